# Optimizing a Trainium2 kernel written in Bass

```python
import jax, jax.numpy as jnp
from jax import lax
import numpy as np

D_MODEL = 1024
BATCH = 16
SEQ = 4096
DEPTH = 1

SSD_EXPAND = 2
SSD_D_INNER = SSD_EXPAND * D_MODEL
SSD_HEAD_DIM = 64
SSD_N_HEADS = SSD_D_INNER // SSD_HEAD_DIM
SSD_N_GROUPS = 8
SSD_HEADS_PER_GROUP = SSD_N_HEADS // SSD_N_GROUPS
SSD_D_STATE = 128
SSD_CONV_WIDTH = 5
SSD_CHUNK = 128
SSD_BC_WIDTH = SSD_N_GROUPS * SSD_D_STATE
SSD_CONV_CH = SSD_D_INNER + 2 * SSD_BC_WIDTH
ATT_HEAD_DIM = 64
ATT_HEADS_PER_GROUP = 8
ATT_PATTERNS = ((128, 1), (512, 4), (2048, 16))
ATT_N_GROUPS = 3
ATT_N_HEADS = ATT_N_GROUPS * ATT_HEADS_PER_GROUP
ATT_WIDTH = ATT_N_HEADS * ATT_HEAD_DIM
ATT_OUT_WIDTH = ATT_HEADS_PER_GROUP * ATT_HEAD_DIM
ATT_BLOCK = 64
ROPE_THETA = 500000.0
ROPE_DIM = ATT_HEAD_DIM // 4
NEG_BIG = -1e30
FFN_HIDDEN = (((8 * D_MODEL + 2) // 3 + 255) // 256) * 256
NORM_EPS = 1e-6
IN_SIZES = (SSD_D_INNER, SSD_CONV_CH, 2 * SSD_N_HEADS, 3 * ATT_WIDTH, 2 * D_MODEL)
IN_COLS = SSD_D_INNER + SSD_CONV_CH + 2 * SSD_N_HEADS + 3 * ATT_WIDTH + 2 * D_MODEL

kernel_name = "hybrid_ssd_dilated_attn_block"


def split_columns(t, sizes):
    outs, start = [], 0
    for s in sizes:
        outs.append(t[..., start:start + s])
        start += s
    return outs


def rms_norm(x, w):
    xf = x.astype(jnp.float32)
    y = xf * lax.rsqrt(jnp.mean(xf * xf, axis=-1, keepdims=True) + NORM_EPS)
    return (y * w.astype(jnp.float32)).astype(x.dtype)


def partial_rotary(x):
    S = x.shape[1]
    pos = jnp.arange(S, dtype=jnp.float32)
    inv_freq = ROPE_THETA ** (-jnp.arange(0, ROPE_DIM, 2, dtype=jnp.float32) / ROPE_DIM)
    ang = pos[:, None] * inv_freq[None, :]
    ang = jnp.concatenate([ang, ang], axis=-1)[None, :, None, :]
    xr, xp = x[..., :ROPE_DIM].astype(jnp.float32), x[..., ROPE_DIM:]
    x1, x2 = xr[..., :ROPE_DIM // 2], xr[..., ROPE_DIM // 2:]
    rot = jnp.concatenate([-x2, x1], axis=-1)
    xr = xr * jnp.cos(ang) + rot * jnp.sin(ang)
    return jnp.concatenate([xr.astype(x.dtype), xp], axis=-1)


def centred_depthwise_conv(x, w, b):
    K, C = w.shape
    y = lax.conv_general_dilated(x, w[:, None, :].astype(x.dtype), window_strides=(1,),
                                 padding=[(K // 2, K // 2)],
                                 dimension_numbers=('NWC', 'WIO', 'NWC'),
                                 feature_group_count=C)
    return y + b.astype(x.dtype)


def ssd_chunked_scan(xh, dt, A, Bm, Cm):
    Bsz, S, G, J, P = xh.shape
    N = Bm.shape[-1]
    nc = S // SSD_CHUNK

    def to_chunks(t):
        return t.reshape((Bsz, nc, SSD_CHUNK) + t.shape[2:]).swapaxes(0, 1)

    dA = dt * A
    xs = (to_chunks(xh), to_chunks(dt), to_chunks(dA), to_chunks(Bm), to_chunks(Cm))
    lower = jnp.tril(jnp.ones((SSD_CHUNK, SSD_CHUNK), dtype=bool))[None, :, :, None, None]
    state0 = jnp.zeros((Bsz, G, J, P, N), jnp.float32)

    def step(state, inp):
        xc, dtc, dAc, Bc, Cc = inp
        a_cum = jnp.cumsum(dAc, axis=1)
        seg = a_cum[:, :, None] - a_cum[:, None, :]
        decay = jnp.exp(jnp.where(lower, seg, -jnp.inf))
        cb = jnp.einsum('btgn,bsgn->btsg', Cc, Bc).astype(jnp.float32)
        y_diag = jnp.einsum('btsg,btsgj,bsgj,bsgjp->btgjp', cb, decay, dtc, xc.astype(jnp.float32))
        y_off = jnp.einsum('btgn,bgjpn,btgj->btgjp', Cc.astype(jnp.float32), state, jnp.exp(a_cum))
        a_last = a_cum[:, -1]
        w_s = jnp.exp(a_last[:, None] - a_cum) * dtc
        new_state = state * jnp.exp(a_last)[..., None, None] + jnp.einsum(
            'bsgn,bsgj,bsgjp->bgjpn', Bc.astype(jnp.float32), w_s, xc.astype(jnp.float32))
        return new_state, y_diag + y_off

    _, ys = lax.scan(step, state0, xs)
    return ys.swapaxes(0, 1).reshape(Bsz, S, G, J, P)


def ssd_mixer(z, xbc, dt_raw, conv_w, conv_b, dt_bias, A_log, D_skip, norm_w):
    Bsz, S, _ = z.shape
    G, J, P, N = SSD_N_GROUPS, SSD_HEADS_PER_GROUP, SSD_HEAD_DIM, SSD_D_STATE
    xbc = jax.nn.silu(centred_depthwise_conv(xbc, conv_w, conv_b))
    xs, Bm, Cm = split_columns(xbc, (SSD_D_INNER, SSD_BC_WIDTH, SSD_BC_WIDTH))
    xh = xs.reshape(Bsz, S, G, J, P)
    Bm = Bm.reshape(Bsz, S, G, N)
    Cm = Cm.reshape(Bsz, S, G, N)
    dt = jax.nn.softplus(dt_raw.reshape(Bsz, S, 2, SSD_N_HEADS).astype(jnp.float32)
                         + dt_bias.astype(jnp.float32))
    A = -jnp.exp(A_log.astype(jnp.float32))
    flip = lambda t: jnp.flip(t, axis=1)
    y_fwd = ssd_chunked_scan(xh, dt[:, :, 0].reshape(Bsz, S, G, J), A[0].reshape(G, J), Bm, Cm)
    y_bwd = flip(ssd_chunked_scan(flip(xh), flip(dt[:, :, 1]).reshape(Bsz, S, G, J),
                                  A[1].reshape(G, J), flip(Bm), flip(Cm)))
    y = y_fwd + y_bwd + D_skip.astype(jnp.float32).reshape(G, J)[..., None] * xh.astype(jnp.float32)
    y = y.reshape(Bsz, S, SSD_D_INNER)
    y = rms_norm(y * jax.nn.silu(z.astype(jnp.float32)), norm_w)
    return y.astype(z.dtype)


def dilated_window_attention(q, k, v, window, dilation):
    Bsz, S, H, Dh = q.shape
    half = window // (2 * dilation)
    n = S // dilation
    nb = -(-n // ATT_BLOCK)
    n_pad = nb * ATT_BLOCK
    kw = ATT_BLOCK + 2 * half
    sub = lambda t: t.reshape(Bsz, n, dilation, H, Dh)
    qs = jnp.pad(sub(q), ((0, 0), (0, n_pad - n), (0, 0), (0, 0), (0, 0)))
    qs = qs.reshape(Bsz, nb, ATT_BLOCK, dilation, H, Dh)
    kv_pad = ((0, 0), (half, half + n_pad - n), (0, 0), (0, 0), (0, 0))
    kp = jnp.pad(sub(k), kv_pad)
    vp = jnp.pad(sub(v), kv_pad)
    idx = jnp.arange(nb)[:, None] * ATT_BLOCK + jnp.arange(kw)[None, :]
    kb = kp[:, idx]
    vb = vp[:, idx]
    s = jnp.einsum('bjqrhd,bjkrhd->bjrhqk', qs, kb).astype(jnp.float32)
    qpos = jnp.arange(nb)[:, None] * ATT_BLOCK + jnp.arange(ATT_BLOCK)[None, :]
    kpos = idx - half
    valid = ((jnp.abs(kpos[:, None, :] - qpos[:, :, None]) <= half)
             & (kpos[:, None, :] >= 0) & (kpos[:, None, :] < n))
    s = jnp.where(valid[None, :, None, None], s, NEG_BIG)
    m = jnp.max(s, axis=-1, keepdims=True)
    p = jnp.exp(s - m)
    l = jnp.sum(p, axis=-1, keepdims=True)
    o = jnp.einsum('bjrhqk,bjkrhd->bjrhqd', p, vb.astype(jnp.float32)) / l
    lse = (m + jnp.log(l))[..., 0]
    o = o.transpose(0, 1, 4, 2, 3, 5).reshape(Bsz, n_pad, dilation, H, Dh)[:, :n]
    lse = lse.transpose(0, 1, 4, 2, 3).reshape(Bsz, n_pad, dilation, H)[:, :n]
    return o.reshape(Bsz, S, H, Dh), lse.reshape(Bsz, S, H)


def dilated_attention_mixer(qkv):
    Bsz, S, _ = qkv.shape
    q, k, v = [t.reshape(Bsz, S, ATT_N_HEADS, ATT_HEAD_DIM)
               for t in split_columns(qkv, (ATT_WIDTH, ATT_WIDTH, ATT_WIDTH))]
    q = partial_rotary(q) * (ATT_HEAD_DIM ** -0.5)
    k = partial_rotary(k)
    outs, lses = [], []
    for g in range(ATT_N_GROUPS):
        window, dilation = ATT_PATTERNS[g]
        sl = slice(g * ATT_HEADS_PER_GROUP, (g + 1) * ATT_HEADS_PER_GROUP)
        o, lse = dilated_window_attention(q[:, :, sl], k[:, :, sl], v[:, :, sl], window, dilation)
        outs.append(o)
        lses.append(lse)
    wts = jax.nn.softmax(jnp.stack(lses, axis=0), axis=0)
    o = jnp.sum(wts[..., None] * jnp.stack(outs, axis=0), axis=0)
    return o.reshape(Bsz, S, ATT_OUT_WIDTH).astype(qkv.dtype)


def setup_inputs(seed: int = 0) -> dict:
    key = jax.random.key(seed)
    ks = jax.random.split(key, 20)
    f32 = jnp.float32
    nrm = lambda k, shape, scale: jax.random.normal(k, shape, f32) * scale
    gain = lambda k, n: 1.0 + 0.02 * jax.random.normal(k, (DEPTH, n), f32)
    dt0 = jnp.exp(jax.random.uniform(ks[5], (DEPTH, 2, SSD_N_HEADS), f32)
                  * (np.log(0.1) - np.log(0.001)) + np.log(0.001))
    dt_bias = dt0 + jnp.log(-jnp.expm1(-dt0))
    A_log = jnp.log(jax.random.uniform(ks[6], (DEPTH, 2, SSD_N_HEADS), f32, 1.0, 16.0))
    return {
        "x": jax.random.normal(ks[0], (BATCH, SEQ, D_MODEL), f32),
        "norm_mix_pre": gain(ks[1], D_MODEL),
        "w_in": nrm(ks[2], (DEPTH, D_MODEL, IN_COLS), D_MODEL ** -0.5),
        "ssd_conv_w": nrm(ks[3], (DEPTH, SSD_CONV_WIDTH, SSD_CONV_CH), SSD_CONV_WIDTH ** -0.5),
        "ssd_conv_b": nrm(ks[4], (DEPTH, SSD_CONV_CH), 0.02),
        "ssd_dt_bias": dt_bias,
        "ssd_A_log": A_log,
        "ssd_D": 1.0 + 0.1 * jax.random.normal(ks[7], (DEPTH, SSD_N_HEADS), f32),
        "ssd_norm_w": gain(ks[8], SSD_D_INNER),
        "w_ssd_branch": nrm(ks[9], (DEPTH, SSD_D_INNER, D_MODEL), SSD_D_INNER ** -0.5),
        "w_attn_branch": nrm(ks[10], (DEPTH, ATT_OUT_WIDTH, D_MODEL), ATT_OUT_WIDTH ** -0.5),
        "w_out": nrm(ks[11], (DEPTH, D_MODEL, D_MODEL), D_MODEL ** -0.5),
        "norm_mix_post": gain(ks[12], D_MODEL),
        "norm_ffn_pre": gain(ks[13], D_MODEL),
        "w_ffn_in": nrm(ks[14], (DEPTH, D_MODEL, 2 * FFN_HIDDEN), D_MODEL ** -0.5),
        "w_ffn_down": nrm(ks[15], (DEPTH, FFN_HIDDEN, D_MODEL), FFN_HIDDEN ** -0.5),
        "norm_ffn_post": gain(ks[16], D_MODEL),
    }


def reference(x, norm_mix_pre, w_in, ssd_conv_w, ssd_conv_b, ssd_dt_bias, ssd_A_log, ssd_D,
              ssd_norm_w, w_ssd_branch, w_attn_branch, w_out, norm_mix_post, norm_ffn_pre,
              w_ffn_in, w_ffn_down, norm_ffn_post):
    for l in range(DEPTH):
        h = rms_norm(x, norm_mix_pre[l])
        proj = jnp.einsum('bsd,dc->bsc', h, w_in[l])
        z, xbc, dt_raw, qkv, gates = split_columns(proj, IN_SIZES)
        y_ssd = ssd_mixer(z, xbc, dt_raw, ssd_conv_w[l], ssd_conv_b[l], ssd_dt_bias[l],
                          ssd_A_log[l], ssd_D[l], ssd_norm_w[l])
        y_att = dilated_attention_mixer(qkv)
        g_ssd, g_att = split_columns(jax.nn.sigmoid(gates), (D_MODEL, D_MODEL))
        merged = (g_ssd * jnp.einsum('bsc,cd->bsd', y_ssd, w_ssd_branch[l])
                  + g_att * jnp.einsum('bsc,cd->bsd', y_att, w_attn_branch[l]))
        mix = jnp.einsum('bsd,de->bse', merged, w_out[l])
        x = x + rms_norm(mix, norm_mix_post[l]).astype(x.dtype)
        h = rms_norm(x, norm_ffn_pre[l])
        gu = jnp.einsum('bsd,df->bsf', h, w_ffn_in[l])
        gt, up = split_columns(gu, (FFN_HIDDEN, FFN_HIDDEN))
        y = jnp.einsum('bsf,fd->bsd', jax.nn.silu(gt) * up, w_ffn_down[l])
        x = x + rms_norm(y, norm_ffn_post[l]).astype(x.dtype)
    return x
```

```python
import numpy as np
from contextlib import ExitStack
import concourse.bass as bass
import concourse.mybir as mybir
from concourse.bass_utils import run_bass_kernel_spmd

F32 = mybir.dt.float32
BF16 = mybir.dt.bfloat16
I32 = mybir.dt.int32
AF = mybir.ActivationFunctionType
ALU = mybir.AluOpType

EPOCH = 20000
D = 1024
DIN = 2048
FFN = 2816
INC = 12864
OFF_Z, OFF_X, OFF_B, OFF_C, OFF_DT, OFF_Q, OFF_K, OFF_V, OFF_G = 0, 2048, 4096, 5120, 6144, 6208, 7744, 9280, 10816
DIL = (1, 4, 16)
EPS = 1e-6
NEGV = -30000.0


class _Rec:
    def __getattr__(self, name):
        return lambda *a, **kw: (name, a, kw)


_REC = _Rec()


class Op:
    __slots__ = ("eng", "fn", "idx", "deps", "signal", "dma_key", "sig_n")

    def __init__(self, eng, fn, dma_key=None):
        self.eng = eng
        self.fn = fn(_REC)
        self.deps = []
        self.signal = False
        self.dma_key = dma_key
        self.sig_n = 0


class Sched:
    ENGS = ("pe", "act", "dve", "pool", "sp")

    def __init__(self, nc):
        self.nc = nc
        self.eng_ops = {e: [] for e in self.ENGS}
        self.last_writer = {}
        self.readers = {}
        self.dma_count = {}
        self.waited = {e: {} for e in self.ENGS}

    def add(self, eng, fn, reads=(), writes=(), dma_key=None):
        op = Op(eng, fn, dma_key)
        op.idx = len(self.eng_ops[eng])
        self.eng_ops[eng].append(op)
        deps = set()
        for r in reads:
            w = self.last_writer.get(r)
            if w is not None:
                deps.add(w)
        for w in writes:
            lw = self.last_writer.get(w)
            if lw is not None:
                deps.add(lw)
            for rd in self.readers.get(w, ()):
                deps.add(rd)
        self._attach(op, deps)
        if dma_key is not None:
            self.dma_count[dma_key] = self.dma_count.get(dma_key, 0) + 1
        for r in reads:
            self.readers.setdefault(r, []).append(op)
        for w in writes:
            self.last_writer[w] = op
            self.readers[w] = []
        return op

    def _attach(self, op, deps):
        eng = op.eng
        best = {}
        for d in deps:
            if d is op:
                continue
            if d.dma_key is not None:
                k = ("dma", d.dma_key)
                v = self.dma_count[d.dma_key]
            else:
                if d.eng == "pe" and eng == "pe" and op.dma_key is None:
                    continue
                k = ("eng", d.eng)
                v = d.idx
            if k not in best or best[k][0] < v:
                best[k] = (v, d)
        wd = self.waited[eng]
        for k, (v, d) in best.items():
            if k in wd and wd[k] >= v:
                continue
            wd[k] = v
            if k[0] == "eng":
                d.signal = True
            op.deps.append((k, v, d))

    def barrier(self):
        lasts = []
        for e in self.ENGS:
            ops = [o for o in self.eng_ops[e] if o.dma_key is None]
            if ops:
                lasts.append(ops[-1])
        dmas = {}
        for e in self.ENGS:
            for o in self.eng_ops[e]:
                if o.dma_key is not None:
                    dmas[o.dma_key] = o
        for e in self.ENGS:
            op = Op(e, lambda eng: eng.nop())
            op.idx = len(self.eng_ops[e])
            self.eng_ops[e].append(op)
            self._attach(op, set(lasts) | set(dmas.values()))
        self.last_writer = {}
        self.readers = {}

    def emit(self, final_waits=()):
        nc = self.nc
        nsig = {}
        for e in self.ENGS:
            n = 0
            for op in self.eng_ops[e]:
                if op.dma_key is None and op.signal:
                    n += 1
                    op.sig_n = n
            nsig[e] = n
        with ExitStack() as st:
            esems = {}
            for e in self.ENGS:
                for ep in range((nsig[e] + EPOCH - 1) // EPOCH):
                    esems[(e, ep)] = st.enter_context(nc.semaphore(f"s_{e}_{ep}"))
            dsems = {}
            for i, k in enumerate(self.dma_count):
                dsems[k] = st.enter_context(nc.semaphore(f"d{i}"))
            block = st.enter_context(nc.Block())

            def run(e, engh):
                for op in self.eng_ops[e]:
                    for (k, v, d) in op.deps:
                        if k[0] == "dma":
                            engh.wait_ge(dsems[k[1]], 16 * v)
                        else:
                            n = d.sig_n
                            engh.wait_ge(esems[(d.eng, (n - 1) // EPOCH)], (n - 1) % EPOCH + 1)
                    name, a_, kw_ = op.fn
                    ins = getattr(engh, name)(*a_, **kw_)
                    if op.dma_key is not None:
                        ins.then_inc(dsems[op.dma_key], 16)
                    elif op.signal:
                        n = op.sig_n
                        ins.then_inc(esems[(e, (n - 1) // EPOCH)], 1)
                if e == "sp":
                    for k in final_waits:
                        engh.wait_ge(dsems[k], 16 * self.dma_count[k])

            @block.tensor
            def _(eng):
                run("pe", eng)

            @block.scalar
            def _(eng):
                run("act", eng)

            @block.vector
            def _(eng):
                run("dve", eng)

            @block.gpsimd
            def _(eng):
                run("pool", eng)

            @block.sync
            def _(eng):
                run("sp", eng)


def build(S, NSEQ, debug=False, stop_after=99, cut=99):
    nc = bass.Bass("TRN2", target_bir_lowering=False)
    NT = S // 128
    TB = S // 512

    def din(name, shape):
        return nc.dram_tensor(name, shape, F32, kind="ExternalInput").ap()

    x_in = din("x", [NSEQ, S, D])
    norm_mix_pre = din("norm_mix_pre", [1, D])
    w_in = din("w_in", [D, INC])
    conv_w = din("ssd_conv_w", [5, 4096])
    conv_b = din("ssd_conv_b", [1, 4096])
    dt_bias = din("ssd_dt_bias", [1, 64])
    A_log = din("ssd_A_log", [1, 64])
    D_skip = din("ssd_D", [1, 32])
    ssd_norm_w = din("ssd_norm_w", [1, DIN])
    w_ssd = din("w_ssd_branch", [DIN, D])
    w_attn = din("w_attn_branch", [512, D])
    w_out = din("w_out", [D, D])
    norm_mix_post = din("norm_mix_post", [1, D])
    norm_ffn_pre = din("norm_ffn_pre", [1, D])
    w_ffn_in = din("w_ffn_in", [D, 2 * FFN])
    w_ffn_down = din("w_ffn_down", [FFN, D])
    norm_ffn_post = din("norm_ffn_post", [1, D])
    rot_cos = din("rot_cos", [128, S])
    rot_sin = din("rot_sin", [128, S])
    cw_l = din("cw_l", [128, 160])
    cb_l = din("cb_l", [128, 32])
    nw_l = din("nw_l", [128, 16])
    y_out = nc.dram_tensor("y", [NSEQ, S, D], F32, kind="ExternalOutput").ap()

    skind = "ExternalOutput" if debug else "Internal"

    def scr(name, shape, dt=BF16):
        return nc.dram_tensor(name, shape, dt, kind=skind).ap()

    scr_z = scr("scr_z", [S, DIN])
    scr_gate = scr("scr_gate", [S, 2 * D])
    scr_x = scr("scr_x", [S, DIN])
    scr_B = scr("scr_B", [S, 1024])
    scr_BT = scr("scr_BT", [1024, S])
    scr_CT = scr("scr_CT", [1024, S])
    scr_dt = scr("scr_dt", [S, 64], F32)
    scr_yf = scr("scr_yf", [S, DIN])
    scr_qT = scr("scr_qT", [1536, S])
    scr_kT = scr("scr_kT", [1536, S])
    scr_v = scr("scr_v", [3, S, 512])
    scr_o = scr("scr_o", [3, S, 8, 65])
    scr_m1 = scr("scr_m1", [S, D])
    wfi_bf = nc.dram_tensor("wfi_bf", [D, 2 * FFN], BF16).ap()
    wfd_bf = nc.dram_tensor("wfd_bf", [FFN, D], BF16).ap()

    S_ = Sched(nc)
    A = S_.add

    with ExitStack() as gst:
        def gsb(name, shape, dt):
            return gst.enter_context(nc.sbuf_tensor(name, shape, dt))

        ident = gsb("ident", [128, 128], BF16)
        cst_f = gsb("cst_f", [128, 4, 128], F32)
        cst_b = gsb("cst_b", [128, 8, 128], BF16)
        band = gsb("band", [128, 3, 128], BF16)
        tmpc = gsb("tmpc", [128, 128], F32)
        smallc = gsb("smallc", [128, 64 + 64 + 32], F32)

        def mk_const(dst_f32_ap, fill_base, selects):
            A("pool", lambda e: e.memset(dst_f32_ap, fill_base), writes=["tmpc"])
            for (pat, cm, base, fill) in selects:
                A("pool", lambda e, pat=pat, cm=cm, base=base, fill=fill: e.affine_select(
                    dst_f32_ap, dst_f32_ap, [[pat, 128]], ALU.is_ge, fill, base=base, channel_multiplier=cm),
                  reads=["tmpc"], writes=["tmpc"])

        def to_bf(dst_ap, src_ap, scale=None):
            if scale is None:
                A("dve", lambda e: e.tensor_copy(dst_ap, src_ap), reads=["tmpc"], writes=["consts"])
            else:
                A("dve", lambda e: e.tensor_scalar(dst_ap, src_ap, scale, None, ALU.mult), reads=["tmpc"], writes=["consts"])

        mk_const(tmpc[:], 1.0, [(1, -1, 0, 0.0), (-1, 1, 0, 0.0)])
        to_bf(ident[:], tmpc[:])
        A("dve", lambda e: e.tensor_copy(cst_f[:, 3, :], tmpc[:]), reads=["tmpc"], writes=["consts"])
        mk_const(tmpc[:], 1.0, [(1, -1, 0, 0.0)])
        to_bf(cst_b[:, 0, :], tmpc[:])
        to_bf(cst_b[:, 1, :], tmpc[:], -1.0)
        to_bf(cst_b[:, 6, :], tmpc[:])
        A("dve", lambda e: e.tensor_copy(cst_f[:, 0, :], tmpc[:]), reads=["tmpc"], writes=["consts"])
        mk_const(tmpc[:], 1.0, [(-1, 1, 0, 0.0)])
        to_bf(cst_b[:, 2, :], tmpc[:])
        to_bf(cst_b[:, 3, :], tmpc[:], -1.0)
        to_bf(cst_b[:, 7, :], tmpc[:])
        A("dve", lambda e: e.tensor_copy(cst_f[:, 1, :], tmpc[:]), reads=["tmpc"], writes=["consts"])
        mk_const(tmpc[:], 0.0, [(1, -1, 0, NEGV)])
        to_bf(cst_b[:, 4, :], tmpc[:])
        mk_const(tmpc[:], 0.0, [(-1, 1, 0, NEGV)])
        to_bf(cst_b[:, 5, :], tmpc[:])
        A("dve", lambda e: e.memset(cst_f[:, 2, :], 1.0), writes=["consts"])
        for oi, o in enumerate((-1, 0, 1)):
            mk_const(tmpc[:], 0.0, [(-1, 1, 128 * o + 64, NEGV), (1, -1, 64 - 128 * o, NEGV)])
            to_bf(band[:, oi, :], tmpc[:])
        A("sp", lambda e: e.dma_start(out=smallc[:, 0:64], in_=dt_bias.partition_broadcast(128)), writes=["smallc"], dma_key="c0")
        A("sp", lambda e: e.dma_start(out=smallc[:, 64:128], in_=A_log.partition_broadcast(128)), writes=["smallc"], dma_key="c0")
        A("sp", lambda e: e.dma_start(out=smallc[:, 128:160], in_=D_skip.partition_broadcast(128)), writes=["smallc"], dma_key="c0")
        A("act", lambda e: e.activation(smallc[:, 64:128], smallc[:, 64:128], AF.Exp), reads=["smallc"], writes=["smallc"])
        A("dve", lambda e: e.tensor_scalar(smallc[:, 64:128], smallc[:, 64:128], -1.0, None, ALU.mult), reads=["smallc"], writes=["smallc"])
        for seq in range(NSEQ):
            with ExitStack() as st:
                def sb(name, shape, dt):
                    return st.enter_context(nc.sbuf_tensor(f"{name}_{seq}", shape, dt))

                def ps(name, shape, dt):
                    return st.enter_context(nc.psum_tensor(f"{name}_{seq}", shape, dt))

                hT = sb("hT", [128, 8, S], BF16)
                gpre = sb("gpre", [128, D], F32)
                xt = [sb(f"xt{i}", [128, D], F32) for i in range(2)]
                hb = [sb(f"hb{i}", [128, D], BF16) for i in range(2)]
                junk = sb("junk", [128, D], F32)
                st8 = sb("st8", [128, 8], F32)
                pT = [ps(f"pT{i}", [128, 1024], BF16) for i in range(2)]
                pA = [ps(f"pA{i}", [128, 512], F32) for i in range(2)]
                pB = [ps(f"pB{i}", [128, 512], F32) for i in range(2)]

                A("sp", lambda e: e.dma_start(out=gpre[:], in_=norm_mix_pre.partition_broadcast(128)), writes=["gpre"], dma_key="gpre")
                for t in range(NT):
                    b = t % 2
                    A("sp", lambda e, t=t, b=b: e.dma_start(out=xt[b][:], in_=x_in[seq, t * 128:(t + 1) * 128, :]),
                      writes=[f"xt{b}"], dma_key=f"xt{b}")
                    A("act", lambda e, b=b: e.activation(junk[:], xt[b][:], AF.Square, accum_out=st8[:, b:b + 1]),
                      reads=[f"xt{b}"], writes=["junk", f"ss{b}"])
                    A("act", lambda e, b=b: e.activation(st8[:, 2 + b:3 + b], st8[:, b:b + 1], AF.Sqrt, bias=EPS, scale=1.0 / D),
                      reads=[f"ss{b}"], writes=[f"sd{b}"])
                    A("dve", lambda e, b=b: e.reciprocal(st8[:, 4 + b:5 + b], st8[:, 2 + b:3 + b]), reads=[f"sd{b}"], writes=[f"rs{b}"])
                    A("dve", lambda e, b=b: e.scalar_tensor_tensor(hb[b][:], xt[b][:], st8[:, 4 + b:5 + b], gpre[:], ALU.mult, ALU.mult),
                      reads=[f"xt{b}", f"rs{b}", "gpre"], writes=[f"hb{b}"])
                    for k in range(8):
                        A("pe", lambda e, b=b, k=k: e.transpose(pT[b][:, k * 128:(k + 1) * 128], hb[b][:, k * 128:(k + 1) * 128], ident[:]),
                          reads=[f"hb{b}", "consts"], writes=[f"pT{b}"])
                    A("act", lambda e, b=b, t=t: e.copy(hT[:, :, t * 128:(t + 1) * 128], pT[b][:].rearrange("p (k t) -> p k t", k=8)),
                      reads=[f"pT{b}"], writes=[("hT", t)])
                hT_all = [("hT", t) for t in range(NT)]
                if cut <= 1:
                    S_.barrier(); continue

                wsl = [sb(f"wsl{i}", [128, 8, 512], BF16) for i in range(2)]
                wp = sb("wp", [128, 8, 512], BF16)
                stg = [sb(f"stg{i}", [128, 512], BF16) for i in range(3)]
                stgf = [sb(f"stgf{i}", [128, 64], F32) for i in range(2)]
                outX = [sb(f"outX{i}", [128, S], BF16) for i in range(4)]
                rawT = [sb(f"rawT{i}", [128, S + 4], BF16) for i in range(2)]
                tmp1 = [sb(f"tmp1{i}", [128, 512], F32) for i in range(2)]
                tmp2 = [sb(f"tmp2{i}", [128, 512], F32) for i in range(2)]
                cosT = sb("cosT", [128, S], BF16)
                sinT = sb("sinT", [128, S], BF16)
                cw = sb("cw", [128, 32, 5], F32)
                cb = sb("cb", [128, 32], F32)
                dg = sb("dg", [128, 5, 128], BF16)
                wcnt = [0]
                scnt = [0]

                def load_w(lo, ncols):
                    i = wcnt[0] % 2
                    wcnt[0] += 1
                    A("pool", lambda e: e.dma_start(out=wsl[i][:, :, 0:ncols],
                                                    in_=w_in[:, lo:lo + ncols].rearrange("(k p) c -> p k c", p=128)),
                      writes=[f"wsl{i}"], dma_key=f"wsl{i}")
                    return i

                def stage():
                    i = scnt[0] % 3
                    scnt[0] += 1
                    return i

                A("pool", lambda e: e.memset(wp[:], 0.0), writes=["wp"])
                for i in range(2):
                    A("pool", lambda e, i=i: e.memset(rawT[i][:, 0:2], 0.0), writes=[f"rawT{i}"])
                    A("pool", lambda e, i=i: e.memset(rawT[i][:, S + 2:S + 4], 0.0), writes=[f"rawT{i}"])
                A("sp", lambda e: e.dma_start(out=cw[:].rearrange("p f k -> p (f k)"), in_=cw_l), writes=["cw"], dma_key="cw")
                A("sp", lambda e: e.dma_start(out=cb[:], in_=cb_l), writes=["cb"], dma_key="cw")
                A("pool", lambda e: e.dma_start(out=cosT[:], in_=rot_cos), writes=["cosT"], dma_key="rot")
                A("pool", lambda e: e.dma_start(out=sinT[:], in_=rot_sin), writes=["sinT"], dma_key="rot")
                if cut <= 2:
                    S_.barrier(); continue
                def tok_block(lo, ncols, func, dst, dst_lo, dkey):
                    wi = load_w(lo, ncols)
                    for t in range(NT):
                        b = t % 2
                        for k in range(8):
                            A("pe", lambda e, t=t, k=k, b=b: e.matmul(pA[b][:, 0:ncols], hT[:, k, t * 128:(t + 1) * 128], wsl[wi][:, k, 0:ncols],
                                                                      start=(k == 0), stop=(k == 7)),
                              reads=[("hT", t), f"wsl{wi}"], writes=[f"pA{b}"])
                        si = stage()
                        A("act", lambda e, b=b, si=si: e.activation(stg[si][:, 0:ncols], pA[b][:, 0:ncols], func),
                          reads=[f"pA{b}"], writes=[f"stg{si}"])
                        A("sp", lambda e, t=t, si=si: e.dma_start(out=dst[t * 128:(t + 1) * 128, dst_lo:dst_lo + ncols], in_=stg[si][:, 0:ncols]),
                          reads=[f"stg{si}"], writes=[(dkey, t)], dma_key=f"stg{si}")

                for blk in range(4):
                    tok_block(OFF_Z + blk * 512, 512, AF.Silu, scr_z, blk * 512, "scr_z")
                for blk in range(4):
                    tok_block(OFF_G + blk * 512, 512, AF.Sigmoid, scr_gate, blk * 512, "scr_gate")
                if cut <= 3:
                    S_.barrier(); continue
                wi = load_w(OFF_DT, 64)
                for t in range(NT):
                    b = t % 2
                    for k in range(8):
                        A("pe", lambda e, t=t, k=k, b=b: e.matmul(pA[b][:, 0:64], hT[:, k, t * 128:(t + 1) * 128], wsl[wi][:, k, 0:64],
                                                                  start=(k == 0), stop=(k == 7)),
                          reads=[("hT", t), f"wsl{wi}"], writes=[f"pA{b}"])
                    A("act", lambda e, b=b: e.copy(stgf[b][:], pA[b][:, 0:64]), reads=[f"pA{b}"], writes=[f"stgf{b}"])
                    A("sp", lambda e, t=t, b=b: e.dma_start(out=scr_dt[t * 128:(t + 1) * 128, :], in_=stgf[b][:]),
                      reads=[f"stgf{b}"], writes=[("scr_dt", t)], dma_key=f"stgf{b}")
                if cut <= 4:
                    S_.barrier(); continue
                for g in range(3):
                    d = DIL[g]
                    n = S // d
                    wi = load_w(OFF_V + g * 512, 512)
                    cnt = 0
                    for r in range(d):
                        for i in range(n // 128):
                            b = cnt % 2
                            cnt += 1
                            base = i * 128 * d + r
                            tl = sorted(set((base + j * d) // 128 for j in (0, 127)))
                            tl = list(range(tl[0], tl[-1] + 1))
                            for k in range(8):
                                A("pe", lambda e, k=k, b=b, base=base, d=d: e.matmul(
                                    pA[b][:], hT[:, k, base:base + 127 * d + 1:d], wsl[wi][:, k, :], start=(k == 0), stop=(k == 7)),
                                  reads=[("hT", tt) for tt in tl] + [f"wsl{wi}"], writes=[f"pA{b}"])
                            si = stage()
                            A("act", lambda e, b=b, si=si: e.copy(stg[si][:], pA[b][:]), reads=[f"pA{b}"], writes=[f"stg{si}"])
                            row = r * n + i * 128
                            A("sp", lambda e, si=si, row=row, g=g: e.dma_start(out=scr_v[g, row:row + 128, :], in_=stg[si][:]),
                              reads=[f"stg{si}"], writes=[("scr_v", g)], dma_key=f"stg{si}")
                if cut <= 5:
                    S_.barrier(); continue
                for qk in range(2):
                    off = OFF_Q if qk == 0 else OFF_K
                    dstT = scr_qT if qk == 0 else scr_kT
                    for blk in range(3):
                        g = blk
                        d = DIL[g]
                        n = S // d
                        wi = load_w(off + blk * 512, 512)
                        if qk == 0:
                            A("dve", lambda e, wi=wi: e.tensor_scalar(wsl[wi][:], wsl[wi][:], 0.125, None, ALU.mult),
                              reads=[f"wsl{wi}"], writes=[f"wsl{wi}"])
                        wv = wsl[wi][:].rearrange("p k (h c) -> p k h c", h=8)
                        wpv = wp[:].rearrange("p k (h c) -> p k h c", h=8)
                        A("dve", lambda e, wv=wv, wpv=wpv: e.tensor_copy(wpv[:, :, :, 0:8], wv[:, :, :, 8:16]), reads=[f"wsl{wi}"], writes=["wp"])
                        A("dve", lambda e, wv=wv, wpv=wpv: e.tensor_copy(wpv[:, :, :, 8:16], wv[:, :, :, 0:8]), reads=[f"wsl{wi}"], writes=["wp"])
                        for hp in range(4):
                            ox = outX[hp]
                            for tb in range(TB):
                                b = tb % 2
                                for k in range(8):
                                    A("pe", lambda e, k=k, b=b, tb=tb, hp=hp, wi=wi: e.matmul(
                                        pA[b][:], wsl[wi][:, k, hp * 128:(hp + 1) * 128], hT[:, k, tb * 512:(tb + 1) * 512],
                                        start=(k == 0), stop=(k == 7)),
                                      reads=hT_all[tb * 4:(tb + 1) * 4] + [f"wsl{wi}"], writes=[f"pA{b}"])
                                for k in range(8):
                                    A("pe", lambda e, k=k, b=b, tb=tb, hp=hp: e.matmul(
                                        pB[b][:], wp[:, k, hp * 128:(hp + 1) * 128], hT[:, k, tb * 512:(tb + 1) * 512],
                                        start=(k == 0), stop=(k == 7)),
                                      reads=hT_all[tb * 4:(tb + 1) * 4] + ["wp"], writes=[f"pB{b}"])
                                sl = slice(tb * 512, (tb + 1) * 512)
                                A("dve", lambda e, b=b, sl=sl: e.tensor_tensor(tmp1[b][:], pB[b][:], sinT[:, sl], ALU.mult),
                                  reads=[f"pB{b}", "sinT"], writes=[f"tmp1{b}"])
                                A("dve", lambda e, b=b, sl=sl: e.tensor_tensor(tmp2[b][:], pA[b][:], cosT[:, sl], ALU.mult),
                                  reads=[f"pA{b}", "cosT"], writes=[f"tmp2{b}"])
                                i0 = tb * 512 // d
                                ni = 512 // d
                                oap = ox[:].rearrange("p (r i) -> p r i", r=d)[:, :, i0:i0 + ni]
                                A("pool", lambda e, b=b, oap=oap, d=d: e.tensor_tensor(
                                    oap, tmp1[b][:].rearrange("p (i r) -> p r i", r=d), tmp2[b][:].rearrange("p (i r) -> p r i", r=d), ALU.add),
                                  reads=[f"tmp1{b}", f"tmp2{b}"], writes=[f"outX{hp}"])
                            row = blk * 512 + hp * 128
                            A("sp", lambda e, hp=hp, row=row, dstT=dstT: e.dma_start(out=dstT[row:row + 128, :], in_=outX[hp][:]),
                              reads=[f"outX{hp}"], writes=[("scr_qk", qk, blk, hp)], dma_key=f"outX{hp}")
                if cut <= 6:
                    S_.barrier(); continue
                for grp in range(8):
                    wi = load_w(OFF_X + grp * 512, 512)
                    for j in range(4):
                        ft = grp * 4 + j
                        rb = ft % 2
                        for k5 in range(5):
                            A("dve", lambda e, k5=k5, ft=ft: e.tensor_scalar(dg[:, k5, :], ident[:], cw[:, ft, k5:k5 + 1], None, ALU.mult),
                              reads=["consts", "cw"], writes=["dg"])
                        for tb in range(TB):
                            b = tb % 2
                            for k in range(8):
                                A("pe", lambda e, k=k, b=b, tb=tb, j=j, wi=wi: e.matmul(
                                    pA[b][:], wsl[wi][:, k, j * 128:(j + 1) * 128], hT[:, k, tb * 512:(tb + 1) * 512],
                                    start=(k == 0), stop=(k == 7)),
                                  reads=hT_all[tb * 4:(tb + 1) * 4] + [f"wsl{wi}"], writes=[f"pA{b}"])
                            A("dve", lambda e, b=b, tb=tb, rb=rb: e.tensor_copy(rawT[rb][:, 2 + tb * 512:2 + (tb + 1) * 512], pA[b][:]),
                              reads=[f"pA{b}"], writes=[f"rawT{rb}"])
                        for tb in range(TB):
                            b = tb % 2
                            for k5 in range(5):
                                A("pe", lambda e, k5=k5, b=b, tb=tb, rb=rb: e.matmul(
                                    pB[b][:], dg[:, k5, :], rawT[rb][:, tb * 512 + k5:tb * 512 + k5 + 512], start=(k5 == 0), stop=(k5 == 4)),
                                  reads=["dg", f"rawT{rb}"], writes=[f"pB{b}"])
                            A("act", lambda e, b=b, tb=tb, j=j, ft=ft: e.activation(
                                outX[j][:, tb * 512:(tb + 1) * 512], pB[b][:], AF.Silu, bias=cb[:, ft:ft + 1]),
                              reads=[f"pB{b}", "cb"], writes=[f"outX{j}"])
                    if grp < 6:
                        dst, clo, dkey = (scr_x, grp * 512, "scr_x") if grp < 4 else (scr_B, (grp - 4) * 512, "scr_B")
                        for t in range(NT):
                            b = t % 2
                            for j in range(4):
                                A("pe", lambda e, b=b, j=j, t=t: e.transpose(pT[b][:, j * 128:(j + 1) * 128], outX[j][:, t * 128:(t + 1) * 128], ident[:]),
                                  reads=[f"outX{j}", "consts"], writes=[f"pT{b}"])
                            si = stage()
                            A("dve", lambda e, b=b, si=si: e.tensor_copy(stg[si][:], pT[b][:, 0:512]), reads=[f"pT{b}"], writes=[f"stg{si}"])
                            A("sp", lambda e, t=t, si=si, dst=dst, clo=clo: e.dma_start(out=dst[t * 128:(t + 1) * 128, clo:clo + 512], in_=stg[si][:]),
                              reads=[f"stg{si}"], writes=[(dkey, t)], dma_key=f"stg{si}")
                    if grp >= 4:
                        dstT = scr_BT if grp < 6 else scr_CT
                        rlo = (grp - 4) * 512 if grp < 6 else (grp - 6) * 512
                        for j in range(4):
                            A("sp", lambda e, j=j, dstT=dstT, rlo=rlo: e.dma_start(out=dstT[rlo + j * 128:rlo + (j + 1) * 128, :], in_=outX[j][:]),
                              reads=[f"outX{j}"], writes=[("scr_BCT", grp, j)], dma_key=f"outX{j}")
            S_.barrier()
            if stop_after <= 2:
                continue
            with ExitStack() as st:
                def sb(name, shape, dt):
                    return st.enter_context(nc.sbuf_tensor(f"{name}_s{seq}", shape, dt))

                def ps(name, shape, dt):
                    return st.enter_context(nc.psum_tensor(f"{name}_s{seq}", shape, dt))

                wssd = sb("wssd", [128, 16, 1024], BF16)
                nwl = sb("nwl", [128, 16], F32)
                xc = [sb(f"xc{i}", [128, 2048], BF16) for i in range(2)]
                Bc = [sb(f"Bc{i}", [128, 1024], BF16) for i in range(2)]
                BTc = [sb(f"BTc{i}", [128, 8, 128], BF16) for i in range(2)]
                CTc = [sb(f"CTc{i}", [128, 8, 128], BF16) for i in range(2)]
                dtc = [sb(f"dtc{i}", [128, 64], F32) for i in range(2)]
                zc = [sb(f"zc{i}", [128, 2048], BF16) for i in range(2)]
                yfc = [sb(f"yfc{i}", [128, 2048], BF16) for i in range(2)]
                gc = [sb(f"gc{i}", [128, 1024], BF16) for i in range(2)]
                ST = sb("ST", [128, 2048], F32)
                STb = sb("STb", [128, 2048], BF16)
                sm = sb("sm", [128, 128], F32)
                sm2 = sb("sm2", [128, 128], F32)
                dAhi = sb("dAhi", [128, 32], BF16)
                dAlo = sb("dAlo", [128, 32], BF16)
                CBm = sb("CBm", [128, 128], BF16)
                Eb = sb("Eb", [128, 512], BF16)
                Mb = sb("Mb", [128, 4, 128], BF16)
                xw = sb("xw", [128, 256], BF16)
                tq = sb("tq", [128, 256], F32)
                tq2 = sb("tq2", [128, 256], F32)
                Ych = sb("Ych", [128, 2048], F32)
                Yst = [sb(f"Yst{i}", [128, 2048], BF16) for i in range(2)]
                Gb = sb("Gb", [128, 2048], BF16)
                GT = sb("GT", [128, 16, 128], BF16)
                junk2 = sb("junk2", [128, 2048], BF16)
                r8 = sb("r8", [128, 4], F32)
                m1s = [sb(f"m1s{i}", [128, 1024], BF16) for i in range(2)]
                pCB = ps("pCB", [128, 512], F32)
                pSeg = ps("pSeg", [128, 512], F32)
                pYY = ps("pYY", [128, 512], F32)
                pS = ps("pS", [128, 512], F32)
                pT2 = ps("pT2", [128, 1024], BF16)
                pO = ps("pO", [128, 1024], F32)

                for k in range(2):
                    A("pool", lambda e, k=k: e.dma_start(out=wssd[:, k * 8:(k + 1) * 8, :],
                                                         in_=w_ssd[k * 1024:(k + 1) * 1024, :].rearrange("(k p) c -> p k c", p=128)),
                      writes=["wssd"], dma_key="wssd")
                A("sp", lambda e: e.dma_start(out=nwl[:], in_=nw_l), writes=["nwl"], dma_key="nwl")
                for k in range(16):
                    A("dve", lambda e, k=k: e.tensor_scalar(wssd[:, k, :], wssd[:, k, :], nwl[:, k:k + 1], None, ALU.mult),
                      reads=["wssd", "nwl"], writes=["wssd"])
                cnt = 0
                for dr in range(2):
                    A("dve", lambda e: e.memset(ST[:], 0.0), writes=[("ST", g_) for g_ in range(8)])
                    A("dve", lambda e: e.memset(STb[:], 0.0), writes=[("STb", g_) for g_ in range(8)])
                    tri_f = cst_f[:, dr, :]
                    tri_b = cst_b[:, 0 + 2 * dr, :]
                    ntri_b = cst_b[:, 1 + 2 * dr, :]
                    neg_b = cst_b[:, 4 + dr, :]
                    msk_b = cst_b[:, 6 + dr, :]
                    for ci in range(NT):
                        c = ci if dr == 0 else NT - 1 - ci
                        b = cnt % 2
                        cnt += 1
                        rows = slice(c * 128, (c + 1) * 128)
                        A("sp", lambda e, b=b, rows=rows: e.dma_start(out=xc[b][:], in_=scr_x[rows, :]), reads=[("scr_x", c)], writes=[f"xc{b}"], dma_key=f"xc{b}")
                        A("sp", lambda e, b=b, rows=rows: e.dma_start(out=Bc[b][:], in_=scr_B[rows, :]), reads=[("scr_B", c)], writes=[f"Bc{b}"], dma_key=f"Bc{b}")
                        A("sp", lambda e, b=b, rows=rows: e.dma_start(out=BTc[b][:], in_=scr_BT[:, rows].rearrange("(g n) t -> n g t", n=128)),
                          reads=[("scr_BCT", gg, jj) for gg in (4, 5) for jj in range(4)], writes=[f"BTc{b}"], dma_key=f"BTc{b}")
                        A("sp", lambda e, b=b, rows=rows: e.dma_start(out=CTc[b][:], in_=scr_CT[:, rows].rearrange("(g n) t -> n g t", n=128)),
                          reads=[("scr_BCT", gg, jj) for gg in (6, 7) for jj in range(4)], writes=[f"CTc{b}"], dma_key=f"CTc{b}")
                        A("sp", lambda e, b=b, rows=rows: e.dma_start(out=dtc[b][:], in_=scr_dt[rows, :]), reads=[("scr_dt", c)], writes=[f"dtc{b}"], dma_key=f"dtc{b}")
                        if dr == 1:
                            A("sp", lambda e, b=b, rows=rows: e.dma_start(out=zc[b][:], in_=scr_z[rows, :]), reads=[("scr_z", c)], writes=[f"zc{b}"], dma_key=f"zc{b}")
                            A("sp", lambda e, b=b, rows=rows: e.dma_start(out=yfc[b][:], in_=scr_yf[rows, :]), reads=[("scr_yf", c)], writes=[f"yfc{b}"], dma_key=f"yfc{b}")
                            A("sp", lambda e, b=b, rows=rows: e.dma_start(out=gc[b][:], in_=scr_gate[rows, 0:1024]), reads=[("scr_gate", c)], writes=[f"gc{b}"], dma_key=f"gc{b}")
                        o32 = dr * 32
                        A("dve", lambda e, b=b, o32=o32: e.tensor_tensor(sm[:, 0:32], dtc[b][:, o32:o32 + 32], smallc[:, o32:o32 + 32], ALU.add),
                          reads=[f"dtc{b}", "smallc"], writes=["sm0"])
                        A("act", lambda e: e.activation(sm[:, 32:64], sm[:, 0:32], AF.Exp), reads=["sm0"], writes=["sm1"])
                        A("act", lambda e: e.activation(sm[:, 64:96], sm[:, 32:64], AF.Ln, bias=1.0), reads=["sm1"], writes=["dt"])
                        A("dve", lambda e, o32=o32: e.tensor_tensor(sm[:, 96:128], sm[:, 64:96], smallc[:, 64 + o32:96 + o32], ALU.mult),
                          reads=["dt", "smallc"], writes=["dA"])
                        A("dve", lambda e: e.tensor_copy(dAhi[:], sm[:, 96:128]), reads=["dA"], writes=["dAhi"])
                        A("dve", lambda e: e.tensor_tensor(dAlo[:], sm[:, 96:128], dAhi[:], ALU.subtract), reads=["dA", "dAhi"], writes=["dAlo"])
                        A("pe", lambda e, tri_f=tri_f: e.matmul(pS[:, 256:288], tri_f, sm[:, 96:128], start=True, stop=True), reads=["dA", "consts"], writes=["pSm"])
                        A("pe", lambda e: e.matmul(pS[:, 288:320], cst_f[:, 2, :], sm[:, 96:128], start=True, stop=True), reads=["dA", "consts"], writes=["pSm"])
                        A("act", lambda e: e.copy(sm2[:, 0:32], pS[:, 256:288]), reads=["pSm"], writes=["asb"])
                        A("act", lambda e: e.activation(sm2[:, 32:64], pS[:, 256:288], AF.Exp), reads=["pSm"], writes=["ea"])
                        A("act", lambda e: e.activation(sm2[:, 64:96], pS[:, 288:320], AF.Exp), reads=["pSm"], writes=["eal"])
                        A("dve", lambda e: e.tensor_tensor(sm2[:, 96:128], pS[:, 288:320], sm2[:, 0:32], ALU.subtract), reads=["pSm", "asb"], writes=["wv"])
                        A("act", lambda e: e.activation(sm2[:, 96:128], sm2[:, 96:128], AF.Exp), reads=["wv"], writes=["wv"])
                        A("dve", lambda e: e.tensor_tensor(sm2[:, 96:128], sm2[:, 96:128], sm[:, 64:96], ALU.mult), reads=["wv", "dt"], writes=["wv"])
                        for g in range(8):
                            A("pe", lambda e, b=b, g=g: e.matmul(pCB[:, 0:128], BTc[b][:, g, :], CTc[b][:, g, :], start=True, stop=True),
                              reads=[f"BTc{b}", f"CTc{b}"], writes=["pCB"])
                            A("dve", lambda e, msk_b=msk_b: e.tensor_tensor(CBm[:], pCB[:, 0:128], msk_b, ALU.mult), reads=["pCB", "consts"], writes=["CBm"])
                            for j in range(4):
                                h = g * 4 + j
                                osl = pSeg[:, j * 128:(j + 1) * 128]
                                hi = dAhi[:, h:h + 1].to_broadcast([128, 128])
                                lo = dAlo[:, h:h + 1].to_broadcast([128, 128])
                                A("pe", lambda e, osl=osl, hi=hi, tri_b=tri_b: e.matmul(osl, hi, tri_b, start=True, stop=False), reads=["dAhi", "consts"], writes=["pSeg"])
                                A("pe", lambda e, osl=osl, lo=lo, tri_b=tri_b: e.matmul(osl, lo, tri_b, start=False, stop=False), reads=["dAlo", "consts"], writes=["pSeg"])
                                A("pe", lambda e, osl=osl, hi=hi, ntri_b=ntri_b: e.matmul(osl, ntri_b, hi, start=False, stop=False), reads=["dAhi", "consts"], writes=["pSeg"])
                                A("pe", lambda e, osl=osl, lo=lo, ntri_b=ntri_b: e.matmul(osl, ntri_b, lo, start=False, stop=False), reads=["dAlo", "consts"], writes=["pSeg"])
                                A("pe", lambda e, osl=osl, neg_b=neg_b: e.matmul(osl, ident[:], neg_b, start=False, stop=True), reads=["consts"], writes=["pSeg"])
                            A("act", lambda e: e.activation(Eb[:], pSeg[:], AF.Exp), reads=["pSeg"], writes=["Eb"])
                            for j in range(4):
                                h = g * 4 + j
                                A("dve", lambda e, j=j, h=h: e.scalar_tensor_tensor(
                                    Mb[:, j, :], Eb[:, j * 128:(j + 1) * 128], sm[:, 64 + h:65 + h], CBm[:], ALU.mult, ALU.mult),
                                  reads=["Eb", "dt", "CBm"], writes=[("Mb", j)])
                            for j in range(4):
                                h = g * 4 + j
                                A("pe", lambda e, j=j, h=h, b=b: e.matmul(pYY[:, j * 64:(j + 1) * 64], Mb[:, j, :], xc[b][:, h * 64:(h + 1) * 64], start=True, stop=True),
                                  reads=[("Mb", j), f"xc{b}"], writes=["pY"])
                            A("pe", lambda e, g=g, b=b: e.matmul(pYY[:, 256:512], CTc[b][:, g, :], STb[:, g * 256:(g + 1) * 256], start=True, stop=True),
                              reads=[f"CTc{b}", ("STb", g)], writes=["pYo"])
                            eab = sm2[:, 32 + g * 4:36 + g * 4].unsqueeze(2).to_broadcast([128, 4, 64])
                            A("dve", lambda e, eab=eab: e.tensor_tensor(tq[:].rearrange("p (j q) -> p j q", j=4), pYY[:, 256:512].rearrange("p (j q) -> p j q", j=4), eab, ALU.mult),
                              reads=["pYo", "ea"], writes=["tq"])
                            A("dve", lambda e, g=g: e.tensor_tensor(Ych[:, g * 256:(g + 1) * 256], pYY[:, 0:256], tq[:], ALU.add),
                              reads=["pY", "tq"], writes=[("Ych", g)])
                            wb = sm2[:, 96 + g * 4:100 + g * 4].unsqueeze(2).to_broadcast([128, 4, 64])
                            A("pool", lambda e, g=g, b=b, wb=wb: e.tensor_tensor(xw[:].rearrange("p (j q) -> p j q", j=4),
                                                                                xc[b][:, g * 256:(g + 1) * 256].rearrange("p (j q) -> p j q", j=4), wb, ALU.mult),
                              reads=[f"xc{b}", "wv"], writes=["xw"])
                            A("pe", lambda e, g=g, b=b: e.matmul(pS[:, 0:256], Bc[b][:, g * 128:(g + 1) * 128], xw[:], start=True, stop=True),
                              reads=[f"Bc{b}", "xw"], writes=["pS"])
                            elb = sm2[:, 64 + g * 4:68 + g * 4].unsqueeze(2).to_broadcast([128, 4, 64])
                            A("pool", lambda e, g=g, elb=elb: e.tensor_tensor(tq2[:].rearrange("p (j q) -> p j q", j=4),
                                                                             ST[:, g * 256:(g + 1) * 256].rearrange("p (j q) -> p j q", j=4), elb, ALU.mult),
                              reads=[("ST", g), "eal"], writes=["tq2"])
                            A("dve", lambda e, g=g: e.tensor_tensor(ST[:, g * 256:(g + 1) * 256], pS[:, 0:256], tq2[:], ALU.add),
                              reads=["pS", "tq2"], writes=[("ST", g)])
                            A("act", lambda e, g=g: e.copy(STb[:, g * 256:(g + 1) * 256], ST[:, g * 256:(g + 1) * 256]), reads=[("ST", g)], writes=[("STb", g)])
                        Yall = [("Ych", g) for g in range(8)]
                        if dr == 0:
                            A("act", lambda e, b=b: e.copy(Yst[b][:], Ych[:]), reads=Yall, writes=[f"Yst{b}"])
                            A("sp", lambda e, b=b, rows=rows: e.dma_start(out=scr_yf[rows, :], in_=Yst[b][:]), reads=[f"Yst{b}"], writes=[("scr_yf", c)], dma_key=f"Yst{b}")
                            continue
                        A("dve", lambda e, b=b: e.tensor_tensor(Ych[:], Ych[:], yfc[b][:], ALU.add), reads=Yall + [f"yfc{b}"], writes=Yall)
                        Db = smallc[:, 128:160].unsqueeze(2).to_broadcast([128, 32, 64])
                        A("pool", lambda e, b=b, Db=Db: e.tensor_tensor(Yst[0][:].rearrange("p (h q) -> p h q", h=32), xc[b][:].rearrange("p (h q) -> p h q", h=32), Db, ALU.mult),
                          reads=[f"xc{b}", "smallc"], writes=["Yst0"])
                        A("dve", lambda e: e.tensor_tensor(Ych[:], Ych[:], Yst[0][:], ALU.add), reads=Yall + ["Yst0"], writes=Yall)
                        A("dve", lambda e, b=b: e.tensor_tensor(Gb[:], Ych[:], zc[b][:], ALU.mult), reads=Yall + [f"zc{b}"], writes=["Gb"])
                        A("act", lambda e: e.activation(junk2[:], Gb[:], AF.Square, accum_out=r8[:, 0:1]), reads=["Gb"], writes=["junk2", "r0"])
                        A("act", lambda e: e.activation(r8[:, 1:2], r8[:, 0:1], AF.Sqrt, bias=EPS, scale=1.0 / DIN), reads=["r0"], writes=["r1"])
                        A("dve", lambda e: e.reciprocal(r8[:, 2:3], r8[:, 1:2]), reads=["r1"], writes=["r2"])
                        for q4 in range(2):
                            for kk in range(8):
                                k = q4 * 8 + kk
                                A("pe", lambda e, k=k, kk=kk: e.transpose(pT2[:, kk * 128:(kk + 1) * 128], Gb[:, k * 128:(k + 1) * 128], ident[:]),
                                  reads=["Gb", "consts"], writes=["pT2"])
                            A("act", lambda e, q4=q4: e.copy(GT[:, q4 * 8:(q4 + 1) * 8, :], pT2[:].rearrange("p (k t) -> p k t", k=8)), reads=["pT2"], writes=["GT"])
                        for hf in range(2):
                            for k in range(16):
                                A("pe", lambda e, k=k, hf=hf: e.matmul(pO[:, hf * 512:(hf + 1) * 512], GT[:, k, :], wssd[:, k, hf * 512:(hf + 1) * 512],
                                                                        start=(k == 0), stop=(k == 15)),
                                  reads=["GT", "wssd"], writes=["pO"])
                        A("dve", lambda e, b=b: e.scalar_tensor_tensor(m1s[b][:], pO[:], r8[:, 2:3], gc[b][:], ALU.mult, ALU.mult),
                          reads=["pO", "r2", f"gc{b}"], writes=[f"m1s{b}"])
                        A("sp", lambda e, b=b, rows=rows: e.dma_start(out=scr_m1[rows, :], in_=m1s[b][:]), reads=[f"m1s{b}"], writes=[("scr_m1", c)], dma_key=f"m1s{b}")
            S_.barrier()
            if stop_after <= 4:
                continue
            with ExitStack() as st:
                def sb(name, shape, dt):
                    return st.enter_context(nc.sbuf_tensor(f"{name}_a{seq}", shape, dt))

                def ps(name, shape, dt):
                    return st.enter_context(nc.psum_tensor(f"{name}_a{seq}", shape, dt))

                QT = sb("QT", [128, 4, S], BF16)
                KT = sb("KT", [128, 4, S], BF16)
                Vp = sb("Vp", [128, NT, 8, 65], BF16)
                PT = [sb(f"PT{i}", [128, 384], BF16) for i in range(2)]
                ost = [sb(f"ost{i}", [128, 8, 65], BF16) for i in range(2)]
                pSc = [ps(f"pSc{i}", [128, 512], F32) for i in range(2)]
                pOa = [ps(f"pOa{i}", [128, 1024], F32) for i in range(2)]
                A("dve", lambda e: e.memset(Vp[:].rearrange("p t h c -> p (t h) c")[:, :, 64:65], 1.0), writes=["Vp"])
                ucnt = 0
                qcnt = 0
                for g in range(3):
                    d = DIL[g]
                    n = S // d
                    nq = n // 128
                    for r in range(d):
                        for hp in range(4):
                            row = g * 512 + hp * 128
                            A("sp", lambda e, hp=hp, row=row, r=r, n=n: e.dma_start(out=QT[:, hp, 0:n], in_=scr_qT[row:row + 128, r * n:(r + 1) * n]),
                              reads=[("scr_qk", 0, g, hp)], writes=["QT"], dma_key="QT")
                            A("sp", lambda e, hp=hp, row=row, r=r, n=n: e.dma_start(out=KT[:, hp, 0:n], in_=scr_kT[row:row + 128, r * n:(r + 1) * n]),
                              reads=[("scr_qk", 1, g, hp)], writes=["KT"], dma_key="KT")
                        for i in range(nq):
                            A("sp", lambda e, g=g, r=r, n=n, i=i: e.dma_start(out=Vp[:, i, :, 0:64],
                                                                          in_=scr_v[g, r * n + i * 128:r * n + (i + 1) * 128, :].rearrange("k (h c) -> k h c", h=8)),
                              reads=[("scr_v", g)], writes=["Vp"], dma_key="Vp")
                        for i in range(nq):
                            qb = qcnt % 2
                            qcnt += 1
                            offs = [o for o in (-1, 0, 1) if 0 <= i + o < nq]
                            for h in range(8):
                                hp, hh = h // 2, h % 2
                                ub = ucnt % 2
                                ucnt += 1
                                for oi, o in enumerate(offs):
                                    ks = slice((i + o) * 128, (i + o + 1) * 128)
                                    A("pe", lambda e, ub=ub, oi=oi, hp=hp, hh=hh, ks=ks, i=i: e.matmul(
                                        pSc[ub][:, oi * 128:(oi + 1) * 128], KT[hh * 64:(hh + 1) * 64, hp, ks], QT[hh * 64:(hh + 1) * 64, hp, i * 128:(i + 1) * 128],
                                        start=True, stop=False), reads=["QT", "KT"], writes=[f"pSc{ub}"])
                                    A("pe", lambda e, ub=ub, oi=oi, o=o: e.matmul(pSc[ub][:, oi * 128:(oi + 1) * 128], ident[:], band[:, o + 1, :], start=False, stop=True),
                                      reads=["consts"], writes=[f"pSc{ub}"])
                                no = len(offs)
                                A("act", lambda e, ub=ub, no=no: e.activation(PT[ub][:, 0:no * 128], pSc[ub][:, 0:no * 128], AF.Exp),
                                  reads=[f"pSc{ub}"], writes=[f"PT{ub}"])
                                for oi, o in enumerate(offs):
                                    A("pe", lambda e, ub=ub, qb=qb, oi=oi, o=o, h=h, i=i, no=no: e.matmul(
                                        pOa[qb][:, (h // 4) * 512 + (h % 4) * 65:(h // 4) * 512 + (h % 4) * 65 + 65], PT[ub][:, oi * 128:(oi + 1) * 128], Vp[:, i + o, h, :],
                                        start=(oi == 0), stop=(oi == no - 1)), reads=[f"PT{ub}", "Vp"], writes=[f"pOa{qb}"])
                            for hf in range(2):
                                A("act" if hf else "dve", lambda e, qb=qb, hf=hf: (e.copy if hf else e.tensor_copy)(
                                    ost[qb][:, hf * 4:(hf + 1) * 4, :], pOa[qb][:, hf * 512:hf * 512 + 260].rearrange("p (h c) -> p h c", h=4)),
                                  reads=[f"pOa{qb}"], writes=[f"ost{qb}"])
                            t0 = i * 128 * d + r
                            A("sp", lambda e, qb=qb, g=g, t0=t0, d=d: e.dma_start(out=scr_o[g, t0:t0 + 127 * d + 1:d, :, :], in_=ost[qb][:]),
                              reads=[f"ost{qb}"], writes=[("scr_o", g)], dma_key=f"ost{qb}")
            S_.barrier()
            if stop_after <= 5:
                continue
            with ExitStack() as st:
                def sb(name, shape, dt):
                    return st.enter_context(nc.sbuf_tensor(f"{name}_f{seq}", shape, dt))

                def ps(name, shape, dt):
                    return st.enter_context(nc.psum_tensor(f"{name}_f{seq}", shape, dt))

                TBLK = 512
                NTB = TBLK // 128
                watt = sb("watt", [128, 4, 1024], BF16)
                wout = sb("wout", [128, 8, 1024], BF16)
                gns = sb("gns", [128, 3, 1024], F32)
                x1 = sb("x1", [128, NTB, 1024], F32)
                h2T = sb("h2T", [128, 8, TBLK], BF16)
                actT = sb("actT", [128, 22, TBLK], BF16)
                gT = sb("gT", [128, TBLK], BF16)
                wfi = [sb(f"wfi{i}", [128, 8, 128], BF16) for i in range(3)]
                wdn = sb("wdn", [128, 22, 1024], BF16)
                o3 = [sb(f"o3{i}", [128, 3, 8, 65], BF16) for i in range(2)]
                osum = sb("osum", [128, 8, 65], F32)
                rl = sb("rl", [128, 8], F32)
                Ob = sb("Ob", [128, 512], BF16)
                OT = sb("OT", [128, 4, 128], BF16)
                gat = [sb(f"gat{i}", [128, 1024], BF16) for i in range(2)]
                m1c = [sb(f"m1c{i}", [128, 1024], BF16) for i in range(2)]
                mrg = sb("mrg", [128, 1024], BF16)
                mrgT = sb("mrgT", [128, 8, 128], BF16)
                tmpf = sb("tmpf", [128, 1024], F32)
                xin = [sb(f"xin{i}", [128, 1024], F32) for i in range(2)]
                h2 = sb("h2", [128, 1024], BF16)
                jk = sb("jk", [128, 1024], F32)
                s8 = sb("s8", [128, 8], F32)
                yo = [sb(f"yo{i}", [128, 1024], F32) for i in range(2)]
                pT3 = ps("pT3", [128, 1024], BF16)
                pM = ps("pM", [128, 1024], F32)
                pG = [ps(f"pG{i}", [128, 512], F32) for i in range(2)]
                pU = [ps(f"pU{i}", [128, 512], F32) for i in range(2)]

                A("pool", lambda e: e.dma_start(out=watt[:], in_=w_attn.rearrange("(k p) c -> p k c", p=128)), writes=["watt"], dma_key="watt")
                for q2 in range(2):
                    A("pool", lambda e, q2=q2: e.dma_start(out=wdn[:, q2 * 11:(q2 + 1) * 11, :],
                                                           in_=w_ffn_down[q2 * 1408:(q2 + 1) * 1408, :].rearrange("(k p) c -> p k c", p=128)),
                      writes=["wdn"], dma_key="wdn")
                A("pool", lambda e: e.dma_start(out=wout[:], in_=w_out.rearrange("(k p) c -> p k c", p=128)), writes=["wout"], dma_key="wout")
                for gi, gsrc in enumerate((norm_mix_post, norm_ffn_pre, norm_ffn_post)):
                    A("sp", lambda e, gi=gi, gsrc=gsrc: e.dma_start(out=gns[:, gi, :], in_=gsrc.partition_broadcast(128)), writes=["gns"], dma_key="gns")

                def rms_scale(src_ap, ss_col, rd):
                    A("act", lambda e: e.activation(jk[:], src_ap, AF.Square, accum_out=s8[:, ss_col:ss_col + 1]), reads=rd, writes=["jk", ("s8", ss_col)])
                    A("act", lambda e: e.activation(s8[:, ss_col + 1:ss_col + 2], s8[:, ss_col:ss_col + 1], AF.Sqrt, bias=EPS, scale=1.0 / D),
                      reads=[("s8", ss_col)], writes=[("s8", ss_col + 1)])
                    A("dve", lambda e: e.reciprocal(s8[:, ss_col + 1:ss_col + 2], s8[:, ss_col + 1:ss_col + 2]), reads=[("s8", ss_col + 1)], writes=[("s8", ss_col + 1)])

                wcn = [0]
                for blk in range(S // TBLK):
                    for tt in range(NTB):
                        t = blk * NTB + tt
                        b = t % 2
                        rows = slice(t * 128, (t + 1) * 128)
                        for g3 in range(3):
                            A("sp", lambda e, b=b, rows=rows, g3=g3: e.dma_start(out=o3[b][:, g3, :, :], in_=scr_o[g3, rows, :, :]),
                              reads=[("scr_o", g3)], writes=[f"o3{b}"], dma_key=f"o3{b}")
                        A("sp", lambda e, b=b, rows=rows: e.dma_start(out=gat[b][:], in_=scr_gate[rows, 1024:2048]), reads=[("scr_gate", t)], writes=[f"gat{b}"], dma_key=f"gat{b}")
                        A("sp", lambda e, b=b, rows=rows: e.dma_start(out=m1c[b][:], in_=scr_m1[rows, :]), reads=[("scr_m1", t)], writes=[f"m1c{b}"], dma_key=f"m1c{b}")
                        A("sp", lambda e, b=b, rows=rows: e.dma_start(out=xin[b][:], in_=x_in[seq, rows, :]), writes=[f"xin{b}"], dma_key=f"xin{b}")
                        A("dve", lambda e, b=b: e.tensor_tensor(osum[:], o3[b][:, 0, :, :], o3[b][:, 1, :, :], ALU.add), reads=[f"o3{b}"], writes=["osum"])
                        A("dve", lambda e, b=b: e.tensor_tensor(osum[:], osum[:], o3[b][:, 2, :, :], ALU.add), reads=[f"o3{b}", "osum"], writes=["osum"])
                        A("dve", lambda e: e.reciprocal(rl[:], osum[:, :, 64]), reads=["osum"], writes=["rl"])
                        A("dve", lambda e: e.tensor_tensor(Ob[:].rearrange("p (h c) -> p h c", h=8), osum[:, :, 0:64], rl[:].unsqueeze(2).to_broadcast([128, 8, 64]), ALU.mult),
                          reads=["osum", "rl"], writes=["Ob"])
                        for k in range(4):
                            A("pe", lambda e, k=k: e.transpose(pT3[:, k * 128:(k + 1) * 128], Ob[:, k * 128:(k + 1) * 128], ident[:]), reads=["Ob", "consts"], writes=["pT3"])
                        A("act", lambda e: e.copy(OT[:], pT3[:, 0:512].rearrange("p (k t) -> p k t", k=4)), reads=["pT3"], writes=["OT"])
                        for hf in range(2):
                            for k in range(4):
                                A("pe", lambda e, k=k, hf=hf: e.matmul(pM[:, hf * 512:(hf + 1) * 512], OT[:, k, :], watt[:, k, hf * 512:(hf + 1) * 512], start=(k == 0), stop=(k == 3)),
                                  reads=["OT", "watt"], writes=["pM"])
                        A("dve", lambda e, b=b: e.tensor_tensor(tmpf[:], pM[:], gat[b][:], ALU.mult), reads=["pM", f"gat{b}"], writes=["tmpf"])
                        A("pool", lambda e, b=b: e.tensor_tensor(mrg[:], tmpf[:], m1c[b][:], ALU.add), reads=["tmpf", f"m1c{b}"], writes=["mrg"])
                        for k in range(8):
                            A("pe", lambda e, k=k: e.transpose(pT3[:, k * 128:(k + 1) * 128], mrg[:, k * 128:(k + 1) * 128], ident[:]), reads=["mrg", "consts"], writes=["pT3"])
                        A("act", lambda e: e.copy(mrgT[:], pT3[:].rearrange("p (k t) -> p k t", k=8)), reads=["pT3"], writes=["mrgT"])
                        for hf in range(2):
                            for k in range(8):
                                A("pe", lambda e, k=k, hf=hf: e.matmul(pM[:, hf * 512:(hf + 1) * 512], mrgT[:, k, :], wout[:, k, hf * 512:(hf + 1) * 512], start=(k == 0), stop=(k == 7)),
                                  reads=["mrgT", "wout"], writes=["pM"])
                        rms_scale(pM[:], 0, ["pM"])
                        A("dve", lambda e: e.scalar_tensor_tensor(tmpf[:], pM[:], s8[:, 1:2], gns[:, 0, :], ALU.mult, ALU.mult), reads=["pM", ("s8", 1), "gns"], writes=["tmpf"])
                        A("pool", lambda e, b=b, tt=tt: e.tensor_tensor(x1[:, tt, :], tmpf[:], xin[b][:], ALU.add), reads=["tmpf", f"xin{b}"], writes=[("x1", tt)])
                        rms_scale(x1[:, tt, :], 2, [("x1", tt)])
                        A("dve", lambda e, tt=tt: e.scalar_tensor_tensor(h2[:], x1[:, tt, :], s8[:, 3:4], gns[:, 1, :], ALU.mult, ALU.mult), reads=[("x1", tt), ("s8", 3), "gns"], writes=["h2"])
                        for k in range(8):
                            A("pe", lambda e, k=k: e.transpose(pT3[:, k * 128:(k + 1) * 128], h2[:, k * 128:(k + 1) * 128], ident[:]), reads=["h2", "consts"], writes=["pT3"])
                        A("act", lambda e, tt=tt: e.copy(h2T[:, :, tt * 128:(tt + 1) * 128], pT3[:].rearrange("p (k t) -> p k t", k=8)), reads=["pT3"], writes=[("h2T", tt)])
                    h2T_all = [("h2T", tt) for tt in range(NTB)]
                    for f in range(22):
                        wa = wcn[0] % 3
                        wb_ = (wcn[0] + 1) % 3
                        wcn[0] += 2
                        A("pool", lambda e, f=f, wa=wa: e.dma_start(out=wfi[wa][:], in_=w_ffn_in[:, f * 128:(f + 1) * 128].rearrange("(k p) c -> p k c", p=128)),
                          writes=[f"wfi{wa}"], dma_key=f"wfi{wa}")
                        A("pool", lambda e, f=f, wb_=wb_: e.dma_start(out=wfi[wb_][:], in_=w_ffn_in[:, FFN + f * 128:FFN + (f + 1) * 128].rearrange("(k p) c -> p k c", p=128)),
                          writes=[f"wfi{wb_}"], dma_key=f"wfi{wb_}")
                        for half in range(TBLK // 512):
                            pb = (f * 2 + half) % 2
                            ts_ = slice(half * 512, (half + 1) * 512)
                            for k in range(8):
                                A("pe", lambda e, k=k, pb=pb, wa=wa, ts_=ts_: e.matmul(pG[pb][:], wfi[wa][:, k, :], h2T[:, k, ts_], start=(k == 0), stop=(k == 7)),
                                  reads=h2T_all + [f"wfi{wa}"], writes=[f"pG{pb}"])
                            for k in range(8):
                                A("pe", lambda e, k=k, pb=pb, wb_=wb_, ts_=ts_: e.matmul(pU[pb][:], wfi[wb_][:, k, :], h2T[:, k, ts_], start=(k == 0), stop=(k == 7)),
                                  reads=h2T_all + [f"wfi{wb_}"], writes=[f"pU{pb}"])
                            A("act", lambda e, pb=pb, ts_=ts_: e.activation(gT[:, ts_], pG[pb][:], AF.Silu), reads=[f"pG{pb}"], writes=[("gT", half)])
                            A("dve", lambda e, pb=pb, ts_=ts_, f=f: e.tensor_tensor(actT[:, f, ts_], pU[pb][:], gT[:, ts_], ALU.mult),
                              reads=[f"pU{pb}", ("gT", half)], writes=[("actT", f)])
                    act_all = [("actT", f) for f in range(22)]
                    for tt in range(NTB):
                        t = blk * NTB + tt
                        b = t % 2
                        for f in range(22):
                            for hf in range(2):
                                A("pe", lambda e, f=f, hf=hf, tt=tt: e.matmul(pM[:, hf * 512:(hf + 1) * 512], actT[:, f, tt * 128:(tt + 1) * 128], wdn[:, f, hf * 512:(hf + 1) * 512],
                                                                          start=(f == 0), stop=(f == 21)),
                                  reads=act_all + ["wdn"], writes=["pM"])
                        rms_scale(pM[:], 4, ["pM"])
                        A("dve", lambda e: e.scalar_tensor_tensor(tmpf[:], pM[:], s8[:, 5:6], gns[:, 2, :], ALU.mult, ALU.mult), reads=["pM", ("s8", 5), "gns"], writes=["tmpf"])
                        A("pool", lambda e, b=b, tt=tt: e.tensor_tensor(yo[b][:], tmpf[:], x1[:, tt, :], ALU.add), reads=["tmpf", ("x1", tt)], writes=[f"yo{b}"])
                        A("sp", lambda e, b=b, t=t: e.dma_start(out=y_out[seq, t * 128:(t + 1) * 128, :], in_=yo[b][:]), reads=[f"yo{b}"], writes=[("y", seq, t)], dma_key=f"yo{b}")
            S_.barrier()
        S_.emit(final_waits=list(S_.dma_count.keys()))
    return nc


def _host_consts(inp_conv_w, inp_conv_b, inp_norm_w, S):
    p = np.arange(128)
    m = p % 64
    rot = (m < 16)
    h2 = (m >= 8) & rot
    fi = np.where(rot, m % 8, 0)
    invf = (500000.0 ** (-(2.0 * fi) / 16.0)).astype(np.float32)
    ang = np.arange(S, dtype=np.float32)[None, :] * invf[:, None]
    cos = np.where(rot[:, None], np.cos(ang), 1.0).astype(np.float32)
    sgn = np.where(rot, np.where(h2, 1.0, -1.0), 0.0).astype(np.float32)
    sin = (np.sin(ang) * sgn[:, None]).astype(np.float32)
    cw = np.ascontiguousarray(inp_conv_w.reshape(5, 32, 128).transpose(2, 1, 0).reshape(128, 160), dtype=np.float32)
    cb = np.ascontiguousarray(inp_conv_b.reshape(32, 128).T, dtype=np.float32)
    nw = np.ascontiguousarray(inp_norm_w.reshape(16, 128).T, dtype=np.float32)
    return {"rot_cos": cos, "rot_sin": sin, "cw_l": cw, "cb_l": cb, "nw_l": nw}


_NC_CACHE = {}


def kernel(**inputs):
    x = np.asarray(inputs["x"], dtype=np.float32)
    B, S, _ = x.shape
    n_cores = 8
    NSEQ = B // n_cores
    key = (S, NSEQ)
    if key not in _NC_CACHE:
        _NC_CACHE[key] = build(S, NSEQ)
    nc = _NC_CACHE[key]
    f = lambda k: np.ascontiguousarray(np.asarray(inputs[k], dtype=np.float32)[0])
    shared = {
        "norm_mix_pre": f("norm_mix_pre").reshape(1, D), "w_in": f("w_in"),
        "ssd_conv_w": f("ssd_conv_w"), "ssd_conv_b": f("ssd_conv_b").reshape(1, 4096),
        "ssd_dt_bias": f("ssd_dt_bias").reshape(1, 64), "ssd_A_log": f("ssd_A_log").reshape(1, 64),
        "ssd_D": f("ssd_D").reshape(1, 32), "ssd_norm_w": f("ssd_norm_w").reshape(1, DIN),
        "w_ssd_branch": f("w_ssd_branch"), "w_attn_branch": f("w_attn_branch"), "w_out": f("w_out"),
        "norm_mix_post": f("norm_mix_post").reshape(1, D), "norm_ffn_pre": f("norm_ffn_pre").reshape(1, D),
        "w_ffn_in": f("w_ffn_in"), "w_ffn_down": f("w_ffn_down"), "norm_ffn_post": f("norm_ffn_post").reshape(1, D),
    }
    shared.update(_host_consts(shared["ssd_conv_w"], shared["ssd_conv_b"], shared["ssd_norm_w"], S))
    in_maps = []
    for c in range(n_cores):
        m = dict(shared)
        m["x"] = np.ascontiguousarray(x[c * NSEQ:(c + 1) * NSEQ])
        in_maps.append(m)
    res = run_bass_kernel_spmd(nc, in_maps, core_ids=list(range(n_cores)))
    return np.concatenate([np.asarray(r["y"], dtype=np.float32) for r in res.results], axis=0)
```

```python
import numpy as np
from contextlib import ExitStack
import concourse.bass as bass
import concourse.mybir as mybir
from concourse.bass_utils import run_bass_kernel_spmd

F32 = mybir.dt.float32
BF16 = mybir.dt.bfloat16
I32 = mybir.dt.int32
AF = mybir.ActivationFunctionType
ALU = mybir.AluOpType

EPOCH = 20000
D = 1024
DIN = 2048
FFN = 2816
INC = 12864
OFF_Z, OFF_X, OFF_B, OFF_C, OFF_DT, OFF_Q, OFF_K, OFF_V, OFF_G = 0, 2048, 4096, 5120, 6144, 6208, 7744, 9280, 10816
DIL = (1, 4, 16)
EPS = 1e-6
NEGV = -30000.0


class _Rec:
    def __getattr__(self, name):
        return lambda *a, **kw: (name, a, kw)


_REC = _Rec()


class Op:
    __slots__ = ("eng", "fn", "idx", "deps", "signal", "dma_key", "sig_n")

    def __init__(self, eng, fn, dma_key=None):
        self.eng = eng
        self.fn = fn(_REC)
        self.deps = []
        self.signal = False
        self.dma_key = dma_key
        self.sig_n = 0


class Sched:
    ENGS = ("pe", "act", "dve", "pool", "sp")

    def __init__(self, nc):
        self.nc = nc
        self.eng_ops = {e: [] for e in self.ENGS}
        self.last_writer = {}
        self.readers = {}
        self.dma_count = {}
        self.waited = {e: {} for e in self.ENGS}

    def add(self, eng, fn, reads=(), writes=(), dma_key=None):
        op = Op(eng, fn, dma_key)
        op.idx = len(self.eng_ops[eng])
        self.eng_ops[eng].append(op)
        deps = set()
        for r in reads:
            w = self.last_writer.get(r)
            if w is not None:
                deps.add(w)
        for w in writes:
            lw = self.last_writer.get(w)
            if lw is not None:
                deps.add(lw)
            for rd in self.readers.get(w, ()):
                deps.add(rd)
        self._attach(op, deps)
        if dma_key is not None:
            self.dma_count[dma_key] = self.dma_count.get(dma_key, 0) + 1
        for r in reads:
            self.readers.setdefault(r, []).append(op)
        for w in writes:
            self.last_writer[w] = op
            self.readers[w] = []
        return op

    def _attach(self, op, deps):
        eng = op.eng
        best = {}
        for d in deps:
            if d is op:
                continue
            if d.dma_key is not None:
                k = ("dma", d.dma_key)
                v = self.dma_count[d.dma_key]
            else:
                if d.eng == "pe" and eng == "pe" and op.dma_key is None:
                    continue
                k = ("eng", d.eng)
                v = d.idx
            if k not in best or best[k][0] < v:
                best[k] = (v, d)
        wd = self.waited[eng]
        for k, (v, d) in best.items():
            if k in wd and wd[k] >= v:
                continue
            wd[k] = v
            if k[0] == "eng":
                d.signal = True
            op.deps.append((k, v, d))

    def barrier(self):
        lasts = []
        for e in self.ENGS:
            ops = [o for o in self.eng_ops[e] if o.dma_key is None]
            if ops:
                lasts.append(ops[-1])
        dmas = {}
        for e in self.ENGS:
            for o in self.eng_ops[e]:
                if o.dma_key is not None:
                    dmas[o.dma_key] = o
        for e in self.ENGS:
            op = Op(e, lambda eng: eng.nop())
            op.idx = len(self.eng_ops[e])
            self.eng_ops[e].append(op)
            self._attach(op, set(lasts) | set(dmas.values()))
        self.last_writer = {}
        self.readers = {}

    def emit(self, final_waits=()):
        nc = self.nc
        nsig = {}
        for e in self.ENGS:
            n = 0
            for op in self.eng_ops[e]:
                if op.dma_key is None and op.signal:
                    n += 1
                    op.sig_n = n
            nsig[e] = n
        with ExitStack() as st:
            esems = {}
            for e in self.ENGS:
                for ep in range((nsig[e] + EPOCH - 1) // EPOCH):
                    esems[(e, ep)] = st.enter_context(nc.semaphore(f"s_{e}_{ep}"))
            dsems = {}
            for i, k in enumerate(self.dma_count):
                dsems[k] = st.enter_context(nc.semaphore(f"d{i}"))
            block = st.enter_context(nc.Block())

            def run(e, engh):
                for op in self.eng_ops[e]:
                    for (k, v, d) in op.deps:
                        if k[0] == "dma":
                            engh.wait_ge(dsems[k[1]], 16 * v)
                        else:
                            n = d.sig_n
                            engh.wait_ge(esems[(d.eng, (n - 1) // EPOCH)], (n - 1) % EPOCH + 1)
                    name, a_, kw_ = op.fn
                    ins = getattr(engh, name)(*a_, **kw_)
                    if op.dma_key is not None:
                        ins.then_inc(dsems[op.dma_key], 16)
                    elif op.signal:
                        n = op.sig_n
                        ins.then_inc(esems[(e, (n - 1) // EPOCH)], 1)
                if e == "sp":
                    for k in final_waits:
                        engh.wait_ge(dsems[k], 16 * self.dma_count[k])

            @block.tensor
            def _(eng):
                run("pe", eng)

            @block.scalar
            def _(eng):
                run("act", eng)

            @block.vector
            def _(eng):
                run("dve", eng)

            @block.gpsimd
            def _(eng):
                run("pool", eng)

            @block.sync
            def _(eng):
                run("sp", eng)


def build(S, NSEQ, debug=False, stop_after=99, cut=99):
    nc = bass.Bass("TRN2", target_bir_lowering=False)
    NT = S // 128
    TB = S // 512

    def din(name, shape, pad=False):
        if pad:
            return nc.dram_tensor(name, [shape[0] + 1] + list(shape[1:]), F32, kind="ExternalInput").ap()[0:shape[0]]
        return nc.dram_tensor(name, shape, F32, kind="ExternalInput").ap()

    x_in = din("x", [NSEQ, S, D])
    norm_mix_pre = din("norm_mix_pre", [1, D])
    w_in = din("w_in", [D, INC], pad=True)
    conv_w = din("ssd_conv_w", [5, 4096])
    conv_b = din("ssd_conv_b", [1, 4096])
    dt_bias = din("ssd_dt_bias", [1, 64])
    A_log = din("ssd_A_log", [1, 64])
    D_skip = din("ssd_D", [1, 32])
    ssd_norm_w = din("ssd_norm_w", [1, DIN])
    w_ssd = din("w_ssd_branch", [DIN, D], pad=True)
    w_attn = din("w_attn_branch", [512, D], pad=True)
    w_out = din("w_out", [D, D], pad=True)
    norm_mix_post = din("norm_mix_post", [1, D])
    norm_ffn_pre = din("norm_ffn_pre", [1, D])
    w_ffn_in = din("w_ffn_in", [D, 2 * FFN], pad=True)
    w_ffn_down = din("w_ffn_down", [FFN, D], pad=True)
    norm_ffn_post = din("norm_ffn_post", [1, D])
    rot_cos = din("rot_cos", [128, S], pad=True)
    rot_sin = din("rot_sin", [128, S], pad=True)
    cw_l = din("cw_l", [128, 160])
    cb_l = din("cb_l", [128, 32])
    nw_l = din("nw_l", [128, 16])
    y_out = nc.dram_tensor("y", [NSEQ, S, D], F32, kind="ExternalOutput").ap()

    skind = "ExternalOutput" if debug else "Internal"

    def scr(name, shape, dt=BF16):
        return nc.dram_tensor(name, shape, dt, kind=skind).ap()

    scr_z = scr("scr_z", [S, DIN])
    scr_gate = scr("scr_gate", [S, 2 * D])
    scr_x = scr("scr_x", [S, DIN])
    scr_B = scr("scr_B", [S, 1024])
    scr_BT = scr("scr_BT", [1024, S])
    scr_CT = scr("scr_CT", [1024, S])
    scr_dt = scr("scr_dt", [S, 64], F32)
    scr_yf = scr("scr_yf", [S, DIN])
    scr_qT = scr("scr_qT", [1536, S])
    scr_kT = scr("scr_kT", [1536, S])
    scr_v = scr("scr_v", [3, S, 512])
    scr_o = scr("scr_o", [3, S, 8, 65])
    scr_m1 = scr("scr_m1", [S, D])
    wfi_bf = nc.dram_tensor("wfi_bf", [D, 2 * FFN], BF16).ap()
    wfd_bf = nc.dram_tensor("wfd_bf", [FFN, D], BF16).ap()

    S_ = Sched(nc)
    A = S_.add

    with ExitStack() as gst:
        def gsb(name, shape, dt):
            return gst.enter_context(nc.sbuf_tensor(name, shape, dt))

        ident = gsb("ident", [128, 128], BF16)
        cst_f = gsb("cst_f", [128, 4, 128], F32)
        cst_b = gsb("cst_b", [128, 8, 128], BF16)
        band = gsb("band", [128, 3, 128], BF16)
        tmpc = gsb("tmpc", [128, 128], F32)
        smallc = gsb("smallc", [128, 64 + 64 + 32], F32)

        def mk_const(dst_f32_ap, fill_base, selects):
            A("pool", lambda e: e.memset(dst_f32_ap, fill_base), writes=["tmpc"])
            for (pat, cm, base, fill) in selects:
                A("pool", lambda e, pat=pat, cm=cm, base=base, fill=fill: e.affine_select(
                    dst_f32_ap, dst_f32_ap, [[pat, 128]], ALU.is_ge, fill, base=base, channel_multiplier=cm),
                  reads=["tmpc"], writes=["tmpc"])

        def to_bf(dst_ap, src_ap, scale=None):
            if scale is None:
                A("dve", lambda e: e.tensor_copy(dst_ap, src_ap), reads=["tmpc"], writes=["consts"])
            else:
                A("dve", lambda e: e.tensor_scalar(dst_ap, src_ap, scale, None, ALU.mult), reads=["tmpc"], writes=["consts"])

        mk_const(tmpc[:], 1.0, [(1, -1, 0, 0.0), (-1, 1, 0, 0.0)])
        to_bf(ident[:], tmpc[:])
        A("dve", lambda e: e.tensor_copy(cst_f[:, 3, :], tmpc[:]), reads=["tmpc"], writes=["consts"])
        mk_const(tmpc[:], 1.0, [(1, -1, 0, 0.0)])
        to_bf(cst_b[:, 0, :], tmpc[:])
        to_bf(cst_b[:, 1, :], tmpc[:], -1.0)
        to_bf(cst_b[:, 6, :], tmpc[:])
        A("dve", lambda e: e.tensor_copy(cst_f[:, 0, :], tmpc[:]), reads=["tmpc"], writes=["consts"])
        mk_const(tmpc[:], 1.0, [(-1, 1, 0, 0.0)])
        to_bf(cst_b[:, 2, :], tmpc[:])
        to_bf(cst_b[:, 3, :], tmpc[:], -1.0)
        to_bf(cst_b[:, 7, :], tmpc[:])
        A("dve", lambda e: e.tensor_copy(cst_f[:, 1, :], tmpc[:]), reads=["tmpc"], writes=["consts"])
        mk_const(tmpc[:], 0.0, [(1, -1, 0, NEGV)])
        to_bf(cst_b[:, 4, :], tmpc[:])
        mk_const(tmpc[:], 0.0, [(-1, 1, 0, NEGV)])
        to_bf(cst_b[:, 5, :], tmpc[:])
        A("dve", lambda e: e.memset(cst_f[:, 2, :], 1.0), writes=["consts"])
        for oi, o in enumerate((-1, 0, 1)):
            mk_const(tmpc[:], 0.0, [(-1, 1, 128 * o + 64, NEGV), (1, -1, 64 - 128 * o, NEGV)])
            to_bf(band[:, oi, :], tmpc[:])
        A("sp", lambda e: e.dma_start(out=smallc[:, 0:64], in_=dt_bias.partition_broadcast(128)), writes=["smallc"], dma_key="c0")
        A("sp", lambda e: e.dma_start(out=smallc[:, 64:128], in_=A_log.partition_broadcast(128)), writes=["smallc"], dma_key="c0")
        A("sp", lambda e: e.dma_start(out=smallc[:, 128:160], in_=D_skip.partition_broadcast(128)), writes=["smallc"], dma_key="c0")
        A("act", lambda e: e.activation(smallc[:, 64:128], smallc[:, 64:128], AF.Exp), reads=["smallc"], writes=["smallc"])
        A("dve", lambda e: e.tensor_scalar(smallc[:, 64:128], smallc[:, 64:128], -1.0, None, ALU.mult), reads=["smallc"], writes=["smallc"])
        for seq in range(NSEQ):
            with ExitStack() as st:
                def sb(name, shape, dt):
                    return st.enter_context(nc.sbuf_tensor(f"{name}_{seq}", shape, dt))

                def ps(name, shape, dt):
                    return st.enter_context(nc.psum_tensor(f"{name}_{seq}", shape, dt))

                hT = sb("hT", [128, 8, S], BF16)
                gpre = sb("gpre", [128, D], F32)
                xt = [sb(f"xt{i}", [128, D], F32) for i in range(2)]
                hb = [sb(f"hb{i}", [128, D], BF16) for i in range(2)]
                junk = sb("junk", [128, D], F32)
                st8 = sb("st8", [128, 8], F32)
                pT = [ps(f"pT{i}", [128, 1024], BF16) for i in range(2)]
                pA = [ps(f"pA{i}", [128, 512], F32) for i in range(2)]
                pB = [ps(f"pB{i}", [128, 512], F32) for i in range(2)]

                A("sp", lambda e: e.dma_start(out=gpre[:], in_=norm_mix_pre.partition_broadcast(128)), writes=["gpre"], dma_key="gpre")
                for t in range(NT):
                    b = t % 2
                    A("sp", lambda e, t=t, b=b: e.dma_start(out=xt[b][:], in_=x_in[seq, t * 128:(t + 1) * 128, :]),
                      writes=[f"xt{b}"], dma_key=f"xt{b}")
                    A("act", lambda e, b=b: e.activation(junk[:], xt[b][:], AF.Square, accum_out=st8[:, b:b + 1]),
                      reads=[f"xt{b}"], writes=["junk", f"ss{b}"])
                    A("act", lambda e, b=b: e.activation(st8[:, 2 + b:3 + b], st8[:, b:b + 1], AF.Sqrt, bias=EPS, scale=1.0 / D),
                      reads=[f"ss{b}"], writes=[f"sd{b}"])
                    A("dve", lambda e, b=b: e.reciprocal(st8[:, 4 + b:5 + b], st8[:, 2 + b:3 + b]), reads=[f"sd{b}"], writes=[f"rs{b}"])
                    A("dve", lambda e, b=b: e.scalar_tensor_tensor(hb[b][:], xt[b][:], st8[:, 4 + b:5 + b], gpre[:], ALU.mult, ALU.mult),
                      reads=[f"xt{b}", f"rs{b}", "gpre"], writes=[f"hb{b}"])
                    for k in range(8):
                        A("pe", lambda e, b=b, k=k: e.transpose(pT[b][:, k * 128:(k + 1) * 128], hb[b][:, k * 128:(k + 1) * 128], ident[:]),
                          reads=[f"hb{b}", "consts"], writes=[f"pT{b}"])
                    A("act", lambda e, b=b, t=t: e.copy(hT[:, :, t * 128:(t + 1) * 128], pT[b][:].rearrange("p (k t) -> p k t", k=8)),
                      reads=[f"pT{b}"], writes=[("hT", t)])
                hT_all = [("hT", t) for t in range(NT)]
                if cut <= 1:
                    S_.barrier(); continue

                wsl = [sb(f"wsl{i}", [128, 8, 512], BF16) for i in range(2)]
                wp = sb("wp", [128, 8, 512], BF16)
                stg = [sb(f"stg{i}", [128, 512], BF16) for i in range(3)]
                stgf = [sb(f"stgf{i}", [128, 64], F32) for i in range(2)]
                outX = [sb(f"outX{i}", [128, S], BF16) for i in range(4)]
                rawT = [sb(f"rawT{i}", [128, S + 4], BF16) for i in range(2)]
                tmp1 = [sb(f"tmp1{i}", [128, 512], F32) for i in range(2)]
                tmp2 = [sb(f"tmp2{i}", [128, 512], F32) for i in range(2)]
                cosT = sb("cosT", [128, S], BF16)
                sinT = sb("sinT", [128, S], BF16)
                cw = sb("cw", [128, 32, 5], F32)
                cb = sb("cb", [128, 32], F32)
                dg = sb("dg", [128, 5, 128], BF16)
                wcnt = [0]
                scnt = [0]

                def load_w(lo, ncols):
                    i = wcnt[0] % 2
                    wcnt[0] += 1
                    A("pool", lambda e: e.dma_start(out=wsl[i][:, :, 0:ncols],
                                                    in_=w_in[:, lo:lo + ncols].rearrange("(k p) c -> p k c", p=128)),
                      writes=[f"wsl{i}"], dma_key=f"wsl{i}")
                    return i

                def stage():
                    i = scnt[0] % 3
                    scnt[0] += 1
                    return i

                A("pool", lambda e: e.memset(wp[:], 0.0), writes=["wp"])
                for i in range(2):
                    A("pool", lambda e, i=i: e.memset(rawT[i][:, 0:2], 0.0), writes=[f"rawT{i}"])
                    A("pool", lambda e, i=i: e.memset(rawT[i][:, S + 2:S + 4], 0.0), writes=[f"rawT{i}"])
                A("sp", lambda e: e.dma_start(out=cw[:].rearrange("p f k -> p (f k)"), in_=cw_l), writes=["cw"], dma_key="cw")
                A("sp", lambda e: e.dma_start(out=cb[:], in_=cb_l), writes=["cb"], dma_key="cw")
                A("pool", lambda e: e.dma_start(out=cosT[:], in_=rot_cos), writes=["cosT"], dma_key="rot")
                A("pool", lambda e: e.dma_start(out=sinT[:], in_=rot_sin), writes=["sinT"], dma_key="rot")
                if cut <= 2:
                    S_.barrier(); continue
                def tok_block(lo, ncols, func, dst, dst_lo, dkey):
                    wi = load_w(lo, ncols)
                    for t in range(NT):
                        b = t % 2
                        for k in range(8):
                            A("pe", lambda e, t=t, k=k, b=b: e.matmul(pA[b][:, 0:ncols], hT[:, k, t * 128:(t + 1) * 128], wsl[wi][:, k, 0:ncols],
                                                                      start=(k == 0), stop=(k == 7)),
                              reads=[("hT", t), f"wsl{wi}"], writes=[f"pA{b}"])
                        si = stage()
                        A("act", lambda e, b=b, si=si: e.activation(stg[si][:, 0:ncols], pA[b][:, 0:ncols], func),
                          reads=[f"pA{b}"], writes=[f"stg{si}"])
                        A("sp", lambda e, t=t, si=si: e.dma_start(out=dst[t * 128:(t + 1) * 128, dst_lo:dst_lo + ncols], in_=stg[si][:, 0:ncols]),
                          reads=[f"stg{si}"], writes=[(dkey, t)], dma_key=f"stg{si}")

                for blk in range(4):
                    tok_block(OFF_Z + blk * 512, 512, AF.Silu, scr_z, blk * 512, "scr_z")
                for blk in range(4):
                    tok_block(OFF_G + blk * 512, 512, AF.Sigmoid, scr_gate, blk * 512, "scr_gate")
                if cut <= 3:
                    S_.barrier(); continue
                wi = load_w(OFF_DT, 64)
                for t in range(NT):
                    b = t % 2
                    for k in range(8):
                        A("pe", lambda e, t=t, k=k, b=b: e.matmul(pA[b][:, 0:64], hT[:, k, t * 128:(t + 1) * 128], wsl[wi][:, k, 0:64],
                                                                  start=(k == 0), stop=(k == 7)),
                          reads=[("hT", t), f"wsl{wi}"], writes=[f"pA{b}"])
                    A("act", lambda e, b=b: e.copy(stgf[b][:], pA[b][:, 0:64]), reads=[f"pA{b}"], writes=[f"stgf{b}"])
                    A("sp", lambda e, t=t, b=b: e.dma_start(out=scr_dt[t * 128:(t + 1) * 128, :], in_=stgf[b][:]),
                      reads=[f"stgf{b}"], writes=[("scr_dt", t)], dma_key=f"stgf{b}")
                if cut <= 4:
                    S_.barrier(); continue
                for g in range(3):
                    d = DIL[g]
                    n = S // d
                    wi = load_w(OFF_V + g * 512, 512)
                    cnt = 0
                    for r in range(d):
                        for i in range(n // 128):
                            b = cnt % 2
                            cnt += 1
                            base = i * 128 * d + r
                            tl = sorted(set((base + j * d) // 128 for j in (0, 127)))
                            tl = list(range(tl[0], tl[-1] + 1))
                            for k in range(8):
                                A("pe", lambda e, k=k, b=b, base=base, d=d: e.matmul(
                                    pA[b][:], hT[:, k, base:base + 127 * d + 1:d], wsl[wi][:, k, :], start=(k == 0), stop=(k == 7)),
                                  reads=[("hT", tt) for tt in tl] + [f"wsl{wi}"], writes=[f"pA{b}"])
                            si = stage()
                            A("act", lambda e, b=b, si=si: e.copy(stg[si][:], pA[b][:]), reads=[f"pA{b}"], writes=[f"stg{si}"])
                            row = r * n + i * 128
                            A("sp", lambda e, si=si, row=row, g=g: e.dma_start(out=scr_v[g, row:row + 128, :], in_=stg[si][:]),
                              reads=[f"stg{si}"], writes=[("scr_v", g)], dma_key=f"stg{si}")
                if cut <= 5:
                    S_.barrier(); continue
                for qk in range(2):
                    off = OFF_Q if qk == 0 else OFF_K
                    dstT = scr_qT if qk == 0 else scr_kT
                    for blk in range(3):
                        g = blk
                        d = DIL[g]
                        n = S // d
                        wi = load_w(off + blk * 512, 512)
                        if qk == 0:
                            A("dve", lambda e, wi=wi: e.tensor_scalar(wsl[wi][:], wsl[wi][:], 0.125, None, ALU.mult),
                              reads=[f"wsl{wi}"], writes=[f"wsl{wi}"])
                        wv = wsl[wi][:].rearrange("p k (h c) -> p k h c", h=8)
                        wpv = wp[:].rearrange("p k (h c) -> p k h c", h=8)
                        A("dve", lambda e, wv=wv, wpv=wpv: e.tensor_copy(wpv[:, :, :, 0:8], wv[:, :, :, 8:16]), reads=[f"wsl{wi}"], writes=["wp"])
                        A("dve", lambda e, wv=wv, wpv=wpv: e.tensor_copy(wpv[:, :, :, 8:16], wv[:, :, :, 0:8]), reads=[f"wsl{wi}"], writes=["wp"])
                        for hp in range(4):
                            ox = outX[hp]
                            for tb in range(TB):
                                b = tb % 2
                                for k in range(8):
                                    A("pe", lambda e, k=k, b=b, tb=tb, hp=hp, wi=wi: e.matmul(
                                        pA[b][:], wsl[wi][:, k, hp * 128:(hp + 1) * 128], hT[:, k, tb * 512:(tb + 1) * 512],
                                        start=(k == 0), stop=(k == 7)),
                                      reads=hT_all[tb * 4:(tb + 1) * 4] + [f"wsl{wi}"], writes=[f"pA{b}"])
                                for k in range(8):
                                    A("pe", lambda e, k=k, b=b, tb=tb, hp=hp: e.matmul(
                                        pB[b][:], wp[:, k, hp * 128:(hp + 1) * 128], hT[:, k, tb * 512:(tb + 1) * 512],
                                        start=(k == 0), stop=(k == 7)),
                                      reads=hT_all[tb * 4:(tb + 1) * 4] + ["wp"], writes=[f"pB{b}"])
                                sl = slice(tb * 512, (tb + 1) * 512)
                                A("dve", lambda e, b=b, sl=sl: e.tensor_tensor(tmp1[b][:], pB[b][:], sinT[:, sl], ALU.mult),
                                  reads=[f"pB{b}", "sinT"], writes=[f"tmp1{b}"])
                                A("dve", lambda e, b=b, sl=sl: e.tensor_tensor(tmp2[b][:], pA[b][:], cosT[:, sl], ALU.mult),
                                  reads=[f"pA{b}", "cosT"], writes=[f"tmp2{b}"])
                                i0 = tb * 512 // d
                                ni = 512 // d
                                oap = ox[:].rearrange("p (r i) -> p r i", r=d)[:, :, i0:i0 + ni]
                                A("pool", lambda e, b=b, oap=oap, d=d: e.tensor_tensor(
                                    oap, tmp1[b][:].rearrange("p (i r) -> p r i", r=d), tmp2[b][:].rearrange("p (i r) -> p r i", r=d), ALU.add),
                                  reads=[f"tmp1{b}", f"tmp2{b}"], writes=[f"outX{hp}"])
                            row = blk * 512 + hp * 128
                            A("sp", lambda e, hp=hp, row=row, dstT=dstT: e.dma_start(out=dstT[row:row + 128, :], in_=outX[hp][:]),
                              reads=[f"outX{hp}"], writes=[("scr_qk", qk, blk, hp)], dma_key=f"outX{hp}")
                if cut <= 6:
                    S_.barrier(); continue
                for grp in range(8):
                    wi = load_w(OFF_X + grp * 512, 512)
                    for j in range(4):
                        ft = grp * 4 + j
                        rb = ft % 2
                        for k5 in range(5):
                            A("dve", lambda e, k5=k5, ft=ft: e.tensor_scalar(dg[:, k5, :], ident[:], cw[:, ft, k5:k5 + 1], None, ALU.mult),
                              reads=["consts", "cw"], writes=["dg"])
                        for tb in range(TB):
                            b = tb % 2
                            for k in range(8):
                                A("pe", lambda e, k=k, b=b, tb=tb, j=j, wi=wi: e.matmul(
                                    pA[b][:], wsl[wi][:, k, j * 128:(j + 1) * 128], hT[:, k, tb * 512:(tb + 1) * 512],
                                    start=(k == 0), stop=(k == 7)),
                                  reads=hT_all[tb * 4:(tb + 1) * 4] + [f"wsl{wi}"], writes=[f"pA{b}"])
                            A("dve", lambda e, b=b, tb=tb, rb=rb: e.tensor_copy(rawT[rb][:, 2 + tb * 512:2 + (tb + 1) * 512], pA[b][:]),
                              reads=[f"pA{b}"], writes=[f"rawT{rb}"])
                        for tb in range(TB):
                            b = tb % 2
                            for k5 in range(5):
                                A("pe", lambda e, k5=k5, b=b, tb=tb, rb=rb: e.matmul(
                                    pB[b][:], dg[:, k5, :], rawT[rb][:, tb * 512 + k5:tb * 512 + k5 + 512], start=(k5 == 0), stop=(k5 == 4)),
                                  reads=["dg", f"rawT{rb}"], writes=[f"pB{b}"])
                            A("act", lambda e, b=b, tb=tb, j=j, ft=ft: e.activation(
                                outX[j][:, tb * 512:(tb + 1) * 512], pB[b][:], AF.Silu, bias=cb[:, ft:ft + 1]),
                              reads=[f"pB{b}", "cb"], writes=[f"outX{j}"])
                    if grp < 6:
                        dst, clo, dkey = (scr_x, grp * 512, "scr_x") if grp < 4 else (scr_B, (grp - 4) * 512, "scr_B")
                        for t in range(NT):
                            b = t % 2
                            for j in range(4):
                                A("pe", lambda e, b=b, j=j, t=t: e.transpose(pT[b][:, j * 128:(j + 1) * 128], outX[j][:, t * 128:(t + 1) * 128], ident[:]),
                                  reads=[f"outX{j}", "consts"], writes=[f"pT{b}"])
                            si = stage()
                            A("dve", lambda e, b=b, si=si: e.tensor_copy(stg[si][:], pT[b][:, 0:512]), reads=[f"pT{b}"], writes=[f"stg{si}"])
                            A("sp", lambda e, t=t, si=si, dst=dst, clo=clo: e.dma_start(out=dst[t * 128:(t + 1) * 128, clo:clo + 512], in_=stg[si][:]),
                              reads=[f"stg{si}"], writes=[(dkey, t)], dma_key=f"stg{si}")
                    if grp >= 4:
                        dstT = scr_BT if grp < 6 else scr_CT
                        rlo = (grp - 4) * 512 if grp < 6 else (grp - 6) * 512
                        for j in range(4):
                            A("sp", lambda e, j=j, dstT=dstT, rlo=rlo: e.dma_start(out=dstT[rlo + j * 128:rlo + (j + 1) * 128, :], in_=outX[j][:]),
                              reads=[f"outX{j}"], writes=[("scr_BCT", grp, j)], dma_key=f"outX{j}")
            S_.barrier()
            if stop_after <= 2:
                continue
            with ExitStack() as st:
                def sb(name, shape, dt):
                    return st.enter_context(nc.sbuf_tensor(f"{name}_s{seq}", shape, dt))

                def ps(name, shape, dt):
                    return st.enter_context(nc.psum_tensor(f"{name}_s{seq}", shape, dt))

                wssd = sb("wssd", [128, 16, 1024], BF16)
                nwl = sb("nwl", [128, 16], F32)
                xc = [sb(f"xc{i}", [128, 2048], BF16) for i in range(2)]
                Bc = [sb(f"Bc{i}", [128, 1024], BF16) for i in range(2)]
                BTc = [sb(f"BTc{i}", [128, 8, 128], BF16) for i in range(2)]
                CTc = [sb(f"CTc{i}", [128, 8, 128], BF16) for i in range(2)]
                dtc = [sb(f"dtc{i}", [128, 64], F32) for i in range(2)]
                zc = [sb(f"zc{i}", [128, 2048], BF16) for i in range(2)]
                yfc = [sb(f"yfc{i}", [128, 2048], BF16) for i in range(2)]
                gc = [sb(f"gc{i}", [128, 1024], BF16) for i in range(2)]
                ST = sb("ST", [128, 2048], F32)
                STb = sb("STb", [128, 2048], BF16)
                sm = sb("sm", [128, 128], F32)
                sm2 = sb("sm2", [128, 128], F32)
                dAhi = sb("dAhi", [128, 32], BF16)
                dAlo = sb("dAlo", [128, 32], BF16)
                CBm = sb("CBm", [128, 128], BF16)
                Eb = sb("Eb", [128, 512], BF16)
                Mb = sb("Mb", [128, 4, 128], BF16)
                xw = sb("xw", [128, 256], BF16)
                tq = sb("tq", [128, 256], F32)
                tq2 = sb("tq2", [128, 256], F32)
                Ych = sb("Ych", [128, 2048], F32)
                Yst = [sb(f"Yst{i}", [128, 2048], BF16) for i in range(2)]
                Gb = sb("Gb", [128, 2048], BF16)
                GT = sb("GT", [128, 16, 128], BF16)
                junk2 = sb("junk2", [128, 2048], BF16)
                r8 = sb("r8", [128, 4], F32)
                m1s = [sb(f"m1s{i}", [128, 1024], BF16) for i in range(2)]
                pCB = ps("pCB", [128, 512], F32)
                pSeg = ps("pSeg", [128, 512], F32)
                pYY = ps("pYY", [128, 512], F32)
                pS = ps("pS", [128, 512], F32)
                pT2 = ps("pT2", [128, 1024], BF16)
                pO = ps("pO", [128, 1024], F32)

                for k in range(2):
                    A("pool", lambda e, k=k: e.dma_start(out=wssd[:, k * 8:(k + 1) * 8, :],
                                                         in_=w_ssd[k * 1024:(k + 1) * 1024, :].rearrange("(k p) c -> p k c", p=128)),
                      writes=["wssd"], dma_key="wssd")
                A("sp", lambda e: e.dma_start(out=nwl[:], in_=nw_l), writes=["nwl"], dma_key="nwl")
                for k in range(16):
                    A("dve", lambda e, k=k: e.tensor_scalar(wssd[:, k, :], wssd[:, k, :], nwl[:, k:k + 1], None, ALU.mult),
                      reads=["wssd", "nwl"], writes=["wssd"])
                cnt = 0
                for dr in range(2):
                    A("dve", lambda e: e.memset(ST[:], 0.0), writes=[("ST", g_) for g_ in range(8)])
                    A("dve", lambda e: e.memset(STb[:], 0.0), writes=[("STb", g_) for g_ in range(8)])
                    tri_f = cst_f[:, dr, :]
                    tri_b = cst_b[:, 0 + 2 * dr, :]
                    ntri_b = cst_b[:, 1 + 2 * dr, :]
                    neg_b = cst_b[:, 4 + dr, :]
                    msk_b = cst_b[:, 6 + dr, :]
                    for ci in range(NT):
                        c = ci if dr == 0 else NT - 1 - ci
                        b = cnt % 2
                        cnt += 1
                        rows = slice(c * 128, (c + 1) * 128)
                        A("sp", lambda e, b=b, rows=rows: e.dma_start(out=xc[b][:], in_=scr_x[rows, :]), reads=[("scr_x", c)], writes=[f"xc{b}"], dma_key=f"xc{b}")
                        A("sp", lambda e, b=b, rows=rows: e.dma_start(out=Bc[b][:], in_=scr_B[rows, :]), reads=[("scr_B", c)], writes=[f"Bc{b}"], dma_key=f"Bc{b}")
                        A("sp", lambda e, b=b, rows=rows: e.dma_start(out=BTc[b][:], in_=scr_BT[:, rows].rearrange("(g n) t -> n g t", n=128)),
                          reads=[("scr_BCT", gg, jj) for gg in (4, 5) for jj in range(4)], writes=[f"BTc{b}"], dma_key=f"BTc{b}")
                        A("sp", lambda e, b=b, rows=rows: e.dma_start(out=CTc[b][:], in_=scr_CT[:, rows].rearrange("(g n) t -> n g t", n=128)),
                          reads=[("scr_BCT", gg, jj) for gg in (6, 7) for jj in range(4)], writes=[f"CTc{b}"], dma_key=f"CTc{b}")
                        A("sp", lambda e, b=b, rows=rows: e.dma_start(out=dtc[b][:], in_=scr_dt[rows, :]), reads=[("scr_dt", c)], writes=[f"dtc{b}"], dma_key=f"dtc{b}")
                        if dr == 1:
                            A("sp", lambda e, b=b, rows=rows: e.dma_start(out=zc[b][:], in_=scr_z[rows, :]), reads=[("scr_z", c)], writes=[f"zc{b}"], dma_key=f"zc{b}")
                            A("sp", lambda e, b=b, rows=rows: e.dma_start(out=yfc[b][:], in_=scr_yf[rows, :]), reads=[("scr_yf", c)], writes=[f"yfc{b}"], dma_key=f"yfc{b}")
                            A("sp", lambda e, b=b, rows=rows: e.dma_start(out=gc[b][:], in_=scr_gate[rows, 0:1024]), reads=[("scr_gate", c)], writes=[f"gc{b}"], dma_key=f"gc{b}")
                        o32 = dr * 32
                        A("dve", lambda e, b=b, o32=o32: e.tensor_tensor(sm[:, 0:32], dtc[b][:, o32:o32 + 32], smallc[:, o32:o32 + 32], ALU.add),
                          reads=[f"dtc{b}", "smallc"], writes=["sm0"])
                        A("act", lambda e: e.activation(sm[:, 32:64], sm[:, 0:32], AF.Exp), reads=["sm0"], writes=["sm1"])
                        A("act", lambda e: e.activation(sm[:, 64:96], sm[:, 32:64], AF.Ln, bias=1.0), reads=["sm1"], writes=["dt"])
                        A("dve", lambda e, o32=o32: e.tensor_tensor(sm[:, 96:128], sm[:, 64:96], smallc[:, 64 + o32:96 + o32], ALU.mult),
                          reads=["dt", "smallc"], writes=["dA"])
                        A("dve", lambda e: e.tensor_copy(dAhi[:], sm[:, 96:128]), reads=["dA"], writes=["dAhi"])
                        A("dve", lambda e: e.tensor_tensor(dAlo[:], sm[:, 96:128], dAhi[:], ALU.subtract), reads=["dA", "dAhi"], writes=["dAlo"])
                        A("pe", lambda e, tri_f=tri_f: e.matmul(pS[:, 256:288], tri_f, sm[:, 96:128], start=True, stop=True), reads=["dA", "consts"], writes=["pSm"])
                        A("pe", lambda e: e.matmul(pS[:, 288:320], cst_f[:, 2, :], sm[:, 96:128], start=True, stop=True), reads=["dA", "consts"], writes=["pSm"])
                        A("act", lambda e: e.copy(sm2[:, 0:32], pS[:, 256:288]), reads=["pSm"], writes=["asb"])
                        A("act", lambda e: e.activation(sm2[:, 32:64], pS[:, 256:288], AF.Exp), reads=["pSm"], writes=["ea"])
                        A("act", lambda e: e.activation(sm2[:, 64:96], pS[:, 288:320], AF.Exp), reads=["pSm"], writes=["eal"])
                        A("dve", lambda e: e.tensor_tensor(sm2[:, 96:128], pS[:, 288:320], sm2[:, 0:32], ALU.subtract), reads=["pSm", "asb"], writes=["wv"])
                        A("act", lambda e: e.activation(sm2[:, 96:128], sm2[:, 96:128], AF.Exp), reads=["wv"], writes=["wv"])
                        A("dve", lambda e: e.tensor_tensor(sm2[:, 96:128], sm2[:, 96:128], sm[:, 64:96], ALU.mult), reads=["wv", "dt"], writes=["wv"])
                        for g in range(8):
                            A("pe", lambda e, b=b, g=g: e.matmul(pCB[:, 0:128], BTc[b][:, g, :], CTc[b][:, g, :], start=True, stop=True),
                              reads=[f"BTc{b}", f"CTc{b}"], writes=["pCB"])
                            A("dve", lambda e, msk_b=msk_b: e.tensor_tensor(CBm[:], pCB[:, 0:128], msk_b, ALU.mult), reads=["pCB", "consts"], writes=["CBm"])
                            for j in range(4):
                                h = g * 4 + j
                                osl = pSeg[:, j * 128:(j + 1) * 128]
                                hi = dAhi[:, h:h + 1].to_broadcast([128, 128])
                                lo = dAlo[:, h:h + 1].to_broadcast([128, 128])
                                A("pe", lambda e, osl=osl, hi=hi, tri_b=tri_b: e.matmul(osl, hi, tri_b, start=True, stop=False), reads=["dAhi", "consts"], writes=["pSeg"])
                                A("pe", lambda e, osl=osl, lo=lo, tri_b=tri_b: e.matmul(osl, lo, tri_b, start=False, stop=False), reads=["dAlo", "consts"], writes=["pSeg"])
                                A("pe", lambda e, osl=osl, hi=hi, ntri_b=ntri_b: e.matmul(osl, ntri_b, hi, start=False, stop=False), reads=["dAhi", "consts"], writes=["pSeg"])
                                A("pe", lambda e, osl=osl, lo=lo, ntri_b=ntri_b: e.matmul(osl, ntri_b, lo, start=False, stop=False), reads=["dAlo", "consts"], writes=["pSeg"])
                                A("pe", lambda e, osl=osl, neg_b=neg_b: e.matmul(osl, ident[:], neg_b, start=False, stop=True), reads=["consts"], writes=["pSeg"])
                            A("act", lambda e: e.activation(Eb[:], pSeg[:], AF.Exp), reads=["pSeg"], writes=["Eb"])
                            for j in range(4):
                                h = g * 4 + j
                                A("dve", lambda e, j=j, h=h: e.scalar_tensor_tensor(
                                    Mb[:, j, :], Eb[:, j * 128:(j + 1) * 128], sm[:, 64 + h:65 + h], CBm[:], ALU.mult, ALU.mult),
                                  reads=["Eb", "dt", "CBm"], writes=[("Mb", j)])
                            for j in range(4):
                                h = g * 4 + j
                                A("pe", lambda e, j=j, h=h, b=b: e.matmul(pYY[:, j * 64:(j + 1) * 64], Mb[:, j, :], xc[b][:, h * 64:(h + 1) * 64], start=True, stop=True),
                                  reads=[("Mb", j), f"xc{b}"], writes=["pY"])
                            A("pe", lambda e, g=g, b=b: e.matmul(pYY[:, 256:512], CTc[b][:, g, :], STb[:, g * 256:(g + 1) * 256], start=True, stop=True),
                              reads=[f"CTc{b}", ("STb", g)], writes=["pYo"])
                            eab = sm2[:, 32 + g * 4:36 + g * 4].unsqueeze(2).to_broadcast([128, 4, 64])
                            A("dve", lambda e, eab=eab: e.tensor_tensor(tq[:].rearrange("p (j q) -> p j q", j=4), pYY[:, 256:512].rearrange("p (j q) -> p j q", j=4), eab, ALU.mult),
                              reads=["pYo", "ea"], writes=["tq"])
                            A("dve", lambda e, g=g: e.tensor_tensor(Ych[:, g * 256:(g + 1) * 256], pYY[:, 0:256], tq[:], ALU.add),
                              reads=["pY", "tq"], writes=[("Ych", g)])
                            wb = sm2[:, 96 + g * 4:100 + g * 4].unsqueeze(2).to_broadcast([128, 4, 64])
                            A("pool", lambda e, g=g, b=b, wb=wb: e.tensor_tensor(xw[:].rearrange("p (j q) -> p j q", j=4),
                                                                                xc[b][:, g * 256:(g + 1) * 256].rearrange("p (j q) -> p j q", j=4), wb, ALU.mult),
                              reads=[f"xc{b}", "wv"], writes=["xw"])
                            A("pe", lambda e, g=g, b=b: e.matmul(pS[:, 0:256], Bc[b][:, g * 128:(g + 1) * 128], xw[:], start=True, stop=True),
                              reads=[f"Bc{b}", "xw"], writes=["pS"])
                            elb = sm2[:, 64 + g * 4:68 + g * 4].unsqueeze(2).to_broadcast([128, 4, 64])
                            A("pool", lambda e, g=g, elb=elb: e.tensor_tensor(tq2[:].rearrange("p (j q) -> p j q", j=4),
                                                                             ST[:, g * 256:(g + 1) * 256].rearrange("p (j q) -> p j q", j=4), elb, ALU.mult),
                              reads=[("ST", g), "eal"], writes=["tq2"])
                            A("dve", lambda e, g=g: e.tensor_tensor(ST[:, g * 256:(g + 1) * 256], pS[:, 0:256], tq2[:], ALU.add),
                              reads=["pS", "tq2"], writes=[("ST", g)])
                            A("act", lambda e, g=g: e.copy(STb[:, g * 256:(g + 1) * 256], ST[:, g * 256:(g + 1) * 256]), reads=[("ST", g)], writes=[("STb", g)])
                        Yall = [("Ych", g) for g in range(8)]
                        if dr == 0:
                            A("act", lambda e, b=b: e.copy(Yst[b][:], Ych[:]), reads=Yall, writes=[f"Yst{b}"])
                            A("sp", lambda e, b=b, rows=rows: e.dma_start(out=scr_yf[rows, :], in_=Yst[b][:]), reads=[f"Yst{b}"], writes=[("scr_yf", c)], dma_key=f"Yst{b}")
                            continue
                        A("dve", lambda e, b=b: e.tensor_tensor(Ych[:], Ych[:], yfc[b][:], ALU.add), reads=Yall + [f"yfc{b}"], writes=Yall)
                        Db = smallc[:, 128:160].unsqueeze(2).to_broadcast([128, 32, 64])
                        A("pool", lambda e, b=b, Db=Db: e.tensor_tensor(Yst[0][:].rearrange("p (h q) -> p h q", h=32), xc[b][:].rearrange("p (h q) -> p h q", h=32), Db, ALU.mult),
                          reads=[f"xc{b}", "smallc"], writes=["Yst0"])
                        A("dve", lambda e: e.tensor_tensor(Ych[:], Ych[:], Yst[0][:], ALU.add), reads=Yall + ["Yst0"], writes=Yall)
                        A("dve", lambda e, b=b: e.tensor_tensor(Gb[:], Ych[:], zc[b][:], ALU.mult), reads=Yall + [f"zc{b}"], writes=["Gb"])
                        A("act", lambda e: e.activation(junk2[:], Gb[:], AF.Square, accum_out=r8[:, 0:1]), reads=["Gb"], writes=["junk2", "r0"])
                        A("act", lambda e: e.activation(r8[:, 1:2], r8[:, 0:1], AF.Sqrt, bias=EPS, scale=1.0 / DIN), reads=["r0"], writes=["r1"])
                        A("dve", lambda e: e.reciprocal(r8[:, 2:3], r8[:, 1:2]), reads=["r1"], writes=["r2"])
                        for q4 in range(2):
                            for kk in range(8):
                                k = q4 * 8 + kk
                                A("pe", lambda e, k=k, kk=kk: e.transpose(pT2[:, kk * 128:(kk + 1) * 128], Gb[:, k * 128:(k + 1) * 128], ident[:]),
                                  reads=["Gb", "consts"], writes=["pT2"])
                            A("act", lambda e, q4=q4: e.copy(GT[:, q4 * 8:(q4 + 1) * 8, :], pT2[:].rearrange("p (k t) -> p k t", k=8)), reads=["pT2"], writes=["GT"])
                        for hf in range(2):
                            for k in range(16):
                                A("pe", lambda e, k=k, hf=hf: e.matmul(pO[:, hf * 512:(hf + 1) * 512], GT[:, k, :], wssd[:, k, hf * 512:(hf + 1) * 512],
                                                                        start=(k == 0), stop=(k == 15)),
                                  reads=["GT", "wssd"], writes=["pO"])
                        A("dve", lambda e, b=b: e.scalar_tensor_tensor(m1s[b][:], pO[:], r8[:, 2:3], gc[b][:], ALU.mult, ALU.mult),
                          reads=["pO", "r2", f"gc{b}"], writes=[f"m1s{b}"])
                        A("sp", lambda e, b=b, rows=rows: e.dma_start(out=scr_m1[rows, :], in_=m1s[b][:]), reads=[f"m1s{b}"], writes=[("scr_m1", c)], dma_key=f"m1s{b}")
            S_.barrier()
            if stop_after <= 4:
                continue
            with ExitStack() as st:
                def sb(name, shape, dt):
                    return st.enter_context(nc.sbuf_tensor(f"{name}_a{seq}", shape, dt))

                def ps(name, shape, dt):
                    return st.enter_context(nc.psum_tensor(f"{name}_a{seq}", shape, dt))

                QT = sb("QT", [128, 4, S], BF16)
                KT = sb("KT", [128, 4, S], BF16)
                Vp = sb("Vp", [128, NT, 8, 65], BF16)
                PT = [sb(f"PT{i}", [128, 384], BF16) for i in range(2)]
                ost = [sb(f"ost{i}", [128, 8, 65], BF16) for i in range(2)]
                pSc = [ps(f"pSc{i}", [128, 512], F32) for i in range(2)]
                pOa = [ps(f"pOa{i}", [128, 1024], F32) for i in range(2)]
                A("dve", lambda e: e.memset(Vp[:].rearrange("p t h c -> p (t h) c")[:, :, 64:65], 1.0), writes=["Vp"])
                ucnt = 0
                qcnt = 0
                for g in range(3):
                    d = DIL[g]
                    n = S // d
                    nq = n // 128
                    for r in range(d):
                        for hp in range(4):
                            row = g * 512 + hp * 128
                            A("sp", lambda e, hp=hp, row=row, r=r, n=n: e.dma_start(out=QT[:, hp, 0:n], in_=scr_qT[row:row + 128, r * n:(r + 1) * n]),
                              reads=[("scr_qk", 0, g, hp)], writes=["QT"], dma_key="QT")
                            A("sp", lambda e, hp=hp, row=row, r=r, n=n: e.dma_start(out=KT[:, hp, 0:n], in_=scr_kT[row:row + 128, r * n:(r + 1) * n]),
                              reads=[("scr_qk", 1, g, hp)], writes=["KT"], dma_key="KT")
                        for i in range(nq):
                            A("sp", lambda e, g=g, r=r, n=n, i=i: e.dma_start(out=Vp[:, i, :, 0:64],
                                                                          in_=scr_v[g, r * n + i * 128:r * n + (i + 1) * 128, :].rearrange("k (h c) -> k h c", h=8)),
                              reads=[("scr_v", g)], writes=["Vp"], dma_key="Vp")
                        for i in range(nq):
                            qb = qcnt % 2
                            qcnt += 1
                            offs = [o for o in (-1, 0, 1) if 0 <= i + o < nq]
                            for h in range(8):
                                hp, hh = h // 2, h % 2
                                ub = ucnt % 2
                                ucnt += 1
                                for oi, o in enumerate(offs):
                                    ks = slice((i + o) * 128, (i + o + 1) * 128)
                                    A("pe", lambda e, ub=ub, oi=oi, hp=hp, hh=hh, ks=ks, i=i: e.matmul(
                                        pSc[ub][:, oi * 128:(oi + 1) * 128], KT[hh * 64:(hh + 1) * 64, hp, ks], QT[hh * 64:(hh + 1) * 64, hp, i * 128:(i + 1) * 128],
                                        start=True, stop=False), reads=["QT", "KT"], writes=[f"pSc{ub}"])
                                    A("pe", lambda e, ub=ub, oi=oi, o=o: e.matmul(pSc[ub][:, oi * 128:(oi + 1) * 128], ident[:], band[:, o + 1, :], start=False, stop=True),
                                      reads=["consts"], writes=[f"pSc{ub}"])
                                no = len(offs)
                                A("act", lambda e, ub=ub, no=no: e.activation(PT[ub][:, 0:no * 128], pSc[ub][:, 0:no * 128], AF.Exp),
                                  reads=[f"pSc{ub}"], writes=[f"PT{ub}"])
                                for oi, o in enumerate(offs):
                                    A("pe", lambda e, ub=ub, qb=qb, oi=oi, o=o, h=h, i=i, no=no: e.matmul(
                                        pOa[qb][:, (h // 4) * 512 + (h % 4) * 65:(h // 4) * 512 + (h % 4) * 65 + 65], PT[ub][:, oi * 128:(oi + 1) * 128], Vp[:, i + o, h, :],
                                        start=(oi == 0), stop=(oi == no - 1)), reads=[f"PT{ub}", "Vp"], writes=[f"pOa{qb}"])
                            for hf in range(2):
                                A("act" if hf else "dve", lambda e, qb=qb, hf=hf: (e.copy if hf else e.tensor_copy)(
                                    ost[qb][:, hf * 4:(hf + 1) * 4, :], pOa[qb][:, hf * 512:hf * 512 + 260].rearrange("p (h c) -> p h c", h=4)),
                                  reads=[f"pOa{qb}"], writes=[f"ost{qb}"])
                            t0 = i * 128 * d + r
                            A("sp", lambda e, qb=qb, g=g, t0=t0, d=d: e.dma_start(out=scr_o[g, t0:t0 + 127 * d + 1:d, :, :], in_=ost[qb][:]),
                              reads=[f"ost{qb}"], writes=[("scr_o", g)], dma_key=f"ost{qb}")
            S_.barrier()
            if stop_after <= 5:
                continue
            with ExitStack() as st:
                def sb(name, shape, dt):
                    return st.enter_context(nc.sbuf_tensor(f"{name}_f{seq}", shape, dt))

                def ps(name, shape, dt):
                    return st.enter_context(nc.psum_tensor(f"{name}_f{seq}", shape, dt))

                TBLK = 512
                NTB = TBLK // 128
                watt = sb("watt", [128, 4, 1024], BF16)
                wout = sb("wout", [128, 8, 1024], BF16)
                gns = sb("gns", [128, 3, 1024], F32)
                x1 = sb("x1", [128, NTB, 1024], F32)
                h2T = sb("h2T", [128, 8, TBLK], BF16)
                actT = sb("actT", [128, 22, TBLK], BF16)
                gT = sb("gT", [128, TBLK], BF16)
                wfi = [sb(f"wfi{i}", [128, 8, 128], BF16) for i in range(3)]
                wdn = sb("wdn", [128, 22, 1024], BF16)
                o3 = [sb(f"o3{i}", [128, 3, 8, 65], BF16) for i in range(2)]
                osum = sb("osum", [128, 8, 65], F32)
                rl = sb("rl", [128, 8], F32)
                Ob = sb("Ob", [128, 512], BF16)
                OT = sb("OT", [128, 4, 128], BF16)
                gat = [sb(f"gat{i}", [128, 1024], BF16) for i in range(2)]
                m1c = [sb(f"m1c{i}", [128, 1024], BF16) for i in range(2)]
                mrg = sb("mrg", [128, 1024], BF16)
                mrgT = sb("mrgT", [128, 8, 128], BF16)
                tmpf = sb("tmpf", [128, 1024], F32)
                xin = [sb(f"xin{i}", [128, 1024], F32) for i in range(2)]
                h2 = sb("h2", [128, 1024], BF16)
                jk = sb("jk", [128, 1024], F32)
                s8 = sb("s8", [128, 8], F32)
                yo = [sb(f"yo{i}", [128, 1024], F32) for i in range(2)]
                pT3 = ps("pT3", [128, 1024], BF16)
                pM = ps("pM", [128, 1024], F32)
                pG = [ps(f"pG{i}", [128, 512], F32) for i in range(2)]
                pU = [ps(f"pU{i}", [128, 512], F32) for i in range(2)]

                A("pool", lambda e: e.dma_start(out=watt[:], in_=w_attn.rearrange("(k p) c -> p k c", p=128)), writes=["watt"], dma_key="watt")
                for q2 in range(2):
                    A("pool", lambda e, q2=q2: e.dma_start(out=wdn[:, q2 * 11:(q2 + 1) * 11, :],
                                                           in_=w_ffn_down[q2 * 1408:(q2 + 1) * 1408, :].rearrange("(k p) c -> p k c", p=128)),
                      writes=["wdn"], dma_key="wdn")
                A("pool", lambda e: e.dma_start(out=wout[:], in_=w_out.rearrange("(k p) c -> p k c", p=128)), writes=["wout"], dma_key="wout")
                for gi, gsrc in enumerate((norm_mix_post, norm_ffn_pre, norm_ffn_post)):
                    A("sp", lambda e, gi=gi, gsrc=gsrc: e.dma_start(out=gns[:, gi, :], in_=gsrc.partition_broadcast(128)), writes=["gns"], dma_key="gns")

                def rms_scale(src_ap, ss_col, rd):
                    A("act", lambda e: e.activation(jk[:], src_ap, AF.Square, accum_out=s8[:, ss_col:ss_col + 1]), reads=rd, writes=["jk", ("s8", ss_col)])
                    A("act", lambda e: e.activation(s8[:, ss_col + 1:ss_col + 2], s8[:, ss_col:ss_col + 1], AF.Sqrt, bias=EPS, scale=1.0 / D),
                      reads=[("s8", ss_col)], writes=[("s8", ss_col + 1)])
                    A("dve", lambda e: e.reciprocal(s8[:, ss_col + 1:ss_col + 2], s8[:, ss_col + 1:ss_col + 2]), reads=[("s8", ss_col + 1)], writes=[("s8", ss_col + 1)])

                wcn = [0]
                for blk in range(S // TBLK):
                    for tt in range(NTB):
                        t = blk * NTB + tt
                        b = t % 2
                        rows = slice(t * 128, (t + 1) * 128)
                        for g3 in range(3):
                            A("sp", lambda e, b=b, rows=rows, g3=g3: e.dma_start(out=o3[b][:, g3, :, :], in_=scr_o[g3, rows, :, :]),
                              reads=[("scr_o", g3)], writes=[f"o3{b}"], dma_key=f"o3{b}")
                        A("sp", lambda e, b=b, rows=rows: e.dma_start(out=gat[b][:], in_=scr_gate[rows, 1024:2048]), reads=[("scr_gate", t)], writes=[f"gat{b}"], dma_key=f"gat{b}")
                        A("sp", lambda e, b=b, rows=rows: e.dma_start(out=m1c[b][:], in_=scr_m1[rows, :]), reads=[("scr_m1", t)], writes=[f"m1c{b}"], dma_key=f"m1c{b}")
                        A("sp", lambda e, b=b, rows=rows: e.dma_start(out=xin[b][:], in_=x_in[seq, rows, :]), writes=[f"xin{b}"], dma_key=f"xin{b}")
                        A("dve", lambda e, b=b: e.tensor_tensor(osum[:], o3[b][:, 0, :, :], o3[b][:, 1, :, :], ALU.add), reads=[f"o3{b}"], writes=["osum"])
                        A("dve", lambda e, b=b: e.tensor_tensor(osum[:], osum[:], o3[b][:, 2, :, :], ALU.add), reads=[f"o3{b}", "osum"], writes=["osum"])
                        A("dve", lambda e: e.reciprocal(rl[:], osum[:, :, 64]), reads=["osum"], writes=["rl"])
                        A("dve", lambda e: e.tensor_tensor(Ob[:].rearrange("p (h c) -> p h c", h=8), osum[:, :, 0:64], rl[:].unsqueeze(2).to_broadcast([128, 8, 64]), ALU.mult),
                          reads=["osum", "rl"], writes=["Ob"])
                        for k in range(4):
                            A("pe", lambda e, k=k: e.transpose(pT3[:, k * 128:(k + 1) * 128], Ob[:, k * 128:(k + 1) * 128], ident[:]), reads=["Ob", "consts"], writes=["pT3"])
                        A("act", lambda e: e.copy(OT[:], pT3[:, 0:512].rearrange("p (k t) -> p k t", k=4)), reads=["pT3"], writes=["OT"])
                        for hf in range(2):
                            for k in range(4):
                                A("pe", lambda e, k=k, hf=hf: e.matmul(pM[:, hf * 512:(hf + 1) * 512], OT[:, k, :], watt[:, k, hf * 512:(hf + 1) * 512], start=(k == 0), stop=(k == 3)),
                                  reads=["OT", "watt"], writes=["pM"])
                        A("dve", lambda e, b=b: e.tensor_tensor(tmpf[:], pM[:], gat[b][:], ALU.mult), reads=["pM", f"gat{b}"], writes=["tmpf"])
                        A("pool", lambda e, b=b: e.tensor_tensor(mrg[:], tmpf[:], m1c[b][:], ALU.add), reads=["tmpf", f"m1c{b}"], writes=["mrg"])
                        for k in range(8):
                            A("pe", lambda e, k=k: e.transpose(pT3[:, k * 128:(k + 1) * 128], mrg[:, k * 128:(k + 1) * 128], ident[:]), reads=["mrg", "consts"], writes=["pT3"])
                        A("act", lambda e: e.copy(mrgT[:], pT3[:].rearrange("p (k t) -> p k t", k=8)), reads=["pT3"], writes=["mrgT"])
                        for hf in range(2):
                            for k in range(8):
                                A("pe", lambda e, k=k, hf=hf: e.matmul(pM[:, hf * 512:(hf + 1) * 512], mrgT[:, k, :], wout[:, k, hf * 512:(hf + 1) * 512], start=(k == 0), stop=(k == 7)),
                                  reads=["mrgT", "wout"], writes=["pM"])
                        rms_scale(pM[:], 0, ["pM"])
                        A("dve", lambda e: e.scalar_tensor_tensor(tmpf[:], pM[:], s8[:, 1:2], gns[:, 0, :], ALU.mult, ALU.mult), reads=["pM", ("s8", 1), "gns"], writes=["tmpf"])
                        A("pool", lambda e, b=b, tt=tt: e.tensor_tensor(x1[:, tt, :], tmpf[:], xin[b][:], ALU.add), reads=["tmpf", f"xin{b}"], writes=[("x1", tt)])
                        rms_scale(x1[:, tt, :], 2, [("x1", tt)])
                        A("dve", lambda e, tt=tt: e.scalar_tensor_tensor(h2[:], x1[:, tt, :], s8[:, 3:4], gns[:, 1, :], ALU.mult, ALU.mult), reads=[("x1", tt), ("s8", 3), "gns"], writes=["h2"])
                        for k in range(8):
                            A("pe", lambda e, k=k: e.transpose(pT3[:, k * 128:(k + 1) * 128], h2[:, k * 128:(k + 1) * 128], ident[:]), reads=["h2", "consts"], writes=["pT3"])
                        A("act", lambda e, tt=tt: e.copy(h2T[:, :, tt * 128:(tt + 1) * 128], pT3[:].rearrange("p (k t) -> p k t", k=8)), reads=["pT3"], writes=[("h2T", tt)])
                    h2T_all = [("h2T", tt) for tt in range(NTB)]
                    for f in range(22):
                        wa = wcn[0] % 3
                        wb_ = (wcn[0] + 1) % 3
                        wcn[0] += 2
                        A("pool", lambda e, f=f, wa=wa: e.dma_start(out=wfi[wa][:], in_=w_ffn_in[:, f * 128:(f + 1) * 128].rearrange("(k p) c -> p k c", p=128)),
                          writes=[f"wfi{wa}"], dma_key=f"wfi{wa}")
                        A("pool", lambda e, f=f, wb_=wb_: e.dma_start(out=wfi[wb_][:], in_=w_ffn_in[:, FFN + f * 128:FFN + (f + 1) * 128].rearrange("(k p) c -> p k c", p=128)),
                          writes=[f"wfi{wb_}"], dma_key=f"wfi{wb_}")
                        for half in range(TBLK // 512):
                            pb = (f * 2 + half) % 2
                            ts_ = slice(half * 512, (half + 1) * 512)
                            for k in range(8):
                                A("pe", lambda e, k=k, pb=pb, wa=wa, ts_=ts_: e.matmul(pG[pb][:], wfi[wa][:, k, :], h2T[:, k, ts_], start=(k == 0), stop=(k == 7)),
                                  reads=h2T_all + [f"wfi{wa}"], writes=[f"pG{pb}"])
                            for k in range(8):
                                A("pe", lambda e, k=k, pb=pb, wb_=wb_, ts_=ts_: e.matmul(pU[pb][:], wfi[wb_][:, k, :], h2T[:, k, ts_], start=(k == 0), stop=(k == 7)),
                                  reads=h2T_all + [f"wfi{wb_}"], writes=[f"pU{pb}"])
                            A("act", lambda e, pb=pb, ts_=ts_: e.activation(gT[:, ts_], pG[pb][:], AF.Silu), reads=[f"pG{pb}"], writes=[("gT", half)])
                            A("dve", lambda e, pb=pb, ts_=ts_, f=f: e.tensor_tensor(actT[:, f, ts_], pU[pb][:], gT[:, ts_], ALU.mult),
                              reads=[f"pU{pb}", ("gT", half)], writes=[("actT", f)])
                    act_all = [("actT", f) for f in range(22)]
                    for tt in range(NTB):
                        t = blk * NTB + tt
                        b = t % 2
                        for f in range(22):
                            for hf in range(2):
                                A("pe", lambda e, f=f, hf=hf, tt=tt: e.matmul(pM[:, hf * 512:(hf + 1) * 512], actT[:, f, tt * 128:(tt + 1) * 128], wdn[:, f, hf * 512:(hf + 1) * 512],
                                                                          start=(f == 0), stop=(f == 21)),
                                  reads=act_all + ["wdn"], writes=["pM"])
                        rms_scale(pM[:], 4, ["pM"])
                        A("dve", lambda e: e.scalar_tensor_tensor(tmpf[:], pM[:], s8[:, 5:6], gns[:, 2, :], ALU.mult, ALU.mult), reads=["pM", ("s8", 5), "gns"], writes=["tmpf"])
                        A("pool", lambda e, b=b, tt=tt: e.tensor_tensor(yo[b][:], tmpf[:], x1[:, tt, :], ALU.add), reads=["tmpf", ("x1", tt)], writes=[f"yo{b}"])
                        A("sp", lambda e, b=b, t=t: e.dma_start(out=y_out[seq, t * 128:(t + 1) * 128, :], in_=yo[b][:]), reads=[f"yo{b}"], writes=[("y", seq, t)], dma_key=f"yo{b}")
            S_.barrier()
        S_.emit(final_waits=list(S_.dma_count.keys()))
    return nc


def _host_consts(inp_conv_w, inp_conv_b, inp_norm_w, S):
    p = np.arange(128)
    m = p % 64
    rot = (m < 16)
    h2 = (m >= 8) & rot
    fi = np.where(rot, m % 8, 0)
    invf = (500000.0 ** (-(2.0 * fi) / 16.0)).astype(np.float32)
    ang = np.arange(S, dtype=np.float32)[None, :] * invf[:, None]
    cos = np.where(rot[:, None], np.cos(ang), 1.0).astype(np.float32)
    sgn = np.where(rot, np.where(h2, 1.0, -1.0), 0.0).astype(np.float32)
    sin = (np.sin(ang) * sgn[:, None]).astype(np.float32)
    cw = np.ascontiguousarray(inp_conv_w.reshape(5, 32, 128).transpose(2, 1, 0).reshape(128, 160), dtype=np.float32)
    cb = np.ascontiguousarray(inp_conv_b.reshape(32, 128).T, dtype=np.float32)
    nw = np.ascontiguousarray(inp_norm_w.reshape(16, 128).T, dtype=np.float32)
    return {"rot_cos": cos, "rot_sin": sin, "cw_l": cw, "cb_l": cb, "nw_l": nw}


_NC_CACHE = {}


def kernel(**inputs):
    x = np.asarray(inputs["x"], dtype=np.float32)
    B, S, _ = x.shape
    n_cores = 8
    NSEQ = B // n_cores
    key = (S, NSEQ)
    if key not in _NC_CACHE:
        _NC_CACHE[key] = build(S, NSEQ)
    nc = _NC_CACHE[key]
    f = lambda k: np.ascontiguousarray(np.asarray(inputs[k], dtype=np.float32)[0])
    shared = {
        "norm_mix_pre": f("norm_mix_pre").reshape(1, D), "w_in": f("w_in"),
        "ssd_conv_w": f("ssd_conv_w"), "ssd_conv_b": f("ssd_conv_b").reshape(1, 4096),
        "ssd_dt_bias": f("ssd_dt_bias").reshape(1, 64), "ssd_A_log": f("ssd_A_log").reshape(1, 64),
        "ssd_D": f("ssd_D").reshape(1, 32), "ssd_norm_w": f("ssd_norm_w").reshape(1, DIN),
        "w_ssd_branch": f("w_ssd_branch"), "w_attn_branch": f("w_attn_branch"), "w_out": f("w_out"),
        "norm_mix_post": f("norm_mix_post").reshape(1, D), "norm_ffn_pre": f("norm_ffn_pre").reshape(1, D),
        "w_ffn_in": f("w_ffn_in"), "w_ffn_down": f("w_ffn_down"), "norm_ffn_post": f("norm_ffn_post").reshape(1, D),
    }
    shared.update(_host_consts(shared["ssd_conv_w"], shared["ssd_conv_b"], shared["ssd_norm_w"], S))
    in_maps = []
    big = ("w_in", "w_ssd_branch", "w_attn_branch", "w_out", "w_ffn_in", "w_ffn_down", "rot_cos", "rot_sin")
    for c in range(n_cores):
        m = dict(shared)
        for k in big:
            a = shared[k]
            m[k] = np.concatenate([a, np.full((1, a.shape[1]), float(c), np.float32)], axis=0)
        m["x"] = np.ascontiguousarray(x[c * NSEQ:(c + 1) * NSEQ])
        in_maps.append(m)
    res = run_bass_kernel_spmd(nc, in_maps, core_ids=list(range(n_cores)))
    return np.concatenate([np.asarray(r["y"], dtype=np.float32) for r in res.results], axis=0)
```

```python
import numpy as np
from contextlib import ExitStack
import concourse.bass as bass
import concourse.mybir as mybir
from concourse.bass_utils import run_bass_kernel_spmd

F32 = mybir.dt.float32
BF16 = mybir.dt.bfloat16
I32 = mybir.dt.int32
AF = mybir.ActivationFunctionType
ALU = mybir.AluOpType

EPOCH = 20000
D = 1024
DIN = 2048
FFN = 2816
INC = 12864
OFF_Z, OFF_X, OFF_B, OFF_C, OFF_DT, OFF_Q, OFF_K, OFF_V, OFF_G = 0, 2048, 4096, 5120, 6144, 6208, 7744, 9280, 10816
DIL = (1, 4, 16)
EPS = 1e-6
NEGV = -30000.0


class _Rec:
    def __getattr__(self, name):
        return lambda *a, **kw: (name, a, kw)


_REC = _Rec()


class Op:
    __slots__ = ("eng", "fn", "idx", "deps", "signal", "dma_key", "sig_n")

    def __init__(self, eng, fn, dma_key=None):
        self.eng = eng
        self.fn = fn(_REC)
        self.deps = []
        self.signal = False
        self.dma_key = dma_key
        self.sig_n = 0


class Sched:
    ENGS = ("pe", "act", "dve", "pool", "sp")

    def __init__(self, nc):
        self.nc = nc
        self.eng_ops = {e: [] for e in self.ENGS}
        self.last_writer = {}
        self.readers = {}
        self.dma_count = {}
        self.waited = {e: {} for e in self.ENGS}

    def add(self, eng, fn, reads=(), writes=(), dma_key=None):
        op = Op(eng, fn, dma_key)
        op.idx = len(self.eng_ops[eng])
        self.eng_ops[eng].append(op)
        deps = set()
        for r in reads:
            w = self.last_writer.get(r)
            if w is not None:
                deps.add(w)
        for w in writes:
            lw = self.last_writer.get(w)
            if lw is not None:
                deps.add(lw)
            for rd in self.readers.get(w, ()):
                deps.add(rd)
        self._attach(op, deps)
        if dma_key is not None:
            self.dma_count[dma_key] = self.dma_count.get(dma_key, 0) + 1
        for r in reads:
            self.readers.setdefault(r, []).append(op)
        for w in writes:
            self.last_writer[w] = op
            self.readers[w] = []
        return op

    def _attach(self, op, deps):
        eng = op.eng
        best = {}
        for d in deps:
            if d is op:
                continue
            if d.dma_key is not None:
                k = ("dma", d.dma_key)
                v = self.dma_count[d.dma_key]
            else:
                if d.eng == "pe" and eng == "pe" and op.dma_key is None:
                    continue
                k = ("eng", d.eng)
                v = d.idx
            if k not in best or best[k][0] < v:
                best[k] = (v, d)
        wd = self.waited[eng]
        for k, (v, d) in best.items():
            if k in wd and wd[k] >= v:
                continue
            wd[k] = v
            if k[0] == "eng":
                d.signal = True
            op.deps.append((k, v, d))

    def barrier(self):
        if not hasattr(self, "marks"):
            self.marks = []
        self.marks.append({e: sum(1 + len(o.deps) for o in self.eng_ops[e]) for e in self.ENGS})
        lasts = []
        for e in self.ENGS:
            ops = [o for o in self.eng_ops[e] if o.dma_key is None]
            if ops:
                lasts.append(ops[-1])
        dmas = {}
        for e in self.ENGS:
            for o in self.eng_ops[e]:
                if o.dma_key is not None:
                    dmas[o.dma_key] = o
        for e in self.ENGS:
            op = Op(e, lambda eng: eng.nop())
            op.idx = len(self.eng_ops[e])
            self.eng_ops[e].append(op)
            self._attach(op, set(lasts) | set(dmas.values()))
        self.last_writer = {}
        self.readers = {}

    def emit(self, final_waits=()):
        nc = self.nc
        nsig = {}
        for e in self.ENGS:
            n = 0
            for op in self.eng_ops[e]:
                if op.dma_key is None and op.signal:
                    n += 1
                    op.sig_n = n
            nsig[e] = n
        with ExitStack() as st:
            esems = {}
            for e in self.ENGS:
                for ep in range((nsig[e] + EPOCH - 1) // EPOCH):
                    esems[(e, ep)] = st.enter_context(nc.semaphore(f"s_{e}_{ep}"))
            dsems = {}
            for i, k in enumerate(self.dma_count):
                dsems[k] = st.enter_context(nc.semaphore(f"d{i}"))
            block = st.enter_context(nc.Block())

            def run(e, engh):
                for op in self.eng_ops[e]:
                    for (k, v, d) in op.deps:
                        if k[0] == "dma":
                            engh.wait_ge(dsems[k[1]], 16 * v)
                        else:
                            n = d.sig_n
                            engh.wait_ge(esems[(d.eng, (n - 1) // EPOCH)], (n - 1) % EPOCH + 1)
                    name, a_, kw_ = op.fn
                    ins = getattr(engh, name)(*a_, **kw_)
                    if op.dma_key is not None:
                        ins.then_inc(dsems[op.dma_key], 16)
                    elif op.signal:
                        n = op.sig_n
                        ins.then_inc(esems[(e, (n - 1) // EPOCH)], 1)
                if e == "sp":
                    for k in final_waits:
                        engh.wait_ge(dsems[k], 16 * self.dma_count[k])

            @block.tensor
            def _(eng):
                run("pe", eng)

            @block.scalar
            def _(eng):
                run("act", eng)

            @block.vector
            def _(eng):
                run("dve", eng)

            @block.gpsimd
            def _(eng):
                run("pool", eng)

            @block.sync
            def _(eng):
                run("sp", eng)


def build(S, NSEQ, debug=False, stop_after=99, cut=99):
    nc = bass.Bass("TRN2", target_bir_lowering=False)
    NT = S // 128
    TB = S // 512

    def din(name, shape, pad=False):
        if pad:
            return nc.dram_tensor(name, [shape[0] + 1] + list(shape[1:]), F32, kind="ExternalInput").ap()[0:shape[0]]
        return nc.dram_tensor(name, shape, F32, kind="ExternalInput").ap()

    x_in = din("x", [NSEQ, S, D])
    norm_mix_pre = din("norm_mix_pre", [1, D])
    w_in = din("w_in", [D, INC], pad=True)
    conv_w = din("ssd_conv_w", [5, 4096])
    conv_b = din("ssd_conv_b", [1, 4096])
    dt_bias = din("ssd_dt_bias", [1, 64])
    A_log = din("ssd_A_log", [1, 64])
    D_skip = din("ssd_D", [1, 32])
    ssd_norm_w = din("ssd_norm_w", [1, DIN])
    w_ssd = din("w_ssd_branch", [DIN, D], pad=True)
    w_attn = din("w_attn_branch", [512, D], pad=True)
    w_out = din("w_out", [D, D], pad=True)
    norm_mix_post = din("norm_mix_post", [1, D])
    norm_ffn_pre = din("norm_ffn_pre", [1, D])
    w_ffn_in = din("w_ffn_in", [D, 2 * FFN], pad=True)
    w_ffn_down = din("w_ffn_down", [FFN, D], pad=True)
    norm_ffn_post = din("norm_ffn_post", [1, D])
    rot_cos = din("rot_cos", [128, S], pad=True)
    rot_sin = din("rot_sin", [128, S], pad=True)
    cw_l = din("cw_l", [128, 160])
    cb_l = din("cb_l", [128, 32])
    nw_l = din("nw_l", [128, 16])
    y_out = nc.dram_tensor("y", [NSEQ, S, D], F32, kind="ExternalOutput").ap()

    skind = "ExternalOutput" if debug else "Internal"

    def scr(name, shape, dt=BF16):
        return nc.dram_tensor(name, shape, dt, kind=skind).ap()

    scr_z = scr("scr_z", [S, DIN])
    scr_gate = scr("scr_gate", [S, 2 * D])
    scr_x = scr("scr_x", [S, DIN])
    scr_B = scr("scr_B", [S, 1024])
    scr_BT = scr("scr_BT", [1024, S])
    scr_CT = scr("scr_CT", [1024, S])
    scr_dt = scr("scr_dt", [S, 64], F32)
    scr_yf = scr("scr_yf", [S, DIN])
    scr_qT = scr("scr_qT", [1536, S])
    scr_kT = scr("scr_kT", [1536, S])
    scr_v = scr("scr_v", [3, S, 512])
    scr_o = scr("scr_o", [3, S, 8, 65])
    scr_m1 = scr("scr_m1", [S, D])
    scr_h2 = scr("scr_h2", [NSEQ, S, D])
    wfi_bf = nc.dram_tensor("wfi_bf", [D, 2 * FFN], BF16).ap()
    wfd_bf = nc.dram_tensor("wfd_bf", [FFN, D], BF16).ap()

    S_ = Sched(nc)
    A = S_.add

    with ExitStack() as gst:
        def gsb(name, shape, dt):
            return gst.enter_context(nc.sbuf_tensor(name, shape, dt))

        ident = gsb("ident", [128, 128], BF16)
        cst_f = gsb("cst_f", [128, 4, 128], F32)
        cst_b = gsb("cst_b", [128, 8, 128], BF16)
        band = gsb("band", [128, 3, 128], BF16)
        tmpc = gsb("tmpc", [128, 128], F32)
        smallc = gsb("smallc", [128, 64 + 64 + 32], F32)

        def mk_const(dst_f32_ap, fill_base, selects):
            A("pool", lambda e: e.memset(dst_f32_ap, fill_base), writes=["tmpc"])
            for (pat, cm, base, fill) in selects:
                A("pool", lambda e, pat=pat, cm=cm, base=base, fill=fill: e.affine_select(
                    dst_f32_ap, dst_f32_ap, [[pat, 128]], ALU.is_ge, fill, base=base, channel_multiplier=cm),
                  reads=["tmpc"], writes=["tmpc"])

        def to_bf(dst_ap, src_ap, scale=None):
            if scale is None:
                A("dve", lambda e: e.tensor_copy(dst_ap, src_ap), reads=["tmpc"], writes=["consts"])
            else:
                A("dve", lambda e: e.tensor_scalar(dst_ap, src_ap, scale, None, ALU.mult), reads=["tmpc"], writes=["consts"])

        mk_const(tmpc[:], 1.0, [(1, -1, 0, 0.0), (-1, 1, 0, 0.0)])
        to_bf(ident[:], tmpc[:])
        A("dve", lambda e: e.tensor_copy(cst_f[:, 3, :], tmpc[:]), reads=["tmpc"], writes=["consts"])
        mk_const(tmpc[:], 1.0, [(1, -1, 0, 0.0)])
        to_bf(cst_b[:, 0, :], tmpc[:])
        to_bf(cst_b[:, 1, :], tmpc[:], -1.0)
        to_bf(cst_b[:, 6, :], tmpc[:])
        A("dve", lambda e: e.tensor_copy(cst_f[:, 0, :], tmpc[:]), reads=["tmpc"], writes=["consts"])
        mk_const(tmpc[:], 1.0, [(-1, 1, 0, 0.0)])
        to_bf(cst_b[:, 2, :], tmpc[:])
        to_bf(cst_b[:, 3, :], tmpc[:], -1.0)
        to_bf(cst_b[:, 7, :], tmpc[:])
        A("dve", lambda e: e.tensor_copy(cst_f[:, 1, :], tmpc[:]), reads=["tmpc"], writes=["consts"])
        mk_const(tmpc[:], 0.0, [(1, -1, 0, NEGV)])
        to_bf(cst_b[:, 4, :], tmpc[:])
        mk_const(tmpc[:], 0.0, [(-1, 1, 0, NEGV)])
        to_bf(cst_b[:, 5, :], tmpc[:])
        A("dve", lambda e: e.memset(cst_f[:, 2, :], 1.0), writes=["consts"])
        for oi, o in enumerate((-1, 0, 1)):
            mk_const(tmpc[:], 0.0, [(-1, 1, 128 * o + 64, NEGV), (1, -1, 64 - 128 * o, NEGV)])
            to_bf(band[:, oi, :], tmpc[:])
        A("sp", lambda e: e.dma_start(out=smallc[:, 0:64], in_=dt_bias.partition_broadcast(128)), writes=["smallc"], dma_key="c0")
        A("sp", lambda e: e.dma_start(out=smallc[:, 64:128], in_=A_log.partition_broadcast(128)), writes=["smallc"], dma_key="c0")
        A("sp", lambda e: e.dma_start(out=smallc[:, 128:160], in_=D_skip.partition_broadcast(128)), writes=["smallc"], dma_key="c0")
        A("act", lambda e: e.activation(smallc[:, 64:128], smallc[:, 64:128], AF.Exp), reads=["smallc"], writes=["smallc"])
        A("dve", lambda e: e.tensor_scalar(smallc[:, 64:128], smallc[:, 64:128], -1.0, None, ALU.mult), reads=["smallc"], writes=["smallc"])
        for seq in range(NSEQ):
            with ExitStack() as st:
                def sb(name, shape, dt):
                    return st.enter_context(nc.sbuf_tensor(f"{name}_{seq}", shape, dt))

                def ps(name, shape, dt):
                    return st.enter_context(nc.psum_tensor(f"{name}_{seq}", shape, dt))

                hT = sb("hT", [128, 8, S], BF16)
                gpre = sb("gpre", [128, D], F32)
                xt = [sb(f"xt{i}", [128, D], F32) for i in range(2)]
                hb = [sb(f"hb{i}", [128, D], BF16) for i in range(2)]
                junk = sb("junk", [128, D], F32)
                st8 = sb("st8", [128, 8], F32)
                pT = [ps(f"pT{i}", [128, 1024], BF16) for i in range(2)]
                pA = [ps(f"pA{i}", [128, 512], F32) for i in range(2)]
                pB = [ps(f"pB{i}", [128, 512], F32) for i in range(2)]

                A("sp", lambda e: e.dma_start(out=gpre[:], in_=norm_mix_pre.partition_broadcast(128)), writes=["gpre"], dma_key="gpre")
                for t in range(NT):
                    b = t % 2
                    A("sp", lambda e, t=t, b=b: e.dma_start(out=xt[b][:], in_=x_in[seq, t * 128:(t + 1) * 128, :]),
                      writes=[f"xt{b}"], dma_key=f"xt{b}")
                    A("act", lambda e, b=b: e.activation(junk[:], xt[b][:], AF.Square, accum_out=st8[:, b:b + 1]),
                      reads=[f"xt{b}"], writes=["junk", f"ss{b}"])
                    A("act", lambda e, b=b: e.activation(st8[:, 2 + b:3 + b], st8[:, b:b + 1], AF.Sqrt, bias=EPS, scale=1.0 / D),
                      reads=[f"ss{b}"], writes=[f"sd{b}"])
                    A("dve", lambda e, b=b: e.reciprocal(st8[:, 4 + b:5 + b], st8[:, 2 + b:3 + b]), reads=[f"sd{b}"], writes=[f"rs{b}"])
                    A("dve", lambda e, b=b: e.scalar_tensor_tensor(hb[b][:], xt[b][:], st8[:, 4 + b:5 + b], gpre[:], ALU.mult, ALU.mult),
                      reads=[f"xt{b}", f"rs{b}", "gpre"], writes=[f"hb{b}"])
                    for k in range(8):
                        A("pe", lambda e, b=b, k=k: e.transpose(pT[b][:, k * 128:(k + 1) * 128], hb[b][:, k * 128:(k + 1) * 128], ident[:]),
                          reads=[f"hb{b}", "consts"], writes=[f"pT{b}"])
                    A("act", lambda e, b=b, t=t: e.copy(hT[:, :, t * 128:(t + 1) * 128], pT[b][:].rearrange("p (k t) -> p k t", k=8)),
                      reads=[f"pT{b}"], writes=[("hT", t)])
                hT_all = [("hT", t) for t in range(NT)]
                if cut <= 1:
                    S_.barrier(); continue

                wsl = [sb(f"wsl{i}", [128, 8, 512], BF16) for i in range(2)]
                wp = sb("wp", [128, 8, 512], BF16)
                stg = [sb(f"stg{i}", [128, 512], BF16) for i in range(3)]
                stgf = [sb(f"stgf{i}", [128, 64], F32) for i in range(2)]
                outX = [sb(f"outX{i}", [128, S], BF16) for i in range(4)]
                rawT = [sb(f"rawT{i}", [128, S + 4], BF16) for i in range(2)]
                tmp1 = [sb(f"tmp1{i}", [128, 512], F32) for i in range(2)]
                tmp2 = [sb(f"tmp2{i}", [128, 512], F32) for i in range(2)]
                cosT = sb("cosT", [128, S], BF16)
                sinT = sb("sinT", [128, S], BF16)
                cw = sb("cw", [128, 32, 5], F32)
                cb = sb("cb", [128, 32], F32)
                dg = sb("dg", [128, 5, 128], BF16)
                wcnt = [0]
                scnt = [0]

                def load_w(lo, ncols):
                    i = wcnt[0] % 2
                    wcnt[0] += 1
                    A("pool", lambda e: e.dma_start(out=wsl[i][:, :, 0:ncols],
                                                    in_=w_in[:, lo:lo + ncols].rearrange("(k p) c -> p k c", p=128)),
                      writes=[f"wsl{i}"], dma_key=f"wsl{i}")
                    return i

                def stage():
                    i = scnt[0] % 3
                    scnt[0] += 1
                    return i

                A("pool", lambda e: e.memset(wp[:], 0.0), writes=["wp"])
                for i in range(2):
                    A("pool", lambda e, i=i: e.memset(rawT[i][:, 0:2], 0.0), writes=[f"rawT{i}"])
                    A("pool", lambda e, i=i: e.memset(rawT[i][:, S + 2:S + 4], 0.0), writes=[f"rawT{i}"])
                A("sp", lambda e: e.dma_start(out=cw[:].rearrange("p f k -> p (f k)"), in_=cw_l), writes=["cw"], dma_key="cw")
                A("sp", lambda e: e.dma_start(out=cb[:], in_=cb_l), writes=["cb"], dma_key="cw")
                A("pool", lambda e: e.dma_start(out=cosT[:], in_=rot_cos), writes=["cosT"], dma_key="rot")
                A("pool", lambda e: e.dma_start(out=sinT[:], in_=rot_sin), writes=["sinT"], dma_key="rot")
                if cut <= 2:
                    S_.barrier(); continue
                def tok_block(lo, ncols, func, dst, dst_lo, dkey):
                    wi = load_w(lo, ncols)
                    for t in range(NT):
                        b = t % 2
                        for k in range(8):
                            A("pe", lambda e, t=t, k=k, b=b: e.matmul(pA[b][:, 0:ncols], hT[:, k, t * 128:(t + 1) * 128], wsl[wi][:, k, 0:ncols],
                                                                      start=(k == 0), stop=(k == 7)),
                              reads=[("hT", t), f"wsl{wi}"], writes=[f"pA{b}"])
                        si = stage()
                        A("act", lambda e, b=b, si=si: e.activation(stg[si][:, 0:ncols], pA[b][:, 0:ncols], func),
                          reads=[f"pA{b}"], writes=[f"stg{si}"])
                        A("sp", lambda e, t=t, si=si: e.dma_start(out=dst[t * 128:(t + 1) * 128, dst_lo:dst_lo + ncols], in_=stg[si][:, 0:ncols]),
                          reads=[f"stg{si}"], writes=[(dkey, t)], dma_key=f"stg{si}")

                for blk in range(4):
                    tok_block(OFF_Z + blk * 512, 512, AF.Silu, scr_z, blk * 512, "scr_z")
                for blk in range(4):
                    tok_block(OFF_G + blk * 512, 512, AF.Sigmoid, scr_gate, blk * 512, "scr_gate")
                if cut <= 3:
                    S_.barrier(); continue
                wi = load_w(OFF_DT, 64)
                for t in range(NT):
                    b = t % 2
                    for k in range(8):
                        A("pe", lambda e, t=t, k=k, b=b: e.matmul(pA[b][:, 0:64], hT[:, k, t * 128:(t + 1) * 128], wsl[wi][:, k, 0:64],
                                                                  start=(k == 0), stop=(k == 7)),
                          reads=[("hT", t), f"wsl{wi}"], writes=[f"pA{b}"])
                    A("act", lambda e, b=b: e.copy(stgf[b][:], pA[b][:, 0:64]), reads=[f"pA{b}"], writes=[f"stgf{b}"])
                    A("sp", lambda e, t=t, b=b: e.dma_start(out=scr_dt[t * 128:(t + 1) * 128, :], in_=stgf[b][:]),
                      reads=[f"stgf{b}"], writes=[("scr_dt", t)], dma_key=f"stgf{b}")
                if cut <= 4:
                    S_.barrier(); continue
                for g in range(3):
                    d = DIL[g]
                    n = S // d
                    wi = load_w(OFF_V + g * 512, 512)
                    cnt = 0
                    for r in range(d):
                        for i in range(n // 128):
                            b = cnt % 2
                            cnt += 1
                            base = i * 128 * d + r
                            tl = sorted(set((base + j * d) // 128 for j in (0, 127)))
                            tl = list(range(tl[0], tl[-1] + 1))
                            for k in range(8):
                                A("pe", lambda e, k=k, b=b, base=base, d=d: e.matmul(
                                    pA[b][:], hT[:, k, base:base + 127 * d + 1:d], wsl[wi][:, k, :], start=(k == 0), stop=(k == 7)),
                                  reads=[("hT", tt) for tt in tl] + [f"wsl{wi}"], writes=[f"pA{b}"])
                            si = stage()
                            A("act", lambda e, b=b, si=si: e.copy(stg[si][:], pA[b][:]), reads=[f"pA{b}"], writes=[f"stg{si}"])
                            row = r * n + i * 128
                            A("sp", lambda e, si=si, row=row, g=g: e.dma_start(out=scr_v[g, row:row + 128, :], in_=stg[si][:]),
                              reads=[f"stg{si}"], writes=[("scr_v", g)], dma_key=f"stg{si}")
                if cut <= 5:
                    S_.barrier(); continue
                for qk in range(2):
                    off = OFF_Q if qk == 0 else OFF_K
                    dstT = scr_qT if qk == 0 else scr_kT
                    for blk in range(3):
                        g = blk
                        d = DIL[g]
                        n = S // d
                        wi = load_w(off + blk * 512, 512)
                        if qk == 0:
                            A("dve", lambda e, wi=wi: e.tensor_scalar(wsl[wi][:], wsl[wi][:], 0.125, None, ALU.mult),
                              reads=[f"wsl{wi}"], writes=[f"wsl{wi}"])
                        wv = wsl[wi][:].rearrange("p k (h c) -> p k h c", h=8)
                        wpv = wp[:].rearrange("p k (h c) -> p k h c", h=8)
                        A("dve", lambda e, wv=wv, wpv=wpv: e.tensor_copy(wpv[:, :, :, 0:8], wv[:, :, :, 8:16]), reads=[f"wsl{wi}"], writes=["wp"])
                        A("dve", lambda e, wv=wv, wpv=wpv: e.tensor_copy(wpv[:, :, :, 8:16], wv[:, :, :, 0:8]), reads=[f"wsl{wi}"], writes=["wp"])
                        for hp in range(4):
                            ox = outX[hp]
                            for tb in range(TB):
                                b = tb % 2
                                for k in range(8):
                                    A("pe", lambda e, k=k, b=b, tb=tb, hp=hp, wi=wi: e.matmul(
                                        pA[b][:], wsl[wi][:, k, hp * 128:(hp + 1) * 128], hT[:, k, tb * 512:(tb + 1) * 512],
                                        start=(k == 0), stop=(k == 7)),
                                      reads=hT_all[tb * 4:(tb + 1) * 4] + [f"wsl{wi}"], writes=[f"pA{b}"])
                                for k in range(8):
                                    A("pe", lambda e, k=k, b=b, tb=tb, hp=hp: e.matmul(
                                        pB[b][:], wp[:, k, hp * 128:(hp + 1) * 128], hT[:, k, tb * 512:(tb + 1) * 512],
                                        start=(k == 0), stop=(k == 7)),
                                      reads=hT_all[tb * 4:(tb + 1) * 4] + ["wp"], writes=[f"pB{b}"])
                                sl = slice(tb * 512, (tb + 1) * 512)
                                A("dve", lambda e, b=b, sl=sl: e.tensor_tensor(tmp1[b][:], pB[b][:], sinT[:, sl], ALU.mult),
                                  reads=[f"pB{b}", "sinT"], writes=[f"tmp1{b}"])
                                A("dve", lambda e, b=b, sl=sl: e.tensor_tensor(tmp2[b][:], pA[b][:], cosT[:, sl], ALU.mult),
                                  reads=[f"pA{b}", "cosT"], writes=[f"tmp2{b}"])
                                i0 = tb * 512 // d
                                ni = 512 // d
                                oap = ox[:].rearrange("p (r i) -> p r i", r=d)[:, :, i0:i0 + ni]
                                A("pool", lambda e, b=b, oap=oap, d=d: e.tensor_tensor(
                                    oap, tmp1[b][:].rearrange("p (i r) -> p r i", r=d), tmp2[b][:].rearrange("p (i r) -> p r i", r=d), ALU.add),
                                  reads=[f"tmp1{b}", f"tmp2{b}"], writes=[f"outX{hp}"])
                            row = blk * 512 + hp * 128
                            A("sp", lambda e, hp=hp, row=row, dstT=dstT: e.dma_start(out=dstT[row:row + 128, :], in_=outX[hp][:]),
                              reads=[f"outX{hp}"], writes=[("scr_qk", qk, blk, hp)], dma_key=f"outX{hp}")
                if cut <= 6:
                    S_.barrier(); continue
                for grp in range(8):
                    wi = load_w(OFF_X + grp * 512, 512)
                    for j in range(4):
                        ft = grp * 4 + j
                        rb = ft % 2
                        for k5 in range(5):
                            A("dve", lambda e, k5=k5, ft=ft: e.tensor_scalar(dg[:, k5, :], ident[:], cw[:, ft, k5:k5 + 1], None, ALU.mult),
                              reads=["consts", "cw"], writes=["dg"])
                        for tb in range(TB):
                            b = tb % 2
                            for k in range(8):
                                A("pe", lambda e, k=k, b=b, tb=tb, j=j, wi=wi: e.matmul(
                                    pA[b][:], wsl[wi][:, k, j * 128:(j + 1) * 128], hT[:, k, tb * 512:(tb + 1) * 512],
                                    start=(k == 0), stop=(k == 7)),
                                  reads=hT_all[tb * 4:(tb + 1) * 4] + [f"wsl{wi}"], writes=[f"pA{b}"])
                            A("dve", lambda e, b=b, tb=tb, rb=rb: e.tensor_copy(rawT[rb][:, 2 + tb * 512:2 + (tb + 1) * 512], pA[b][:]),
                              reads=[f"pA{b}"], writes=[f"rawT{rb}"])
                        for tb in range(TB):
                            b = tb % 2
                            for k5 in range(5):
                                A("pe", lambda e, k5=k5, b=b, tb=tb, rb=rb: e.matmul(
                                    pB[b][:], dg[:, k5, :], rawT[rb][:, tb * 512 + k5:tb * 512 + k5 + 512], start=(k5 == 0), stop=(k5 == 4)),
                                  reads=["dg", f"rawT{rb}"], writes=[f"pB{b}"])
                            A("act", lambda e, b=b, tb=tb, j=j, ft=ft: e.activation(
                                outX[j][:, tb * 512:(tb + 1) * 512], pB[b][:], AF.Silu, bias=cb[:, ft:ft + 1]),
                              reads=[f"pB{b}", "cb"], writes=[f"outX{j}"])
                    if grp < 6:
                        dst, clo, dkey = (scr_x, grp * 512, "scr_x") if grp < 4 else (scr_B, (grp - 4) * 512, "scr_B")
                        for t in range(NT):
                            b = t % 2
                            for j in range(4):
                                A("pe", lambda e, b=b, j=j, t=t: e.transpose(pT[b][:, j * 128:(j + 1) * 128], outX[j][:, t * 128:(t + 1) * 128], ident[:]),
                                  reads=[f"outX{j}", "consts"], writes=[f"pT{b}"])
                            si = stage()
                            A("dve", lambda e, b=b, si=si: e.tensor_copy(stg[si][:], pT[b][:, 0:512]), reads=[f"pT{b}"], writes=[f"stg{si}"])
                            A("sp", lambda e, t=t, si=si, dst=dst, clo=clo: e.dma_start(out=dst[t * 128:(t + 1) * 128, clo:clo + 512], in_=stg[si][:]),
                              reads=[f"stg{si}"], writes=[(dkey, t)], dma_key=f"stg{si}")
                    if grp >= 4:
                        dstT = scr_BT if grp < 6 else scr_CT
                        rlo = (grp - 4) * 512 if grp < 6 else (grp - 6) * 512
                        for j in range(4):
                            A("sp", lambda e, j=j, dstT=dstT, rlo=rlo: e.dma_start(out=dstT[rlo + j * 128:rlo + (j + 1) * 128, :], in_=outX[j][:]),
                              reads=[f"outX{j}"], writes=[("scr_BCT", grp, j)], dma_key=f"outX{j}")
            S_.barrier()
            if stop_after <= 2:
                continue
            with ExitStack() as st:
                def sb(name, shape, dt):
                    return st.enter_context(nc.sbuf_tensor(f"{name}_s{seq}", shape, dt))

                def ps(name, shape, dt):
                    return st.enter_context(nc.psum_tensor(f"{name}_s{seq}", shape, dt))

                wssd = sb("wssd", [128, 16, 1024], BF16)
                nwl = sb("nwl", [128, 16], F32)
                xc = [sb(f"xc{i}", [128, 2048], BF16) for i in range(2)]
                Bc = [sb(f"Bc{i}", [128, 1024], BF16) for i in range(2)]
                BTc = [sb(f"BTc{i}", [128, 8, 128], BF16) for i in range(2)]
                CTc = [sb(f"CTc{i}", [128, 8, 128], BF16) for i in range(2)]
                dtc = [sb(f"dtc{i}", [128, 64], F32) for i in range(2)]
                zc = [sb(f"zc{i}", [128, 2048], BF16) for i in range(2)]
                yfc = [sb(f"yfc{i}", [128, 2048], BF16) for i in range(2)]
                gc = [sb(f"gc{i}", [128, 1024], BF16) for i in range(2)]
                ST = sb("ST", [128, 2048], F32)
                STb = sb("STb", [128, 2048], BF16)
                sm = sb("sm", [128, 128], F32)
                sm2 = sb("sm2", [128, 128], F32)
                dAhi = sb("dAhi", [128, 32], BF16)
                dAlo = sb("dAlo", [128, 32], BF16)
                CBm = [sb(f"CBm{i}", [128, 128], BF16) for i in range(2)]
                Eb = [sb(f"Eb{i}", [128, 512], BF16) for i in range(2)]
                Mb = [sb(f"Mb{i}", [128, 4, 128], BF16) for i in range(2)]
                xw = [sb(f"xw{i}", [128, 256], BF16) for i in range(2)]
                xdt = [sb(f"xdt{i}", [128, 256], BF16) for i in range(2)]
                tq = [sb(f"tq{i}", [128, 256], F32) for i in range(2)]
                tq2 = [sb(f"tq2{i}", [128, 256], F32) for i in range(2)]
                Ych = sb("Ych", [128, 2048], F32)
                Yst = [sb(f"Yst{i}", [128, 2048], BF16) for i in range(2)]
                Gb = sb("Gb", [128, 2048], BF16)
                GT = sb("GT", [128, 16, 128], BF16)
                junk2 = sb("junk2", [128, 2048], BF16)
                r8 = sb("r8", [128, 4], F32)
                m1s = [sb(f"m1s{i}", [128, 1024], BF16) for i in range(2)]
                pSeg = [ps(f"pSeg{i}", [128, 512], F32) for i in range(2)]
                pYY = [ps(f"pYY{i}", [128, 512], F32) for i in range(2)]
                pSx = [ps(f"pSx{i}", [128, 512], F32) for i in range(2)]
                pS = pSx[0]
                pT2 = ps("pT2", [128, 1024], BF16)
                pO = ps("pO", [128, 512], F32)

                for k in range(2):
                    A("pool", lambda e, k=k: e.dma_start(out=wssd[:, k * 8:(k + 1) * 8, :],
                                                         in_=w_ssd[k * 1024:(k + 1) * 1024, :].rearrange("(k p) c -> p k c", p=128)),
                      writes=["wssd"], dma_key="wssd")
                A("sp", lambda e: e.dma_start(out=nwl[:], in_=nw_l), writes=["nwl"], dma_key="nwl")
                for k in range(16):
                    A("dve", lambda e, k=k: e.tensor_scalar(wssd[:, k, :], wssd[:, k, :], nwl[:, k:k + 1], None, ALU.mult),
                      reads=["wssd", "nwl"], writes=["wssd"])
                cnt = 0
                for dr in range(2):
                    A("dve", lambda e: e.memset(ST[:], 0.0), writes=[("ST", g_) for g_ in range(8)])
                    A("dve", lambda e: e.memset(STb[:], 0.0), writes=[("STb", g_) for g_ in range(8)])
                    tri_f = cst_f[:, dr, :]
                    tri_b = cst_b[:, 0 + 2 * dr, :]
                    ntri_b = cst_b[:, 1 + 2 * dr, :]
                    neg_b = cst_b[:, 4 + dr, :]
                    msk_b = cst_b[:, 6 + dr, :]
                    for ci in range(NT):
                        c = ci if dr == 0 else NT - 1 - ci
                        b = cnt % 2
                        cnt += 1
                        rows = slice(c * 128, (c + 1) * 128)
                        A("sp", lambda e, b=b, rows=rows: e.dma_start(out=xc[b][:], in_=scr_x[rows, :]), reads=[("scr_x", c)], writes=[f"xc{b}"], dma_key=f"xc{b}")
                        A("sp", lambda e, b=b, rows=rows: e.dma_start(out=Bc[b][:], in_=scr_B[rows, :]), reads=[("scr_B", c)], writes=[f"Bc{b}"], dma_key=f"Bc{b}")
                        A("sp", lambda e, b=b, rows=rows: e.dma_start(out=BTc[b][:], in_=scr_BT[:, rows].rearrange("(g n) t -> n g t", n=128)),
                          reads=[("scr_BCT", gg, jj) for gg in (4, 5) for jj in range(4)], writes=[f"BTc{b}"], dma_key=f"BTc{b}")
                        A("sp", lambda e, b=b, rows=rows: e.dma_start(out=CTc[b][:], in_=scr_CT[:, rows].rearrange("(g n) t -> n g t", n=128)),
                          reads=[("scr_BCT", gg, jj) for gg in (6, 7) for jj in range(4)], writes=[f"CTc{b}"], dma_key=f"CTc{b}")
                        A("sp", lambda e, b=b, rows=rows: e.dma_start(out=dtc[b][:], in_=scr_dt[rows, :]), reads=[("scr_dt", c)], writes=[f"dtc{b}"], dma_key=f"dtc{b}")
                        if dr == 1:
                            A("sp", lambda e, b=b, rows=rows: e.dma_start(out=zc[b][:], in_=scr_z[rows, :]), reads=[("scr_z", c)], writes=[f"zc{b}"], dma_key=f"zc{b}")
                            A("sp", lambda e, b=b, rows=rows: e.dma_start(out=yfc[b][:], in_=scr_yf[rows, :]), reads=[("scr_yf", c)], writes=[f"yfc{b}"], dma_key=f"yfc{b}")
                            A("sp", lambda e, b=b, rows=rows: e.dma_start(out=gc[b][:], in_=scr_gate[rows, 0:1024]), reads=[("scr_gate", c)], writes=[f"gc{b}"], dma_key=f"gc{b}")
                        o32 = dr * 32
                        A("dve", lambda e, b=b, o32=o32: e.tensor_tensor(sm[:, 0:32], dtc[b][:, o32:o32 + 32], smallc[:, o32:o32 + 32], ALU.add),
                          reads=[f"dtc{b}", "smallc"], writes=["sm0"])
                        A("act", lambda e: e.activation(sm[:, 32:64], sm[:, 0:32], AF.Exp), reads=["sm0"], writes=["sm1"])
                        A("act", lambda e: e.activation(sm[:, 64:96], sm[:, 32:64], AF.Ln, bias=1.0), reads=["sm1"], writes=["dt"])
                        A("dve", lambda e, o32=o32: e.tensor_tensor(sm[:, 96:128], sm[:, 64:96], smallc[:, 64 + o32:96 + o32], ALU.mult),
                          reads=["dt", "smallc"], writes=["dA"])
                        A("dve", lambda e: e.tensor_copy(dAhi[:], sm[:, 96:128]), reads=["dA"], writes=["dAhi"])
                        A("dve", lambda e: e.tensor_tensor(dAlo[:], sm[:, 96:128], dAhi[:], ALU.subtract), reads=["dA", "dAhi"], writes=["dAlo"])
                        A("pe", lambda e, tri_f=tri_f: e.matmul(pS[:, 256:288], tri_f, sm[:, 96:128], start=True, stop=True), reads=["dA", "consts"], writes=["pSx0"])
                        A("pe", lambda e: e.matmul(pS[:, 288:320], cst_f[:, 2, :], sm[:, 96:128], start=True, stop=True), reads=["dA", "consts"], writes=["pSx0"])
                        A("act", lambda e: e.copy(sm2[:, 0:32], pS[:, 256:288]), reads=["pSx0"], writes=["asb"])
                        A("act", lambda e: e.activation(sm2[:, 32:64], pS[:, 256:288], AF.Exp), reads=["pSx0"], writes=["ea"])
                        A("act", lambda e: e.activation(sm2[:, 64:96], pS[:, 288:320], AF.Exp), reads=["pSx0"], writes=["eal"])
                        A("dve", lambda e: e.tensor_tensor(sm2[:, 96:128], pS[:, 288:320], sm2[:, 0:32], ALU.subtract), reads=["pSx0", "asb"], writes=["wv"])
                        A("act", lambda e: e.activation(sm2[:, 96:128], sm2[:, 96:128], AF.Exp), reads=["wv"], writes=["wv"])
                        A("dve", lambda e: e.tensor_tensor(sm2[:, 96:128], sm2[:, 96:128], sm[:, 64:96], ALU.mult), reads=["wv", "dt"], writes=["wv"])
                        def stage_a(g, b=b, tri_b=tri_b, ntri_b=ntri_b, neg_b=neg_b, msk_b=msk_b):
                            q = g % 2
                            A("pe", lambda e: e.matmul(pSx[q][:, 320:448], BTc[b][:, g, :], CTc[b][:, g, :], start=True, stop=True),
                              reads=[f"BTc{b}", f"CTc{b}"], writes=[f"pSx{q}"])
                            A("dve", lambda e: e.tensor_tensor(CBm[q][:], pSx[q][:, 320:448], msk_b, ALU.mult), reads=[f"pSx{q}", "consts"], writes=[f"CBm{q}"])
                            for j in range(4):
                                h = g * 4 + j
                                osl = pSeg[q][:, j * 128:(j + 1) * 128]
                                hi = dAhi[:, h:h + 1].to_broadcast([128, 128])
                                lo = dAlo[:, h:h + 1].to_broadcast([128, 128])
                                A("pe", lambda e: e.matmul(osl, hi, tri_b, start=True, stop=False), reads=["dAhi", "consts"], writes=[f"pSeg{q}"])
                                A("pe", lambda e: e.matmul(osl, lo, tri_b, start=False, stop=False), reads=["dAlo", "consts"], writes=[f"pSeg{q}"])
                                A("pe", lambda e: e.matmul(osl, ntri_b, hi, start=False, stop=False), reads=["dAhi", "consts"], writes=[f"pSeg{q}"])
                                A("pe", lambda e: e.matmul(osl, ntri_b, lo, start=False, stop=False), reads=["dAlo", "consts"], writes=[f"pSeg{q}"])
                                A("pe", lambda e: e.matmul(osl, ident[:], neg_b, start=False, stop=True), reads=["consts"], writes=[f"pSeg{q}"])
                            A("act", lambda e: e.activation(Eb[q][:], pSeg[q][:], AF.Exp), reads=[f"pSeg{q}"], writes=[f"Eb{q}"])
                            xg = xc[b][:, g * 256:(g + 1) * 256].rearrange("p (j q) -> p j q", j=4)
                            wb = sm2[:, 96 + g * 4:100 + g * 4].unsqueeze(2).to_broadcast([128, 4, 64])
                            dtb = sm[:, 64 + g * 4:68 + g * 4].unsqueeze(2).to_broadcast([128, 4, 64])
                            A("pool", lambda e: e.tensor_tensor(xw[q][:].rearrange("p (j q) -> p j q", j=4), xg, wb, ALU.mult),
                              reads=[f"xc{b}", "wv"], writes=[f"xw{q}"])
                            A("pool", lambda e: e.tensor_tensor(xdt[q][:].rearrange("p (j q) -> p j q", j=4), xg, dtb, ALU.mult),
                              reads=[f"xc{b}", "dt"], writes=[f"xdt{q}"])

                        def stage_b(g, b=b):
                            q = g % 2
                            for j in range(4):
                                A("pool" if j % 2 else "dve", lambda e, j=j: e.tensor_tensor(Mb[q][:, j, :], Eb[q][:, j * 128:(j + 1) * 128], CBm[q][:], ALU.mult),
                                  reads=[f"Eb{q}", f"CBm{q}"], writes=[("Mb", q, j)])
                            for j in range(4):
                                A("pe", lambda e, j=j: e.matmul(pYY[q][:, j * 64:(j + 1) * 64], Mb[q][:, j, :], xdt[q][:, j * 64:(j + 1) * 64], start=True, stop=True),
                                  reads=[("Mb", q, j), f"xdt{q}"], writes=[f"pY{q}"])
                            A("pe", lambda e: e.matmul(pYY[q][:, 256:512], CTc[b][:, g, :], STb[:, g * 256:(g + 1) * 256], start=True, stop=True),
                              reads=[f"CTc{b}", ("STb", g)], writes=[f"pYo{q}"])
                            A("pe", lambda e: e.matmul(pSx[q][:, 0:256], Bc[b][:, g * 128:(g + 1) * 128], xw[q][:], start=True, stop=True),
                              reads=[f"Bc{b}", f"xw{q}"], writes=[f"pSx{q}"])
                            eab = sm2[:, 32 + g * 4:36 + g * 4].unsqueeze(2).to_broadcast([128, 4, 64])
                            A("dve", lambda e: e.tensor_tensor(tq[q][:].rearrange("p (j q) -> p j q", j=4), pYY[q][:, 256:512].rearrange("p (j q) -> p j q", j=4), eab, ALU.mult),
                              reads=[f"pYo{q}", "ea"], writes=[f"tq{q}"])
                            A("dve", lambda e: e.tensor_tensor(Ych[:, g * 256:(g + 1) * 256], pYY[q][:, 0:256], tq[q][:], ALU.add),
                              reads=[f"pY{q}", f"tq{q}"], writes=[("Ych", g)])
                            elb = sm2[:, 64 + g * 4:68 + g * 4].unsqueeze(2).to_broadcast([128, 4, 64])
                            A("pool", lambda e: e.tensor_tensor(tq2[q][:].rearrange("p (j q) -> p j q", j=4),
                                                                ST[:, g * 256:(g + 1) * 256].rearrange("p (j q) -> p j q", j=4), elb, ALU.mult),
                              reads=[("ST", g), "eal"], writes=[f"tq2{q}"])
                            A("dve", lambda e: e.tensor_tensor(ST[:, g * 256:(g + 1) * 256], pSx[q][:, 0:256], tq2[q][:], ALU.add),
                              reads=[f"pSx{q}", f"tq2{q}"], writes=[("ST", g)])
                            A("act", lambda e: e.copy(STb[:, g * 256:(g + 1) * 256], ST[:, g * 256:(g + 1) * 256]), reads=[("ST", g)], writes=[("STb", g)])

                        stage_a(0)
                        for g in range(8):
                            if g + 1 < 8:
                                stage_a(g + 1)
                            stage_b(g)
                        Yall = [("Ych", g) for g in range(8)]
                        if dr == 0:
                            A("act", lambda e, b=b: e.copy(Yst[b][:], Ych[:]), reads=Yall, writes=[f"Yst{b}"])
                            A("sp", lambda e, b=b, rows=rows: e.dma_start(out=scr_yf[rows, :], in_=Yst[b][:]), reads=[f"Yst{b}"], writes=[("scr_yf", c)], dma_key=f"Yst{b}")
                            continue
                        A("dve", lambda e, b=b: e.tensor_tensor(Ych[:], Ych[:], yfc[b][:], ALU.add), reads=Yall + [f"yfc{b}"], writes=Yall)
                        Db = smallc[:, 128:160].unsqueeze(2).to_broadcast([128, 32, 64])
                        A("pool", lambda e, b=b, Db=Db: e.tensor_tensor(Yst[0][:].rearrange("p (h q) -> p h q", h=32), xc[b][:].rearrange("p (h q) -> p h q", h=32), Db, ALU.mult),
                          reads=[f"xc{b}", "smallc"], writes=["Yst0"])
                        A("dve", lambda e: e.tensor_tensor(Ych[:], Ych[:], Yst[0][:], ALU.add), reads=Yall + ["Yst0"], writes=Yall)
                        A("dve", lambda e, b=b: e.tensor_tensor(Gb[:], Ych[:], zc[b][:], ALU.mult), reads=Yall + [f"zc{b}"], writes=["Gb"])
                        A("act", lambda e: e.activation(junk2[:], Gb[:], AF.Square, accum_out=r8[:, 0:1]), reads=["Gb"], writes=["junk2", "r0"])
                        A("act", lambda e: e.activation(r8[:, 1:2], r8[:, 0:1], AF.Sqrt, bias=EPS, scale=1.0 / DIN), reads=["r0"], writes=["r1"])
                        A("dve", lambda e: e.reciprocal(r8[:, 2:3], r8[:, 1:2]), reads=["r1"], writes=["r2"])
                        for q4 in range(2):
                            for kk in range(8):
                                k = q4 * 8 + kk
                                A("pe", lambda e, k=k, kk=kk: e.transpose(pT2[:, kk * 128:(kk + 1) * 128], Gb[:, k * 128:(k + 1) * 128], ident[:]),
                                  reads=["Gb", "consts"], writes=["pT2"])
                            A("act", lambda e, q4=q4: e.copy(GT[:, q4 * 8:(q4 + 1) * 8, :], pT2[:].rearrange("p (k t) -> p k t", k=8)), reads=["pT2"], writes=["GT"])
                        for hf in range(2):
                            hs = slice(hf * 512, (hf + 1) * 512)
                            for k in range(16):
                                A("pe", lambda e, k=k, hs=hs: e.matmul(pO[:], GT[:, k, :], wssd[:, k, hs], start=(k == 0), stop=(k == 15)),
                                  reads=["GT", "wssd"], writes=["pO"])
                            A("dve", lambda e, b=b, hs=hs: e.scalar_tensor_tensor(m1s[b][:, hs], pO[:], r8[:, 2:3], gc[b][:, hs], ALU.mult, ALU.mult),
                              reads=["pO", "r2", f"gc{b}"], writes=[f"m1s{b}"])
                        A("sp", lambda e, b=b, rows=rows: e.dma_start(out=scr_m1[rows, :], in_=m1s[b][:]), reads=[f"m1s{b}"], writes=[("scr_m1", c)], dma_key=f"m1s{b}")
            S_.barrier()
            if stop_after <= 4:
                continue
            with ExitStack() as st:
                def sb(name, shape, dt):
                    return st.enter_context(nc.sbuf_tensor(f"{name}_a{seq}", shape, dt))

                def ps(name, shape, dt):
                    return st.enter_context(nc.psum_tensor(f"{name}_a{seq}", shape, dt))

                QT = sb("QT", [128, 4, S], BF16)
                KT = sb("KT", [128, 4, S], BF16)
                Vp = sb("Vp", [128, NT, 8, 65], BF16)
                PT = [sb(f"PT{i}", [128, 384], BF16) for i in range(2)]
                ost = [sb(f"ost{i}", [128, 8, 65], BF16) for i in range(2)]
                pSc = [ps(f"pSc{i}", [128, 512], F32) for i in range(2)]
                pOa = [ps(f"pOa{i}", [128, 1024], F32) for i in range(2)]
                A("dve", lambda e: e.memset(Vp[:].rearrange("p t h c -> p (t h) c")[:, :, 64:65], 1.0), writes=["Vp"])
                ucnt = 0
                qcnt = 0
                for g in range(3):
                    d = DIL[g]
                    n = S // d
                    nq = n // 128
                    for r in range(d):
                        for hp in range(4):
                            row = g * 512 + hp * 128
                            A("sp", lambda e, hp=hp, row=row, r=r, n=n: e.dma_start(out=QT[:, hp, 0:n], in_=scr_qT[row:row + 128, r * n:(r + 1) * n]),
                              reads=[("scr_qk", 0, g, hp)], writes=["QT"], dma_key="QT")
                            A("sp", lambda e, hp=hp, row=row, r=r, n=n: e.dma_start(out=KT[:, hp, 0:n], in_=scr_kT[row:row + 128, r * n:(r + 1) * n]),
                              reads=[("scr_qk", 1, g, hp)], writes=["KT"], dma_key="KT")
                        for i in range(nq):
                            A("sp", lambda e, g=g, r=r, n=n, i=i: e.dma_start(out=Vp[:, i, :, 0:64],
                                                                          in_=scr_v[g, r * n + i * 128:r * n + (i + 1) * 128, :].rearrange("k (h c) -> k h c", h=8)),
                              reads=[("scr_v", g)], writes=["Vp"], dma_key="Vp")
                        for i in range(nq):
                            qb = qcnt % 2
                            qcnt += 1
                            offs = [o for o in (-1, 0, 1) if 0 <= i + o < nq]
                            for h in range(8):
                                hp, hh = h // 2, h % 2
                                ub = ucnt % 2
                                ucnt += 1
                                for oi, o in enumerate(offs):
                                    ks = slice((i + o) * 128, (i + o + 1) * 128)
                                    A("pe", lambda e, ub=ub, oi=oi, hp=hp, hh=hh, ks=ks, i=i: e.matmul(
                                        pSc[ub][:, oi * 128:(oi + 1) * 128], KT[hh * 64:(hh + 1) * 64, hp, ks], QT[hh * 64:(hh + 1) * 64, hp, i * 128:(i + 1) * 128],
                                        start=True, stop=False), reads=["QT", "KT"], writes=[f"pSc{ub}"])
                                    A("pe", lambda e, ub=ub, oi=oi, o=o: e.matmul(pSc[ub][:, oi * 128:(oi + 1) * 128], ident[:], band[:, o + 1, :], start=False, stop=True),
                                      reads=["consts"], writes=[f"pSc{ub}"])
                                no = len(offs)
                                A("act", lambda e, ub=ub, no=no: e.activation(PT[ub][:, 0:no * 128], pSc[ub][:, 0:no * 128], AF.Exp),
                                  reads=[f"pSc{ub}"], writes=[f"PT{ub}"])
                                for oi, o in enumerate(offs):
                                    A("pe", lambda e, ub=ub, qb=qb, oi=oi, o=o, h=h, i=i, no=no: e.matmul(
                                        pOa[qb][:, (h // 4) * 512 + (h % 4) * 65:(h // 4) * 512 + (h % 4) * 65 + 65], PT[ub][:, oi * 128:(oi + 1) * 128], Vp[:, i + o, h, :],
                                        start=(oi == 0), stop=(oi == no - 1)), reads=[f"PT{ub}", "Vp"], writes=[f"pOa{qb}"])
                            for hf in range(2):
                                A("act" if hf else "dve", lambda e, qb=qb, hf=hf: (e.copy if hf else e.tensor_copy)(
                                    ost[qb][:, hf * 4:(hf + 1) * 4, :], pOa[qb][:, hf * 512:hf * 512 + 260].rearrange("p (h c) -> p h c", h=4)),
                                  reads=[f"pOa{qb}"], writes=[f"ost{qb}"])
                            t0 = i * 128 * d + r
                            A("sp", lambda e, qb=qb, g=g, t0=t0, d=d: e.dma_start(out=scr_o[g, t0:t0 + 127 * d + 1:d, :, :], in_=ost[qb][:]),
                              reads=[f"ost{qb}"], writes=[("scr_o", g)], dma_key=f"ost{qb}")
            S_.barrier()
            if stop_after <= 5:
                continue
            with ExitStack() as st:
                def sb(name, shape, dt):
                    return st.enter_context(nc.sbuf_tensor(f"{name}_f{seq}", shape, dt))

                def ps(name, shape, dt):
                    return st.enter_context(nc.psum_tensor(f"{name}_f{seq}", shape, dt))

                watt = sb("watt", [128, 4, 1024], BF16)
                wout = sb("wout", [128, 8, 1024], BF16)
                gns = sb("gns", [128, 2, 1024], F32)
                o3 = [sb(f"o3{i}", [128, 3, 8, 65], BF16) for i in range(2)]
                osum = sb("osum", [128, 8, 65], F32)
                rl = sb("rl", [128, 8], F32)
                Ob = sb("Ob", [128, 512], BF16)
                OT = sb("OT", [128, 4, 128], BF16)
                gat = [sb(f"gat{i}", [128, 1024], BF16) for i in range(2)]
                m1c = [sb(f"m1c{i}", [128, 1024], BF16) for i in range(2)]
                mrg = sb("mrg", [128, 1024], BF16)
                mrgT = sb("mrgT", [128, 8, 128], BF16)
                tmpf = sb("tmpf", [128, 1024], F32)
                xin = [sb(f"xin{i}", [128, 1024], F32) for i in range(2)]
                x1 = [sb(f"x1{i}", [128, 1024], F32) for i in range(2)]
                h2 = [sb(f"h2{i}", [128, 1024], BF16) for i in range(2)]
                jk = sb("jk", [128, 1024], F32)
                s8 = sb("s8", [128, 8], F32)
                pT3 = [ps(f"pT3{i}", [128, 1024], BF16) for i in range(2)]
                pM = [ps(f"pM{i}", [128, 1024], F32) for i in range(2)]

                A("pool", lambda e: e.dma_start(out=watt[:], in_=w_attn.rearrange("(k p) c -> p k c", p=128)), writes=["watt"], dma_key="watt")
                A("pool", lambda e: e.dma_start(out=wout[:], in_=w_out.rearrange("(k p) c -> p k c", p=128)), writes=["wout"], dma_key="wout")
                for gi, gsrc in enumerate((norm_mix_post, norm_ffn_pre)):
                    A("sp", lambda e, gi=gi, gsrc=gsrc: e.dma_start(out=gns[:, gi, :], in_=gsrc.partition_broadcast(128)), writes=["gns"], dma_key="gns")

                def rms_scale(src_ap, ss_col, rd):
                    A("act", lambda e: e.activation(jk[:], src_ap, AF.Square, accum_out=s8[:, ss_col:ss_col + 1]), reads=rd, writes=["jk", ("s8", ss_col)])
                    A("act", lambda e: e.activation(s8[:, ss_col + 1:ss_col + 2], s8[:, ss_col:ss_col + 1], AF.Sqrt, bias=EPS, scale=1.0 / D),
                      reads=[("s8", ss_col)], writes=[("s8", ss_col + 1)])
                    A("dve", lambda e: e.reciprocal(s8[:, ss_col + 1:ss_col + 2], s8[:, ss_col + 1:ss_col + 2]), reads=[("s8", ss_col + 1)], writes=[("s8", ss_col + 1)])

                for t in range(NT):
                    b = t % 2
                    rows = slice(t * 128, (t + 1) * 128)
                    for g3 in range(3):
                        A("sp", lambda e, g3=g3: e.dma_start(out=o3[b][:, g3, :, :], in_=scr_o[g3, rows, :, :]),
                          reads=[("scr_o", g3)], writes=[f"o3{b}"], dma_key=f"o3{b}")
                    A("sp", lambda e: e.dma_start(out=gat[b][:], in_=scr_gate[rows, 1024:2048]), reads=[("scr_gate", t)], writes=[f"gat{b}"], dma_key=f"gat{b}")
                    A("sp", lambda e: e.dma_start(out=m1c[b][:], in_=scr_m1[rows, :]), reads=[("scr_m1", t)], writes=[f"m1c{b}"], dma_key=f"m1c{b}")
                    A("sp", lambda e: e.dma_start(out=xin[b][:], in_=x_in[seq, rows, :]), writes=[f"xin{b}"], dma_key=f"xin{b}")
                    A("dve", lambda e: e.tensor_tensor(osum[:], o3[b][:, 0, :, :], o3[b][:, 1, :, :], ALU.add), reads=[f"o3{b}"], writes=["osum"])
                    A("dve", lambda e: e.tensor_tensor(osum[:], osum[:], o3[b][:, 2, :, :], ALU.add), reads=[f"o3{b}", "osum"], writes=["osum"])
                    A("dve", lambda e: e.reciprocal(rl[:], osum[:, :, 64]), reads=["osum"], writes=["rl"])
                    A("dve", lambda e: e.tensor_tensor(Ob[:].rearrange("p (h c) -> p h c", h=8), osum[:, :, 0:64], rl[:].unsqueeze(2).to_broadcast([128, 8, 64]), ALU.mult),
                      reads=["osum", "rl"], writes=["Ob"])
                    for k in range(4):
                        A("pe", lambda e, k=k: e.transpose(pT3[b][:, k * 128:(k + 1) * 128], Ob[:, k * 128:(k + 1) * 128], ident[:]), reads=["Ob", "consts"], writes=[f"pT3{b}"])
                    A("act", lambda e: e.copy(OT[:], pT3[b][:, 0:512].rearrange("p (k t) -> p k t", k=4)), reads=[f"pT3{b}"], writes=["OT"])
                    for hf in range(2):
                        for k in range(4):
                            A("pe", lambda e, k=k, hf=hf: e.matmul(pM[b][:, hf * 512:(hf + 1) * 512], OT[:, k, :], watt[:, k, hf * 512:(hf + 1) * 512], start=(k == 0), stop=(k == 3)),
                              reads=["OT", "watt"], writes=[f"pM{b}"])
                    A("dve", lambda e: e.tensor_tensor(tmpf[:], pM[b][:], gat[b][:], ALU.mult), reads=[f"pM{b}", f"gat{b}"], writes=["tmpf"])
                    A("pool", lambda e: e.tensor_tensor(mrg[:], tmpf[:], m1c[b][:], ALU.add), reads=["tmpf", f"m1c{b}"], writes=["mrg"])
                    for k in range(8):
                        A("pe", lambda e, k=k: e.transpose(pT3[b][:, k * 128:(k + 1) * 128], mrg[:, k * 128:(k + 1) * 128], ident[:]), reads=["mrg", "consts"], writes=[f"pT3{b}"])
                    A("act", lambda e: e.copy(mrgT[:], pT3[b][:].rearrange("p (k t) -> p k t", k=8)), reads=[f"pT3{b}"], writes=["mrgT"])
                    for hf in range(2):
                        for k in range(8):
                            A("pe", lambda e, k=k, hf=hf: e.matmul(pM[b][:, hf * 512:(hf + 1) * 512], mrgT[:, k, :], wout[:, k, hf * 512:(hf + 1) * 512], start=(k == 0), stop=(k == 7)),
                              reads=["mrgT", "wout"], writes=[f"pM{b}"])
                    rms_scale(pM[b][:], 0, [f"pM{b}"])
                    A("dve", lambda e: e.scalar_tensor_tensor(tmpf[:], pM[b][:], s8[:, 1:2], gns[:, 0, :], ALU.mult, ALU.mult), reads=[f"pM{b}", ("s8", 1), "gns"], writes=["tmpf"])
                    A("pool", lambda e: e.tensor_tensor(x1[b][:], tmpf[:], xin[b][:], ALU.add), reads=["tmpf", f"xin{b}"], writes=[f"x1{b}"])
                    A("sp", lambda e: e.dma_start(out=y_out[seq, rows, :], in_=x1[b][:]), reads=[f"x1{b}"], writes=[("y", seq, t)], dma_key=f"x1{b}")
                    rms_scale(x1[b][:], 2, [f"x1{b}"])
                    A("dve", lambda e: e.scalar_tensor_tensor(h2[b][:], x1[b][:], s8[:, 3:4], gns[:, 1, :], ALU.mult, ALU.mult), reads=[f"x1{b}", ("s8", 3), "gns"], writes=[f"h2{b}"])
                    A("sp", lambda e: e.dma_start(out=scr_h2[seq, rows, :], in_=h2[b][:]), reads=[f"h2{b}"], writes=[("scr_h2", seq, t)], dma_key=f"h2{b}")
            S_.barrier()
        if stop_after > 5:
            with ExitStack() as st:
                def sb(name, shape, dt):
                    return st.enter_context(nc.sbuf_tensor(f"{name}_ffn", shape, dt))

                def ps(name, shape, dt):
                    return st.enter_context(nc.psum_tensor(f"{name}_ffn", shape, dt))

                TBLK = 512
                NTB = TBLK // 128
                wfi = sb("wfi", [128, 8, 2 * FFN], BF16)
                wdn = sb("wdn", [128, 22, 1024], BF16)
                gpo = sb("gpo", [128, 1024], F32)
                h2b = [sb(f"h2b{i}", [128, 1024], BF16) for i in range(2)]
                h2T = sb("h2T", [128, 8, TBLK], BF16)
                actT = sb("actT", [128, 22, TBLK], BF16)
                gT = [sb(f"gT{i}", [128, TBLK], BF16) for i in range(2)]
                x1b = [sb(f"x1b{i}", [128, 1024], F32) for i in range(2)]
                tmpf = sb("tmpf", [128, 1024], F32)
                jk = sb("jk", [128, 1024], F32)
                s8 = sb("s8", [128, 8], F32)
                yo = [sb(f"yo{i}", [128, 1024], F32) for i in range(2)]
                pT3 = ps("pT3", [128, 1024], BF16)
                pM = ps("pM", [128, 1024], F32)
                pG = [ps(f"pG{i}", [128, 512], F32) for i in range(2)]
                pU = [ps(f"pU{i}", [128, 512], F32) for i in range(2)]
                for c4 in range(11):
                    A("pool", lambda e, c4=c4: e.dma_start(out=wfi[:, :, c4 * 512:(c4 + 1) * 512],
                                                           in_=w_ffn_in[:, c4 * 512:(c4 + 1) * 512].rearrange("(k p) c -> p k c", p=128)),
                      writes=[("wfi", c4)], dma_key="wfi")
                for q2 in range(2):
                    A("pool", lambda e, q2=q2: e.dma_start(out=wdn[:, q2 * 11:(q2 + 1) * 11, :],
                                                           in_=w_ffn_down[q2 * 1408:(q2 + 1) * 1408, :].rearrange("(k p) c -> p k c", p=128)),
                      writes=["wdn"], dma_key="wdn")
                A("sp", lambda e: e.dma_start(out=gpo[:], in_=norm_ffn_post.partition_broadcast(128)), writes=["gpo"], dma_key="gpo")
                wfi_all = [("wfi", c4) for c4 in range(11)]
                cntt = 0
                for seq in range(NSEQ):
                    for blk in range(S // TBLK):
                        for tt in range(NTB):
                            t = blk * NTB + tt
                            b = cntt % 2
                            cntt += 1
                            rows = slice(t * 128, (t + 1) * 128)
                            A("sp", lambda e: e.dma_start(out=h2b[b][:], in_=scr_h2[seq, rows, :]), reads=[("scr_h2", seq, t)], writes=[f"h2b{b}"], dma_key=f"h2b{b}")
                            for k in range(8):
                                A("pe", lambda e, k=k: e.transpose(pT3[:, k * 128:(k + 1) * 128], h2b[b][:, k * 128:(k + 1) * 128], ident[:]), reads=[f"h2b{b}", "consts"], writes=["pT3"])
                            A("act", lambda e: e.copy(h2T[:, :, tt * 128:(tt + 1) * 128], pT3[:].rearrange("p (k t) -> p k t", k=8)), reads=["pT3"], writes=[("h2T", tt)])
                        h2T_all = [("h2T", tt) for tt in range(NTB)]
                        for f in range(22):
                            pb = f % 2
                            for k in range(8):
                                A("pe", lambda e, k=k: e.matmul(pG[pb][:], wfi[:, k, f * 128:(f + 1) * 128], h2T[:, k, :], start=(k == 0), stop=(k == 7)),
                                  reads=h2T_all + wfi_all, writes=[f"pG{pb}"])
                            for k in range(8):
                                A("pe", lambda e, k=k: e.matmul(pU[pb][:], wfi[:, k, FFN + f * 128:FFN + (f + 1) * 128], h2T[:, k, :], start=(k == 0), stop=(k == 7)),
                                  reads=h2T_all + wfi_all, writes=[f"pU{pb}"])
                            A("act", lambda e: e.activation(gT[pb][:], pG[pb][:], AF.Silu), reads=[f"pG{pb}"], writes=[f"gT{pb}"])
                            A("dve", lambda e: e.tensor_tensor(actT[:, f, :], pU[pb][:], gT[pb][:], ALU.mult),
                              reads=[f"pU{pb}", f"gT{pb}"], writes=[("actT", f)])
                        act_all = [("actT", f) for f in range(22)]
                        for tt in range(NTB):
                            t = blk * NTB + tt
                            b = (cntt + tt) % 2
                            rows = slice(t * 128, (t + 1) * 128)
                            A("sp", lambda e: e.dma_start(out=x1b[b][:], in_=y_out[seq, rows, :]), reads=[("y", seq, t)], writes=[f"x1b{b}"], dma_key=f"x1b{b}")
                            for f in range(22):
                                for hf in range(2):
                                    A("pe", lambda e, f=f, hf=hf: e.matmul(pM[:, hf * 512:(hf + 1) * 512], actT[:, f, tt * 128:(tt + 1) * 128], wdn[:, f, hf * 512:(hf + 1) * 512],
                                                                      start=(f == 0), stop=(f == 21)),
                                      reads=act_all + ["wdn"], writes=["pM"])
                            A("act", lambda e: e.activation(jk[:], pM[:], AF.Square, accum_out=s8[:, 0:1]), reads=["pM"], writes=["jk", "ss"])
                            A("act", lambda e: e.activation(s8[:, 1:2], s8[:, 0:1], AF.Sqrt, bias=EPS, scale=1.0 / D), reads=["ss"], writes=["sd"])
                            A("dve", lambda e: e.reciprocal(s8[:, 1:2], s8[:, 1:2]), reads=["sd"], writes=["sd"])
                            A("dve", lambda e: e.scalar_tensor_tensor(tmpf[:], pM[:], s8[:, 1:2], gpo[:], ALU.mult, ALU.mult), reads=["pM", "sd", "gpo"], writes=["tmpf"])
                            A("pool", lambda e: e.tensor_tensor(yo[b][:], tmpf[:], x1b[b][:], ALU.add), reads=["tmpf", f"x1b{b}"], writes=[f"yo{b}"])
                            A("sp", lambda e: e.dma_start(out=y_out[seq, rows, :], in_=yo[b][:]), reads=[f"yo{b}"], writes=[("y", seq, t)], dma_key=f"yo{b}")
            S_.barrier()
        S_.emit(final_waits=list(S_.dma_count.keys()))
    nc._marks = getattr(S_, 'marks', [])
    return nc


def _host_consts(inp_conv_w, inp_conv_b, inp_norm_w, S):
    p = np.arange(128)
    m = p % 64
    rot = (m < 16)
    h2 = (m >= 8) & rot
    fi = np.where(rot, m % 8, 0)
    invf = (500000.0 ** (-(2.0 * fi) / 16.0)).astype(np.float32)
    ang = np.arange(S, dtype=np.float32)[None, :] * invf[:, None]
    cos = np.where(rot[:, None], np.cos(ang), 1.0).astype(np.float32)
    sgn = np.where(rot, np.where(h2, 1.0, -1.0), 0.0).astype(np.float32)
    sin = (np.sin(ang) * sgn[:, None]).astype(np.float32)
    cw = np.ascontiguousarray(inp_conv_w.reshape(5, 32, 128).transpose(2, 1, 0).reshape(128, 160), dtype=np.float32)
    cb = np.ascontiguousarray(inp_conv_b.reshape(32, 128).T, dtype=np.float32)
    nw = np.ascontiguousarray(inp_norm_w.reshape(16, 128).T, dtype=np.float32)
    return {"rot_cos": cos, "rot_sin": sin, "cw_l": cw, "cb_l": cb, "nw_l": nw}


_NC_CACHE = {}


def kernel(**inputs):
    x = np.asarray(inputs["x"], dtype=np.float32)
    B, S, _ = x.shape
    n_cores = 8
    NSEQ = B // n_cores
    key = (S, NSEQ)
    if key not in _NC_CACHE:
        _NC_CACHE[key] = build(S, NSEQ)
    nc = _NC_CACHE[key]
    f = lambda k: np.ascontiguousarray(np.asarray(inputs[k], dtype=np.float32)[0])
    shared = {
        "norm_mix_pre": f("norm_mix_pre").reshape(1, D), "w_in": f("w_in"),
        "ssd_conv_w": f("ssd_conv_w"), "ssd_conv_b": f("ssd_conv_b").reshape(1, 4096),
        "ssd_dt_bias": f("ssd_dt_bias").reshape(1, 64), "ssd_A_log": f("ssd_A_log").reshape(1, 64),
        "ssd_D": f("ssd_D").reshape(1, 32), "ssd_norm_w": f("ssd_norm_w").reshape(1, DIN),
        "w_ssd_branch": f("w_ssd_branch"), "w_attn_branch": f("w_attn_branch"), "w_out": f("w_out"),
        "norm_mix_post": f("norm_mix_post").reshape(1, D), "norm_ffn_pre": f("norm_ffn_pre").reshape(1, D),
        "w_ffn_in": f("w_ffn_in"), "w_ffn_down": f("w_ffn_down"), "norm_ffn_post": f("norm_ffn_post").reshape(1, D),
    }
    shared.update(_host_consts(shared["ssd_conv_w"], shared["ssd_conv_b"], shared["ssd_norm_w"], S))
    in_maps = []
    big = ("w_in", "w_ssd_branch", "w_attn_branch", "w_out", "w_ffn_in", "w_ffn_down", "rot_cos", "rot_sin")
    for c in range(n_cores):
        m = dict(shared)
        for k in big:
            a = shared[k]
            m[k] = np.concatenate([a, np.full((1, a.shape[1]), float(c), np.float32)], axis=0)
        m["x"] = np.ascontiguousarray(x[c * NSEQ:(c + 1) * NSEQ])
        in_maps.append(m)
    res = run_bass_kernel_spmd(nc, in_maps, core_ids=list(range(n_cores)))
    return np.concatenate([np.asarray(r["y"], dtype=np.float32) for r in res.results], axis=0)
```

```python
import numpy as np
from contextlib import ExitStack
import concourse.bass as bass
import concourse.mybir as mybir
from concourse.bass_utils import run_bass_kernel_spmd

F32 = mybir.dt.float32
BF16 = mybir.dt.bfloat16
I32 = mybir.dt.int32
AF = mybir.ActivationFunctionType
ALU = mybir.AluOpType

EPOCH = 20000
D = 1024
DIN = 2048
FFN = 2816
INC = 12864
OFF_Z, OFF_X, OFF_B, OFF_C, OFF_DT, OFF_Q, OFF_K, OFF_V, OFF_G = 0, 2048, 4096, 5120, 6144, 6208, 7744, 9280, 10816
DIL = (1, 4, 16)
EPS = 1e-6
NEGV = -30000.0


class _Rec:
    def __getattr__(self, name):
        return lambda *a, **kw: (name, a, kw)


_REC = _Rec()


class Op:
    __slots__ = ("eng", "fn", "idx", "deps", "signal", "dma_key", "sig_n")

    def __init__(self, eng, fn, dma_key=None):
        self.eng = eng
        self.fn = fn(_REC)
        self.deps = []
        self.signal = False
        self.dma_key = dma_key
        self.sig_n = 0


class Sched:
    ENGS = ("pe", "act", "dve", "pool", "sp")

    def __init__(self, nc):
        self.nc = nc
        self.eng_ops = {e: [] for e in self.ENGS}
        self.last_writer = {}
        self.readers = {}
        self.dma_count = {}
        self.waited = {e: {} for e in self.ENGS}

    def add(self, eng, fn, reads=(), writes=(), dma_key=None):
        op = Op(eng, fn, dma_key)
        op.idx = len(self.eng_ops[eng])
        self.eng_ops[eng].append(op)
        deps = set()
        for r in reads:
            w = self.last_writer.get(r)
            if w is not None:
                deps.add(w)
        for w in writes:
            lw = self.last_writer.get(w)
            if lw is not None:
                deps.add(lw)
            for rd in self.readers.get(w, ()):
                deps.add(rd)
        self._attach(op, deps)
        if dma_key is not None:
            self.dma_count[dma_key] = self.dma_count.get(dma_key, 0) + 1
        for r in reads:
            self.readers.setdefault(r, []).append(op)
        for w in writes:
            self.last_writer[w] = op
            self.readers[w] = []
        return op

    def _attach(self, op, deps):
        eng = op.eng
        best = {}
        for d in deps:
            if d is op:
                continue
            if d.dma_key is not None:
                k = ("dma", d.dma_key)
                v = self.dma_count[d.dma_key]
            else:
                if d.eng == "pe" and eng == "pe" and op.dma_key is None:
                    continue
                k = ("eng", d.eng)
                v = d.idx
            if k not in best or best[k][0] < v:
                best[k] = (v, d)
        wd = self.waited[eng]
        for k, (v, d) in best.items():
            if k in wd and wd[k] >= v:
                continue
            wd[k] = v
            if k[0] == "eng":
                d.signal = True
            op.deps.append((k, v, d))

    def barrier(self):
        if not hasattr(self, "marks"):
            self.marks = []
        self.marks.append({e: sum(1 + len(o.deps) for o in self.eng_ops[e]) for e in self.ENGS})
        lasts = []
        for e in self.ENGS:
            ops = [o for o in self.eng_ops[e] if o.dma_key is None]
            if ops:
                lasts.append(ops[-1])
        dmas = {}
        for e in self.ENGS:
            for o in self.eng_ops[e]:
                if o.dma_key is not None:
                    dmas[o.dma_key] = o
        for e in self.ENGS:
            op = Op(e, lambda eng: eng.nop())
            op.idx = len(self.eng_ops[e])
            self.eng_ops[e].append(op)
            self._attach(op, set(lasts) | set(dmas.values()))
        self.last_writer = {}
        self.readers = {}

    def emit(self, final_waits=()):
        nc = self.nc
        nsig = {}
        for e in self.ENGS:
            n = 0
            for op in self.eng_ops[e]:
                if op.dma_key is None and op.signal:
                    n += 1
                    op.sig_n = n
            nsig[e] = n
        with ExitStack() as st:
            esems = {}
            for e in self.ENGS:
                for ep in range((nsig[e] + EPOCH - 1) // EPOCH):
                    esems[(e, ep)] = st.enter_context(nc.semaphore(f"s_{e}_{ep}"))
            dsems = {}
            for i, k in enumerate(self.dma_count):
                dsems[k] = st.enter_context(nc.semaphore(f"d{i}"))
            block = st.enter_context(nc.Block())

            def run(e, engh):
                for op in self.eng_ops[e]:
                    for (k, v, d) in op.deps:
                        if k[0] == "dma":
                            engh.wait_ge(dsems[k[1]], 16 * v)
                        else:
                            n = d.sig_n
                            engh.wait_ge(esems[(d.eng, (n - 1) // EPOCH)], (n - 1) % EPOCH + 1)
                    name, a_, kw_ = op.fn
                    ins = getattr(engh, name)(*a_, **kw_)
                    if op.dma_key is not None:
                        ins.then_inc(dsems[op.dma_key], 16)
                    elif op.signal:
                        n = op.sig_n
                        ins.then_inc(esems[(e, (n - 1) // EPOCH)], 1)
                if e == "sp":
                    for k in final_waits:
                        engh.wait_ge(dsems[k], 16 * self.dma_count[k])

            @block.tensor
            def _(eng):
                run("pe", eng)

            @block.scalar
            def _(eng):
                run("act", eng)

            @block.vector
            def _(eng):
                run("dve", eng)

            @block.gpsimd
            def _(eng):
                run("pool", eng)

            @block.sync
            def _(eng):
                run("sp", eng)


def build(S, NSEQ, debug=False, stop_after=99, cut=99):
    nc = bass.Bass("TRN2", target_bir_lowering=False)
    NT = S // 128
    TB = S // 512

    def din(name, shape, pad=False):
        if pad:
            return nc.dram_tensor(name, [shape[0] + 1] + list(shape[1:]), F32, kind="ExternalInput").ap()[0:shape[0]]
        return nc.dram_tensor(name, shape, F32, kind="ExternalInput").ap()

    x_in = din("x", [NSEQ, S, D])
    norm_mix_pre = din("norm_mix_pre", [1, D])
    w_in = din("w_in", [D, INC], pad=True)
    conv_w = din("ssd_conv_w", [5, 4096])
    conv_b = din("ssd_conv_b", [1, 4096])
    dt_bias = din("ssd_dt_bias", [1, 64])
    A_log = din("ssd_A_log", [1, 64])
    D_skip = din("ssd_D", [1, 32])
    ssd_norm_w = din("ssd_norm_w", [1, DIN])
    w_ssd = din("w_ssd_branch", [DIN, D], pad=True)
    w_attn = din("w_attn_branch", [512, D], pad=True)
    w_out = din("w_out", [D, D], pad=True)
    norm_mix_post = din("norm_mix_post", [1, D])
    norm_ffn_pre = din("norm_ffn_pre", [1, D])
    w_ffn_in = din("w_ffn_in", [D, 2 * FFN], pad=True)
    w_ffn_down = din("w_ffn_down", [FFN, D], pad=True)
    norm_ffn_post = din("norm_ffn_post", [1, D])
    rot_cos = din("rot_cos", [128, S], pad=True)
    rot_sin = din("rot_sin", [128, S], pad=True)
    cw_l = din("cw_l", [128, 160])
    cb_l = din("cb_l", [128, 32])
    nw_l = din("nw_l", [128, 16])
    y_out = nc.dram_tensor("y", [NSEQ, S, D], F32, kind="ExternalOutput").ap()

    skind = "ExternalOutput" if debug else "Internal"

    def scr(name, shape, dt=BF16):
        return nc.dram_tensor(name, shape, dt, kind=skind).ap()

    scr_z = scr("scr_z", [S, DIN])
    scr_gate = scr("scr_gate", [S, 2 * D])
    scr_x = scr("scr_x", [S, DIN])
    scr_B = scr("scr_B", [S, 1024])
    scr_BT = scr("scr_BT", [1024, S])
    scr_CT = scr("scr_CT", [1024, S])
    scr_dt = scr("scr_dt", [S, 64], F32)
    scr_yf = scr("scr_yf", [S, DIN])
    scr_qT = scr("scr_qT", [1536, S])
    scr_kT = scr("scr_kT", [1536, S])
    scr_v = scr("scr_v", [3, S, 512])
    scr_o = scr("scr_o", [3, S, 8, 65])
    scr_m1 = scr("scr_m1", [S, D])
    scr_h2 = scr("scr_h2", [NSEQ, S, D])
    wfi_bf = nc.dram_tensor("wfi_bf", [D, 2 * FFN], BF16).ap()
    wfd_bf = nc.dram_tensor("wfd_bf", [FFN, D], BF16).ap()

    S_ = Sched(nc)
    A = S_.add

    with ExitStack() as gst:
        def gsb(name, shape, dt):
            return gst.enter_context(nc.sbuf_tensor(name, shape, dt))

        ident = gsb("ident", [128, 128], BF16)
        cst_f = gsb("cst_f", [128, 4, 128], F32)
        cst_b = gsb("cst_b", [128, 8, 128], BF16)
        band = gsb("band", [128, 3, 128], BF16)
        tmpc = gsb("tmpc", [128, 128], F32)
        smallc = gsb("smallc", [128, 64 + 64 + 32], F32)

        def mk_const(dst_f32_ap, fill_base, selects):
            A("pool", lambda e: e.memset(dst_f32_ap, fill_base), writes=["tmpc"])
            for (pat, cm, base, fill) in selects:
                A("pool", lambda e, pat=pat, cm=cm, base=base, fill=fill: e.affine_select(
                    dst_f32_ap, dst_f32_ap, [[pat, 128]], ALU.is_ge, fill, base=base, channel_multiplier=cm),
                  reads=["tmpc"], writes=["tmpc"])

        def to_bf(dst_ap, src_ap, scale=None):
            if scale is None:
                A("dve", lambda e: e.tensor_copy(dst_ap, src_ap), reads=["tmpc"], writes=["consts"])
            else:
                A("dve", lambda e: e.tensor_scalar(dst_ap, src_ap, scale, None, ALU.mult), reads=["tmpc"], writes=["consts"])

        mk_const(tmpc[:], 1.0, [(1, -1, 0, 0.0), (-1, 1, 0, 0.0)])
        to_bf(ident[:], tmpc[:])
        A("dve", lambda e: e.tensor_copy(cst_f[:, 3, :], tmpc[:]), reads=["tmpc"], writes=["consts"])
        mk_const(tmpc[:], 1.0, [(1, -1, 0, 0.0)])
        to_bf(cst_b[:, 0, :], tmpc[:])
        to_bf(cst_b[:, 1, :], tmpc[:], -1.0)
        to_bf(cst_b[:, 6, :], tmpc[:])
        A("dve", lambda e: e.tensor_copy(cst_f[:, 0, :], tmpc[:]), reads=["tmpc"], writes=["consts"])
        mk_const(tmpc[:], 1.0, [(-1, 1, 0, 0.0)])
        to_bf(cst_b[:, 2, :], tmpc[:])
        to_bf(cst_b[:, 3, :], tmpc[:], -1.0)
        to_bf(cst_b[:, 7, :], tmpc[:])
        A("dve", lambda e: e.tensor_copy(cst_f[:, 1, :], tmpc[:]), reads=["tmpc"], writes=["consts"])
        mk_const(tmpc[:], 0.0, [(1, -1, 0, NEGV)])
        to_bf(cst_b[:, 4, :], tmpc[:])
        mk_const(tmpc[:], 0.0, [(-1, 1, 0, NEGV)])
        to_bf(cst_b[:, 5, :], tmpc[:])
        A("dve", lambda e: e.memset(cst_f[:, 2, :], 1.0), writes=["consts"])
        for oi, o in enumerate((-1, 0, 1)):
            mk_const(tmpc[:], 0.0, [(-1, 1, 128 * o + 64, NEGV), (1, -1, 64 - 128 * o, NEGV)])
            to_bf(band[:, oi, :], tmpc[:])
        A("sp", lambda e: e.dma_start(out=smallc[:, 0:64], in_=dt_bias.partition_broadcast(128)), writes=["smallc"], dma_key="c0")
        A("sp", lambda e: e.dma_start(out=smallc[:, 64:128], in_=A_log.partition_broadcast(128)), writes=["smallc"], dma_key="c0")
        A("sp", lambda e: e.dma_start(out=smallc[:, 128:160], in_=D_skip.partition_broadcast(128)), writes=["smallc"], dma_key="c0")
        A("act", lambda e: e.activation(smallc[:, 64:128], smallc[:, 64:128], AF.Exp), reads=["smallc"], writes=["smallc"])
        A("dve", lambda e: e.tensor_scalar(smallc[:, 64:128], smallc[:, 64:128], -1.0, None, ALU.mult), reads=["smallc"], writes=["smallc"])
        for seq in range(NSEQ):
            with ExitStack() as st:
                def sb(name, shape, dt):
                    return st.enter_context(nc.sbuf_tensor(f"{name}_{seq}", shape, dt))

                def ps(name, shape, dt):
                    return st.enter_context(nc.psum_tensor(f"{name}_{seq}", shape, dt))

                hT = sb("hT", [128, 8, S], BF16)
                gpre = sb("gpre", [128, D], F32)
                xt = [sb(f"xt{i}", [128, D], F32) for i in range(2)]
                hb = [sb(f"hb{i}", [128, D], BF16) for i in range(2)]
                junk = sb("junk", [128, D], F32)
                st8 = sb("st8", [128, 8], F32)
                pT = [ps(f"pT{i}", [128, 1024], BF16) for i in range(2)]
                pA = [ps(f"pA{i}", [128, 512], F32) for i in range(2)]
                pB = [ps(f"pB{i}", [128, 512], F32) for i in range(2)]

                A("sp", lambda e: e.dma_start(out=gpre[:], in_=norm_mix_pre.partition_broadcast(128)), writes=["gpre"], dma_key="gpre")
                for t in range(NT):
                    b = t % 2
                    A("sp", lambda e, t=t, b=b: e.dma_start(out=xt[b][:], in_=x_in[seq, t * 128:(t + 1) * 128, :]),
                      writes=[f"xt{b}"], dma_key=f"xt{b}")
                    A("act", lambda e, b=b: e.activation(junk[:], xt[b][:], AF.Square, accum_out=st8[:, b:b + 1]),
                      reads=[f"xt{b}"], writes=["junk", f"ss{b}"])
                    A("act", lambda e, b=b: e.activation(st8[:, 2 + b:3 + b], st8[:, b:b + 1], AF.Sqrt, bias=EPS, scale=1.0 / D),
                      reads=[f"ss{b}"], writes=[f"sd{b}"])
                    A("dve", lambda e, b=b: e.reciprocal(st8[:, 4 + b:5 + b], st8[:, 2 + b:3 + b]), reads=[f"sd{b}"], writes=[f"rs{b}"])
                    A("dve", lambda e, b=b: e.scalar_tensor_tensor(hb[b][:], xt[b][:], st8[:, 4 + b:5 + b], gpre[:], ALU.mult, ALU.mult),
                      reads=[f"xt{b}", f"rs{b}", "gpre"], writes=[f"hb{b}"])
                    for k in range(8):
                        A("pe", lambda e, b=b, k=k: e.transpose(pT[b][:, k * 128:(k + 1) * 128], hb[b][:, k * 128:(k + 1) * 128], ident[:]),
                          reads=[f"hb{b}", "consts"], writes=[f"pT{b}"])
                    A("act", lambda e, b=b, t=t: e.copy(hT[:, :, t * 128:(t + 1) * 128], pT[b][:].rearrange("p (k t) -> p k t", k=8)),
                      reads=[f"pT{b}"], writes=[("hT", t)])
                hT_all = [("hT", t) for t in range(NT)]
                if cut <= 1:
                    S_.barrier(); continue

                wsl = [sb(f"wsl{i}", [128, 8, 512], BF16) for i in range(2)]
                wp = sb("wp", [128, 8, 512], BF16)
                stg = [sb(f"stg{i}", [128, 512], BF16) for i in range(3)]
                stgf = [sb(f"stgf{i}", [128, 64], F32) for i in range(2)]
                outX = [sb(f"outX{i}", [128, S], BF16) for i in range(4)]
                rawT = [sb(f"rawT{i}", [128, S + 4], BF16) for i in range(2)]
                tmp1 = [sb(f"tmp1{i}", [128, 512], F32) for i in range(2)]
                tmp2 = [sb(f"tmp2{i}", [128, 512], F32) for i in range(2)]
                cosT = sb("cosT", [128, S], BF16)
                sinT = sb("sinT", [128, S], BF16)
                cw = sb("cw", [128, 32, 5], F32)
                cb = sb("cb", [128, 32], F32)
                dg = sb("dg", [128, 5, 128], BF16)
                wcnt = [0]
                scnt = [0]

                def load_w(lo, ncols):
                    i = wcnt[0] % 2
                    wcnt[0] += 1
                    A("pool", lambda e: e.dma_start(out=wsl[i][:, :, 0:ncols],
                                                    in_=w_in[:, lo:lo + ncols].rearrange("(k p) c -> p k c", p=128)),
                      writes=[f"wsl{i}"], dma_key=f"wsl{i}")
                    return i

                def stage():
                    i = scnt[0] % 3
                    scnt[0] += 1
                    return i

                A("pool", lambda e: e.memset(wp[:], 0.0), writes=["wp"])
                for i in range(2):
                    A("pool", lambda e, i=i: e.memset(rawT[i][:, 0:2], 0.0), writes=[f"rawT{i}"])
                    A("pool", lambda e, i=i: e.memset(rawT[i][:, S + 2:S + 4], 0.0), writes=[f"rawT{i}"])
                A("sp", lambda e: e.dma_start(out=cw[:].rearrange("p f k -> p (f k)"), in_=cw_l), writes=["cw"], dma_key="cw")
                A("sp", lambda e: e.dma_start(out=cb[:], in_=cb_l), writes=["cb"], dma_key="cw")
                A("pool", lambda e: e.dma_start(out=cosT[:], in_=rot_cos), writes=["cosT"], dma_key="rot")
                A("pool", lambda e: e.dma_start(out=sinT[:], in_=rot_sin), writes=["sinT"], dma_key="rot")
                if cut <= 2:
                    S_.barrier(); continue
                def tok_block(lo, ncols, func, dst, dst_lo, dkey):
                    wi = load_w(lo, ncols)
                    for t in range(NT):
                        b = t % 2
                        for k in range(8):
                            A("pe", lambda e, t=t, k=k, b=b: e.matmul(pA[b][:, 0:ncols], hT[:, k, t * 128:(t + 1) * 128], wsl[wi][:, k, 0:ncols],
                                                                      start=(k == 0), stop=(k == 7)),
                              reads=[("hT", t), f"wsl{wi}"], writes=[f"pA{b}"])
                        si = stage()
                        A("act", lambda e, b=b, si=si: e.activation(stg[si][:, 0:ncols], pA[b][:, 0:ncols], func),
                          reads=[f"pA{b}"], writes=[f"stg{si}"])
                        A("sp", lambda e, t=t, si=si: e.dma_start(out=dst[t * 128:(t + 1) * 128, dst_lo:dst_lo + ncols], in_=stg[si][:, 0:ncols]),
                          reads=[f"stg{si}"], writes=[(dkey, t)], dma_key=f"stg{si}")

                for blk in range(4):
                    tok_block(OFF_Z + blk * 512, 512, AF.Silu, scr_z, blk * 512, "scr_z")
                for blk in range(4):
                    tok_block(OFF_G + blk * 512, 512, AF.Sigmoid, scr_gate, blk * 512, "scr_gate")
                if cut <= 3:
                    S_.barrier(); continue
                wi = load_w(OFF_DT, 64)
                for t in range(NT):
                    b = t % 2
                    for k in range(8):
                        A("pe", lambda e, t=t, k=k, b=b: e.matmul(pA[b][:, 0:64], hT[:, k, t * 128:(t + 1) * 128], wsl[wi][:, k, 0:64],
                                                                  start=(k == 0), stop=(k == 7)),
                          reads=[("hT", t), f"wsl{wi}"], writes=[f"pA{b}"])
                    A("act", lambda e, b=b: e.copy(stgf[b][:], pA[b][:, 0:64]), reads=[f"pA{b}"], writes=[f"stgf{b}"])
                    A("sp", lambda e, t=t, b=b: e.dma_start(out=scr_dt[t * 128:(t + 1) * 128, :], in_=stgf[b][:]),
                      reads=[f"stgf{b}"], writes=[("scr_dt", t)], dma_key=f"stgf{b}")
                if cut <= 4:
                    S_.barrier(); continue
                for g in range(3):
                    d = DIL[g]
                    n = S // d
                    wi = load_w(OFF_V + g * 512, 512)
                    cnt = 0
                    for r in range(d):
                        for i in range(n // 128):
                            b = cnt % 2
                            cnt += 1
                            base = i * 128 * d + r
                            tl = sorted(set((base + j * d) // 128 for j in (0, 127)))
                            tl = list(range(tl[0], tl[-1] + 1))
                            for k in range(8):
                                A("pe", lambda e, k=k, b=b, base=base, d=d: e.matmul(
                                    pA[b][:], hT[:, k, base:base + 127 * d + 1:d], wsl[wi][:, k, :], start=(k == 0), stop=(k == 7)),
                                  reads=[("hT", tt) for tt in tl] + [f"wsl{wi}"], writes=[f"pA{b}"])
                            si = stage()
                            A("act", lambda e, b=b, si=si: e.copy(stg[si][:], pA[b][:]), reads=[f"pA{b}"], writes=[f"stg{si}"])
                            row = r * n + i * 128
                            A("sp", lambda e, si=si, row=row, g=g: e.dma_start(out=scr_v[g, row:row + 128, :], in_=stg[si][:]),
                              reads=[f"stg{si}"], writes=[("scr_v", g)], dma_key=f"stg{si}")
                if cut <= 5:
                    S_.barrier(); continue
                for qk in range(2):
                    off = OFF_Q if qk == 0 else OFF_K
                    dstT = scr_qT if qk == 0 else scr_kT
                    for blk in range(3):
                        g = blk
                        d = DIL[g]
                        n = S // d
                        wi = load_w(off + blk * 512, 512)
                        if qk == 0:
                            A("dve", lambda e, wi=wi: e.tensor_scalar(wsl[wi][:], wsl[wi][:], 0.125, None, ALU.mult),
                              reads=[f"wsl{wi}"], writes=[f"wsl{wi}"])
                        wv = wsl[wi][:].rearrange("p k (h c) -> p k h c", h=8)
                        wpv = wp[:].rearrange("p k (h c) -> p k h c", h=8)
                        A("dve", lambda e, wv=wv, wpv=wpv: e.tensor_copy(wpv[:, :, :, 0:8], wv[:, :, :, 8:16]), reads=[f"wsl{wi}"], writes=["wp"])
                        A("dve", lambda e, wv=wv, wpv=wpv: e.tensor_copy(wpv[:, :, :, 8:16], wv[:, :, :, 0:8]), reads=[f"wsl{wi}"], writes=["wp"])
                        for hp in range(4):
                            ox = outX[hp]
                            for tb in range(TB):
                                b = tb % 2
                                for k in range(8):
                                    A("pe", lambda e, k=k, b=b, tb=tb, hp=hp, wi=wi: e.matmul(
                                        pA[b][:], wsl[wi][:, k, hp * 128:(hp + 1) * 128], hT[:, k, tb * 512:(tb + 1) * 512],
                                        start=(k == 0), stop=(k == 7)),
                                      reads=hT_all[tb * 4:(tb + 1) * 4] + [f"wsl{wi}"], writes=[f"pA{b}"])
                                for k in range(8):
                                    A("pe", lambda e, k=k, b=b, tb=tb, hp=hp: e.matmul(
                                        pB[b][:], wp[:, k, hp * 128:(hp + 1) * 128], hT[:, k, tb * 512:(tb + 1) * 512],
                                        start=(k == 0), stop=(k == 7)),
                                      reads=hT_all[tb * 4:(tb + 1) * 4] + ["wp"], writes=[f"pB{b}"])
                                sl = slice(tb * 512, (tb + 1) * 512)
                                A("dve", lambda e, b=b, sl=sl: e.tensor_tensor(tmp1[b][:], pB[b][:], sinT[:, sl], ALU.mult),
                                  reads=[f"pB{b}", "sinT"], writes=[f"tmp1{b}"])
                                A("dve", lambda e, b=b, sl=sl: e.tensor_tensor(tmp2[b][:], pA[b][:], cosT[:, sl], ALU.mult),
                                  reads=[f"pA{b}", "cosT"], writes=[f"tmp2{b}"])
                                i0 = tb * 512 // d
                                ni = 512 // d
                                oap = ox[:].rearrange("p (r i) -> p r i", r=d)[:, :, i0:i0 + ni]
                                A("pool", lambda e, b=b, oap=oap, d=d: e.tensor_tensor(
                                    oap, tmp1[b][:].rearrange("p (i r) -> p r i", r=d), tmp2[b][:].rearrange("p (i r) -> p r i", r=d), ALU.add),
                                  reads=[f"tmp1{b}", f"tmp2{b}"], writes=[f"outX{hp}"])
                            row = blk * 512 + hp * 128
                            A("sp", lambda e, hp=hp, row=row, dstT=dstT: e.dma_start(out=dstT[row:row + 128, :], in_=outX[hp][:]),
                              reads=[f"outX{hp}"], writes=[("scr_qk", qk, blk, hp)], dma_key=f"outX{hp}")
                if cut <= 6:
                    S_.barrier(); continue
                for grp in range(8):
                    wi = load_w(OFF_X + grp * 512, 512)
                    for j in range(4):
                        ft = grp * 4 + j
                        rb = ft % 2
                        for k5 in range(5):
                            A("dve", lambda e, k5=k5, ft=ft: e.tensor_scalar(dg[:, k5, :], ident[:], cw[:, ft, k5:k5 + 1], None, ALU.mult),
                              reads=["consts", "cw"], writes=["dg"])
                        for tb in range(TB):
                            b = tb % 2
                            for k in range(8):
                                A("pe", lambda e, k=k, b=b, tb=tb, j=j, wi=wi: e.matmul(
                                    pA[b][:], wsl[wi][:, k, j * 128:(j + 1) * 128], hT[:, k, tb * 512:(tb + 1) * 512],
                                    start=(k == 0), stop=(k == 7)),
                                  reads=hT_all[tb * 4:(tb + 1) * 4] + [f"wsl{wi}"], writes=[f"pA{b}"])
                            A("dve", lambda e, b=b, tb=tb, rb=rb: e.tensor_copy(rawT[rb][:, 2 + tb * 512:2 + (tb + 1) * 512], pA[b][:]),
                              reads=[f"pA{b}"], writes=[f"rawT{rb}"])
                        for tb in range(TB):
                            b = tb % 2
                            for k5 in range(5):
                                A("pe", lambda e, k5=k5, b=b, tb=tb, rb=rb: e.matmul(
                                    pB[b][:], dg[:, k5, :], rawT[rb][:, tb * 512 + k5:tb * 512 + k5 + 512], start=(k5 == 0), stop=(k5 == 4)),
                                  reads=["dg", f"rawT{rb}"], writes=[f"pB{b}"])
                            A("act", lambda e, b=b, tb=tb, j=j, ft=ft: e.activation(
                                outX[j][:, tb * 512:(tb + 1) * 512], pB[b][:], AF.Silu, bias=cb[:, ft:ft + 1]),
                              reads=[f"pB{b}", "cb"], writes=[f"outX{j}"])
                    if grp < 6:
                        dst, clo, dkey = (scr_x, grp * 512, "scr_x") if grp < 4 else (scr_B, (grp - 4) * 512, "scr_B")
                        for t in range(NT):
                            b = t % 2
                            for j in range(4):
                                A("pe", lambda e, b=b, j=j, t=t: e.transpose(pT[b][:, j * 128:(j + 1) * 128], outX[j][:, t * 128:(t + 1) * 128], ident[:]),
                                  reads=[f"outX{j}", "consts"], writes=[f"pT{b}"])
                            si = stage()
                            A("dve", lambda e, b=b, si=si: e.tensor_copy(stg[si][:], pT[b][:, 0:512]), reads=[f"pT{b}"], writes=[f"stg{si}"])
                            A("sp", lambda e, t=t, si=si, dst=dst, clo=clo: e.dma_start(out=dst[t * 128:(t + 1) * 128, clo:clo + 512], in_=stg[si][:]),
                              reads=[f"stg{si}"], writes=[(dkey, t)], dma_key=f"stg{si}")
                    if grp >= 4:
                        dstT = scr_BT if grp < 6 else scr_CT
                        rlo = (grp - 4) * 512 if grp < 6 else (grp - 6) * 512
                        for j in range(4):
                            A("sp", lambda e, j=j, dstT=dstT, rlo=rlo: e.dma_start(out=dstT[rlo + j * 128:rlo + (j + 1) * 128, :], in_=outX[j][:]),
                              reads=[f"outX{j}"], writes=[("scr_BCT", grp, j)], dma_key=f"outX{j}")
            S_.barrier()
            if stop_after <= 2:
                continue
            with ExitStack() as st:
                def sb(name, shape, dt):
                    return st.enter_context(nc.sbuf_tensor(f"{name}_s{seq}", shape, dt))

                def ps(name, shape, dt):
                    return st.enter_context(nc.psum_tensor(f"{name}_s{seq}", shape, dt))

                wssd = sb("wssd", [128, 16, 1024], BF16)
                nwl = sb("nwl", [128, 16], F32)
                xc = [sb(f"xc{i}", [128, 2048], BF16) for i in range(2)]
                Bc = [sb(f"Bc{i}", [128, 1024], BF16) for i in range(2)]
                BTc = [sb(f"BTc{i}", [128, 8, 128], BF16) for i in range(2)]
                CTc = [sb(f"CTc{i}", [128, 8, 128], BF16) for i in range(2)]
                dtc = [sb(f"dtc{i}", [128, 64], F32) for i in range(2)]
                zc = [sb(f"zc{i}", [128, 2048], BF16) for i in range(2)]
                yfc = [sb(f"yfc{i}", [128, 2048], BF16) for i in range(2)]
                gc = [sb(f"gc{i}", [128, 1024], BF16) for i in range(2)]
                ST = sb("ST", [128, 2048], F32)
                STb = sb("STb", [128, 2048], BF16)
                sm = sb("sm", [128, 128], F32)
                sm2 = sb("sm2", [128, 128], F32)
                dAhi = sb("dAhi", [128, 32], BF16)
                dAlo = sb("dAlo", [128, 32], BF16)
                CBm = [sb(f"CBm{i}", [128, 128], BF16) for i in range(2)]
                Eb = [sb(f"Eb{i}", [128, 512], BF16) for i in range(2)]
                Mb = [sb(f"Mb{i}", [128, 4, 128], BF16) for i in range(2)]
                xw = [sb(f"xw{i}", [128, 256], BF16) for i in range(2)]
                xdt = [sb(f"xdt{i}", [128, 256], BF16) for i in range(2)]
                tq = [sb(f"tq{i}", [128, 256], F32) for i in range(2)]
                tq2 = [sb(f"tq2{i}", [128, 256], F32) for i in range(2)]
                Ych = sb("Ych", [128, 2048], F32)
                Yst = [sb(f"Yst{i}", [128, 2048], BF16) for i in range(2)]
                Gb = sb("Gb", [128, 2048], BF16)
                GT = sb("GT", [128, 16, 128], BF16)
                junk2 = sb("junk2", [128, 2048], BF16)
                r8 = sb("r8", [128, 4], F32)
                m1s = [sb(f"m1s{i}", [128, 1024], BF16) for i in range(2)]
                pSeg = [ps(f"pSeg{i}", [128, 512], F32) for i in range(2)]
                pYY = [ps(f"pYY{i}", [128, 512], F32) for i in range(2)]
                pSx = [ps(f"pSx{i}", [128, 512], F32) for i in range(2)]
                pS = pSx[0]
                pT2 = ps("pT2", [128, 1024], BF16)
                pO = ps("pO", [128, 512], F32)

                for k in range(2):
                    A("pool", lambda e, k=k: e.dma_start(out=wssd[:, k * 8:(k + 1) * 8, :],
                                                         in_=w_ssd[k * 1024:(k + 1) * 1024, :].rearrange("(k p) c -> p k c", p=128)),
                      writes=["wssd"], dma_key="wssd")
                A("sp", lambda e: e.dma_start(out=nwl[:], in_=nw_l), writes=["nwl"], dma_key="nwl")
                for k in range(16):
                    A("dve", lambda e, k=k: e.tensor_scalar(wssd[:, k, :], wssd[:, k, :], nwl[:, k:k + 1], None, ALU.mult),
                      reads=["wssd", "nwl"], writes=["wssd"])
                iters = [(dr_, ci_) for dr_ in range(2) for ci_ in range(NT)]

                def issue_loads(k):
                    dr_, ci_ = iters[k]
                    c = ci_ if dr_ == 0 else NT - 1 - ci_
                    b = k % 2
                    rows = slice(c * 128, (c + 1) * 128)
                    A("sp", lambda e: e.dma_start(out=xc[b][:], in_=scr_x[rows, :]), reads=[("scr_x", c)], writes=[f"xc{b}"], dma_key=f"xc{b}")
                    A("sp", lambda e: e.dma_start(out=Bc[b][:], in_=scr_B[rows, :]), reads=[("scr_B", c)], writes=[f"Bc{b}"], dma_key=f"Bc{b}")
                    A("sp", lambda e: e.dma_start(out=BTc[b][:], in_=scr_BT[:, rows].rearrange("(g n) t -> n g t", n=128)),
                      reads=[("scr_BCT", gg, jj) for gg in (4, 5) for jj in range(4)], writes=[f"BTc{b}"], dma_key=f"BTc{b}")
                    A("sp", lambda e: e.dma_start(out=CTc[b][:], in_=scr_CT[:, rows].rearrange("(g n) t -> n g t", n=128)),
                      reads=[("scr_BCT", gg, jj) for gg in (6, 7) for jj in range(4)], writes=[f"CTc{b}"], dma_key=f"CTc{b}")
                    A("sp", lambda e: e.dma_start(out=dtc[b][:], in_=scr_dt[rows, :]), reads=[("scr_dt", c)], writes=[f"dtc{b}"], dma_key=f"dtc{b}")
                    if dr_ == 1:
                        A("sp", lambda e: e.dma_start(out=zc[b][:], in_=scr_z[rows, :]), reads=[("scr_z", c)], writes=[f"zc{b}"], dma_key=f"zc{b}")
                        if ci_ > 0:
                            A("sp", lambda e: e.dma_start(out=yfc[b][:], in_=scr_yf[rows, :]), reads=[("scr_yf", c)], writes=[f"yfc{b}"], dma_key=f"yfc{b}")
                        A("sp", lambda e: e.dma_start(out=gc[b][:], in_=scr_gate[rows, 0:1024]), reads=[("scr_gate", c)], writes=[f"gc{b}"], dma_key=f"gc{b}")

                cnt = 0
                issue_loads(0)
                for dr in range(2):
                    A("dve", lambda e: e.memset(ST[:], 0.0), writes=[("ST", g_) for g_ in range(8)])
                    A("dve", lambda e: e.memset(STb[:], 0.0), writes=[("STb", g_) for g_ in range(8)])
                    tri_f = cst_f[:, dr, :]
                    tri_b = cst_b[:, 0 + 2 * dr, :]
                    ntri_b = cst_b[:, 1 + 2 * dr, :]
                    neg_b = cst_b[:, 4 + dr, :]
                    msk_b = cst_b[:, 6 + dr, :]
                    for ci in range(NT):
                        c = ci if dr == 0 else NT - 1 - ci
                        b = cnt % 2
                        cnt += 1
                        rows = slice(c * 128, (c + 1) * 128)
                        if dr == 1 and ci == 0:
                            A("sp", lambda e, b=b, rows=rows: e.dma_start(out=yfc[b][:], in_=scr_yf[rows, :]), reads=[("scr_yf", c)], writes=[f"yfc{b}"], dma_key=f"yfc{b}")
                        if cnt < len(iters):
                            issue_loads(cnt)
                        o32 = dr * 32
                        A("dve", lambda e, b=b, o32=o32: e.tensor_tensor(sm[:, 0:32], dtc[b][:, o32:o32 + 32], smallc[:, o32:o32 + 32], ALU.add),
                          reads=[f"dtc{b}", "smallc"], writes=["sm0"])
                        A("act", lambda e: e.activation(sm[:, 32:64], sm[:, 0:32], AF.Exp), reads=["sm0"], writes=["sm1"])
                        A("act", lambda e: e.activation(sm[:, 64:96], sm[:, 32:64], AF.Ln, bias=1.0), reads=["sm1"], writes=["dt"])
                        A("dve", lambda e, o32=o32: e.tensor_tensor(sm[:, 96:128], sm[:, 64:96], smallc[:, 64 + o32:96 + o32], ALU.mult),
                          reads=["dt", "smallc"], writes=["dA"])
                        A("dve", lambda e: e.tensor_copy(dAhi[:], sm[:, 96:128]), reads=["dA"], writes=["dAhi"])
                        A("dve", lambda e: e.tensor_tensor(dAlo[:], sm[:, 96:128], dAhi[:], ALU.subtract), reads=["dA", "dAhi"], writes=["dAlo"])
                        A("pe", lambda e, tri_f=tri_f: e.matmul(pS[:, 256:288], tri_f, sm[:, 96:128], start=True, stop=True), reads=["dA", "consts"], writes=["pSx0"])
                        A("pe", lambda e: e.matmul(pS[:, 288:320], cst_f[:, 2, :], sm[:, 96:128], start=True, stop=True), reads=["dA", "consts"], writes=["pSx0"])
                        A("act", lambda e: e.copy(sm2[:, 0:32], pS[:, 256:288]), reads=["pSx0"], writes=["asb"])
                        A("act", lambda e: e.activation(sm2[:, 32:64], pS[:, 256:288], AF.Exp), reads=["pSx0"], writes=["ea"])
                        A("act", lambda e: e.activation(sm2[:, 64:96], pS[:, 288:320], AF.Exp), reads=["pSx0"], writes=["eal"])
                        A("dve", lambda e: e.tensor_tensor(sm2[:, 96:128], pS[:, 288:320], sm2[:, 0:32], ALU.subtract), reads=["pSx0", "asb"], writes=["wv"])
                        A("act", lambda e: e.activation(sm2[:, 96:128], sm2[:, 96:128], AF.Exp), reads=["wv"], writes=["wv"])
                        A("dve", lambda e: e.tensor_tensor(sm2[:, 96:128], sm2[:, 96:128], sm[:, 64:96], ALU.mult), reads=["wv", "dt"], writes=["wv"])
                        def stage_a(g, b=b, tri_b=tri_b, ntri_b=ntri_b, neg_b=neg_b, msk_b=msk_b):
                            q = g % 2
                            A("pe", lambda e: e.matmul(pSx[q][:, 320:448], BTc[b][:, g, :], CTc[b][:, g, :], start=True, stop=True),
                              reads=[f"BTc{b}", f"CTc{b}"], writes=[f"pSx{q}"])
                            A("dve", lambda e: e.tensor_tensor(CBm[q][:], pSx[q][:, 320:448], msk_b, ALU.mult), reads=[f"pSx{q}", "consts"], writes=[f"CBm{q}"])
                            for j in range(4):
                                h = g * 4 + j
                                osl = pSeg[q][:, j * 128:(j + 1) * 128]
                                hi = dAhi[:, h:h + 1].to_broadcast([128, 128])
                                lo = dAlo[:, h:h + 1].to_broadcast([128, 128])
                                A("pe", lambda e: e.matmul(osl, hi, tri_b, start=True, stop=False), reads=["dAhi", "consts"], writes=[f"pSeg{q}"])
                                A("pe", lambda e: e.matmul(osl, lo, tri_b, start=False, stop=False), reads=["dAlo", "consts"], writes=[f"pSeg{q}"])
                                A("pe", lambda e: e.matmul(osl, ntri_b, hi, start=False, stop=False), reads=["dAhi", "consts"], writes=[f"pSeg{q}"])
                                A("pe", lambda e: e.matmul(osl, ntri_b, lo, start=False, stop=False), reads=["dAlo", "consts"], writes=[f"pSeg{q}"])
                                A("pe", lambda e: e.matmul(osl, ident[:], neg_b, start=False, stop=True), reads=["consts"], writes=[f"pSeg{q}"])
                            A("act", lambda e: e.activation(Eb[q][:], pSeg[q][:], AF.Exp), reads=[f"pSeg{q}"], writes=[f"Eb{q}"])
                            xg = xc[b][:, g * 256:(g + 1) * 256].rearrange("p (j q) -> p j q", j=4)
                            wb = sm2[:, 96 + g * 4:100 + g * 4].unsqueeze(2).to_broadcast([128, 4, 64])
                            dtb = sm[:, 64 + g * 4:68 + g * 4].unsqueeze(2).to_broadcast([128, 4, 64])
                            A("pool", lambda e: e.tensor_tensor(xw[q][:].rearrange("p (j q) -> p j q", j=4), xg, wb, ALU.mult),
                              reads=[f"xc{b}", "wv"], writes=[f"xw{q}"])
                            A("pool", lambda e: e.tensor_tensor(xdt[q][:].rearrange("p (j q) -> p j q", j=4), xg, dtb, ALU.mult),
                              reads=[f"xc{b}", "dt"], writes=[f"xdt{q}"])

                        def stage_b(g, b=b):
                            q = g % 2
                            for j in range(4):
                                A("pool" if j % 2 else "dve", lambda e, j=j: e.tensor_tensor(Mb[q][:, j, :], Eb[q][:, j * 128:(j + 1) * 128], CBm[q][:], ALU.mult),
                                  reads=[f"Eb{q}", f"CBm{q}"], writes=[("Mb", q, j)])
                            for j in range(4):
                                A("pe", lambda e, j=j: e.matmul(pYY[q][:, j * 64:(j + 1) * 64], Mb[q][:, j, :], xdt[q][:, j * 64:(j + 1) * 64], start=True, stop=True),
                                  reads=[("Mb", q, j), f"xdt{q}"], writes=[f"pY{q}"])
                            A("pe", lambda e: e.matmul(pYY[q][:, 256:512], CTc[b][:, g, :], STb[:, g * 256:(g + 1) * 256], start=True, stop=True),
                              reads=[f"CTc{b}", ("STb", g)], writes=[f"pYo{q}"])
                            A("pe", lambda e: e.matmul(pSx[q][:, 0:256], Bc[b][:, g * 128:(g + 1) * 128], xw[q][:], start=True, stop=True),
                              reads=[f"Bc{b}", f"xw{q}"], writes=[f"pSx{q}"])
                            eab = sm2[:, 32 + g * 4:36 + g * 4].unsqueeze(2).to_broadcast([128, 4, 64])
                            A("dve", lambda e: e.tensor_tensor(tq[q][:].rearrange("p (j q) -> p j q", j=4), pYY[q][:, 256:512].rearrange("p (j q) -> p j q", j=4), eab, ALU.mult),
                              reads=[f"pYo{q}", "ea"], writes=[f"tq{q}"])
                            A("dve", lambda e: e.tensor_tensor(Ych[:, g * 256:(g + 1) * 256], pYY[q][:, 0:256], tq[q][:], ALU.add),
                              reads=[f"pY{q}", f"tq{q}"], writes=[("Ych", g)])
                            elb = sm2[:, 64 + g * 4:68 + g * 4].unsqueeze(2).to_broadcast([128, 4, 64])
                            A("pool", lambda e: e.tensor_tensor(tq2[q][:].rearrange("p (j q) -> p j q", j=4),
                                                                ST[:, g * 256:(g + 1) * 256].rearrange("p (j q) -> p j q", j=4), elb, ALU.mult),
                              reads=[("ST", g), "eal"], writes=[f"tq2{q}"])
                            A("dve", lambda e: e.tensor_tensor(ST[:, g * 256:(g + 1) * 256], pSx[q][:, 0:256], tq2[q][:], ALU.add),
                              reads=[f"pSx{q}", f"tq2{q}"], writes=[("ST", g)])
                            A("act", lambda e: e.copy(STb[:, g * 256:(g + 1) * 256], ST[:, g * 256:(g + 1) * 256]), reads=[("ST", g)], writes=[("STb", g)])

                        stage_a(0)
                        for g in range(8):
                            if g + 1 < 8:
                                stage_a(g + 1)
                            stage_b(g)
                        Yall = [("Ych", g) for g in range(8)]
                        if dr == 0:
                            A("act", lambda e, b=b: e.copy(Yst[b][:], Ych[:]), reads=Yall, writes=[f"Yst{b}"])
                            A("act", lambda e, b=b, rows=rows: e.dma_start(out=scr_yf[rows, :], in_=Yst[b][:]), reads=[f"Yst{b}"], writes=[("scr_yf", c)], dma_key=f"Yst{b}")
                            continue
                        A("dve", lambda e, b=b: e.tensor_tensor(Ych[:], Ych[:], yfc[b][:], ALU.add), reads=Yall + [f"yfc{b}"], writes=Yall)
                        Db = smallc[:, 128:160].unsqueeze(2).to_broadcast([128, 32, 64])
                        A("pool", lambda e, b=b, Db=Db: e.tensor_tensor(Yst[0][:].rearrange("p (h q) -> p h q", h=32), xc[b][:].rearrange("p (h q) -> p h q", h=32), Db, ALU.mult),
                          reads=[f"xc{b}", "smallc"], writes=["Yst0"])
                        A("dve", lambda e: e.tensor_tensor(Ych[:], Ych[:], Yst[0][:], ALU.add), reads=Yall + ["Yst0"], writes=Yall)
                        A("dve", lambda e, b=b: e.tensor_tensor(Gb[:], Ych[:], zc[b][:], ALU.mult), reads=Yall + [f"zc{b}"], writes=["Gb"])
                        A("act", lambda e: e.activation(junk2[:], Gb[:], AF.Square, accum_out=r8[:, 0:1]), reads=["Gb"], writes=["junk2", "r0"])
                        A("act", lambda e: e.activation(r8[:, 1:2], r8[:, 0:1], AF.Sqrt, bias=EPS, scale=1.0 / DIN), reads=["r0"], writes=["r1"])
                        A("dve", lambda e: e.reciprocal(r8[:, 2:3], r8[:, 1:2]), reads=["r1"], writes=["r2"])
                        for q4 in range(2):
                            for kk in range(8):
                                k = q4 * 8 + kk
                                A("pe", lambda e, k=k, kk=kk: e.transpose(pT2[:, kk * 128:(kk + 1) * 128], Gb[:, k * 128:(k + 1) * 128], ident[:]),
                                  reads=["Gb", "consts"], writes=["pT2"])
                            A("act", lambda e, q4=q4: e.copy(GT[:, q4 * 8:(q4 + 1) * 8, :], pT2[:].rearrange("p (k t) -> p k t", k=8)), reads=["pT2"], writes=["GT"])
                        for hf in range(2):
                            hs = slice(hf * 512, (hf + 1) * 512)
                            for k in range(16):
                                A("pe", lambda e, k=k, hs=hs: e.matmul(pO[:], GT[:, k, :], wssd[:, k, hs], start=(k == 0), stop=(k == 15)),
                                  reads=["GT", "wssd"], writes=["pO"])
                            A("dve", lambda e, b=b, hs=hs: e.scalar_tensor_tensor(m1s[b][:, hs], pO[:], r8[:, 2:3], gc[b][:, hs], ALU.mult, ALU.mult),
                              reads=["pO", "r2", f"gc{b}"], writes=[f"m1s{b}"])
                        A("act", lambda e, b=b, rows=rows: e.dma_start(out=scr_m1[rows, :], in_=m1s[b][:]), reads=[f"m1s{b}"], writes=[("scr_m1", c)], dma_key=f"m1s{b}")
            S_.barrier()
            if stop_after <= 4:
                continue
            with ExitStack() as st:
                def sb(name, shape, dt):
                    return st.enter_context(nc.sbuf_tensor(f"{name}_a{seq}", shape, dt))

                def ps(name, shape, dt):
                    return st.enter_context(nc.psum_tensor(f"{name}_a{seq}", shape, dt))

                QT = sb("QT", [128, 4, S], BF16)
                KT = sb("KT", [128, 4, S], BF16)
                Vp = sb("Vp", [128, NT, 8, 65], BF16)
                NPS = 4
                PT = [sb(f"PT{i}", [128, 384], BF16) for i in range(NPS)]
                ubase = [0]
                qbase = [0]
                ost = [sb(f"ost{i}", [128, 8, 65], BF16) for i in range(2)]
                pSc = [ps(f"pSc{i}", [128, 512], F32) for i in range(NPS)]
                pOa = [ps(f"pOa{i}", [128, 1024], F32) for i in range(2)]
                A("dve", lambda e: e.memset(Vp[:].rearrange("p t h c -> p (t h) c")[:, :, 64:65], 1.0), writes=["Vp"])
                ucnt = 0
                qcnt = 0
                for g in range(3):
                    d = DIL[g]
                    n = S // d
                    nq = n // 128
                    for r in range(d):
                        for hp in range(4):
                            row = g * 512 + hp * 128
                            A("sp", lambda e, hp=hp, row=row, r=r, n=n: e.dma_start(out=QT[:, hp, 0:n], in_=scr_qT[row:row + 128, r * n:(r + 1) * n]),
                              reads=[("scr_qk", 0, g, hp)], writes=["QT"], dma_key="QT")
                            A("sp", lambda e, hp=hp, row=row, r=r, n=n: e.dma_start(out=KT[:, hp, 0:n], in_=scr_kT[row:row + 128, r * n:(r + 1) * n]),
                              reads=[("scr_qk", 1, g, hp)], writes=["KT"], dma_key="KT")
                        for i in range(nq):
                            A("sp", lambda e, g=g, r=r, n=n, i=i: e.dma_start(out=Vp[:, i, :, 0:64],
                                                                          in_=scr_v[g, r * n + i * 128:r * n + (i + 1) * 128, :].rearrange("k (h c) -> k h c", h=8)),
                              reads=[("scr_v", g)], writes=["Vp"], dma_key="Vp")
                        units = [(i, h) for i in range(nq) for h in range(8)]

                        def emit_scores(u):
                            i, h = units[u]
                            hp, hh = h // 2, h % 2
                            ub = (ubase[0] + u) % NPS
                            offs = [o for o in (-1, 0, 1) if 0 <= i + o < nq]
                            for oi, o in enumerate(offs):
                                ks = slice((i + o) * 128, (i + o + 1) * 128)
                                A("pe", lambda e: e.matmul(pSc[ub][:, oi * 128:(oi + 1) * 128], KT[hh * 64:(hh + 1) * 64, hp, ks],
                                                           QT[hh * 64:(hh + 1) * 64, hp, i * 128:(i + 1) * 128], start=True, stop=False),
                                  reads=["QT", "KT"], writes=[f"pSc{ub}"])
                                A("pe", lambda e: e.matmul(pSc[ub][:, oi * 128:(oi + 1) * 128], ident[:], band[:, o + 1, :], start=False, stop=True),
                                  reads=["consts"], writes=[f"pSc{ub}"])
                            no = len(offs)
                            A("act", lambda e: e.activation(PT[ub][:, 0:no * 128], pSc[ub][:, 0:no * 128], AF.Exp),
                              reads=[f"pSc{ub}"], writes=[f"PT{ub}"])

                        def emit_pv(u):
                            i, h = units[u]
                            ub = (ubase[0] + u) % NPS
                            qb = (qbase[0] + i) % 2
                            offs = [o for o in (-1, 0, 1) if 0 <= i + o < nq]
                            no = len(offs)
                            c0 = (h // 4) * 512 + (h % 4) * 65
                            for oi, o in enumerate(offs):
                                A("pe", lambda e: e.matmul(pOa[qb][:, c0:c0 + 65], PT[ub][:, oi * 128:(oi + 1) * 128], Vp[:, i + o, h, :],
                                                           start=(oi == 0), stop=(oi == no - 1)), reads=[f"PT{ub}", "Vp"], writes=[f"pOa{qb}"])
                            if h == 7:
                                for hf in range(2):
                                    A("act" if hf else "dve", lambda e: (e.copy if hf else e.tensor_copy)(
                                        ost[qb][:, hf * 4:(hf + 1) * 4, :], pOa[qb][:, hf * 512:hf * 512 + 260].rearrange("p (h c) -> p h c", h=4)),
                                      reads=[f"pOa{qb}"], writes=[f"ost{qb}"])
                                t0 = i * 128 * d + r
                                A("act", lambda e: e.dma_start(out=scr_o[g, t0:t0 + 127 * d + 1:d, :, :], in_=ost[qb][:]),
                                  reads=[f"ost{qb}"], writes=[("scr_o", g)], dma_key=f"ost{qb}")

                        LA = NPS - 1
                        for u in range(min(LA, len(units))):
                            emit_scores(u)
                        for u in range(len(units)):
                            if u + LA < len(units):
                                emit_scores(u + LA)
                            emit_pv(u)
                        ubase[0] += len(units)
                        qbase[0] += nq
            S_.barrier()
            if stop_after <= 5:
                continue
            with ExitStack() as st:
                def sb(name, shape, dt):
                    return st.enter_context(nc.sbuf_tensor(f"{name}_f{seq}", shape, dt))

                def ps(name, shape, dt):
                    return st.enter_context(nc.psum_tensor(f"{name}_f{seq}", shape, dt))

                watt = sb("watt", [128, 4, 1024], BF16)
                wout = sb("wout", [128, 8, 1024], BF16)
                gns = sb("gns", [128, 2, 1024], F32)
                o3 = [sb(f"o3{i}", [128, 3, 8, 65], BF16) for i in range(2)]
                osum_ = [sb(f"osum{i}", [128, 8, 65], F32) for i in range(2)]
                rl_ = [sb(f"rl{i}", [128, 8], F32) for i in range(2)]
                Ob_ = [sb(f"Ob{i}", [128, 512], BF16) for i in range(2)]
                OT_ = [sb(f"OT{i}", [128, 4, 128], BF16) for i in range(2)]
                gat = [sb(f"gat{i}", [128, 1024], BF16) for i in range(2)]
                m1c = [sb(f"m1c{i}", [128, 1024], BF16) for i in range(2)]
                mrg_ = [sb(f"mrg{i}", [128, 1024], BF16) for i in range(2)]
                mrgT_ = [sb(f"mrgT{i}", [128, 8, 128], BF16) for i in range(2)]
                tmpf_ = [sb(f"tmpf{i}", [128, 1024], F32) for i in range(2)]
                xin = [sb(f"xin{i}", [128, 1024], F32) for i in range(2)]
                x1 = [sb(f"x1{i}", [128, 1024], F32) for i in range(2)]
                h2 = [sb(f"h2{i}", [128, 1024], BF16) for i in range(2)]
                jk_ = [sb(f"jk{i}", [128, 1024], F32) for i in range(2)]
                s8_ = [sb(f"s8{i}", [128, 8], F32) for i in range(2)]
                pT3 = [ps(f"pT3{i}", [128, 1024], BF16) for i in range(2)]
                pM = [ps(f"pM{i}", [128, 1024], F32) for i in range(2)]

                A("pool", lambda e: e.dma_start(out=watt[:], in_=w_attn.rearrange("(k p) c -> p k c", p=128)), writes=["watt"], dma_key="watt")
                A("pool", lambda e: e.dma_start(out=wout[:], in_=w_out.rearrange("(k p) c -> p k c", p=128)), writes=["wout"], dma_key="wout")
                for gi, gsrc in enumerate((norm_mix_post, norm_ffn_pre)):
                    A("sp", lambda e, gi=gi, gsrc=gsrc: e.dma_start(out=gns[:, gi, :], in_=gsrc.partition_broadcast(128)), writes=["gns"], dma_key="gns")

                def rms_scale(src_ap, ss_col, rd, jk, s8, sfx):
                    A("act", lambda e: e.activation(jk[:], src_ap, AF.Square, accum_out=s8[:, ss_col:ss_col + 1]), reads=rd, writes=["jk" + sfx, ("s8" + sfx, ss_col)])
                    A("act", lambda e: e.activation(s8[:, ss_col + 1:ss_col + 2], s8[:, ss_col:ss_col + 1], AF.Sqrt, bias=EPS, scale=1.0 / D),
                      reads=[("s8" + sfx, ss_col)], writes=[("s8" + sfx, ss_col + 1)])
                    A("dve", lambda e: e.reciprocal(s8[:, ss_col + 1:ss_col + 2], s8[:, ss_col + 1:ss_col + 2]), reads=[("s8" + sfx, ss_col + 1)], writes=[("s8" + sfx, ss_col + 1)])

                def p6a_loads(t):
                    b = t % 2
                    rows = slice(t * 128, (t + 1) * 128)
                    for g3 in range(3):
                        A("sp", lambda e, g3=g3: e.dma_start(out=o3[b][:, g3, :, :], in_=scr_o[g3, rows, :, :]),
                          reads=[("scr_o", g3)], writes=[f"o3{b}"], dma_key=f"o3{b}")
                    A("sp", lambda e: e.dma_start(out=gat[b][:], in_=scr_gate[rows, 1024:2048]), reads=[("scr_gate", t)], writes=[f"gat{b}"], dma_key=f"gat{b}")
                    A("sp", lambda e: e.dma_start(out=m1c[b][:], in_=scr_m1[rows, :]), reads=[("scr_m1", t)], writes=[f"m1c{b}"], dma_key=f"m1c{b}")
                    A("sp", lambda e: e.dma_start(out=xin[b][:], in_=x_in[seq, rows, :]), writes=[f"xin{b}"], dma_key=f"xin{b}")

                for t in range(NT):
                    b = t % 2
                    rows = slice(t * 128, (t + 1) * 128)
                    osum, rl, Ob, OT, mrg, mrgT, tmpf, jk, s8 = osum_[b], rl_[b], Ob_[b], OT_[b], mrg_[b], mrgT_[b], tmpf_[b], jk_[b], s8_[b]
                    sfx = f"_{b}"
                    if t == 0:
                        p6a_loads(0)
                    if t + 1 < NT:
                        p6a_loads(t + 1)
                    A("dve", lambda e: e.tensor_tensor(osum[:], o3[b][:, 0, :, :], o3[b][:, 1, :, :], ALU.add), reads=[f"o3{b}"], writes=["osum" + sfx])
                    A("dve", lambda e: e.tensor_tensor(osum[:], osum[:], o3[b][:, 2, :, :], ALU.add), reads=[f"o3{b}", "osum" + sfx], writes=["osum" + sfx])
                    A("dve", lambda e: e.reciprocal(rl[:], osum[:, :, 64]), reads=["osum" + sfx], writes=["rl" + sfx])
                    A("dve", lambda e: e.tensor_tensor(Ob[:].rearrange("p (h c) -> p h c", h=8), osum[:, :, 0:64], rl[:].unsqueeze(2).to_broadcast([128, 8, 64]), ALU.mult),
                      reads=["osum" + sfx, "rl" + sfx], writes=["Ob" + sfx])
                    for k in range(4):
                        A("pe", lambda e, k=k: e.transpose(pT3[b][:, k * 128:(k + 1) * 128], Ob[:, k * 128:(k + 1) * 128], ident[:]), reads=["Ob" + sfx, "consts"], writes=[f"pT3{b}"])
                    A("act", lambda e: e.copy(OT[:], pT3[b][:, 0:512].rearrange("p (k t) -> p k t", k=4)), reads=[f"pT3{b}"], writes=["OT" + sfx])
                    for hf in range(2):
                        for k in range(4):
                            A("pe", lambda e, k=k, hf=hf: e.matmul(pM[b][:, hf * 512:(hf + 1) * 512], OT[:, k, :], watt[:, k, hf * 512:(hf + 1) * 512], start=(k == 0), stop=(k == 3)),
                              reads=["OT" + sfx, "watt"], writes=[f"pM{b}"])
                    A("dve", lambda e: e.tensor_tensor(tmpf[:], pM[b][:], gat[b][:], ALU.mult), reads=[f"pM{b}", f"gat{b}"], writes=["tmpf" + sfx])
                    A("pool", lambda e: e.tensor_tensor(mrg[:], tmpf[:], m1c[b][:], ALU.add), reads=["tmpf" + sfx, f"m1c{b}"], writes=["mrg" + sfx])
                    for k in range(8):
                        A("pe", lambda e, k=k: e.transpose(pT3[b][:, k * 128:(k + 1) * 128], mrg[:, k * 128:(k + 1) * 128], ident[:]), reads=["mrg" + sfx, "consts"], writes=[f"pT3{b}"])
                    A("act", lambda e: e.copy(mrgT[:], pT3[b][:].rearrange("p (k t) -> p k t", k=8)), reads=[f"pT3{b}"], writes=["mrgT" + sfx])
                    for hf in range(2):
                        for k in range(8):
                            A("pe", lambda e, k=k, hf=hf: e.matmul(pM[b][:, hf * 512:(hf + 1) * 512], mrgT[:, k, :], wout[:, k, hf * 512:(hf + 1) * 512], start=(k == 0), stop=(k == 7)),
                              reads=["mrgT" + sfx, "wout"], writes=[f"pM{b}"])
                    rms_scale(pM[b][:], 0, [f"pM{b}"], jk, s8, sfx)
                    A("dve", lambda e: e.scalar_tensor_tensor(tmpf[:], pM[b][:], s8[:, 1:2], gns[:, 0, :], ALU.mult, ALU.mult), reads=[f"pM{b}", ("s8" + sfx, 1), "gns"], writes=["tmpf" + sfx])
                    A("pool", lambda e: e.tensor_tensor(x1[b][:], tmpf[:], xin[b][:], ALU.add), reads=["tmpf" + sfx, f"xin{b}"], writes=[f"x1{b}"])
                    A("act", lambda e: e.dma_start(out=y_out[seq, rows, :], in_=x1[b][:]), reads=[f"x1{b}"], writes=[("y", seq, t)], dma_key=f"x1{b}")
                    rms_scale(x1[b][:], 2, [f"x1{b}"], jk, s8, sfx)
                    A("dve", lambda e: e.scalar_tensor_tensor(h2[b][:], x1[b][:], s8[:, 3:4], gns[:, 1, :], ALU.mult, ALU.mult), reads=[f"x1{b}", ("s8" + sfx, 3), "gns"], writes=[f"h2{b}"])
                    A("act", lambda e: e.dma_start(out=scr_h2[seq, rows, :], in_=h2[b][:]), reads=[f"h2{b}"], writes=[("scr_h2", seq, t)], dma_key=f"h2{b}")
            S_.barrier()
        if stop_after > 5:
            with ExitStack() as st:
                def sb(name, shape, dt):
                    return st.enter_context(nc.sbuf_tensor(f"{name}_ffn", shape, dt))

                def ps(name, shape, dt):
                    return st.enter_context(nc.psum_tensor(f"{name}_ffn", shape, dt))

                TBLK = 512
                NTB = TBLK // 128
                wfi = sb("wfi", [128, 8, 2 * FFN], BF16)
                wdn = sb("wdn", [128, 22, 1024], BF16)
                gpo = sb("gpo", [128, 1024], F32)
                h2b = [sb(f"h2b{i}", [128, 1024], BF16) for i in range(2)]
                h2T = sb("h2T", [128, 8, TBLK], BF16)
                actT = sb("actT", [128, 22, TBLK], BF16)
                gT = [sb(f"gT{i}", [128, TBLK], BF16) for i in range(2)]
                x1b = [sb(f"x1b{i}", [128, 1024], F32) for i in range(2)]
                tmpf = sb("tmpf", [128, 1024], F32)
                jk = sb("jk", [128, 1024], F32)
                s8 = sb("s8", [128, 8], F32)
                yo = [sb(f"yo{i}", [128, 1024], F32) for i in range(2)]
                pT3 = ps("pT3", [128, 1024], BF16)
                pM = ps("pM", [128, 1024], F32)
                pG = [ps(f"pG{i}", [128, 512], F32) for i in range(2)]
                pU = [ps(f"pU{i}", [128, 512], F32) for i in range(2)]
                for c4 in range(11):
                    A("pool", lambda e, c4=c4: e.dma_start(out=wfi[:, :, c4 * 512:(c4 + 1) * 512],
                                                           in_=w_ffn_in[:, c4 * 512:(c4 + 1) * 512].rearrange("(k p) c -> p k c", p=128)),
                      writes=[("wfi", c4)], dma_key="wfi")
                for q2 in range(2):
                    A("pool", lambda e, q2=q2: e.dma_start(out=wdn[:, q2 * 11:(q2 + 1) * 11, :],
                                                           in_=w_ffn_down[q2 * 1408:(q2 + 1) * 1408, :].rearrange("(k p) c -> p k c", p=128)),
                      writes=["wdn"], dma_key="wdn")
                A("sp", lambda e: e.dma_start(out=gpo[:], in_=norm_ffn_post.partition_broadcast(128)), writes=["gpo"], dma_key="gpo")
                wfi_all = [("wfi", c4) for c4 in range(11)]
                cntt = 0
                for seq in range(NSEQ):
                    for blk in range(S // TBLK):
                        for tt in range(NTB):
                            t = blk * NTB + tt
                            b = cntt % 2
                            cntt += 1
                            rows = slice(t * 128, (t + 1) * 128)
                            A("sp", lambda e: e.dma_start(out=h2b[b][:], in_=scr_h2[seq, rows, :]), reads=[("scr_h2", seq, t)], writes=[f"h2b{b}"], dma_key=f"h2b{b}")
                            for k in range(8):
                                A("pe", lambda e, k=k: e.transpose(pT3[:, k * 128:(k + 1) * 128], h2b[b][:, k * 128:(k + 1) * 128], ident[:]), reads=[f"h2b{b}", "consts"], writes=["pT3"])
                            A("act", lambda e: e.copy(h2T[:, :, tt * 128:(tt + 1) * 128], pT3[:].rearrange("p (k t) -> p k t", k=8)), reads=["pT3"], writes=[("h2T", tt)])
                        h2T_all = [("h2T", tt) for tt in range(NTB)]
                        for f in range(22):
                            pb = f % 2
                            for k in range(8):
                                A("pe", lambda e, k=k: e.matmul(pG[pb][:], wfi[:, k, f * 128:(f + 1) * 128], h2T[:, k, :], start=(k == 0), stop=(k == 7)),
                                  reads=h2T_all + wfi_all, writes=[f"pG{pb}"])
                            for k in range(8):
                                A("pe", lambda e, k=k: e.matmul(pU[pb][:], wfi[:, k, FFN + f * 128:FFN + (f + 1) * 128], h2T[:, k, :], start=(k == 0), stop=(k == 7)),
                                  reads=h2T_all + wfi_all, writes=[f"pU{pb}"])
                            A("act", lambda e: e.activation(gT[pb][:], pG[pb][:], AF.Silu), reads=[f"pG{pb}"], writes=[f"gT{pb}"])
                            A("dve", lambda e: e.tensor_tensor(actT[:, f, :], pU[pb][:], gT[pb][:], ALU.mult),
                              reads=[f"pU{pb}", f"gT{pb}"], writes=[("actT", f)])
                        act_all = [("actT", f) for f in range(22)]
                        for tt in range(NTB):
                            t = blk * NTB + tt
                            b = (cntt + tt) % 2
                            rows = slice(t * 128, (t + 1) * 128)
                            A("sp", lambda e: e.dma_start(out=x1b[b][:], in_=y_out[seq, rows, :]), reads=[("y", seq, t)], writes=[f"x1b{b}"], dma_key=f"x1b{b}")
                            for f in range(22):
                                for hf in range(2):
                                    A("pe", lambda e, f=f, hf=hf: e.matmul(pM[:, hf * 512:(hf + 1) * 512], actT[:, f, tt * 128:(tt + 1) * 128], wdn[:, f, hf * 512:(hf + 1) * 512],
                                                                      start=(f == 0), stop=(f == 21)),
                                      reads=act_all + ["wdn"], writes=["pM"])
                            A("act", lambda e: e.activation(jk[:], pM[:], AF.Square, accum_out=s8[:, 0:1]), reads=["pM"], writes=["jk", "ss"])
                            A("act", lambda e: e.activation(s8[:, 1:2], s8[:, 0:1], AF.Sqrt, bias=EPS, scale=1.0 / D), reads=["ss"], writes=["sd"])
                            A("dve", lambda e: e.reciprocal(s8[:, 1:2], s8[:, 1:2]), reads=["sd"], writes=["sd"])
                            A("dve", lambda e: e.scalar_tensor_tensor(tmpf[:], pM[:], s8[:, 1:2], gpo[:], ALU.mult, ALU.mult), reads=["pM", "sd", "gpo"], writes=["tmpf"])
                            A("pool", lambda e: e.tensor_tensor(yo[b][:], tmpf[:], x1b[b][:], ALU.add), reads=["tmpf", f"x1b{b}"], writes=[f"yo{b}"])
                            A("act", lambda e: e.dma_start(out=y_out[seq, rows, :], in_=yo[b][:]), reads=[f"yo{b}"], writes=[("y", seq, t)], dma_key=f"yo{b}")
            S_.barrier()
        S_.emit(final_waits=list(S_.dma_count.keys()))
    nc._marks = getattr(S_, 'marks', [])
    return nc


def _host_consts(inp_conv_w, inp_conv_b, inp_norm_w, S):
    p = np.arange(128)
    m = p % 64
    rot = (m < 16)
    h2 = (m >= 8) & rot
    fi = np.where(rot, m % 8, 0)
    invf = (500000.0 ** (-(2.0 * fi) / 16.0)).astype(np.float32)
    ang = np.arange(S, dtype=np.float32)[None, :] * invf[:, None]
    cos = np.where(rot[:, None], np.cos(ang), 1.0).astype(np.float32)
    sgn = np.where(rot, np.where(h2, 1.0, -1.0), 0.0).astype(np.float32)
    sin = (np.sin(ang) * sgn[:, None]).astype(np.float32)
    cw = np.ascontiguousarray(inp_conv_w.reshape(5, 32, 128).transpose(2, 1, 0).reshape(128, 160), dtype=np.float32)
    cb = np.ascontiguousarray(inp_conv_b.reshape(32, 128).T, dtype=np.float32)
    nw = np.ascontiguousarray(inp_norm_w.reshape(16, 128).T, dtype=np.float32)
    return {"rot_cos": cos, "rot_sin": sin, "cw_l": cw, "cb_l": cb, "nw_l": nw}


_NC_CACHE = {}


def kernel(**inputs):
    x = np.asarray(inputs["x"], dtype=np.float32)
    B, S, _ = x.shape
    n_cores = 8
    NSEQ = B // n_cores
    key = (S, NSEQ)
    if key not in _NC_CACHE:
        _NC_CACHE[key] = build(S, NSEQ)
    nc = _NC_CACHE[key]
    f = lambda k: np.ascontiguousarray(np.asarray(inputs[k], dtype=np.float32)[0])
    shared = {
        "norm_mix_pre": f("norm_mix_pre").reshape(1, D), "w_in": f("w_in"),
        "ssd_conv_w": f("ssd_conv_w"), "ssd_conv_b": f("ssd_conv_b").reshape(1, 4096),
        "ssd_dt_bias": f("ssd_dt_bias").reshape(1, 64), "ssd_A_log": f("ssd_A_log").reshape(1, 64),
        "ssd_D": f("ssd_D").reshape(1, 32), "ssd_norm_w": f("ssd_norm_w").reshape(1, DIN),
        "w_ssd_branch": f("w_ssd_branch"), "w_attn_branch": f("w_attn_branch"), "w_out": f("w_out"),
        "norm_mix_post": f("norm_mix_post").reshape(1, D), "norm_ffn_pre": f("norm_ffn_pre").reshape(1, D),
        "w_ffn_in": f("w_ffn_in"), "w_ffn_down": f("w_ffn_down"), "norm_ffn_post": f("norm_ffn_post").reshape(1, D),
    }
    shared.update(_host_consts(shared["ssd_conv_w"], shared["ssd_conv_b"], shared["ssd_norm_w"], S))
    in_maps = []
    big = ("w_in", "w_ssd_branch", "w_attn_branch", "w_out", "w_ffn_in", "w_ffn_down", "rot_cos", "rot_sin")
    for c in range(n_cores):
        m = dict(shared)
        for k in big:
            a = shared[k]
            m[k] = np.concatenate([a, np.full((1, a.shape[1]), float(c), np.float32)], axis=0)
        m["x"] = np.ascontiguousarray(x[c * NSEQ:(c + 1) * NSEQ])
        in_maps.append(m)
    res = run_bass_kernel_spmd(nc, in_maps, core_ids=list(range(n_cores)))
    return np.concatenate([np.asarray(r["y"], dtype=np.float32) for r in res.results], axis=0)
```

```python
import numpy as np
from contextlib import ExitStack
import concourse.bass as bass
import concourse.mybir as mybir
from concourse.bass_utils import run_bass_kernel_spmd

F32 = mybir.dt.float32
BF16 = mybir.dt.bfloat16
I32 = mybir.dt.int32
AF = mybir.ActivationFunctionType
ALU = mybir.AluOpType

EPOCH = 20000
D = 1024
DIN = 2048
FFN = 2816
INC = 12864
OFF_Z, OFF_X, OFF_B, OFF_C, OFF_DT, OFF_Q, OFF_K, OFF_V, OFF_G = 0, 2048, 4096, 5120, 6144, 6208, 7744, 9280, 10816
DIL = (1, 4, 16)
EPS = 1e-6
NEGV = -30000.0


class _Rec:
    def __getattr__(self, name):
        return lambda *a, **kw: (name, a, kw)


_REC = _Rec()


class Op:
    __slots__ = ("eng", "fn", "idx", "deps", "signal", "dma_key", "sig_n")

    def __init__(self, eng, fn, dma_key=None):
        self.eng = eng
        self.fn = fn(_REC)
        self.deps = []
        self.signal = False
        self.dma_key = dma_key
        self.sig_n = 0


class Sched:
    ENGS = ("pe", "act", "dve", "pool", "sp")

    def __init__(self, nc):
        self.nc = nc
        self.eng_ops = {e: [] for e in self.ENGS}
        self.last_writer = {}
        self.readers = {}
        self.dma_count = {}
        self.waited = {e: {} for e in self.ENGS}

    def add(self, eng, fn, reads=(), writes=(), dma_key=None):
        op = Op(eng, fn, dma_key)
        op.idx = len(self.eng_ops[eng])
        self.eng_ops[eng].append(op)
        deps = set()
        for r in reads:
            w = self.last_writer.get(r)
            if w is not None:
                deps.add(w)
        for w in writes:
            lw = self.last_writer.get(w)
            if lw is not None:
                deps.add(lw)
            for rd in self.readers.get(w, ()):
                deps.add(rd)
        self._attach(op, deps)
        if dma_key is not None:
            self.dma_count[dma_key] = self.dma_count.get(dma_key, 0) + 1
        for r in reads:
            self.readers.setdefault(r, []).append(op)
        for w in writes:
            self.last_writer[w] = op
            self.readers[w] = []
        return op

    def _attach(self, op, deps):
        eng = op.eng
        best = {}
        for d in deps:
            if d is op:
                continue
            if d.dma_key is not None:
                k = ("dma", d.dma_key)
                v = self.dma_count[d.dma_key]
            else:
                if d.eng == "pe" and eng == "pe" and op.dma_key is None:
                    continue
                k = ("eng", d.eng)
                v = d.idx
            if k not in best or best[k][0] < v:
                best[k] = (v, d)
        wd = self.waited[eng]
        for k, (v, d) in best.items():
            if k in wd and wd[k] >= v:
                continue
            wd[k] = v
            if k[0] == "eng":
                d.signal = True
            op.deps.append((k, v, d))

    def barrier(self):
        if not hasattr(self, "marks"):
            self.marks = []
        self.marks.append({e: sum(1 + len(o.deps) for o in self.eng_ops[e]) for e in self.ENGS})
        lasts = []
        for e in self.ENGS:
            ops = [o for o in self.eng_ops[e] if o.dma_key is None]
            if ops:
                lasts.append(ops[-1])
        dmas = {}
        for e in self.ENGS:
            for o in self.eng_ops[e]:
                if o.dma_key is not None:
                    dmas[o.dma_key] = o
        for e in self.ENGS:
            op = Op(e, lambda eng: eng.nop())
            op.idx = len(self.eng_ops[e])
            self.eng_ops[e].append(op)
            self._attach(op, set(lasts) | set(dmas.values()))
        self.last_writer = {}
        self.readers = {}

    def emit(self, final_waits=()):
        nc = self.nc
        nsig = {}
        for e in self.ENGS:
            n = 0
            for op in self.eng_ops[e]:
                if op.dma_key is None and op.signal:
                    n += 1
                    op.sig_n = n
            nsig[e] = n
        with ExitStack() as st:
            esems = {}
            for e in self.ENGS:
                for ep in range((nsig[e] + EPOCH - 1) // EPOCH):
                    esems[(e, ep)] = st.enter_context(nc.semaphore(f"s_{e}_{ep}"))
            dsems = {}
            for i, k in enumerate(self.dma_count):
                dsems[k] = st.enter_context(nc.semaphore(f"d{i}"))
            block = st.enter_context(nc.Block())

            def run(e, engh):
                for op in self.eng_ops[e]:
                    for (k, v, d) in op.deps:
                        if k[0] == "dma":
                            engh.wait_ge(dsems[k[1]], 16 * v)
                        else:
                            n = d.sig_n
                            engh.wait_ge(esems[(d.eng, (n - 1) // EPOCH)], (n - 1) % EPOCH + 1)
                    name, a_, kw_ = op.fn
                    ins = getattr(engh, name)(*a_, **kw_)
                    if op.dma_key is not None:
                        ins.then_inc(dsems[op.dma_key], 16)
                    elif op.signal:
                        n = op.sig_n
                        ins.then_inc(esems[(e, (n - 1) // EPOCH)], 1)
                if e == "sp":
                    for k in final_waits:
                        engh.wait_ge(dsems[k], 16 * self.dma_count[k])

            @block.tensor
            def _(eng):
                run("pe", eng)

            @block.scalar
            def _(eng):
                run("act", eng)

            @block.vector
            def _(eng):
                run("dve", eng)

            @block.gpsimd
            def _(eng):
                run("pool", eng)

            @block.sync
            def _(eng):
                run("sp", eng)


def build(S, NSEQ, debug=False, stop_after=99, cut=99):
    nc = bass.Bass("TRN2", target_bir_lowering=False)
    NT = S // 128
    TB = S // 512

    def din(name, shape, pad=False):
        if pad:
            return nc.dram_tensor(name, [shape[0] + 1] + list(shape[1:]), F32, kind="ExternalInput").ap()[0:shape[0]]
        return nc.dram_tensor(name, shape, F32, kind="ExternalInput").ap()

    x_in = din("x", [NSEQ, S, D])
    norm_mix_pre = din("norm_mix_pre", [1, D])
    w_in = din("w_in", [D, INC], pad=True)
    conv_w = din("ssd_conv_w", [5, 4096])
    conv_b = din("ssd_conv_b", [1, 4096])
    dt_bias = din("ssd_dt_bias", [1, 64])
    A_log = din("ssd_A_log", [1, 64])
    D_skip = din("ssd_D", [1, 32])
    ssd_norm_w = din("ssd_norm_w", [1, DIN])
    w_ssd = din("w_ssd_branch", [DIN, D], pad=True)
    w_attn = din("w_attn_branch", [512, D], pad=True)
    w_out = din("w_out", [D, D], pad=True)
    norm_mix_post = din("norm_mix_post", [1, D])
    norm_ffn_pre = din("norm_ffn_pre", [1, D])
    w_ffn_in = din("w_ffn_in", [D, 2 * FFN], pad=True)
    w_ffn_down = din("w_ffn_down", [FFN, D], pad=True)
    norm_ffn_post = din("norm_ffn_post", [1, D])
    rot_cos = din("rot_cos", [128, S], pad=True)
    rot_sin = din("rot_sin", [128, S], pad=True)
    cw_l = din("cw_l", [128, 160])
    cb_l = din("cb_l", [128, 32])
    nw_l = din("nw_l", [128, 16])
    y_out = nc.dram_tensor("y", [NSEQ, S, D], F32, kind="ExternalOutput").ap()

    skind = "ExternalOutput" if debug else "Internal"

    def scr(name, shape, dt=BF16):
        return nc.dram_tensor(name, shape, dt, kind=skind).ap()

    scr_z = scr("scr_z", [S, DIN])
    scr_gate = scr("scr_gate", [S, 2 * D])
    scr_x = scr("scr_x", [S, DIN])
    scr_B = scr("scr_B", [S, 1024])
    scr_BT = scr("scr_BT", [1024, S])
    scr_CT = scr("scr_CT", [1024, S])
    scr_dt = scr("scr_dt", [S, 64], F32)
    scr_yf = scr("scr_yf", [S, DIN])
    scr_qT = scr("scr_qT", [1536, S])
    scr_kT = scr("scr_kT", [1536, S])
    scr_v = scr("scr_v", [3, S, 512])
    scr_o = scr("scr_o", [3, S, 8, 65])
    scr_m1 = scr("scr_m1", [S, D])
    scr_h2 = scr("scr_h2", [NSEQ, S, D])
    wfi_bf = nc.dram_tensor("wfi_bf", [D, 2 * FFN], BF16).ap()
    wfd_bf = nc.dram_tensor("wfd_bf", [FFN, D], BF16).ap()

    S_ = Sched(nc)
    A = S_.add

    with ExitStack() as gst:
        def gsb(name, shape, dt):
            return gst.enter_context(nc.sbuf_tensor(name, shape, dt))

        ident = gsb("ident", [128, 128], BF16)
        cst_f = gsb("cst_f", [128, 4, 128], F32)
        cst_b = gsb("cst_b", [128, 8, 128], BF16)
        band = gsb("band", [128, 3, 128], BF16)
        tmpc = gsb("tmpc", [128, 128], F32)
        smallc = gsb("smallc", [128, 64 + 64 + 32], F32)

        def mk_const(dst_f32_ap, fill_base, selects):
            A("pool", lambda e: e.memset(dst_f32_ap, fill_base), writes=["tmpc"])
            for (pat, cm, base, fill) in selects:
                A("pool", lambda e, pat=pat, cm=cm, base=base, fill=fill: e.affine_select(
                    dst_f32_ap, dst_f32_ap, [[pat, 128]], ALU.is_ge, fill, base=base, channel_multiplier=cm),
                  reads=["tmpc"], writes=["tmpc"])

        def to_bf(dst_ap, src_ap, scale=None):
            if scale is None:
                A("dve", lambda e: e.tensor_copy(dst_ap, src_ap), reads=["tmpc"], writes=["consts"])
            else:
                A("dve", lambda e: e.tensor_scalar(dst_ap, src_ap, scale, None, ALU.mult), reads=["tmpc"], writes=["consts"])

        mk_const(tmpc[:], 1.0, [(1, -1, 0, 0.0), (-1, 1, 0, 0.0)])
        to_bf(ident[:], tmpc[:])
        A("dve", lambda e: e.tensor_copy(cst_f[:, 3, :], tmpc[:]), reads=["tmpc"], writes=["consts"])
        mk_const(tmpc[:], 1.0, [(1, -1, 0, 0.0)])
        to_bf(cst_b[:, 0, :], tmpc[:])
        to_bf(cst_b[:, 1, :], tmpc[:], -1.0)
        to_bf(cst_b[:, 6, :], tmpc[:])
        A("dve", lambda e: e.tensor_copy(cst_f[:, 0, :], tmpc[:]), reads=["tmpc"], writes=["consts"])
        mk_const(tmpc[:], 1.0, [(-1, 1, 0, 0.0)])
        to_bf(cst_b[:, 2, :], tmpc[:])
        to_bf(cst_b[:, 3, :], tmpc[:], -1.0)
        to_bf(cst_b[:, 7, :], tmpc[:])
        A("dve", lambda e: e.tensor_copy(cst_f[:, 1, :], tmpc[:]), reads=["tmpc"], writes=["consts"])
        mk_const(tmpc[:], 0.0, [(1, -1, 0, NEGV)])
        to_bf(cst_b[:, 4, :], tmpc[:])
        mk_const(tmpc[:], 0.0, [(-1, 1, 0, NEGV)])
        to_bf(cst_b[:, 5, :], tmpc[:])
        A("dve", lambda e: e.memset(cst_f[:, 2, :], 1.0), writes=["consts"])
        for oi, o in enumerate((-1, 0, 1)):
            mk_const(tmpc[:], 0.0, [(-1, 1, 128 * o + 64, NEGV), (1, -1, 64 - 128 * o, NEGV)])
            to_bf(band[:, oi, :], tmpc[:])
        A("sp", lambda e: e.dma_start(out=smallc[:, 0:64], in_=dt_bias.partition_broadcast(128)), writes=["smallc"], dma_key="c0")
        A("sp", lambda e: e.dma_start(out=smallc[:, 64:128], in_=A_log.partition_broadcast(128)), writes=["smallc"], dma_key="c0")
        A("sp", lambda e: e.dma_start(out=smallc[:, 128:160], in_=D_skip.partition_broadcast(128)), writes=["smallc"], dma_key="c0")
        A("act", lambda e: e.activation(smallc[:, 64:128], smallc[:, 64:128], AF.Exp), reads=["smallc"], writes=["smallc"])
        A("dve", lambda e: e.tensor_scalar(smallc[:, 64:128], smallc[:, 64:128], -1.0, None, ALU.mult), reads=["smallc"], writes=["smallc"])
        for seq in range(NSEQ):
            with ExitStack() as st:
                def sb(name, shape, dt):
                    return st.enter_context(nc.sbuf_tensor(f"{name}_{seq}", shape, dt))

                def ps(name, shape, dt):
                    return st.enter_context(nc.psum_tensor(f"{name}_{seq}", shape, dt))

                hT = sb("hT", [128, 8, S], BF16)
                gpre = sb("gpre", [128, D], F32)
                xt = [sb(f"xt{i}", [128, D], F32) for i in range(2)]
                hb = [sb(f"hb{i}", [128, D], BF16) for i in range(2)]
                junk = sb("junk", [128, D], F32)
                st8 = sb("st8", [128, 8], F32)
                pT = [ps(f"pT{i}", [128, 1024], BF16) for i in range(2)]
                pA = [ps(f"pA{i}", [128, 512], F32) for i in range(2)]
                pB = [ps(f"pB{i}", [128, 512], F32) for i in range(2)]

                A("sp", lambda e: e.dma_start(out=gpre[:], in_=norm_mix_pre.partition_broadcast(128)), writes=["gpre"], dma_key="gpre")
                for t in range(NT):
                    b = t % 2
                    A("sp", lambda e, t=t, b=b: e.dma_start(out=xt[b][:], in_=x_in[seq, t * 128:(t + 1) * 128, :]),
                      writes=[f"xt{b}"], dma_key=f"xt{b}")
                    A("act", lambda e, b=b: e.activation(junk[:], xt[b][:], AF.Square, accum_out=st8[:, b:b + 1]),
                      reads=[f"xt{b}"], writes=["junk", f"ss{b}"])
                    A("act", lambda e, b=b: e.activation(st8[:, 2 + b:3 + b], st8[:, b:b + 1], AF.Sqrt, bias=EPS, scale=1.0 / D),
                      reads=[f"ss{b}"], writes=[f"sd{b}"])
                    A("dve", lambda e, b=b: e.reciprocal(st8[:, 4 + b:5 + b], st8[:, 2 + b:3 + b]), reads=[f"sd{b}"], writes=[f"rs{b}"])
                    A("dve", lambda e, b=b: e.scalar_tensor_tensor(hb[b][:], xt[b][:], st8[:, 4 + b:5 + b], gpre[:], ALU.mult, ALU.mult),
                      reads=[f"xt{b}", f"rs{b}", "gpre"], writes=[f"hb{b}"])
                    for k in range(8):
                        A("pe", lambda e, b=b, k=k: e.transpose(pT[b][:, k * 128:(k + 1) * 128], hb[b][:, k * 128:(k + 1) * 128], ident[:]),
                          reads=[f"hb{b}", "consts"], writes=[f"pT{b}"])
                    A("act", lambda e, b=b, t=t: e.copy(hT[:, :, t * 128:(t + 1) * 128], pT[b][:].rearrange("p (k t) -> p k t", k=8)),
                      reads=[f"pT{b}"], writes=[("hT", t)])
                hT_all = [("hT", t) for t in range(NT)]
                if cut <= 1:
                    S_.barrier(); continue

                wsl = [sb(f"wsl{i}", [128, 8, 512], BF16) for i in range(2)]
                wp = sb("wp", [128, 8, 512], BF16)
                stg = [sb(f"stg{i}", [128, 512], BF16) for i in range(3)]
                stgf = [sb(f"stgf{i}", [128, 64], F32) for i in range(2)]
                outX = [sb(f"outX{i}", [128, S], BF16) for i in range(4)]
                rawT = [sb(f"rawT{i}", [128, S + 4], BF16) for i in range(2)]
                tmp1 = [sb(f"tmp1{i}", [128, 512], F32) for i in range(2)]
                tmp2 = [sb(f"tmp2{i}", [128, 512], F32) for i in range(2)]
                cosT = sb("cosT", [128, S], BF16)
                sinT = sb("sinT", [128, S], BF16)
                cw = sb("cw", [128, 32, 5], F32)
                cb = sb("cb", [128, 32], F32)
                dg = sb("dg", [128, 5, 128], BF16)
                wcnt = [0]
                scnt = [0]

                def load_w(lo, ncols):
                    i = wcnt[0] % 2
                    wcnt[0] += 1
                    A("pool", lambda e: e.dma_start(out=wsl[i][:, :, 0:ncols],
                                                    in_=w_in[:, lo:lo + ncols].rearrange("(k p) c -> p k c", p=128)),
                      writes=[f"wsl{i}"], dma_key=f"wsl{i}")
                    return i

                def stage():
                    i = scnt[0] % 3
                    scnt[0] += 1
                    return i

                A("pool", lambda e: e.memset(wp[:], 0.0), writes=["wp"])
                for i in range(2):
                    A("pool", lambda e, i=i: e.memset(rawT[i][:, 0:2], 0.0), writes=[f"rawT{i}"])
                    A("pool", lambda e, i=i: e.memset(rawT[i][:, S + 2:S + 4], 0.0), writes=[f"rawT{i}"])
                A("sp", lambda e: e.dma_start(out=cw[:].rearrange("p f k -> p (f k)"), in_=cw_l), writes=["cw"], dma_key="cw")
                A("sp", lambda e: e.dma_start(out=cb[:], in_=cb_l), writes=["cb"], dma_key="cw")
                A("pool", lambda e: e.dma_start(out=cosT[:], in_=rot_cos), writes=["cosT"], dma_key="rot")
                A("pool", lambda e: e.dma_start(out=sinT[:], in_=rot_sin), writes=["sinT"], dma_key="rot")
                if cut <= 2:
                    S_.barrier(); continue
                def tok_block(lo, ncols, func, dst, dst_lo, dkey):
                    wi = load_w(lo, ncols)
                    for t in range(NT):
                        b = t % 2
                        for k in range(8):
                            A("pe", lambda e, t=t, k=k, b=b: e.matmul(pA[b][:, 0:ncols], hT[:, k, t * 128:(t + 1) * 128], wsl[wi][:, k, 0:ncols],
                                                                      start=(k == 0), stop=(k == 7)),
                              reads=[("hT", t), f"wsl{wi}"], writes=[f"pA{b}"])
                        si = stage()
                        A("act", lambda e, b=b, si=si: e.activation(stg[si][:, 0:ncols], pA[b][:, 0:ncols], func),
                          reads=[f"pA{b}"], writes=[f"stg{si}"])
                        A("sp", lambda e, t=t, si=si: e.dma_start(out=dst[t * 128:(t + 1) * 128, dst_lo:dst_lo + ncols], in_=stg[si][:, 0:ncols]),
                          reads=[f"stg{si}"], writes=[(dkey, t)], dma_key=f"stg{si}")

                for blk in range(4):
                    tok_block(OFF_Z + blk * 512, 512, AF.Silu, scr_z, blk * 512, "scr_z")
                for blk in range(4):
                    tok_block(OFF_G + blk * 512, 512, AF.Sigmoid, scr_gate, blk * 512, "scr_gate")
                if cut <= 3:
                    S_.barrier(); continue
                wi = load_w(OFF_DT, 64)
                for t in range(NT):
                    b = t % 2
                    for k in range(8):
                        A("pe", lambda e, t=t, k=k, b=b: e.matmul(pA[b][:, 0:64], hT[:, k, t * 128:(t + 1) * 128], wsl[wi][:, k, 0:64],
                                                                  start=(k == 0), stop=(k == 7)),
                          reads=[("hT", t), f"wsl{wi}"], writes=[f"pA{b}"])
                    A("act", lambda e, b=b: e.copy(stgf[b][:], pA[b][:, 0:64]), reads=[f"pA{b}"], writes=[f"stgf{b}"])
                    A("sp", lambda e, t=t, b=b: e.dma_start(out=scr_dt[t * 128:(t + 1) * 128, :], in_=stgf[b][:]),
                      reads=[f"stgf{b}"], writes=[("scr_dt", t)], dma_key=f"stgf{b}")
                if cut <= 4:
                    S_.barrier(); continue
                for g in range(3):
                    d = DIL[g]
                    n = S // d
                    wi = load_w(OFF_V + g * 512, 512)
                    cnt = 0
                    for r in range(d):
                        for i in range(n // 128):
                            b = cnt % 2
                            cnt += 1
                            base = i * 128 * d + r
                            tl = sorted(set((base + j * d) // 128 for j in (0, 127)))
                            tl = list(range(tl[0], tl[-1] + 1))
                            for k in range(8):
                                A("pe", lambda e, k=k, b=b, base=base, d=d: e.matmul(
                                    pA[b][:], hT[:, k, base:base + 127 * d + 1:d], wsl[wi][:, k, :], start=(k == 0), stop=(k == 7)),
                                  reads=[("hT", tt) for tt in tl] + [f"wsl{wi}"], writes=[f"pA{b}"])
                            si = stage()
                            A("act", lambda e, b=b, si=si: e.copy(stg[si][:], pA[b][:]), reads=[f"pA{b}"], writes=[f"stg{si}"])
                            row = r * n + i * 128
                            A("sp", lambda e, si=si, row=row, g=g: e.dma_start(out=scr_v[g, row:row + 128, :], in_=stg[si][:]),
                              reads=[f"stg{si}"], writes=[("scr_v", g)], dma_key=f"stg{si}")
                if cut <= 5:
                    S_.barrier(); continue
                for qk in range(2):
                    off = OFF_Q if qk == 0 else OFF_K
                    dstT = scr_qT if qk == 0 else scr_kT
                    for blk in range(3):
                        g = blk
                        d = DIL[g]
                        n = S // d
                        wi = load_w(off + blk * 512, 512)
                        if qk == 0:
                            A("dve", lambda e, wi=wi: e.tensor_scalar(wsl[wi][:], wsl[wi][:], 0.125, None, ALU.mult),
                              reads=[f"wsl{wi}"], writes=[f"wsl{wi}"])
                        wv = wsl[wi][:].rearrange("p k (h c) -> p k h c", h=8)
                        wpv = wp[:].rearrange("p k (h c) -> p k h c", h=8)
                        A("dve", lambda e, wv=wv, wpv=wpv: e.tensor_copy(wpv[:, :, :, 0:8], wv[:, :, :, 8:16]), reads=[f"wsl{wi}"], writes=["wp"])
                        A("dve", lambda e, wv=wv, wpv=wpv: e.tensor_copy(wpv[:, :, :, 8:16], wv[:, :, :, 0:8]), reads=[f"wsl{wi}"], writes=["wp"])
                        for hp in range(4):
                            ox = outX[hp]
                            for tb in range(TB):
                                b = tb % 2
                                for k in range(8):
                                    A("pe", lambda e, k=k, b=b, tb=tb, hp=hp, wi=wi: e.matmul(
                                        pA[b][:], wsl[wi][:, k, hp * 128:(hp + 1) * 128], hT[:, k, tb * 512:(tb + 1) * 512],
                                        start=(k == 0), stop=(k == 7)),
                                      reads=hT_all[tb * 4:(tb + 1) * 4] + [f"wsl{wi}"], writes=[f"pA{b}"])
                                for k in range(8):
                                    A("pe", lambda e, k=k, b=b, tb=tb, hp=hp: e.matmul(
                                        pB[b][:], wp[:, k, hp * 128:(hp + 1) * 128], hT[:, k, tb * 512:(tb + 1) * 512],
                                        start=(k == 0), stop=(k == 7)),
                                      reads=hT_all[tb * 4:(tb + 1) * 4] + ["wp"], writes=[f"pB{b}"])
                                sl = slice(tb * 512, (tb + 1) * 512)
                                A("dve", lambda e, b=b, sl=sl: e.tensor_tensor(tmp1[b][:], pB[b][:], sinT[:, sl], ALU.mult),
                                  reads=[f"pB{b}", "sinT"], writes=[f"tmp1{b}"])
                                A("dve", lambda e, b=b, sl=sl: e.tensor_tensor(tmp2[b][:], pA[b][:], cosT[:, sl], ALU.mult),
                                  reads=[f"pA{b}", "cosT"], writes=[f"tmp2{b}"])
                                i0 = tb * 512 // d
                                ni = 512 // d
                                oap = ox[:].rearrange("p (r i) -> p r i", r=d)[:, :, i0:i0 + ni]
                                A("pool", lambda e, b=b, oap=oap, d=d: e.tensor_tensor(
                                    oap, tmp1[b][:].rearrange("p (i r) -> p r i", r=d), tmp2[b][:].rearrange("p (i r) -> p r i", r=d), ALU.add),
                                  reads=[f"tmp1{b}", f"tmp2{b}"], writes=[f"outX{hp}"])
                            row = blk * 512 + hp * 128
                            A("sp", lambda e, hp=hp, row=row, dstT=dstT: e.dma_start(out=dstT[row:row + 128, :], in_=outX[hp][:]),
                              reads=[f"outX{hp}"], writes=[("scr_qk", qk, blk, hp)], dma_key=f"outX{hp}")
                if cut <= 6:
                    S_.barrier(); continue
                for grp in range(8):
                    wi = load_w(OFF_X + grp * 512, 512)
                    for j in range(4):
                        ft = grp * 4 + j
                        rb = ft % 2
                        for k5 in range(5):
                            A("dve", lambda e, k5=k5, ft=ft: e.tensor_scalar(dg[:, k5, :], ident[:], cw[:, ft, k5:k5 + 1], None, ALU.mult),
                              reads=["consts", "cw"], writes=["dg"])
                        for tb in range(TB):
                            b = tb % 2
                            for k in range(8):
                                A("pe", lambda e, k=k, b=b, tb=tb, j=j, wi=wi: e.matmul(
                                    pA[b][:], wsl[wi][:, k, j * 128:(j + 1) * 128], hT[:, k, tb * 512:(tb + 1) * 512],
                                    start=(k == 0), stop=(k == 7)),
                                  reads=hT_all[tb * 4:(tb + 1) * 4] + [f"wsl{wi}"], writes=[f"pA{b}"])
                            A("dve", lambda e, b=b, tb=tb, rb=rb: e.tensor_copy(rawT[rb][:, 2 + tb * 512:2 + (tb + 1) * 512], pA[b][:]),
                              reads=[f"pA{b}"], writes=[f"rawT{rb}"])
                        for tb in range(TB):
                            b = tb % 2
                            for k5 in range(5):
                                A("pe", lambda e, k5=k5, b=b, tb=tb, rb=rb: e.matmul(
                                    pB[b][:], dg[:, k5, :], rawT[rb][:, tb * 512 + k5:tb * 512 + k5 + 512], start=(k5 == 0), stop=(k5 == 4)),
                                  reads=["dg", f"rawT{rb}"], writes=[f"pB{b}"])
                            A("act", lambda e, b=b, tb=tb, j=j, ft=ft: e.activation(
                                outX[j][:, tb * 512:(tb + 1) * 512], pB[b][:], AF.Silu, bias=cb[:, ft:ft + 1]),
                              reads=[f"pB{b}", "cb"], writes=[f"outX{j}"])
                    if grp < 6:
                        dst, clo, dkey = (scr_x, grp * 512, "scr_x") if grp < 4 else (scr_B, (grp - 4) * 512, "scr_B")
                        for t in range(NT):
                            b = t % 2
                            for j in range(4):
                                A("pe", lambda e, b=b, j=j, t=t: e.transpose(pT[b][:, j * 128:(j + 1) * 128], outX[j][:, t * 128:(t + 1) * 128], ident[:]),
                                  reads=[f"outX{j}", "consts"], writes=[f"pT{b}"])
                            si = stage()
                            A("dve", lambda e, b=b, si=si: e.tensor_copy(stg[si][:], pT[b][:, 0:512]), reads=[f"pT{b}"], writes=[f"stg{si}"])
                            A("sp", lambda e, t=t, si=si, dst=dst, clo=clo: e.dma_start(out=dst[t * 128:(t + 1) * 128, clo:clo + 512], in_=stg[si][:]),
                              reads=[f"stg{si}"], writes=[(dkey, t)], dma_key=f"stg{si}")
                    if grp >= 4:
                        dstT = scr_BT if grp < 6 else scr_CT
                        rlo = (grp - 4) * 512 if grp < 6 else (grp - 6) * 512
                        for j in range(4):
                            A("sp", lambda e, j=j, dstT=dstT, rlo=rlo: e.dma_start(out=dstT[rlo + j * 128:rlo + (j + 1) * 128, :], in_=outX[j][:]),
                              reads=[f"outX{j}"], writes=[("scr_BCT", grp, j)], dma_key=f"outX{j}")
            S_.barrier()
            if stop_after <= 2:
                continue
            with ExitStack() as st:
                def sb(name, shape, dt):
                    return st.enter_context(nc.sbuf_tensor(f"{name}_s{seq}", shape, dt))

                def ps(name, shape, dt):
                    return st.enter_context(nc.psum_tensor(f"{name}_s{seq}", shape, dt))

                wssd = sb("wssd", [128, 16, 1024], BF16)
                nwl = sb("nwl", [128, 16], F32)
                xc = [sb(f"xc{i}", [128, 2048], BF16) for i in range(3)]
                Bc = [sb(f"Bc{i}", [128, 1024], BF16) for i in range(3)]
                BTc = [sb(f"BTc{i}", [128, 8, 128], BF16) for i in range(3)]
                CTc = [sb(f"CTc{i}", [128, 8, 128], BF16) for i in range(3)]
                dtc = [sb(f"dtc{i}", [128, 64], F32) for i in range(3)]
                zc = [sb(f"zc{i}", [128, 2048], BF16) for i in range(3)]
                yfc = [sb(f"yfc{i}", [128, 2048], BF16) for i in range(3)]
                gc = [sb(f"gc{i}", [128, 1024], BF16) for i in range(3)]
                ST = sb("ST", [128, 2048], F32)
                STb = sb("STb", [128, 2048], BF16)
                sm_ = [sb(f"sm{i}", [128, 128], F32) for i in range(2)]
                sm2_ = [sb(f"sm2{i}", [128, 128], F32) for i in range(2)]
                dAhi_ = [sb(f"dAhi{i}", [128, 32], BF16) for i in range(2)]
                dAlo_ = [sb(f"dAlo{i}", [128, 32], BF16) for i in range(2)]
                uhi = sb("uhi", [128, 32], BF16)
                ulo = sb("ulo", [128, 32], BF16)
                sm3 = sb("sm3", [128, 64], F32)
                CBm = [sb(f"CBm{i}", [128, 128], BF16) for i in range(2)]
                Eb = [sb(f"Eb{i}", [128, 512], BF16) for i in range(2)]
                Mb = [sb(f"Mb{i}", [128, 4, 128], BF16) for i in range(2)]
                xw = [sb(f"xw{i}", [128, 256], BF16) for i in range(2)]
                xdt = [sb(f"xdt{i}", [128, 256], BF16) for i in range(2)]
                tq = [sb(f"tq{i}", [128, 256], F32) for i in range(2)]
                tq2 = [sb(f"tq2{i}", [128, 256], F32) for i in range(2)]
                Ych_ = [sb(f"Ych{i}", [128, 2048], F32) for i in range(2)]
                Yst = [sb(f"Yst{i}", [128, 2048], BF16) for i in range(2)]
                Gb = sb("Gb", [128, 2048], BF16)
                GT = sb("GT", [128, 16, 128], BF16)
                junk2 = sb("junk2", [128, 2048], BF16)
                r8 = sb("r8", [128, 4], F32)
                m1s = [sb(f"m1s{i}", [128, 1024], BF16) for i in range(2)]
                pSeg = [ps(f"pSeg{i}", [128, 512], F32) for i in range(2)]
                pYY = [ps(f"pYY{i}", [128, 512], F32) for i in range(2)]
                pSx = [ps(f"pSx{i}", [128, 512], F32) for i in range(2)]
                pS = pSx[0]
                pSm = pSx[0][:, 256:320]
                pT2 = ps("pT2", [128, 1024], BF16)
                pO = ps("pO", [128, 512], F32)

                for k in range(2):
                    A("pool", lambda e, k=k: e.dma_start(out=wssd[:, k * 8:(k + 1) * 8, :],
                                                         in_=w_ssd[k * 1024:(k + 1) * 1024, :].rearrange("(k p) c -> p k c", p=128)),
                      writes=["wssd"], dma_key="wssd")
                A("sp", lambda e: e.dma_start(out=nwl[:], in_=nw_l), writes=["nwl"], dma_key="nwl")
                for k in range(16):
                    A("dve", lambda e, k=k: e.tensor_scalar(wssd[:, k, :], wssd[:, k, :], nwl[:, k:k + 1], None, ALU.mult),
                      reads=["wssd", "nwl"], writes=["wssd"])
                iters = [(dr_, ci_) for dr_ in range(2) for ci_ in range(NT)]
                NIT = len(iters)

                def chunk_of(k):
                    dr_, ci_ = iters[k]
                    return dr_, (ci_ if dr_ == 0 else NT - 1 - ci_)

                def issue_loads(k, only_yf=False):
                    dr_, c = chunk_of(k)
                    b = k % 3
                    rows = slice(c * 128, (c + 1) * 128)
                    if only_yf:
                        A("sp", lambda e: e.dma_start(out=yfc[b][:], in_=scr_yf[rows, :]), reads=[("scr_yf", c)], writes=[f"yfc{b}"], dma_key=f"yfc{b}")
                        return
                    A("sp", lambda e: e.dma_start(out=xc[b][:], in_=scr_x[rows, :]), reads=[("scr_x", c)], writes=[f"xc{b}"], dma_key=f"xc{b}")
                    A("sp", lambda e: e.dma_start(out=Bc[b][:], in_=scr_B[rows, :]), reads=[("scr_B", c)], writes=[f"Bc{b}"], dma_key=f"Bc{b}")
                    A("sp", lambda e: e.dma_start(out=BTc[b][:], in_=scr_BT[:, rows].rearrange("(g n) t -> n g t", n=128)),
                      reads=[("scr_BCT", gg, jj) for gg in (4, 5) for jj in range(4)], writes=[f"BTc{b}"], dma_key=f"BTc{b}")
                    A("sp", lambda e: e.dma_start(out=CTc[b][:], in_=scr_CT[:, rows].rearrange("(g n) t -> n g t", n=128)),
                      reads=[("scr_BCT", gg, jj) for gg in (6, 7) for jj in range(4)], writes=[f"CTc{b}"], dma_key=f"CTc{b}")
                    A("sp", lambda e: e.dma_start(out=dtc[b][:], in_=scr_dt[rows, :]), reads=[("scr_dt", c)], writes=[f"dtc{b}"], dma_key=f"dtc{b}")
                    if dr_ == 1:
                        A("sp", lambda e: e.dma_start(out=zc[b][:], in_=scr_z[rows, :]), reads=[("scr_z", c)], writes=[f"zc{b}"], dma_key=f"zc{b}")
                        if iters[k][1] > 0:
                            A("sp", lambda e: e.dma_start(out=yfc[b][:], in_=scr_yf[rows, :]), reads=[("scr_yf", c)], writes=[f"yfc{b}"], dma_key=f"yfc{b}")
                        A("sp", lambda e: e.dma_start(out=gc[b][:], in_=scr_gate[rows, 0:1024]), reads=[("scr_gate", c)], writes=[f"gc{b}"], dma_key=f"gc{b}")

                def prologue(k):
                    dr_, c = chunk_of(k)
                    b = k % 3
                    p = k % 2
                    o32 = dr_ * 32
                    sm, sm2, dAhi, dAlo = sm_[p], sm2_[p], dAhi_[p], dAlo_[p]
                    P = f"_{p}"
                    A("dve", lambda e: e.tensor_tensor(sm[:, 0:32], dtc[b][:, o32:o32 + 32], smallc[:, o32:o32 + 32], ALU.add),
                      reads=[f"dtc{b}", "smallc"], writes=["sm0" + P])
                    A("act", lambda e: e.activation(sm[:, 32:64], sm[:, 0:32], AF.Exp), reads=["sm0" + P], writes=["sm1" + P])
                    A("act", lambda e: e.activation(sm[:, 64:96], sm[:, 32:64], AF.Ln, bias=1.0), reads=["sm1" + P], writes=["dt" + P])
                    A("dve", lambda e: e.tensor_tensor(sm[:, 96:128], sm[:, 64:96], smallc[:, 64 + o32:96 + o32], ALU.mult),
                      reads=["dt" + P, "smallc"], writes=["dA" + P])
                    A("dve", lambda e: e.tensor_copy(dAhi[:], sm[:, 96:128]), reads=["dA" + P], writes=["dAhi" + P])
                    A("dve", lambda e: e.tensor_tensor(dAlo[:], sm[:, 96:128], dAhi[:], ALU.subtract), reads=["dA" + P, "dAhi" + P], writes=["dAlo" + P])
                    A("pe", lambda e: e.matmul(pSm[:, 0:32], cst_f[:, dr_, :], sm[:, 96:128], start=True, stop=True), reads=["dA" + P, "consts"], writes=["pSx0"])
                    A("pe", lambda e: e.matmul(pSm[:, 32:64], cst_f[:, 2, :], sm[:, 96:128], start=True, stop=True), reads=["dA" + P, "consts"], writes=["pSx0"])
                    A("act", lambda e: e.copy(sm2[:, 0:32], pSm[:, 0:32]), reads=["pSx0"], writes=["asb" + P])
                    A("act", lambda e: e.activation(sm2[:, 32:64], pSm[:, 0:32], AF.Exp), reads=["pSx0"], writes=["ea" + P])
                    A("act", lambda e: e.activation(sm2[:, 64:96], pSm[:, 32:64], AF.Exp), reads=["pSx0"], writes=["eal" + P])
                    A("dve", lambda e: e.tensor_tensor(sm2[:, 96:128], pSm[:, 32:64], sm2[:, 0:32], ALU.subtract), reads=["pSx0", "asb" + P], writes=["wv" + P])
                    A("act", lambda e: e.activation(sm2[:, 96:128], sm2[:, 96:128], AF.Exp), reads=["wv" + P], writes=["wv" + P])
                    A("dve", lambda e: e.tensor_tensor(sm2[:, 96:128], sm2[:, 96:128], sm[:, 64:96], ALU.mult), reads=["wv" + P, "dt" + P], writes=["wv" + P])

                def stage_a(k, g):
                    dr_, c = chunk_of(k)
                    b = k % 3
                    p = k % 2
                    P = f"_{p}"
                    sm, sm2, dAhi, dAlo = sm_[p], sm2_[p], dAhi_[p], dAlo_[p]
                    tri_b = cst_b[:, 0 + 2 * dr_, :]
                    ntri_b = cst_b[:, 1 + 2 * dr_, :]
                    neg_b = cst_b[:, 4 + dr_, :]
                    q = g % 2
                    A("pe", lambda e: e.matmul(pSx[q][:, 320:448], BTc[b][:, g, :], CTc[b][:, g, :], start=True, stop=True),
                      reads=[f"BTc{b}", f"CTc{b}"], writes=[f"pSx{q}"])
                    for j in range(4):
                        h = g * 4 + j
                        osl = pSeg[q][:, j * 128:(j + 1) * 128]
                        hi = dAhi[:, h:h + 1].to_broadcast([128, 128])
                        lo = dAlo[:, h:h + 1].to_broadcast([128, 128])
                        A("pe", lambda e: e.matmul(osl, hi, tri_b, start=True, stop=False), reads=["dAhi" + P, "consts"], writes=[f"pSeg{q}"])
                        A("pe", lambda e: e.matmul(osl, lo, tri_b, start=False, stop=False), reads=["dAlo" + P, "consts"], writes=[f"pSeg{q}"])
                        A("pe", lambda e: e.matmul(osl, ntri_b, hi, start=False, stop=False), reads=["dAhi" + P, "consts"], writes=[f"pSeg{q}"])
                        A("pe", lambda e: e.matmul(osl, ntri_b, lo, start=False, stop=False), reads=["dAlo" + P, "consts"], writes=[f"pSeg{q}"])
                        A("pe", lambda e: e.matmul(osl, ident[:], neg_b, start=False, stop=True), reads=["consts"], writes=[f"pSeg{q}"])
                    A("act", lambda e: e.activation(Eb[q][:], pSeg[q][:], AF.Exp), reads=[f"pSeg{q}"], writes=[f"Eb{q}"])
                    xg = xc[b][:, g * 256:(g + 1) * 256].rearrange("p (j q) -> p j q", j=4)
                    wb = sm2[:, 96 + g * 4:100 + g * 4].unsqueeze(2).to_broadcast([128, 4, 64])
                    A("pool", lambda e: e.tensor_tensor(xw[q][:].rearrange("p (j q) -> p j q", j=4), xg, wb, ALU.mult),
                      reads=[f"xc{b}", "wv" + P], writes=[f"xw{q}"])
                    dtb = sm[:, 64 + g * 4:68 + g * 4].unsqueeze(2).to_broadcast([128, 4, 64])
                    A("pool", lambda e: e.tensor_tensor(xdt[q][:].rearrange("p (j q) -> p j q", j=4), xg, dtb, ALU.mult),
                      reads=[f"xc{b}", "dt" + P], writes=[f"xdt{q}"])

                def stage_b(k, g):
                    dr_, c = chunk_of(k)
                    b = k % 3
                    p = k % 2
                    P = f"_{p}"
                    sm2 = sm2_[p]
                    Ych = Ych_[p]
                    q = g % 2
                    A("dve", lambda e: e.tensor_tensor(Mb[q][:], Eb[q][:].rearrange("p (j t) -> p j t", j=4),
                                                       pSx[q][:, 320:448].unsqueeze(1).to_broadcast([128, 4, 128]), ALU.mult),
                      reads=[f"Eb{q}", f"pSx{q}"], writes=[("Mb", q)])
                    for j in range(4):
                        A("pe", lambda e, j=j: e.matmul(pYY[q][:, j * 64:(j + 1) * 64], Mb[q][:, j, :], xdt[q][:, j * 64:(j + 1) * 64], start=True, stop=True),
                          reads=[("Mb", q), f"xdt{q}"], writes=[f"pY{q}"])
                    A("pe", lambda e: e.matmul(pYY[q][:, 256:512], CTc[b][:, g, :], STb[:, g * 256:(g + 1) * 256], start=True, stop=True),
                      reads=[f"CTc{b}", ("STb", g)], writes=[f"pYo{q}"])
                    A("pe", lambda e: e.matmul(pSx[q][:, 0:256], Bc[b][:, g * 128:(g + 1) * 128], xw[q][:], start=True, stop=True),
                      reads=[f"Bc{b}", f"xw{q}"], writes=[f"pSx{q}"])
                    eab = sm2[:, 32 + g * 4:36 + g * 4].unsqueeze(2).to_broadcast([128, 4, 64])
                    A("dve", lambda e: e.tensor_tensor(tq[q][:].rearrange("p (j q) -> p j q", j=4), pYY[q][:, 256:512].rearrange("p (j q) -> p j q", j=4), eab, ALU.mult),
                      reads=[f"pYo{q}", "ea" + P], writes=[f"tq{q}"])
                    A("dve", lambda e: e.tensor_tensor(Ych[:, g * 256:(g + 1) * 256], pYY[q][:, 0:256], tq[q][:], ALU.add),
                      reads=[f"pY{q}", f"tq{q}"], writes=[("Ych", p, g)])
                    elb = sm2[:, 64 + g * 4:68 + g * 4].unsqueeze(2).to_broadcast([128, 4, 64])
                    A("pool", lambda e: e.tensor_tensor(tq2[q][:].rearrange("p (j q) -> p j q", j=4),
                                                        ST[:, g * 256:(g + 1) * 256].rearrange("p (j q) -> p j q", j=4), elb, ALU.mult),
                      reads=[("ST", g), "eal" + P], writes=[f"tq2{q}"])
                    A("dve", lambda e: e.tensor_tensor(ST[:, g * 256:(g + 1) * 256], pSx[q][:, 0:256], tq2[q][:], ALU.add),
                      reads=[f"pSx{q}", f"tq2{q}"], writes=[("ST", g)])
                    A("act", lambda e: e.copy(STb[:, g * 256:(g + 1) * 256], ST[:, g * 256:(g + 1) * 256]), reads=[("ST", g)], writes=[("STb", g)])

                def epilogue_pieces(k):
                    dr_, c = chunk_of(k)
                    b = k % 3
                    p = k % 2
                    Ych = Ych_[p]
                    rows = slice(c * 128, (c + 1) * 128)
                    Yall = [("Ych", p, g) for g in range(8)]
                    pcs = []
                    if dr_ == 0:
                        def f0():
                            A("act", lambda e: e.copy(Yst[p][:], Ych[:]), reads=Yall, writes=[f"Yst{p}"])
                            A("act", lambda e: e.dma_start(out=scr_yf[rows, :], in_=Yst[p][:]), reads=[f"Yst{p}"], writes=[("scr_yf", c)], dma_key=f"Yst{p}")
                        return [f0]

                    def e0():
                        A("dve", lambda e: e.tensor_tensor(Ych[:], Ych[:], yfc[b][:], ALU.add), reads=Yall + [f"yfc{b}"], writes=Yall)
                        Db = smallc[:, 128:160].unsqueeze(2).to_broadcast([128, 32, 64])
                        A("pool", lambda e: e.tensor_tensor(Yst[0][:].rearrange("p (h q) -> p h q", h=32), xc[b][:].rearrange("p (h q) -> p h q", h=32), Db, ALU.mult),
                          reads=[f"xc{b}", "smallc"], writes=["Yst0"])

                    def e1():
                        A("dve", lambda e: e.tensor_tensor(Ych[:], Ych[:], Yst[0][:], ALU.add), reads=Yall + ["Yst0"], writes=Yall)
                        A("dve", lambda e: e.tensor_tensor(Gb[:], Ych[:], zc[b][:], ALU.mult), reads=Yall + [f"zc{b}"], writes=["Gb"])
                        A("act", lambda e: e.activation(junk2[:], Gb[:], AF.Square, accum_out=r8[:, 0:1]), reads=["Gb"], writes=["junk2", "r0"])
                        A("act", lambda e: e.activation(r8[:, 1:2], r8[:, 0:1], AF.Sqrt, bias=EPS, scale=1.0 / DIN), reads=["r0"], writes=["r1"])
                        A("dve", lambda e: e.reciprocal(r8[:, 2:3], r8[:, 1:2]), reads=["r1"], writes=["r2"])

                    def mk_tr(q4):
                        def f():
                            for kk in range(8):
                                kx = q4 * 8 + kk
                                A("pe", lambda e, kx=kx, kk=kk: e.transpose(pT2[:, kk * 128:(kk + 1) * 128], Gb[:, kx * 128:(kx + 1) * 128], ident[:]),
                                  reads=["Gb", "consts"], writes=["pT2"])
                            A("act", lambda e: e.copy(GT[:, q4 * 8:(q4 + 1) * 8, :], pT2[:].rearrange("p (k t) -> p k t", k=8)), reads=["pT2"], writes=["GT"])
                        return f

                    def mk_op(hf):
                        def f():
                            hs = slice(hf * 512, (hf + 1) * 512)
                            for kx in range(16):
                                A("pe", lambda e, kx=kx: e.matmul(pO[:], GT[:, kx, :], wssd[:, kx, hs], start=(kx == 0), stop=(kx == 15)),
                                  reads=["GT", "wssd"], writes=["pO"])
                            A("dve", lambda e: e.scalar_tensor_tensor(m1s[p][:, hs], pO[:], r8[:, 2:3], gc[b][:, hs], ALU.mult, ALU.mult),
                              reads=["pO", "r2", f"gc{b}"], writes=[f"m1s{p}"])
                            if hf == 1:
                                A("act", lambda e: e.dma_start(out=scr_m1[rows, :], in_=m1s[p][:]), reads=[f"m1s{p}"], writes=[("scr_m1", c)], dma_key=f"m1s{p}")
                        return f
                    return [e0, e1, mk_tr(0), mk_tr(1), mk_op(0), mk_op(1)]

                issue_loads(0)
                prologue(0)
                pending = []
                for k in range(NIT):
                    dr, ci = iters[k]
                    if ci == 0:
                        A("dve", lambda e: e.memset(ST[:], 0.0), writes=[("ST", g_) for g_ in range(8)])
                        A("dve", lambda e: e.memset(STb[:], 0.0), writes=[("STb", g_) for g_ in range(8)])
                    if dr == 1 and ci == 0:
                        for pc in pending:
                            pc()
                        pending = []
                        issue_loads(k, only_yf=True)
                    if k + 1 < NIT:
                        issue_loads(k + 1)
                        prologue(k + 1)
                    stage_a(k, 0)
                    for g in range(8):
                        if g + 1 < 8:
                            stage_a(k, g + 1)
                        stage_b(k, g)
                        if pending:
                            pending.pop(0)()
                    for pc in pending:
                        pc()
                    pending = epilogue_pieces(k)
                for pc in pending:
                    pc()
            S_.barrier()
            if stop_after <= 4:
                continue
            with ExitStack() as st:
                def sb(name, shape, dt):
                    return st.enter_context(nc.sbuf_tensor(f"{name}_a{seq}", shape, dt))

                def ps(name, shape, dt):
                    return st.enter_context(nc.psum_tensor(f"{name}_a{seq}", shape, dt))

                QT = sb("QT", [128, 4, S], BF16)
                KT = sb("KT", [128, 4, S], BF16)
                Vp = sb("Vp", [128, NT, 8, 65], BF16)
                NPS = 4
                PT = [sb(f"PT{i}", [128, 384], BF16) for i in range(NPS)]
                ubase = [0]
                qbase = [0]
                ost = [sb(f"ost{i}", [128, 8, 65], BF16) for i in range(2)]
                pSc = [ps(f"pSc{i}", [128, 512], F32) for i in range(NPS)]
                pOa = [ps(f"pOa{i}", [128, 1024], F32) for i in range(2)]
                A("dve", lambda e: e.memset(Vp[:].rearrange("p t h c -> p (t h) c")[:, :, 64:65], 1.0), writes=["Vp"])
                ucnt = 0
                qcnt = 0
                for g in range(3):
                    d = DIL[g]
                    n = S // d
                    nq = n // 128
                    for r in range(d):
                        for hp in range(4):
                            row = g * 512 + hp * 128
                            A("sp", lambda e, hp=hp, row=row, r=r, n=n: e.dma_start(out=QT[:, hp, 0:n], in_=scr_qT[row:row + 128, r * n:(r + 1) * n]),
                              reads=[("scr_qk", 0, g, hp)], writes=["QT"], dma_key="QT")
                            A("sp", lambda e, hp=hp, row=row, r=r, n=n: e.dma_start(out=KT[:, hp, 0:n], in_=scr_kT[row:row + 128, r * n:(r + 1) * n]),
                              reads=[("scr_qk", 1, g, hp)], writes=["KT"], dma_key="KT")
                        for i in range(nq):
                            A("sp", lambda e, g=g, r=r, n=n, i=i: e.dma_start(out=Vp[:, i, :, 0:64],
                                                                          in_=scr_v[g, r * n + i * 128:r * n + (i + 1) * 128, :].rearrange("k (h c) -> k h c", h=8)),
                              reads=[("scr_v", g)], writes=["Vp"], dma_key="Vp")
                        units = [(i, h) for i in range(nq) for h in range(8)]

                        def emit_scores(u):
                            i, h = units[u]
                            hp, hh = h // 2, h % 2
                            ub = (ubase[0] + u) % NPS
                            offs = [o for o in (-1, 0, 1) if 0 <= i + o < nq]
                            for oi, o in enumerate(offs):
                                ks = slice((i + o) * 128, (i + o + 1) * 128)
                                A("pe", lambda e: e.matmul(pSc[ub][:, oi * 128:(oi + 1) * 128], KT[hh * 64:(hh + 1) * 64, hp, ks],
                                                           QT[hh * 64:(hh + 1) * 64, hp, i * 128:(i + 1) * 128], start=True, stop=False),
                                  reads=["QT", "KT"], writes=[f"pSc{ub}"])
                                A("pe", lambda e: e.matmul(pSc[ub][:, oi * 128:(oi + 1) * 128], ident[:], band[:, o + 1, :], start=False, stop=True),
                                  reads=["consts"], writes=[f"pSc{ub}"])
                            no = len(offs)
                            A("act", lambda e: e.activation(PT[ub][:, 0:no * 128], pSc[ub][:, 0:no * 128], AF.Exp),
                              reads=[f"pSc{ub}"], writes=[f"PT{ub}"])

                        def emit_pv(u):
                            i, h = units[u]
                            ub = (ubase[0] + u) % NPS
                            qb = (qbase[0] + i) % 2
                            offs = [o for o in (-1, 0, 1) if 0 <= i + o < nq]
                            no = len(offs)
                            c0 = (h // 4) * 512 + (h % 4) * 65
                            for oi, o in enumerate(offs):
                                A("pe", lambda e: e.matmul(pOa[qb][:, c0:c0 + 65], PT[ub][:, oi * 128:(oi + 1) * 128], Vp[:, i + o, h, :],
                                                           start=(oi == 0), stop=(oi == no - 1)), reads=[f"PT{ub}", "Vp"], writes=[f"pOa{qb}"])
                            if h == 7:
                                for hf in range(2):
                                    A("act" if hf else "dve", lambda e: (e.copy if hf else e.tensor_copy)(
                                        ost[qb][:, hf * 4:(hf + 1) * 4, :], pOa[qb][:, hf * 512:hf * 512 + 260].rearrange("p (h c) -> p h c", h=4)),
                                      reads=[f"pOa{qb}"], writes=[f"ost{qb}"])
                                t0 = i * 128 * d + r
                                A("act", lambda e: e.dma_start(out=scr_o[g, t0:t0 + 127 * d + 1:d, :, :], in_=ost[qb][:]),
                                  reads=[f"ost{qb}"], writes=[("scr_o", g)], dma_key=f"ost{qb}")

                        LA = NPS - 1
                        for u in range(min(LA, len(units))):
                            emit_scores(u)
                        for u in range(len(units)):
                            if u + LA < len(units):
                                emit_scores(u + LA)
                            emit_pv(u)
                        ubase[0] += len(units)
                        qbase[0] += nq
            S_.barrier()
            if stop_after <= 5:
                continue
            with ExitStack() as st:
                def sb(name, shape, dt):
                    return st.enter_context(nc.sbuf_tensor(f"{name}_f{seq}", shape, dt))

                def ps(name, shape, dt):
                    return st.enter_context(nc.psum_tensor(f"{name}_f{seq}", shape, dt))

                watt = sb("watt", [128, 4, 1024], BF16)
                wout = sb("wout", [128, 8, 1024], BF16)
                gns = sb("gns", [128, 2, 1024], F32)
                o3 = [sb(f"o3{i}", [128, 3, 8, 65], BF16) for i in range(4)]
                osum_ = [sb(f"osum{i}", [128, 8, 65], F32) for i in range(4)]
                rl_ = [sb(f"rl{i}", [128, 8], F32) for i in range(4)]
                Ob_ = [sb(f"Ob{i}", [128, 512], BF16) for i in range(4)]
                OT_ = [sb(f"OT{i}", [128, 4, 128], BF16) for i in range(4)]
                gat = [sb(f"gat{i}", [128, 1024], BF16) for i in range(4)]
                m1c = [sb(f"m1c{i}", [128, 1024], BF16) for i in range(4)]
                mrg_ = [sb(f"mrg{i}", [128, 1024], BF16) for i in range(4)]
                mrgT_ = [sb(f"mrgT{i}", [128, 8, 128], BF16) for i in range(4)]
                tmpf_ = [sb(f"tmpf{i}", [128, 1024], F32) for i in range(4)]
                xin = [sb(f"xin{i}", [128, 1024], F32) for i in range(4)]
                x1 = [sb(f"x1{i}", [128, 1024], F32) for i in range(4)]
                h2 = [sb(f"h2{i}", [128, 1024], BF16) for i in range(4)]
                jk_ = [sb(f"jk{i}", [128, 1024], F32) for i in range(4)]
                s8_ = [sb(f"s8{i}", [128, 8], F32) for i in range(4)]
                pT3 = [ps(f"pT3{i}", [128, 1024], BF16) for i in range(2)]
                pM = [ps(f"pM{i}", [128, 1024], F32) for i in range(2)]

                A("pool", lambda e: e.dma_start(out=watt[:], in_=w_attn.rearrange("(k p) c -> p k c", p=128)), writes=["watt"], dma_key="watt")
                A("pool", lambda e: e.dma_start(out=wout[:], in_=w_out.rearrange("(k p) c -> p k c", p=128)), writes=["wout"], dma_key="wout")
                for gi, gsrc in enumerate((norm_mix_post, norm_ffn_pre)):
                    A("sp", lambda e, gi=gi, gsrc=gsrc: e.dma_start(out=gns[:, gi, :], in_=gsrc.partition_broadcast(128)), writes=["gns"], dma_key="gns")

                def rms_scale(src_ap, ss_col, rd, jk, s8, sfx):
                    A("act", lambda e: e.activation(jk[:], src_ap, AF.Square, accum_out=s8[:, ss_col:ss_col + 1]), reads=rd, writes=["jk" + sfx, ("s8" + sfx, ss_col)])
                    A("act", lambda e: e.activation(s8[:, ss_col + 1:ss_col + 2], s8[:, ss_col:ss_col + 1], AF.Sqrt, bias=EPS, scale=1.0 / D),
                      reads=[("s8" + sfx, ss_col)], writes=[("s8" + sfx, ss_col + 1)])
                    A("dve", lambda e: e.reciprocal(s8[:, ss_col + 1:ss_col + 2], s8[:, ss_col + 1:ss_col + 2]), reads=[("s8" + sfx, ss_col + 1)], writes=[("s8" + sfx, ss_col + 1)])

                def p6a_loads(t):
                    c4 = t % 4
                    rows = slice(t * 128, (t + 1) * 128)
                    for g3 in range(3):
                        A("sp", lambda e, g3=g3: e.dma_start(out=o3[c4][:, g3, :, :], in_=scr_o[g3, rows, :, :]),
                          reads=[("scr_o", g3)], writes=[f"o3{c4}"], dma_key=f"o3{c4}")
                    A("sp", lambda e: e.dma_start(out=gat[c4][:], in_=scr_gate[rows, 1024:2048]), reads=[("scr_gate", t)], writes=[f"gat{c4}"], dma_key=f"gat{c4}")
                    A("sp", lambda e: e.dma_start(out=m1c[c4][:], in_=scr_m1[rows, :]), reads=[("scr_m1", t)], writes=[f"m1c{c4}"], dma_key=f"m1c{c4}")
                    A("sp", lambda e: e.dma_start(out=xin[c4][:], in_=x_in[seq, rows, :]), writes=[f"xin{c4}"], dma_key=f"xin{c4}")

                def p6a_stage(t, stage):
                    b = t % 2
                    c4 = t % 4
                    rows = slice(t * 128, (t + 1) * 128)
                    osum, rl, Ob, OT, mrg, mrgT, tmpf, jk, s8 = osum_[c4], rl_[c4], Ob_[c4], OT_[c4], mrg_[c4], mrgT_[c4], tmpf_[c4], jk_[c4], s8_[c4]
                    sfx = f"_{c4}"
                    if stage == 1:
                        A("dve", lambda e: e.tensor_tensor(osum[:], o3[c4][:, 0, :, :], o3[c4][:, 1, :, :], ALU.add), reads=[f"o3{c4}"], writes=["osum" + sfx])
                        A("dve", lambda e: e.tensor_tensor(osum[:], osum[:], o3[c4][:, 2, :, :], ALU.add), reads=[f"o3{c4}", "osum" + sfx], writes=["osum" + sfx])
                        A("dve", lambda e: e.reciprocal(rl[:], osum[:, :, 64]), reads=["osum" + sfx], writes=["rl" + sfx])
                        A("dve", lambda e: e.tensor_tensor(Ob[:].rearrange("p (h c) -> p h c", h=8), osum[:, :, 0:64], rl[:].unsqueeze(2).to_broadcast([128, 8, 64]), ALU.mult),
                          reads=["osum" + sfx, "rl" + sfx], writes=["Ob" + sfx])
                        for k in range(4):
                            A("pe", lambda e, k=k: e.transpose(pT3[b][:, k * 128:(k + 1) * 128], Ob[:, k * 128:(k + 1) * 128], ident[:]), reads=["Ob" + sfx, "consts"], writes=[f"pT3{b}"])
                        A("act", lambda e: e.copy(OT[:], pT3[b][:, 0:512].rearrange("p (k t) -> p k t", k=4)), reads=[f"pT3{b}"], writes=["OT" + sfx])
                        for hf in range(2):
                            for k in range(4):
                                A("pe", lambda e, k=k, hf=hf: e.matmul(pM[b][:, hf * 512:(hf + 1) * 512], OT[:, k, :], watt[:, k, hf * 512:(hf + 1) * 512], start=(k == 0), stop=(k == 3)),
                                  reads=["OT" + sfx, "watt"], writes=[f"pM{b}"])
                        A("dve", lambda e: e.tensor_tensor(tmpf[:], pM[b][:], gat[c4][:], ALU.mult), reads=[f"pM{b}", f"gat{c4}"], writes=["tmpf" + sfx])
                        A("pool", lambda e: e.tensor_tensor(mrg[:], tmpf[:], m1c[c4][:], ALU.add), reads=["tmpf" + sfx, f"m1c{c4}"], writes=["mrg" + sfx])
                    if stage == 2:
                        for k in range(8):
                            A("pe", lambda e, k=k: e.transpose(pT3[b][:, k * 128:(k + 1) * 128], mrg[:, k * 128:(k + 1) * 128], ident[:]), reads=["mrg" + sfx, "consts"], writes=[f"pT3{b}"])
                        A("act", lambda e: e.copy(mrgT[:], pT3[b][:].rearrange("p (k t) -> p k t", k=8)), reads=[f"pT3{b}"], writes=["mrgT" + sfx])
                        for hf in range(2):
                            for k in range(8):
                                A("pe", lambda e, k=k, hf=hf: e.matmul(pM[b][:, hf * 512:(hf + 1) * 512], mrgT[:, k, :], wout[:, k, hf * 512:(hf + 1) * 512], start=(k == 0), stop=(k == 7)),
                                  reads=["mrgT" + sfx, "wout"], writes=[f"pM{b}"])
                        rms_scale(pM[b][:], 0, [f"pM{b}"], jk, s8, sfx)
                        A("dve", lambda e: e.scalar_tensor_tensor(tmpf[:], pM[b][:], s8[:, 1:2], gns[:, 0, :], ALU.mult, ALU.mult), reads=[f"pM{b}", ("s8" + sfx, 1), "gns"], writes=["tmpf" + sfx])
                        A("pool", lambda e: e.tensor_tensor(x1[c4][:], tmpf[:], xin[c4][:], ALU.add), reads=["tmpf" + sfx, f"xin{c4}"], writes=[f"x1{c4}"])
                        A("act", lambda e: e.dma_start(out=y_out[seq, rows, :], in_=x1[c4][:]), reads=[f"x1{c4}"], writes=[("y", seq, t)], dma_key=f"x1{c4}")
                    if stage == 3:
                        rms_scale(x1[c4][:], 2, [f"x1{c4}"], jk, s8, sfx)
                        A("dve", lambda e: e.scalar_tensor_tensor(h2[c4][:], x1[c4][:], s8[:, 3:4], gns[:, 1, :], ALU.mult, ALU.mult), reads=[f"x1{c4}", ("s8" + sfx, 3), "gns"], writes=[f"h2{c4}"])
                        A("act", lambda e: e.dma_start(out=scr_h2[seq, rows, :], in_=h2[c4][:]), reads=[f"h2{c4}"], writes=[("scr_h2", seq, t)], dma_key=f"h2{c4}")


                p6a_loads(0)
                p6a_loads(1)
                import os as _os
                PIPE = _os.environ.get("P6A_PIPE", "1") == "1"
                for kk in range(NT + 2):
                    if kk + 2 < NT:
                        p6a_loads(kk + 2)
                    if PIPE:
                        if kk < NT:
                            p6a_stage(kk, 1)
                        if 0 <= kk - 1 < NT:
                            p6a_stage(kk - 1, 2)
                        if 0 <= kk - 2 < NT:
                            p6a_stage(kk - 2, 3)
                    elif kk < NT:
                        p6a_stage(kk, 1)
                        p6a_stage(kk, 2)
                        p6a_stage(kk, 3)
            S_.barrier()
        if stop_after > 5:
            with ExitStack() as st:
                def sb(name, shape, dt):
                    return st.enter_context(nc.sbuf_tensor(f"{name}_ffn", shape, dt))

                def ps(name, shape, dt):
                    return st.enter_context(nc.psum_tensor(f"{name}_ffn", shape, dt))

                TBLK = 512
                NTB = TBLK // 128
                wfi = sb("wfi", [128, 8, 2 * FFN], BF16)
                wdn = sb("wdn", [128, 22, 1024], BF16)
                gpo = sb("gpo", [128, 1024], F32)
                h2b = [sb(f"h2b{i}", [128, 1024], BF16) for i in range(2)]
                h2T = sb("h2T", [128, 8, TBLK], BF16)
                actT = sb("actT", [128, 22, TBLK], BF16)
                gT = [sb(f"gT{i}", [128, TBLK], BF16) for i in range(2)]
                x1b = [sb(f"x1b{i}", [128, 1024], F32) for i in range(2)]
                tmpf = sb("tmpf", [128, 1024], F32)
                jk = sb("jk", [128, 1024], F32)
                s8 = sb("s8", [128, 8], F32)
                yo = [sb(f"yo{i}", [128, 1024], F32) for i in range(2)]
                pT3 = ps("pT3", [128, 1024], BF16)
                pM = ps("pM", [128, 1024], F32)
                pG = [ps(f"pG{i}", [128, 512], F32) for i in range(2)]
                pU = [ps(f"pU{i}", [128, 512], F32) for i in range(2)]
                for c4 in range(11):
                    A("pool", lambda e, c4=c4: e.dma_start(out=wfi[:, :, c4 * 512:(c4 + 1) * 512],
                                                           in_=w_ffn_in[:, c4 * 512:(c4 + 1) * 512].rearrange("(k p) c -> p k c", p=128)),
                      writes=[("wfi", c4)], dma_key="wfi")
                for q2 in range(2):
                    A("pool", lambda e, q2=q2: e.dma_start(out=wdn[:, q2 * 11:(q2 + 1) * 11, :],
                                                           in_=w_ffn_down[q2 * 1408:(q2 + 1) * 1408, :].rearrange("(k p) c -> p k c", p=128)),
                      writes=["wdn"], dma_key="wdn")
                A("sp", lambda e: e.dma_start(out=gpo[:], in_=norm_ffn_post.partition_broadcast(128)), writes=["gpo"], dma_key="gpo")
                wfi_all = [("wfi", c4) for c4 in range(11)]
                cntt = 0
                for seq in range(NSEQ):
                    for blk in range(S // TBLK):
                        for tt in range(NTB):
                            t = blk * NTB + tt
                            b = cntt % 2
                            cntt += 1
                            rows = slice(t * 128, (t + 1) * 128)
                            A("sp", lambda e: e.dma_start(out=h2b[b][:], in_=scr_h2[seq, rows, :]), reads=[("scr_h2", seq, t)], writes=[f"h2b{b}"], dma_key=f"h2b{b}")
                            for k in range(8):
                                A("pe", lambda e, k=k: e.transpose(pT3[:, k * 128:(k + 1) * 128], h2b[b][:, k * 128:(k + 1) * 128], ident[:]), reads=[f"h2b{b}", "consts"], writes=["pT3"])
                            A("act", lambda e: e.copy(h2T[:, :, tt * 128:(tt + 1) * 128], pT3[:].rearrange("p (k t) -> p k t", k=8)), reads=["pT3"], writes=[("h2T", tt)])
                        h2T_all = [("h2T", tt) for tt in range(NTB)]
                        for f in range(22):
                            pb = f % 2
                            for k in range(8):
                                A("pe", lambda e, k=k: e.matmul(pG[pb][:], wfi[:, k, f * 128:(f + 1) * 128], h2T[:, k, :], start=(k == 0), stop=(k == 7)),
                                  reads=h2T_all + wfi_all, writes=[f"pG{pb}"])
                            for k in range(8):
                                A("pe", lambda e, k=k: e.matmul(pU[pb][:], wfi[:, k, FFN + f * 128:FFN + (f + 1) * 128], h2T[:, k, :], start=(k == 0), stop=(k == 7)),
                                  reads=h2T_all + wfi_all, writes=[f"pU{pb}"])
                            A("act", lambda e: e.activation(gT[pb][:], pG[pb][:], AF.Silu), reads=[f"pG{pb}"], writes=[f"gT{pb}"])
                            A("dve", lambda e: e.tensor_tensor(actT[:, f, :], pU[pb][:], gT[pb][:], ALU.mult),
                              reads=[f"pU{pb}", f"gT{pb}"], writes=[("actT", f)])
                        act_all = [("actT", f) for f in range(22)]
                        for tt in range(NTB):
                            t = blk * NTB + tt
                            b = (cntt + tt) % 2
                            rows = slice(t * 128, (t + 1) * 128)
                            A("sp", lambda e: e.dma_start(out=x1b[b][:], in_=y_out[seq, rows, :]), reads=[("y", seq, t)], writes=[f"x1b{b}"], dma_key=f"x1b{b}")
                            for f in range(22):
                                for hf in range(2):
                                    A("pe", lambda e, f=f, hf=hf: e.matmul(pM[:, hf * 512:(hf + 1) * 512], actT[:, f, tt * 128:(tt + 1) * 128], wdn[:, f, hf * 512:(hf + 1) * 512],
                                                                      start=(f == 0), stop=(f == 21)),
                                      reads=act_all + ["wdn"], writes=["pM"])
                            A("act", lambda e: e.activation(jk[:], pM[:], AF.Square, accum_out=s8[:, 0:1]), reads=["pM"], writes=["jk", "ss"])
                            A("act", lambda e: e.activation(s8[:, 1:2], s8[:, 0:1], AF.Sqrt, bias=EPS, scale=1.0 / D), reads=["ss"], writes=["sd"])
                            A("dve", lambda e: e.reciprocal(s8[:, 1:2], s8[:, 1:2]), reads=["sd"], writes=["sd"])
                            A("dve", lambda e: e.scalar_tensor_tensor(tmpf[:], pM[:], s8[:, 1:2], gpo[:], ALU.mult, ALU.mult), reads=["pM", "sd", "gpo"], writes=["tmpf"])
                            A("pool", lambda e: e.tensor_tensor(yo[b][:], tmpf[:], x1b[b][:], ALU.add), reads=["tmpf", f"x1b{b}"], writes=[f"yo{b}"])
                            A("act", lambda e: e.dma_start(out=y_out[seq, rows, :], in_=yo[b][:]), reads=[f"yo{b}"], writes=[("y", seq, t)], dma_key=f"yo{b}")
            S_.barrier()
        S_.emit(final_waits=list(S_.dma_count.keys()))
    nc._marks = getattr(S_, 'marks', [])
    return nc


def _host_consts(inp_conv_w, inp_conv_b, inp_norm_w, S):
    p = np.arange(128)
    m = p % 64
    rot = (m < 16)
    h2 = (m >= 8) & rot
    fi = np.where(rot, m % 8, 0)
    invf = (500000.0 ** (-(2.0 * fi) / 16.0)).astype(np.float32)
    ang = np.arange(S, dtype=np.float32)[None, :] * invf[:, None]
    cos = np.where(rot[:, None], np.cos(ang), 1.0).astype(np.float32)
    sgn = np.where(rot, np.where(h2, 1.0, -1.0), 0.0).astype(np.float32)
    sin = (np.sin(ang) * sgn[:, None]).astype(np.float32)
    cw = np.ascontiguousarray(inp_conv_w.reshape(5, 32, 128).transpose(2, 1, 0).reshape(128, 160), dtype=np.float32)
    cb = np.ascontiguousarray(inp_conv_b.reshape(32, 128).T, dtype=np.float32)
    nw = np.ascontiguousarray(inp_norm_w.reshape(16, 128).T, dtype=np.float32)
    return {"rot_cos": cos, "rot_sin": sin, "cw_l": cw, "cb_l": cb, "nw_l": nw}


_NC_CACHE = {}


def kernel(**inputs):
    x = np.asarray(inputs["x"], dtype=np.float32)
    B, S, _ = x.shape
    n_cores = 8
    NSEQ = B // n_cores
    key = (S, NSEQ)
    if key not in _NC_CACHE:
        _NC_CACHE[key] = build(S, NSEQ)
    nc = _NC_CACHE[key]
    f = lambda k: np.ascontiguousarray(np.asarray(inputs[k], dtype=np.float32)[0])
    shared = {
        "norm_mix_pre": f("norm_mix_pre").reshape(1, D), "w_in": f("w_in"),
        "ssd_conv_w": f("ssd_conv_w"), "ssd_conv_b": f("ssd_conv_b").reshape(1, 4096),
        "ssd_dt_bias": f("ssd_dt_bias").reshape(1, 64), "ssd_A_log": f("ssd_A_log").reshape(1, 64),
        "ssd_D": f("ssd_D").reshape(1, 32), "ssd_norm_w": f("ssd_norm_w").reshape(1, DIN),
        "w_ssd_branch": f("w_ssd_branch"), "w_attn_branch": f("w_attn_branch"), "w_out": f("w_out"),
        "norm_mix_post": f("norm_mix_post").reshape(1, D), "norm_ffn_pre": f("norm_ffn_pre").reshape(1, D),
        "w_ffn_in": f("w_ffn_in"), "w_ffn_down": f("w_ffn_down"), "norm_ffn_post": f("norm_ffn_post").reshape(1, D),
    }
    shared.update(_host_consts(shared["ssd_conv_w"], shared["ssd_conv_b"], shared["ssd_norm_w"], S))
    in_maps = []
    big = ("w_in", "w_ssd_branch", "w_attn_branch", "w_out", "w_ffn_in", "w_ffn_down", "rot_cos", "rot_sin")
    for c in range(n_cores):
        m = dict(shared)
        for k in big:
            a = shared[k]
            m[k] = np.concatenate([a, np.full((1, a.shape[1]), float(c), np.float32)], axis=0)
        m["x"] = np.ascontiguousarray(x[c * NSEQ:(c + 1) * NSEQ])
        in_maps.append(m)
    res = run_bass_kernel_spmd(nc, in_maps, core_ids=list(range(n_cores)))
    return np.concatenate([np.asarray(r["y"], dtype=np.float32) for r in res.results], axis=0)
```

```python
import numpy as np
from contextlib import ExitStack
import concourse.bass as bass
import concourse.mybir as mybir
from concourse.bass_utils import run_bass_kernel_spmd

F32 = mybir.dt.float32
BF16 = mybir.dt.bfloat16
I32 = mybir.dt.int32
AF = mybir.ActivationFunctionType
ALU = mybir.AluOpType

EPOCH = 20000
D = 1024
DIN = 2048
FFN = 2816
INC = 12864
OFF_Z, OFF_X, OFF_B, OFF_C, OFF_DT, OFF_Q, OFF_K, OFF_V, OFF_G = 0, 2048, 4096, 5120, 6144, 6208, 7744, 9280, 10816
DIL = (1, 4, 16)
EPS = 1e-6
NEGV = -30000.0


class _Rec:
    def __getattr__(self, name):
        return lambda *a, **kw: (name, a, kw)


_REC = _Rec()


class Op:
    __slots__ = ("eng", "fn", "idx", "deps", "signal", "dma_key", "sig_n")

    def __init__(self, eng, fn, dma_key=None):
        self.eng = eng
        self.fn = fn(_REC)
        self.deps = []
        self.signal = False
        self.dma_key = dma_key
        self.sig_n = 0


class Sched:
    ENGS = ("pe", "act", "dve", "pool", "sp")

    def __init__(self, nc):
        self.nc = nc
        self.eng_ops = {e: [] for e in self.ENGS}
        self.last_writer = {}
        self.readers = {}
        self.dma_count = {}
        self.waited = {e: {} for e in self.ENGS}

    def add(self, eng, fn, reads=(), writes=(), dma_key=None):
        op = Op(eng, fn, dma_key)
        op.idx = len(self.eng_ops[eng])
        self.eng_ops[eng].append(op)
        deps = set()
        for r in reads:
            w = self.last_writer.get(r)
            if w is not None:
                deps.add(w)
        for w in writes:
            lw = self.last_writer.get(w)
            if lw is not None:
                deps.add(lw)
            for rd in self.readers.get(w, ()):
                deps.add(rd)
        self._attach(op, deps)
        if dma_key is not None:
            self.dma_count[dma_key] = self.dma_count.get(dma_key, 0) + 1
        for r in reads:
            self.readers.setdefault(r, []).append(op)
        for w in writes:
            self.last_writer[w] = op
            self.readers[w] = []
        return op

    def _attach(self, op, deps):
        eng = op.eng
        best = {}
        for d in deps:
            if d is op:
                continue
            if d.dma_key is not None:
                k = ("dma", d.dma_key)
                v = self.dma_count[d.dma_key]
            else:
                if d.eng == "pe" and eng == "pe" and op.dma_key is None:
                    continue
                k = ("eng", d.eng)
                v = d.idx
            if k not in best or best[k][0] < v:
                best[k] = (v, d)
        wd = self.waited[eng]
        for k, (v, d) in best.items():
            if k in wd and wd[k] >= v:
                continue
            wd[k] = v
            if k[0] == "eng":
                d.signal = True
            op.deps.append((k, v, d))

    def barrier(self):
        if not hasattr(self, "marks"):
            self.marks = []
        self.marks.append({e: sum(1 + len(o.deps) for o in self.eng_ops[e]) for e in self.ENGS})
        lasts = []
        for e in self.ENGS:
            ops = [o for o in self.eng_ops[e] if o.dma_key is None]
            if ops:
                lasts.append(ops[-1])
        dmas = {}
        for e in self.ENGS:
            for o in self.eng_ops[e]:
                if o.dma_key is not None:
                    dmas[o.dma_key] = o
        for e in self.ENGS:
            op = Op(e, lambda eng: eng.nop())
            op.idx = len(self.eng_ops[e])
            self.eng_ops[e].append(op)
            self._attach(op, set(lasts) | set(dmas.values()))
        self.last_writer = {}
        self.readers = {}

    def emit(self, final_waits=()):
        nc = self.nc
        nsig = {}
        for e in self.ENGS:
            n = 0
            for op in self.eng_ops[e]:
                if op.dma_key is None and op.signal:
                    n += 1
                    op.sig_n = n
            nsig[e] = n
        with ExitStack() as st:
            esems = {}
            for e in self.ENGS:
                for ep in range((nsig[e] + EPOCH - 1) // EPOCH):
                    esems[(e, ep)] = st.enter_context(nc.semaphore(f"s_{e}_{ep}"))
            dsems = {}
            for i, k in enumerate(self.dma_count):
                dsems[k] = st.enter_context(nc.semaphore(f"d{i}"))
            block = st.enter_context(nc.Block())

            def run(e, engh):
                for op in self.eng_ops[e]:
                    for (k, v, d) in op.deps:
                        if k[0] == "dma":
                            engh.wait_ge(dsems[k[1]], 16 * v)
                        else:
                            n = d.sig_n
                            engh.wait_ge(esems[(d.eng, (n - 1) // EPOCH)], (n - 1) % EPOCH + 1)
                    name, a_, kw_ = op.fn
                    ins = getattr(engh, name)(*a_, **kw_)
                    if op.dma_key is not None:
                        ins.then_inc(dsems[op.dma_key], 16)
                    elif op.signal:
                        n = op.sig_n
                        ins.then_inc(esems[(e, (n - 1) // EPOCH)], 1)
                if e == "sp":
                    for k in final_waits:
                        engh.wait_ge(dsems[k], 16 * self.dma_count[k])

            @block.tensor
            def _(eng):
                run("pe", eng)

            @block.scalar
            def _(eng):
                run("act", eng)

            @block.vector
            def _(eng):
                run("dve", eng)

            @block.gpsimd
            def _(eng):
                run("pool", eng)

            @block.sync
            def _(eng):
                run("sp", eng)


def build(S, NSEQ, debug=False, stop_after=99, cut=99):
    nc = bass.Bass("TRN2", target_bir_lowering=False)
    NT = S // 128
    TB = S // 512

    def din(name, shape, pad=False):
        if pad:
            return nc.dram_tensor(name, [shape[0] + 1] + list(shape[1:]), F32, kind="ExternalInput").ap()[0:shape[0]]
        return nc.dram_tensor(name, shape, F32, kind="ExternalInput").ap()

    x_in = din("x", [NSEQ, S, D])
    norm_mix_pre = din("norm_mix_pre", [1, D])
    w_in = din("w_in", [D, INC], pad=True)
    conv_w = din("ssd_conv_w", [5, 4096])
    conv_b = din("ssd_conv_b", [1, 4096])
    dt_bias = din("ssd_dt_bias", [1, 64])
    A_log = din("ssd_A_log", [1, 64])
    D_skip = din("ssd_D", [1, 32])
    ssd_norm_w = din("ssd_norm_w", [1, DIN])
    w_ssd = din("w_ssd_branch", [DIN, D], pad=True)
    w_attn = din("w_attn_branch", [512, D], pad=True)
    w_out = din("w_out", [D, D], pad=True)
    norm_mix_post = din("norm_mix_post", [1, D])
    norm_ffn_pre = din("norm_ffn_pre", [1, D])
    w_ffn_in = din("w_ffn_in", [D, 2 * FFN], pad=True)
    w_ffn_down = din("w_ffn_down", [FFN, D], pad=True)
    norm_ffn_post = din("norm_ffn_post", [1, D])
    rot_cos = din("rot_cos", [128, S], pad=True)
    rot_sin = din("rot_sin", [128, S], pad=True)
    cw_l = din("cw_l", [128, 160])
    cb_l = din("cb_l", [128, 32])
    nw_l = din("nw_l", [128, 16])
    y_out = nc.dram_tensor("y", [NSEQ, S, D], F32, kind="ExternalOutput").ap()

    skind = "ExternalOutput" if debug else "Internal"

    def scr(name, shape, dt=BF16):
        return nc.dram_tensor(name, shape, dt, kind=skind).ap()

    scr_z = scr("scr_z", [S, DIN])
    scr_gate = scr("scr_gate", [S, 2 * D])
    scr_x = scr("scr_x", [S, DIN])
    scr_B = scr("scr_B", [S, 1024])
    scr_BT = scr("scr_BT", [1024, S])
    scr_CT = scr("scr_CT", [1024, S])
    scr_dt = scr("scr_dt", [S, 64], F32)
    scr_yf = scr("scr_yf", [S, DIN])
    scr_qT = scr("scr_qT", [1536, S])
    scr_kT = scr("scr_kT", [1536, S])
    scr_v = scr("scr_v", [3, S, 512])
    scr_o = scr("scr_o", [3, S, 8, 65])
    scr_m1 = scr("scr_m1", [S, D])
    scr_h2 = scr("scr_h2", [NSEQ, S, D])
    wfi_bf = nc.dram_tensor("wfi_bf", [D, 2 * FFN], BF16).ap()
    wfd_bf = nc.dram_tensor("wfd_bf", [FFN, D], BF16).ap()

    S_ = Sched(nc)
    A = S_.add

    with ExitStack() as gst:
        def gsb(name, shape, dt):
            return gst.enter_context(nc.sbuf_tensor(name, shape, dt))

        ident = gsb("ident", [128, 128], BF16)
        cst_f = gsb("cst_f", [128, 4, 128], F32)
        cst_b = gsb("cst_b", [128, 8, 128], BF16)
        band = gsb("band", [128, 3, 128], BF16)
        tmpc = gsb("tmpc", [128, 128], F32)
        smallc = gsb("smallc", [128, 64 + 64 + 32], F32)

        def mk_const(dst_f32_ap, fill_base, selects):
            A("pool", lambda e: e.memset(dst_f32_ap, fill_base), writes=["tmpc"])
            for (pat, cm, base, fill) in selects:
                A("pool", lambda e, pat=pat, cm=cm, base=base, fill=fill: e.affine_select(
                    dst_f32_ap, dst_f32_ap, [[pat, 128]], ALU.is_ge, fill, base=base, channel_multiplier=cm),
                  reads=["tmpc"], writes=["tmpc"])

        def to_bf(dst_ap, src_ap, scale=None):
            if scale is None:
                A("dve", lambda e: e.tensor_copy(dst_ap, src_ap), reads=["tmpc"], writes=["consts"])
            else:
                A("dve", lambda e: e.tensor_scalar(dst_ap, src_ap, scale, None, ALU.mult), reads=["tmpc"], writes=["consts"])

        mk_const(tmpc[:], 1.0, [(1, -1, 0, 0.0), (-1, 1, 0, 0.0)])
        to_bf(ident[:], tmpc[:])
        A("dve", lambda e: e.tensor_copy(cst_f[:, 3, :], tmpc[:]), reads=["tmpc"], writes=["consts"])
        mk_const(tmpc[:], 1.0, [(1, -1, 0, 0.0)])
        to_bf(cst_b[:, 0, :], tmpc[:])
        to_bf(cst_b[:, 1, :], tmpc[:], -1.0)
        to_bf(cst_b[:, 6, :], tmpc[:])
        A("dve", lambda e: e.tensor_copy(cst_f[:, 0, :], tmpc[:]), reads=["tmpc"], writes=["consts"])
        mk_const(tmpc[:], 1.0, [(-1, 1, 0, 0.0)])
        to_bf(cst_b[:, 2, :], tmpc[:])
        to_bf(cst_b[:, 3, :], tmpc[:], -1.0)
        to_bf(cst_b[:, 7, :], tmpc[:])
        A("dve", lambda e: e.tensor_copy(cst_f[:, 1, :], tmpc[:]), reads=["tmpc"], writes=["consts"])
        mk_const(tmpc[:], 0.0, [(1, -1, 0, NEGV)])
        to_bf(cst_b[:, 4, :], tmpc[:])
        mk_const(tmpc[:], 0.0, [(-1, 1, 0, NEGV)])
        to_bf(cst_b[:, 5, :], tmpc[:])
        A("dve", lambda e: e.memset(cst_f[:, 2, :], 1.0), writes=["consts"])
        for oi, o in enumerate((-1, 0, 1)):
            mk_const(tmpc[:], 0.0, [(-1, 1, 128 * o + 64, NEGV), (1, -1, 64 - 128 * o, NEGV)])
            to_bf(band[:, oi, :], tmpc[:])
        A("sp", lambda e: e.dma_start(out=smallc[:, 0:64], in_=dt_bias.partition_broadcast(128)), writes=["smallc"], dma_key="c0")
        A("sp", lambda e: e.dma_start(out=smallc[:, 64:128], in_=A_log.partition_broadcast(128)), writes=["smallc"], dma_key="c0")
        A("sp", lambda e: e.dma_start(out=smallc[:, 128:160], in_=D_skip.partition_broadcast(128)), writes=["smallc"], dma_key="c0")
        A("act", lambda e: e.activation(smallc[:, 64:128], smallc[:, 64:128], AF.Exp), reads=["smallc"], writes=["smallc"])
        A("dve", lambda e: e.tensor_scalar(smallc[:, 64:128], smallc[:, 64:128], -1.0, None, ALU.mult), reads=["smallc"], writes=["smallc"])
        for seq in range(NSEQ):
            with ExitStack() as st:
                def sb(name, shape, dt):
                    return st.enter_context(nc.sbuf_tensor(f"{name}_{seq}", shape, dt))

                def ps(name, shape, dt):
                    return st.enter_context(nc.psum_tensor(f"{name}_{seq}", shape, dt))

                hT = sb("hT", [128, 8, S], BF16)
                gpre = sb("gpre", [128, D], F32)
                xt = [sb(f"xt{i}", [128, D], F32) for i in range(2)]
                hb = [sb(f"hb{i}", [128, D], BF16) for i in range(2)]
                junk = sb("junk", [128, D], F32)
                st8 = sb("st8", [128, 8], F32)
                pT = [ps(f"pT{i}", [128, 1024], BF16) for i in range(2)]
                pA = [ps(f"pA{i}", [128, 512], F32) for i in range(2)]
                pB = [ps(f"pB{i}", [128, 512], F32) for i in range(2)]

                A("sp", lambda e: e.dma_start(out=gpre[:], in_=norm_mix_pre.partition_broadcast(128)), writes=["gpre"], dma_key="gpre")
                for t in range(NT):
                    b = t % 2
                    A("sp", lambda e, t=t, b=b: e.dma_start(out=xt[b][:], in_=x_in[seq, t * 128:(t + 1) * 128, :]),
                      writes=[f"xt{b}"], dma_key=f"xt{b}")
                    A("act", lambda e, b=b: e.activation(junk[:], xt[b][:], AF.Square, accum_out=st8[:, b:b + 1]),
                      reads=[f"xt{b}"], writes=["junk", f"ss{b}"])
                    A("act", lambda e, b=b: e.activation(st8[:, 2 + b:3 + b], st8[:, b:b + 1], AF.Sqrt, bias=EPS, scale=1.0 / D),
                      reads=[f"ss{b}"], writes=[f"sd{b}"])
                    A("dve", lambda e, b=b: e.reciprocal(st8[:, 4 + b:5 + b], st8[:, 2 + b:3 + b]), reads=[f"sd{b}"], writes=[f"rs{b}"])
                    A("dve", lambda e, b=b: e.scalar_tensor_tensor(hb[b][:], xt[b][:], st8[:, 4 + b:5 + b], gpre[:], ALU.mult, ALU.mult),
                      reads=[f"xt{b}", f"rs{b}", "gpre"], writes=[f"hb{b}"])
                    for k in range(8):
                        A("pe", lambda e, b=b, k=k: e.transpose(pT[b][:, k * 128:(k + 1) * 128], hb[b][:, k * 128:(k + 1) * 128], ident[:]),
                          reads=[f"hb{b}", "consts"], writes=[f"pT{b}"])
                    A("act", lambda e, b=b, t=t: e.copy(hT[:, :, t * 128:(t + 1) * 128], pT[b][:].rearrange("p (k t) -> p k t", k=8)),
                      reads=[f"pT{b}"], writes=[("hT", t)])
                hT_all = [("hT", t) for t in range(NT)]
                if cut <= 1:
                    S_.barrier(); continue

                wsl = [sb(f"wsl{i}", [128, 8, 512], BF16) for i in range(2)]
                wp = sb("wp", [128, 8, 512], BF16)
                stg = [sb(f"stg{i}", [128, 512], BF16) for i in range(3)]
                stgf = [sb(f"stgf{i}", [128, 64], F32) for i in range(2)]
                outX = [sb(f"outX{i}", [128, S], BF16) for i in range(4)]
                rawT = [sb(f"rawT{i}", [128, S + 4], BF16) for i in range(2)]
                tmp1 = [sb(f"tmp1{i}", [128, 512], F32) for i in range(2)]
                tmp2 = [sb(f"tmp2{i}", [128, 512], F32) for i in range(2)]
                cosT = sb("cosT", [128, S], BF16)
                sinT = sb("sinT", [128, S], BF16)
                cw = sb("cw", [128, 32, 5], F32)
                cb = sb("cb", [128, 32], F32)
                dg = sb("dg", [128, 5, 128], BF16)
                wcnt = [0]
                scnt = [0]

                def load_w(lo, ncols):
                    i = wcnt[0] % 2
                    wcnt[0] += 1
                    A("pool", lambda e: e.dma_start(out=wsl[i][:, :, 0:ncols],
                                                    in_=w_in[:, lo:lo + ncols].rearrange("(k p) c -> p k c", p=128)),
                      writes=[f"wsl{i}"], dma_key=f"wsl{i}")
                    return i

                def stage():
                    i = scnt[0] % 3
                    scnt[0] += 1
                    return i

                A("pool", lambda e: e.memset(wp[:], 0.0), writes=["wp"])
                for i in range(2):
                    A("pool", lambda e, i=i: e.memset(rawT[i][:, 0:2], 0.0), writes=[f"rawT{i}"])
                    A("pool", lambda e, i=i: e.memset(rawT[i][:, S + 2:S + 4], 0.0), writes=[f"rawT{i}"])
                A("sp", lambda e: e.dma_start(out=cw[:].rearrange("p f k -> p (f k)"), in_=cw_l), writes=["cw"], dma_key="cw")
                A("sp", lambda e: e.dma_start(out=cb[:], in_=cb_l), writes=["cb"], dma_key="cw")
                A("pool", lambda e: e.dma_start(out=cosT[:], in_=rot_cos), writes=["cosT"], dma_key="rot")
                A("pool", lambda e: e.dma_start(out=sinT[:], in_=rot_sin), writes=["sinT"], dma_key="rot")
                if cut <= 2:
                    S_.barrier(); continue
                def tok_block(lo, ncols, func, dst, dst_lo, dkey):
                    wi = load_w(lo, ncols)
                    for t in range(NT):
                        b = t % 2
                        for k in range(8):
                            A("pe", lambda e, t=t, k=k, b=b: e.matmul(pA[b][:, 0:ncols], hT[:, k, t * 128:(t + 1) * 128], wsl[wi][:, k, 0:ncols],
                                                                      start=(k == 0), stop=(k == 7)),
                              reads=[("hT", t), f"wsl{wi}"], writes=[f"pA{b}"])
                        si = stage()
                        A("act", lambda e, b=b, si=si: e.activation(stg[si][:, 0:ncols], pA[b][:, 0:ncols], func),
                          reads=[f"pA{b}"], writes=[f"stg{si}"])
                        A("sp", lambda e, t=t, si=si: e.dma_start(out=dst[t * 128:(t + 1) * 128, dst_lo:dst_lo + ncols], in_=stg[si][:, 0:ncols]),
                          reads=[f"stg{si}"], writes=[(dkey, t)], dma_key=f"stg{si}")

                for blk in range(4):
                    tok_block(OFF_Z + blk * 512, 512, AF.Silu, scr_z, blk * 512, "scr_z")
                for blk in range(4):
                    tok_block(OFF_G + blk * 512, 512, AF.Sigmoid, scr_gate, blk * 512, "scr_gate")
                if cut <= 3:
                    S_.barrier(); continue
                wi = load_w(OFF_DT, 64)
                for t in range(NT):
                    b = t % 2
                    for k in range(8):
                        A("pe", lambda e, t=t, k=k, b=b: e.matmul(pA[b][:, 0:64], hT[:, k, t * 128:(t + 1) * 128], wsl[wi][:, k, 0:64],
                                                                  start=(k == 0), stop=(k == 7)),
                          reads=[("hT", t), f"wsl{wi}"], writes=[f"pA{b}"])
                    A("act", lambda e, b=b: e.copy(stgf[b][:], pA[b][:, 0:64]), reads=[f"pA{b}"], writes=[f"stgf{b}"])
                    A("sp", lambda e, t=t, b=b: e.dma_start(out=scr_dt[t * 128:(t + 1) * 128, :], in_=stgf[b][:]),
                      reads=[f"stgf{b}"], writes=[("scr_dt", t)], dma_key=f"stgf{b}")
                if cut <= 4:
                    S_.barrier(); continue
                for g in range(3):
                    d = DIL[g]
                    n = S // d
                    wi = load_w(OFF_V + g * 512, 512)
                    cnt = 0
                    for r in range(d):
                        for i in range(n // 128):
                            b = cnt % 2
                            cnt += 1
                            base = i * 128 * d + r
                            tl = sorted(set((base + j * d) // 128 for j in (0, 127)))
                            tl = list(range(tl[0], tl[-1] + 1))
                            for k in range(8):
                                A("pe", lambda e, k=k, b=b, base=base, d=d: e.matmul(
                                    pA[b][:], hT[:, k, base:base + 127 * d + 1:d], wsl[wi][:, k, :], start=(k == 0), stop=(k == 7)),
                                  reads=[("hT", tt) for tt in tl] + [f"wsl{wi}"], writes=[f"pA{b}"])
                            si = stage()
                            A("act", lambda e, b=b, si=si: e.copy(stg[si][:], pA[b][:]), reads=[f"pA{b}"], writes=[f"stg{si}"])
                            row = r * n + i * 128
                            A("sp", lambda e, si=si, row=row, g=g: e.dma_start(out=scr_v[g, row:row + 128, :], in_=stg[si][:]),
                              reads=[f"stg{si}"], writes=[("scr_v", g)], dma_key=f"stg{si}")
                if cut <= 5:
                    S_.barrier(); continue
                for qk in range(2):
                    off = OFF_Q if qk == 0 else OFF_K
                    dstT = scr_qT if qk == 0 else scr_kT
                    for blk in range(3):
                        g = blk
                        d = DIL[g]
                        n = S // d
                        wi = load_w(off + blk * 512, 512)
                        if qk == 0:
                            A("dve", lambda e, wi=wi: e.tensor_scalar(wsl[wi][:], wsl[wi][:], 0.125, None, ALU.mult),
                              reads=[f"wsl{wi}"], writes=[f"wsl{wi}"])
                        wv = wsl[wi][:].rearrange("p k (h c) -> p k h c", h=8)
                        wpv = wp[:].rearrange("p k (h c) -> p k h c", h=8)
                        A("dve", lambda e, wv=wv, wpv=wpv: e.tensor_copy(wpv[:, :, :, 0:8], wv[:, :, :, 8:16]), reads=[f"wsl{wi}"], writes=["wp"])
                        A("dve", lambda e, wv=wv, wpv=wpv: e.tensor_copy(wpv[:, :, :, 8:16], wv[:, :, :, 0:8]), reads=[f"wsl{wi}"], writes=["wp"])
                        for hp in range(4):
                            ox = outX[hp]
                            for tb in range(TB):
                                b = tb % 2
                                for k in range(8):
                                    A("pe", lambda e, k=k, b=b, tb=tb, hp=hp, wi=wi: e.matmul(
                                        pA[b][:], wsl[wi][:, k, hp * 128:(hp + 1) * 128], hT[:, k, tb * 512:(tb + 1) * 512],
                                        start=(k == 0), stop=(k == 7)),
                                      reads=hT_all[tb * 4:(tb + 1) * 4] + [f"wsl{wi}"], writes=[f"pA{b}"])
                                for k in range(8):
                                    A("pe", lambda e, k=k, b=b, tb=tb, hp=hp: e.matmul(
                                        pB[b][:], wp[:, k, hp * 128:(hp + 1) * 128], hT[:, k, tb * 512:(tb + 1) * 512],
                                        start=(k == 0), stop=(k == 7)),
                                      reads=hT_all[tb * 4:(tb + 1) * 4] + ["wp"], writes=[f"pB{b}"])
                                sl = slice(tb * 512, (tb + 1) * 512)
                                A("dve", lambda e, b=b, sl=sl: e.tensor_tensor(tmp1[b][:], pB[b][:], sinT[:, sl], ALU.mult),
                                  reads=[f"pB{b}", "sinT"], writes=[f"tmp1{b}"])
                                A("dve", lambda e, b=b, sl=sl: e.tensor_tensor(tmp2[b][:], pA[b][:], cosT[:, sl], ALU.mult),
                                  reads=[f"pA{b}", "cosT"], writes=[f"tmp2{b}"])
                                i0 = tb * 512 // d
                                ni = 512 // d
                                oap = ox[:].rearrange("p (r i) -> p r i", r=d)[:, :, i0:i0 + ni]
                                A("pool", lambda e, b=b, oap=oap, d=d: e.tensor_tensor(
                                    oap, tmp1[b][:].rearrange("p (i r) -> p r i", r=d), tmp2[b][:].rearrange("p (i r) -> p r i", r=d), ALU.add),
                                  reads=[f"tmp1{b}", f"tmp2{b}"], writes=[f"outX{hp}"])
                            row = blk * 512 + hp * 128
                            A("sp", lambda e, hp=hp, row=row, dstT=dstT: e.dma_start(out=dstT[row:row + 128, :], in_=outX[hp][:]),
                              reads=[f"outX{hp}"], writes=[("scr_qk", qk, blk, hp)], dma_key=f"outX{hp}")
                if cut <= 6:
                    S_.barrier(); continue
                for grp in range(8):
                    wi = load_w(OFF_X + grp * 512, 512)
                    for j in range(4):
                        ft = grp * 4 + j
                        rb = ft % 2
                        for k5 in range(5):
                            A("dve", lambda e, k5=k5, ft=ft: e.tensor_scalar(dg[:, k5, :], ident[:], cw[:, ft, k5:k5 + 1], None, ALU.mult),
                              reads=["consts", "cw"], writes=["dg"])
                        for tb in range(TB):
                            b = tb % 2
                            for k in range(8):
                                A("pe", lambda e, k=k, b=b, tb=tb, j=j, wi=wi: e.matmul(
                                    pA[b][:], wsl[wi][:, k, j * 128:(j + 1) * 128], hT[:, k, tb * 512:(tb + 1) * 512],
                                    start=(k == 0), stop=(k == 7)),
                                  reads=hT_all[tb * 4:(tb + 1) * 4] + [f"wsl{wi}"], writes=[f"pA{b}"])
                            A("dve", lambda e, b=b, tb=tb, rb=rb: e.tensor_copy(rawT[rb][:, 2 + tb * 512:2 + (tb + 1) * 512], pA[b][:]),
                              reads=[f"pA{b}"], writes=[f"rawT{rb}"])
                        for tb in range(TB):
                            b = tb % 2
                            for k5 in range(5):
                                A("pe", lambda e, k5=k5, b=b, tb=tb, rb=rb: e.matmul(
                                    pB[b][:], dg[:, k5, :], rawT[rb][:, tb * 512 + k5:tb * 512 + k5 + 512], start=(k5 == 0), stop=(k5 == 4)),
                                  reads=["dg", f"rawT{rb}"], writes=[f"pB{b}"])
                            A("act", lambda e, b=b, tb=tb, j=j, ft=ft: e.activation(
                                outX[j][:, tb * 512:(tb + 1) * 512], pB[b][:], AF.Silu, bias=cb[:, ft:ft + 1]),
                              reads=[f"pB{b}", "cb"], writes=[f"outX{j}"])
                    if grp < 6:
                        dst, clo, dkey = (scr_x, grp * 512, "scr_x") if grp < 4 else (scr_B, (grp - 4) * 512, "scr_B")
                        for t in range(NT):
                            b = t % 2
                            for j in range(4):
                                A("pe", lambda e, b=b, j=j, t=t: e.transpose(pT[b][:, j * 128:(j + 1) * 128], outX[j][:, t * 128:(t + 1) * 128], ident[:]),
                                  reads=[f"outX{j}", "consts"], writes=[f"pT{b}"])
                            si = stage()
                            A("dve", lambda e, b=b, si=si: e.tensor_copy(stg[si][:], pT[b][:, 0:512]), reads=[f"pT{b}"], writes=[f"stg{si}"])
                            A("sp", lambda e, t=t, si=si, dst=dst, clo=clo: e.dma_start(out=dst[t * 128:(t + 1) * 128, clo:clo + 512], in_=stg[si][:]),
                              reads=[f"stg{si}"], writes=[(dkey, t)], dma_key=f"stg{si}")
                    if grp >= 4:
                        dstT = scr_BT if grp < 6 else scr_CT
                        rlo = (grp - 4) * 512 if grp < 6 else (grp - 6) * 512
                        for j in range(4):
                            A("sp", lambda e, j=j, dstT=dstT, rlo=rlo: e.dma_start(out=dstT[rlo + j * 128:rlo + (j + 1) * 128, :], in_=outX[j][:]),
                              reads=[f"outX{j}"], writes=[("scr_BCT", grp, j)], dma_key=f"outX{j}")
            S_.barrier()
            if stop_after <= 2:
                continue
            with ExitStack() as st:
                def sb(name, shape, dt):
                    return st.enter_context(nc.sbuf_tensor(f"{name}_s{seq}", shape, dt))

                def ps(name, shape, dt):
                    return st.enter_context(nc.psum_tensor(f"{name}_s{seq}", shape, dt))

                wssd = sb("wssd", [128, 16, 1024], BF16)
                nwl = sb("nwl", [128, 16], F32)
                xc = [sb(f"xc{i}", [128, 2048], BF16) for i in range(3)]
                Bc = [sb(f"Bc{i}", [128, 1024], BF16) for i in range(3)]
                BTc = [sb(f"BTc{i}", [128, 8, 128], BF16) for i in range(3)]
                CTc = [sb(f"CTc{i}", [128, 8, 128], BF16) for i in range(3)]
                dtc = [sb(f"dtc{i}", [128, 64], F32) for i in range(3)]
                zc = [sb(f"zc{i}", [128, 2048], BF16) for i in range(3)]
                yfc = [sb(f"yfc{i}", [128, 2048], BF16) for i in range(3)]
                gc = [sb(f"gc{i}", [128, 1024], BF16) for i in range(3)]
                ST = sb("ST", [128, 2048], F32)
                STb = sb("STb", [128, 2048], BF16)
                sm_ = [sb(f"sm{i}", [128, 128], F32) for i in range(2)]
                sm2_ = [sb(f"sm2{i}", [128, 128], F32) for i in range(2)]
                dAhi_ = [sb(f"dAhi{i}", [128, 32], BF16) for i in range(2)]
                dAlo_ = [sb(f"dAlo{i}", [128, 32], BF16) for i in range(2)]
                uhi = sb("uhi", [128, 32], BF16)
                ulo = sb("ulo", [128, 32], BF16)
                sm3 = sb("sm3", [128, 64], F32)
                CBm = [sb(f"CBm{i}", [128, 128], BF16) for i in range(2)]
                Eb = [sb(f"Eb{i}", [128, 512], BF16) for i in range(2)]
                Mb = [sb(f"Mb{i}", [128, 4, 128], BF16) for i in range(2)]
                xw = [sb(f"xw{i}", [128, 256], BF16) for i in range(2)]
                xdt = [sb(f"xdt{i}", [128, 256], BF16) for i in range(2)]
                tq = [sb(f"tq{i}", [128, 256], F32) for i in range(2)]
                tq2 = [sb(f"tq2{i}", [128, 256], F32) for i in range(2)]
                Ych_ = [sb(f"Ych{i}", [128, 2048], F32) for i in range(2)]
                Yst = [sb(f"Yst{i}", [128, 2048], BF16) for i in range(2)]
                Gb = sb("Gb", [128, 2048], BF16)
                GT = sb("GT", [128, 16, 128], BF16)
                junk2 = sb("junk2", [128, 2048], BF16)
                r8 = sb("r8", [128, 4], F32)
                m1s = [sb(f"m1s{i}", [128, 1024], BF16) for i in range(2)]
                pSeg = [ps(f"pSeg{i}", [128, 512], F32) for i in range(2)]
                pYY = [ps(f"pYY{i}", [128, 512], F32) for i in range(2)]
                pSx = [ps(f"pSx{i}", [128, 512], F32) for i in range(2)]
                pS = pSx[0]
                pSm = pSx[0][:, 256:320]
                pT2 = ps("pT2", [128, 1024], BF16)
                pO = ps("pO", [128, 512], F32)

                for k in range(2):
                    A("pool", lambda e, k=k: e.dma_start(out=wssd[:, k * 8:(k + 1) * 8, :],
                                                         in_=w_ssd[k * 1024:(k + 1) * 1024, :].rearrange("(k p) c -> p k c", p=128)),
                      writes=["wssd"], dma_key="wssd")
                A("sp", lambda e: e.dma_start(out=nwl[:], in_=nw_l), writes=["nwl"], dma_key="nwl")
                for k in range(16):
                    A("dve", lambda e, k=k: e.tensor_scalar(wssd[:, k, :], wssd[:, k, :], nwl[:, k:k + 1], None, ALU.mult),
                      reads=["wssd", "nwl"], writes=["wssd"])
                iters = [(dr_, ci_) for dr_ in range(2) for ci_ in range(NT)]
                NIT = len(iters)

                def chunk_of(k):
                    dr_, ci_ = iters[k]
                    return dr_, (ci_ if dr_ == 0 else NT - 1 - ci_)

                def issue_loads(k, only_yf=False):
                    dr_, c = chunk_of(k)
                    b = k % 3
                    rows = slice(c * 128, (c + 1) * 128)
                    if only_yf:
                        A("sp", lambda e: e.dma_start(out=yfc[b][:], in_=scr_yf[rows, :]), reads=[("scr_yf", c)], writes=[f"yfc{b}"], dma_key=f"yfc{b}")
                        return
                    A("sp", lambda e: e.dma_start(out=xc[b][:], in_=scr_x[rows, :]), reads=[("scr_x", c)], writes=[f"xc{b}"], dma_key=f"xc{b}")
                    A("sp", lambda e: e.dma_start(out=Bc[b][:], in_=scr_B[rows, :]), reads=[("scr_B", c)], writes=[f"Bc{b}"], dma_key=f"Bc{b}")
                    A("sp", lambda e: e.dma_start(out=BTc[b][:], in_=scr_BT[:, rows].rearrange("(g n) t -> n g t", n=128)),
                      reads=[("scr_BCT", gg, jj) for gg in (4, 5) for jj in range(4)], writes=[f"BTc{b}"], dma_key=f"BTc{b}")
                    A("sp", lambda e: e.dma_start(out=CTc[b][:], in_=scr_CT[:, rows].rearrange("(g n) t -> n g t", n=128)),
                      reads=[("scr_BCT", gg, jj) for gg in (6, 7) for jj in range(4)], writes=[f"CTc{b}"], dma_key=f"CTc{b}")
                    A("sp", lambda e: e.dma_start(out=dtc[b][:], in_=scr_dt[rows, :]), reads=[("scr_dt", c)], writes=[f"dtc{b}"], dma_key=f"dtc{b}")
                    if dr_ == 1:
                        A("sp", lambda e: e.dma_start(out=zc[b][:], in_=scr_z[rows, :]), reads=[("scr_z", c)], writes=[f"zc{b}"], dma_key=f"zc{b}")
                        if iters[k][1] > 0:
                            A("sp", lambda e: e.dma_start(out=yfc[b][:], in_=scr_yf[rows, :]), reads=[("scr_yf", c)], writes=[f"yfc{b}"], dma_key=f"yfc{b}")
                        A("sp", lambda e: e.dma_start(out=gc[b][:], in_=scr_gate[rows, 0:1024]), reads=[("scr_gate", c)], writes=[f"gc{b}"], dma_key=f"gc{b}")

                def prologue(k):
                    dr_, c = chunk_of(k)
                    b = k % 3
                    p = k % 2
                    o32 = dr_ * 32
                    sm, sm2, dAhi, dAlo = sm_[p], sm2_[p], dAhi_[p], dAlo_[p]
                    P = f"_{p}"
                    A("dve", lambda e: e.tensor_tensor(sm[:, 0:32], dtc[b][:, o32:o32 + 32], smallc[:, o32:o32 + 32], ALU.add),
                      reads=[f"dtc{b}", "smallc"], writes=["sm0" + P])
                    A("act", lambda e: e.activation(sm[:, 32:64], sm[:, 0:32], AF.Exp), reads=["sm0" + P], writes=["sm1" + P])
                    A("act", lambda e: e.activation(sm[:, 64:96], sm[:, 32:64], AF.Ln, bias=1.0), reads=["sm1" + P], writes=["dt" + P])
                    A("dve", lambda e: e.tensor_tensor(sm[:, 96:128], sm[:, 64:96], smallc[:, 64 + o32:96 + o32], ALU.mult),
                      reads=["dt" + P, "smallc"], writes=["dA" + P])
                    A("dve", lambda e: e.tensor_copy(dAhi[:], sm[:, 96:128]), reads=["dA" + P], writes=["dAhi" + P])
                    A("dve", lambda e: e.tensor_tensor(dAlo[:], sm[:, 96:128], dAhi[:], ALU.subtract), reads=["dA" + P, "dAhi" + P], writes=["dAlo" + P])
                    A("pe", lambda e: e.matmul(pSm[:, 0:32], cst_f[:, dr_, :], sm[:, 96:128], start=True, stop=True), reads=["dA" + P, "consts"], writes=["pSx0"])
                    A("pe", lambda e: e.matmul(pSm[:, 32:64], cst_f[:, 2, :], sm[:, 96:128], start=True, stop=True), reads=["dA" + P, "consts"], writes=["pSx0"])
                    A("act", lambda e: e.copy(sm2[:, 0:32], pSm[:, 0:32]), reads=["pSx0"], writes=["asb" + P])
                    A("act", lambda e: e.activation(sm2[:, 32:64], pSm[:, 0:32], AF.Exp), reads=["pSx0"], writes=["ea" + P])
                    A("act", lambda e: e.activation(sm2[:, 64:96], pSm[:, 32:64], AF.Exp), reads=["pSx0"], writes=["eal" + P])
                    A("dve", lambda e: e.tensor_tensor(sm2[:, 96:128], pSm[:, 32:64], sm2[:, 0:32], ALU.subtract), reads=["pSx0", "asb" + P], writes=["wv" + P])
                    A("act", lambda e: e.activation(sm2[:, 96:128], sm2[:, 96:128], AF.Exp), reads=["wv" + P], writes=["wv" + P])
                    A("dve", lambda e: e.tensor_tensor(sm2[:, 96:128], sm2[:, 96:128], sm[:, 64:96], ALU.mult), reads=["wv" + P, "dt" + P], writes=["wv" + P])

                def stage_a(k, g):
                    dr_, c = chunk_of(k)
                    b = k % 3
                    p = k % 2
                    P = f"_{p}"
                    sm, sm2, dAhi, dAlo = sm_[p], sm2_[p], dAhi_[p], dAlo_[p]
                    tri_b = cst_b[:, 0 + 2 * dr_, :]
                    ntri_b = cst_b[:, 1 + 2 * dr_, :]
                    neg_b = cst_b[:, 4 + dr_, :]
                    q = g % 2
                    A("pe", lambda e: e.matmul(pSx[q][:, 320:448], BTc[b][:, g, :], CTc[b][:, g, :], start=True, stop=True),
                      reads=[f"BTc{b}", f"CTc{b}"], writes=[f"pSx{q}"])
                    for j in range(4):
                        h = g * 4 + j
                        osl = pSeg[q][:, j * 128:(j + 1) * 128]
                        hi = dAhi[:, h:h + 1].to_broadcast([128, 128])
                        lo = dAlo[:, h:h + 1].to_broadcast([128, 128])
                        A("pe", lambda e: e.matmul(osl, hi, tri_b, start=True, stop=False), reads=["dAhi" + P, "consts"], writes=[f"pSeg{q}"])
                        A("pe", lambda e: e.matmul(osl, lo, tri_b, start=False, stop=False), reads=["dAlo" + P, "consts"], writes=[f"pSeg{q}"])
                        A("pe", lambda e: e.matmul(osl, ntri_b, hi, start=False, stop=False), reads=["dAhi" + P, "consts"], writes=[f"pSeg{q}"])
                        A("pe", lambda e: e.matmul(osl, ntri_b, lo, start=False, stop=False), reads=["dAlo" + P, "consts"], writes=[f"pSeg{q}"])
                        A("pe", lambda e: e.matmul(osl, ident[:], neg_b, start=False, stop=True), reads=["consts"], writes=[f"pSeg{q}"])
                    A("act", lambda e: e.activation(Eb[q][:], pSeg[q][:], AF.Exp), reads=[f"pSeg{q}"], writes=[f"Eb{q}"])
                    xg = xc[b][:, g * 256:(g + 1) * 256].rearrange("p (j q) -> p j q", j=4)
                    wb = sm2[:, 96 + g * 4:100 + g * 4].unsqueeze(2).to_broadcast([128, 4, 64])
                    A("pool", lambda e: e.tensor_tensor(xw[q][:].rearrange("p (j q) -> p j q", j=4), xg, wb, ALU.mult),
                      reads=[f"xc{b}", "wv" + P], writes=[f"xw{q}"])
                    dtb = sm[:, 64 + g * 4:68 + g * 4].unsqueeze(2).to_broadcast([128, 4, 64])
                    A("pool", lambda e: e.tensor_tensor(xdt[q][:].rearrange("p (j q) -> p j q", j=4), xg, dtb, ALU.mult),
                      reads=[f"xc{b}", "dt" + P], writes=[f"xdt{q}"])

                def stage_b(k, g):
                    dr_, c = chunk_of(k)
                    b = k % 3
                    p = k % 2
                    P = f"_{p}"
                    sm2 = sm2_[p]
                    Ych = Ych_[p]
                    q = g % 2
                    A("dve", lambda e: e.tensor_tensor(Mb[q][:], Eb[q][:].rearrange("p (j t) -> p j t", j=4),
                                                       pSx[q][:, 320:448].unsqueeze(1).to_broadcast([128, 4, 128]), ALU.mult),
                      reads=[f"Eb{q}", f"pSx{q}"], writes=[("Mb", q)])
                    for j in range(4):
                        A("pe", lambda e, j=j: e.matmul(pYY[q][:, j * 64:(j + 1) * 64], Mb[q][:, j, :], xdt[q][:, j * 64:(j + 1) * 64], start=True, stop=True),
                          reads=[("Mb", q), f"xdt{q}"], writes=[f"pY{q}"])
                    A("pe", lambda e: e.matmul(pYY[q][:, 256:512], CTc[b][:, g, :], STb[:, g * 256:(g + 1) * 256], start=True, stop=True),
                      reads=[f"CTc{b}", ("STb", g)], writes=[f"pYo{q}"])
                    A("pe", lambda e: e.matmul(pSx[q][:, 0:256], Bc[b][:, g * 128:(g + 1) * 128], xw[q][:], start=True, stop=True),
                      reads=[f"Bc{b}", f"xw{q}"], writes=[f"pSx{q}"])
                    eab = sm2[:, 32 + g * 4:36 + g * 4].unsqueeze(2).to_broadcast([128, 4, 64])
                    A("dve", lambda e: e.tensor_tensor(tq[q][:].rearrange("p (j q) -> p j q", j=4), pYY[q][:, 256:512].rearrange("p (j q) -> p j q", j=4), eab, ALU.mult),
                      reads=[f"pYo{q}", "ea" + P], writes=[f"tq{q}"])
                    A("dve", lambda e: e.tensor_tensor(Ych[:, g * 256:(g + 1) * 256], pYY[q][:, 0:256], tq[q][:], ALU.add),
                      reads=[f"pY{q}", f"tq{q}"], writes=[("Ych", p, g)])
                    elb = sm2[:, 64 + g * 4:68 + g * 4].unsqueeze(2).to_broadcast([128, 4, 64])
                    A("pool", lambda e: e.tensor_tensor(tq2[q][:].rearrange("p (j q) -> p j q", j=4),
                                                        ST[:, g * 256:(g + 1) * 256].rearrange("p (j q) -> p j q", j=4), elb, ALU.mult),
                      reads=[("ST", g), "eal" + P], writes=[f"tq2{q}"])
                    A("dve", lambda e: e.tensor_tensor(ST[:, g * 256:(g + 1) * 256], pSx[q][:, 0:256], tq2[q][:], ALU.add),
                      reads=[f"pSx{q}", f"tq2{q}"], writes=[("ST", g)])
                    A("act", lambda e: e.copy(STb[:, g * 256:(g + 1) * 256], ST[:, g * 256:(g + 1) * 256]), reads=[("ST", g)], writes=[("STb", g)])

                def epilogue_pieces(k):
                    dr_, c = chunk_of(k)
                    b = k % 3
                    p = k % 2
                    Ych = Ych_[p]
                    rows = slice(c * 128, (c + 1) * 128)
                    Yall = [("Ych", p, g) for g in range(8)]
                    pcs = []
                    if dr_ == 0:
                        def f0():
                            A("act", lambda e: e.copy(Yst[p][:], Ych[:]), reads=Yall, writes=[f"Yst{p}"])
                            A("act", lambda e: e.dma_start(out=scr_yf[rows, :], in_=Yst[p][:]), reads=[f"Yst{p}"], writes=[("scr_yf", c)], dma_key=f"Yst{p}")
                        return [f0]

                    def e0():
                        A("dve", lambda e: e.tensor_tensor(Ych[:], Ych[:], yfc[b][:], ALU.add), reads=Yall + [f"yfc{b}"], writes=Yall)
                        Db = smallc[:, 128:160].unsqueeze(2).to_broadcast([128, 32, 64])
                        A("pool", lambda e: e.tensor_tensor(Yst[0][:].rearrange("p (h q) -> p h q", h=32), xc[b][:].rearrange("p (h q) -> p h q", h=32), Db, ALU.mult),
                          reads=[f"xc{b}", "smallc"], writes=["Yst0"])

                    def e1():
                        A("dve", lambda e: e.tensor_tensor(Ych[:], Ych[:], Yst[0][:], ALU.add), reads=Yall + ["Yst0"], writes=Yall)
                        A("dve", lambda e: e.tensor_tensor(Gb[:], Ych[:], zc[b][:], ALU.mult), reads=Yall + [f"zc{b}"], writes=["Gb"])
                        A("act", lambda e: e.activation(junk2[:], Gb[:], AF.Square, accum_out=r8[:, 0:1]), reads=["Gb"], writes=["junk2", "r0"])
                        A("act", lambda e: e.activation(r8[:, 1:2], r8[:, 0:1], AF.Sqrt, bias=EPS, scale=1.0 / DIN), reads=["r0"], writes=["r1"])
                        A("dve", lambda e: e.reciprocal(r8[:, 2:3], r8[:, 1:2]), reads=["r1"], writes=["r2"])

                    def mk_tr(q4):
                        def f():
                            for kk in range(8):
                                kx = q4 * 8 + kk
                                A("pe", lambda e, kx=kx, kk=kk: e.transpose(pT2[:, kk * 128:(kk + 1) * 128], Gb[:, kx * 128:(kx + 1) * 128], ident[:]),
                                  reads=["Gb", "consts"], writes=["pT2"])
                            A("act", lambda e: e.copy(GT[:, q4 * 8:(q4 + 1) * 8, :], pT2[:].rearrange("p (k t) -> p k t", k=8)), reads=["pT2"], writes=["GT"])
                        return f

                    def mk_op(hf):
                        def f():
                            hs = slice(hf * 512, (hf + 1) * 512)
                            for kx in range(16):
                                A("pe", lambda e, kx=kx: e.matmul(pO[:], GT[:, kx, :], wssd[:, kx, hs], start=(kx == 0), stop=(kx == 15)),
                                  reads=["GT", "wssd"], writes=["pO"])
                            A("dve", lambda e: e.scalar_tensor_tensor(m1s[p][:, hs], pO[:], r8[:, 2:3], gc[b][:, hs], ALU.mult, ALU.mult),
                              reads=["pO", "r2", f"gc{b}"], writes=[f"m1s{p}"])
                            if hf == 1:
                                A("act", lambda e: e.dma_start(out=scr_m1[rows, :], in_=m1s[p][:]), reads=[f"m1s{p}"], writes=[("scr_m1", c)], dma_key=f"m1s{p}")
                        return f
                    return [e0, e1, mk_tr(0), mk_tr(1), mk_op(0), mk_op(1)]

                issue_loads(0)
                prologue(0)
                pending = []
                for k in range(NIT):
                    dr, ci = iters[k]
                    if ci == 0:
                        A("dve", lambda e: e.memset(ST[:], 0.0), writes=[("ST", g_) for g_ in range(8)])
                        A("dve", lambda e: e.memset(STb[:], 0.0), writes=[("STb", g_) for g_ in range(8)])
                    if dr == 1 and ci == 0:
                        for pc in pending:
                            pc()
                        pending = []
                        issue_loads(k, only_yf=True)
                    if k + 1 < NIT:
                        issue_loads(k + 1)
                        prologue(k + 1)
                    stage_a(k, 0)
                    for g in range(8):
                        if g + 1 < 8:
                            stage_a(k, g + 1)
                        stage_b(k, g)
                        if pending:
                            pending.pop(0)()
                    for pc in pending:
                        pc()
                    pending = epilogue_pieces(k)
                for pc in pending:
                    pc()
            S_.barrier()
            if stop_after <= 4:
                continue
            with ExitStack() as st:
                def sb(name, shape, dt):
                    return st.enter_context(nc.sbuf_tensor(f"{name}_a{seq}", shape, dt))

                def ps(name, shape, dt):
                    return st.enter_context(nc.psum_tensor(f"{name}_a{seq}", shape, dt))

                QTs = [sb("QT0", [128, 4, S], BF16), sb("QT1", [128, 4, S // 4], BF16)]
                KTs = [sb("KT0", [128, 4, S], BF16), sb("KT1", [128, 4, S // 4], BF16)]
                Vps = [sb("Vp0", [128, NT, 8, 65], BF16), sb("Vp1", [128, NT // 4, 8, 65], BF16)]
                NPS = 4
                PT = [sb(f"PT{i}", [128, 384], BF16) for i in range(NPS)]
                ubase = [0]
                qbase = [0]
                ost = [sb(f"ost{i}", [128, 8, 65], BF16) for i in range(2)]
                pSc = [ps(f"pSc{i}", [128, 512], F32) for i in range(NPS)]
                pOa = [ps(f"pOa{i}", [128, 1024], F32) for i in range(2)]
                for si_ in range(2):
                    A("dve", lambda e, si_=si_: e.memset(Vps[si_][:].rearrange("p t h c -> p (t h) c")[:, :, 64:65], 1.0), writes=[f"Vp{si_}"])
                combos = [(g_, r_) for g_ in range(3) for r_ in range(DIL[g_])]

                def att_loads(ci_):
                    g, r = combos[ci_]
                    n = S // DIL[g]
                    nq = n // 128
                    si = ci_ % 2
                    for hp in range(4):
                        row = g * 512 + hp * 128
                        A("sp", lambda e, hp=hp, row=row: e.dma_start(out=QTs[si][:, hp, 0:n], in_=scr_qT[row:row + 128, r * n:(r + 1) * n]),
                          reads=[("scr_qk", 0, g, hp)], writes=[f"QT{si}"], dma_key=f"QT{si}")
                        A("sp", lambda e, hp=hp, row=row: e.dma_start(out=KTs[si][:, hp, 0:n], in_=scr_kT[row:row + 128, r * n:(r + 1) * n]),
                          reads=[("scr_qk", 1, g, hp)], writes=[f"KT{si}"], dma_key=f"KT{si}")
                    for i in range(nq):
                        A("sp", lambda e, i=i: e.dma_start(out=Vps[si][:, i, :, 0:64],
                                                           in_=scr_v[g, r * n + i * 128:r * n + (i + 1) * 128, :].rearrange("k (h c) -> k h c", h=8)),
                          reads=[("scr_v", g)], writes=[f"Vp{si}"], dma_key=f"Vp{si}")

                att_loads(0)
                for ci_ in range(len(combos)):
                    g, r = combos[ci_]
                    d = DIL[g]
                    n = S // d
                    nq = n // 128
                    si = 0 if ci_ % 2 == 0 else 1
                    QT, KT, Vp = QTs[si], KTs[si], Vps[si]
                    if ci_ + 1 < len(combos):
                        att_loads(ci_ + 1)
                    if True:
                        units = [(i, h) for i in range(nq) for h in range(8)]

                        def emit_scores(u):
                            i, h = units[u]
                            hp, hh = h // 2, h % 2
                            ub = (ubase[0] + u) % NPS
                            offs = [o for o in (-1, 0, 1) if 0 <= i + o < nq]
                            for oi, o in enumerate(offs):
                                ks = slice((i + o) * 128, (i + o + 1) * 128)
                                A("pe", lambda e: e.matmul(pSc[ub][:, oi * 128:(oi + 1) * 128], KT[hh * 64:(hh + 1) * 64, hp, ks],
                                                           QT[hh * 64:(hh + 1) * 64, hp, i * 128:(i + 1) * 128], start=True, stop=False),
                                  reads=[f"QT{si}", f"KT{si}"], writes=[f"pSc{ub}"])
                                A("pe", lambda e: e.matmul(pSc[ub][:, oi * 128:(oi + 1) * 128], ident[:], band[:, o + 1, :], start=False, stop=True),
                                  reads=["consts"], writes=[f"pSc{ub}"])
                            no = len(offs)
                            A("act", lambda e: e.activation(PT[ub][:, 0:no * 128], pSc[ub][:, 0:no * 128], AF.Exp),
                              reads=[f"pSc{ub}"], writes=[f"PT{ub}"])

                        def emit_pv(u):
                            i, h = units[u]
                            ub = (ubase[0] + u) % NPS
                            qb = (qbase[0] + i) % 2
                            offs = [o for o in (-1, 0, 1) if 0 <= i + o < nq]
                            no = len(offs)
                            c0 = (h // 4) * 512 + (h % 4) * 65
                            for oi, o in enumerate(offs):
                                A("pe", lambda e: e.matmul(pOa[qb][:, c0:c0 + 65], PT[ub][:, oi * 128:(oi + 1) * 128], Vp[:, i + o, h, :],
                                                           start=(oi == 0), stop=(oi == no - 1)), reads=[f"PT{ub}", f"Vp{si}"], writes=[f"pOa{qb}"])
                            if h == 7:
                                for hf in range(2):
                                    A("act" if hf else "dve", lambda e: (e.copy if hf else e.tensor_copy)(
                                        ost[qb][:, hf * 4:(hf + 1) * 4, :], pOa[qb][:, hf * 512:hf * 512 + 260].rearrange("p (h c) -> p h c", h=4)),
                                      reads=[f"pOa{qb}"], writes=[f"ost{qb}"])
                                t0 = i * 128 * d + r
                                A("act", lambda e: e.dma_start(out=scr_o[g, t0:t0 + 127 * d + 1:d, :, :], in_=ost[qb][:]),
                                  reads=[f"ost{qb}"], writes=[("scr_o", g)], dma_key=f"ost{qb}")

                        LA = NPS - 1
                        for u in range(min(LA, len(units))):
                            emit_scores(u)
                        for u in range(len(units)):
                            if u + LA < len(units):
                                emit_scores(u + LA)
                            emit_pv(u)
                        ubase[0] += len(units)
                        qbase[0] += nq
            S_.barrier()
            if stop_after <= 5:
                continue
            with ExitStack() as st:
                def sb(name, shape, dt):
                    return st.enter_context(nc.sbuf_tensor(f"{name}_f{seq}", shape, dt))

                def ps(name, shape, dt):
                    return st.enter_context(nc.psum_tensor(f"{name}_f{seq}", shape, dt))

                watt = sb("watt", [128, 4, 1024], BF16)
                wout = sb("wout", [128, 8, 1024], BF16)
                gns = sb("gns", [128, 2, 1024], F32)
                o3 = [sb(f"o3{i}", [128, 3, 8, 65], BF16) for i in range(4)]
                osum_ = [sb(f"osum{i}", [128, 8, 65], F32) for i in range(4)]
                rl_ = [sb(f"rl{i}", [128, 8], F32) for i in range(4)]
                Ob_ = [sb(f"Ob{i}", [128, 512], BF16) for i in range(4)]
                OT_ = [sb(f"OT{i}", [128, 4, 128], BF16) for i in range(4)]
                gat = [sb(f"gat{i}", [128, 1024], BF16) for i in range(4)]
                m1c = [sb(f"m1c{i}", [128, 1024], BF16) for i in range(4)]
                mrg_ = [sb(f"mrg{i}", [128, 1024], BF16) for i in range(4)]
                mrgT_ = [sb(f"mrgT{i}", [128, 8, 128], BF16) for i in range(4)]
                tmpf_ = [sb(f"tmpf{i}", [128, 1024], F32) for i in range(4)]
                xin = [sb(f"xin{i}", [128, 1024], F32) for i in range(4)]
                x1 = [sb(f"x1{i}", [128, 1024], F32) for i in range(4)]
                h2 = [sb(f"h2{i}", [128, 1024], BF16) for i in range(4)]
                jk_ = [sb(f"jk{i}", [128, 1024], F32) for i in range(4)]
                s8_ = [sb(f"s8{i}", [128, 8], F32) for i in range(4)]
                pT3 = [ps(f"pT3{i}", [128, 1024], BF16) for i in range(2)]
                pM = [ps(f"pM{i}", [128, 1024], F32) for i in range(2)]

                A("pool", lambda e: e.dma_start(out=watt[:], in_=w_attn.rearrange("(k p) c -> p k c", p=128)), writes=["watt"], dma_key="watt")
                A("pool", lambda e: e.dma_start(out=wout[:], in_=w_out.rearrange("(k p) c -> p k c", p=128)), writes=["wout"], dma_key="wout")
                for gi, gsrc in enumerate((norm_mix_post, norm_ffn_pre)):
                    A("sp", lambda e, gi=gi, gsrc=gsrc: e.dma_start(out=gns[:, gi, :], in_=gsrc.partition_broadcast(128)), writes=["gns"], dma_key="gns")

                def rms_scale(src_ap, ss_col, rd, jk, s8, sfx):
                    A("act", lambda e: e.activation(jk[:], src_ap, AF.Square, accum_out=s8[:, ss_col:ss_col + 1]), reads=rd, writes=["jk" + sfx, ("s8" + sfx, ss_col)])
                    A("act", lambda e: e.activation(s8[:, ss_col + 1:ss_col + 2], s8[:, ss_col:ss_col + 1], AF.Sqrt, bias=EPS, scale=1.0 / D),
                      reads=[("s8" + sfx, ss_col)], writes=[("s8" + sfx, ss_col + 1)])
                    A("dve", lambda e: e.reciprocal(s8[:, ss_col + 1:ss_col + 2], s8[:, ss_col + 1:ss_col + 2]), reads=[("s8" + sfx, ss_col + 1)], writes=[("s8" + sfx, ss_col + 1)])

                def p6a_loads(t):
                    c4 = t % 4
                    rows = slice(t * 128, (t + 1) * 128)
                    for g3 in range(3):
                        A("sp", lambda e, g3=g3: e.dma_start(out=o3[c4][:, g3, :, :], in_=scr_o[g3, rows, :, :]),
                          reads=[("scr_o", g3)], writes=[f"o3{c4}"], dma_key=f"o3{c4}")
                    A("sp", lambda e: e.dma_start(out=gat[c4][:], in_=scr_gate[rows, 1024:2048]), reads=[("scr_gate", t)], writes=[f"gat{c4}"], dma_key=f"gat{c4}")
                    A("sp", lambda e: e.dma_start(out=m1c[c4][:], in_=scr_m1[rows, :]), reads=[("scr_m1", t)], writes=[f"m1c{c4}"], dma_key=f"m1c{c4}")
                    A("sp", lambda e: e.dma_start(out=xin[c4][:], in_=x_in[seq, rows, :]), writes=[f"xin{c4}"], dma_key=f"xin{c4}")

                def p6a_stage(t, stage):
                    b = t % 2
                    c4 = t % 4
                    rows = slice(t * 128, (t + 1) * 128)
                    osum, rl, Ob, OT, mrg, mrgT, tmpf, jk, s8 = osum_[c4], rl_[c4], Ob_[c4], OT_[c4], mrg_[c4], mrgT_[c4], tmpf_[c4], jk_[c4], s8_[c4]
                    sfx = f"_{c4}"
                    if stage == 1:
                        A("dve", lambda e: e.tensor_tensor(osum[:], o3[c4][:, 0, :, :], o3[c4][:, 1, :, :], ALU.add), reads=[f"o3{c4}"], writes=["osum" + sfx])
                        A("dve", lambda e: e.tensor_tensor(osum[:], osum[:], o3[c4][:, 2, :, :], ALU.add), reads=[f"o3{c4}", "osum" + sfx], writes=["osum" + sfx])
                        A("dve", lambda e: e.reciprocal(rl[:], osum[:, :, 64]), reads=["osum" + sfx], writes=["rl" + sfx])
                        A("dve", lambda e: e.tensor_tensor(Ob[:].rearrange("p (h c) -> p h c", h=8), osum[:, :, 0:64], rl[:].unsqueeze(2).to_broadcast([128, 8, 64]), ALU.mult),
                          reads=["osum" + sfx, "rl" + sfx], writes=["Ob" + sfx])
                        for k in range(4):
                            A("pe", lambda e, k=k: e.transpose(pT3[b][:, k * 128:(k + 1) * 128], Ob[:, k * 128:(k + 1) * 128], ident[:]), reads=["Ob" + sfx, "consts"], writes=[f"pT3{b}"])
                        A("act", lambda e: e.copy(OT[:], pT3[b][:, 0:512].rearrange("p (k t) -> p k t", k=4)), reads=[f"pT3{b}"], writes=["OT" + sfx])
                        for hf in range(2):
                            for k in range(4):
                                A("pe", lambda e, k=k, hf=hf: e.matmul(pM[b][:, hf * 512:(hf + 1) * 512], OT[:, k, :], watt[:, k, hf * 512:(hf + 1) * 512], start=(k == 0), stop=(k == 3)),
                                  reads=["OT" + sfx, "watt"], writes=[f"pM{b}"])
                        A("dve", lambda e: e.tensor_tensor(tmpf[:], pM[b][:], gat[c4][:], ALU.mult), reads=[f"pM{b}", f"gat{c4}"], writes=["tmpf" + sfx])
                        A("pool", lambda e: e.tensor_tensor(mrg[:], tmpf[:], m1c[c4][:], ALU.add), reads=["tmpf" + sfx, f"m1c{c4}"], writes=["mrg" + sfx])
                    if stage == 2:
                        for k in range(8):
                            A("pe", lambda e, k=k: e.transpose(pT3[b][:, k * 128:(k + 1) * 128], mrg[:, k * 128:(k + 1) * 128], ident[:]), reads=["mrg" + sfx, "consts"], writes=[f"pT3{b}"])
                        A("act", lambda e: e.copy(mrgT[:], pT3[b][:].rearrange("p (k t) -> p k t", k=8)), reads=[f"pT3{b}"], writes=["mrgT" + sfx])
                        for hf in range(2):
                            for k in range(8):
                                A("pe", lambda e, k=k, hf=hf: e.matmul(pM[b][:, hf * 512:(hf + 1) * 512], mrgT[:, k, :], wout[:, k, hf * 512:(hf + 1) * 512], start=(k == 0), stop=(k == 7)),
                                  reads=["mrgT" + sfx, "wout"], writes=[f"pM{b}"])
                        rms_scale(pM[b][:], 0, [f"pM{b}"], jk, s8, sfx)
                        A("dve", lambda e: e.scalar_tensor_tensor(tmpf[:], pM[b][:], s8[:, 1:2], gns[:, 0, :], ALU.mult, ALU.mult), reads=[f"pM{b}", ("s8" + sfx, 1), "gns"], writes=["tmpf" + sfx])
                        A("pool", lambda e: e.tensor_tensor(x1[c4][:], tmpf[:], xin[c4][:], ALU.add), reads=["tmpf" + sfx, f"xin{c4}"], writes=[f"x1{c4}"])
                        A("act", lambda e: e.dma_start(out=y_out[seq, rows, :], in_=x1[c4][:]), reads=[f"x1{c4}"], writes=[("y", seq, t)], dma_key=f"x1{c4}")
                    if stage == 3:
                        rms_scale(x1[c4][:], 2, [f"x1{c4}"], jk, s8, sfx)
                        A("dve", lambda e: e.scalar_tensor_tensor(h2[c4][:], x1[c4][:], s8[:, 3:4], gns[:, 1, :], ALU.mult, ALU.mult), reads=[f"x1{c4}", ("s8" + sfx, 3), "gns"], writes=[f"h2{c4}"])
                        A("act", lambda e: e.dma_start(out=scr_h2[seq, rows, :], in_=h2[c4][:]), reads=[f"h2{c4}"], writes=[("scr_h2", seq, t)], dma_key=f"h2{c4}")


                p6a_loads(0)
                p6a_loads(1)
                import os as _os
                PIPE = _os.environ.get("P6A_PIPE", "1") == "1"
                for kk in range(NT + 2):
                    if kk + 2 < NT:
                        p6a_loads(kk + 2)
                    if PIPE:
                        if kk < NT:
                            p6a_stage(kk, 1)
                        if 0 <= kk - 1 < NT:
                            p6a_stage(kk - 1, 2)
                        if 0 <= kk - 2 < NT:
                            p6a_stage(kk - 2, 3)
                    elif kk < NT:
                        p6a_stage(kk, 1)
                        p6a_stage(kk, 2)
                        p6a_stage(kk, 3)
            S_.barrier()
        if stop_after > 5:
            with ExitStack() as st:
                def sb(name, shape, dt):
                    return st.enter_context(nc.sbuf_tensor(f"{name}_ffn", shape, dt))

                def ps(name, shape, dt):
                    return st.enter_context(nc.psum_tensor(f"{name}_ffn", shape, dt))

                TBLK = 512
                NTB = TBLK // 128
                wfi = sb("wfi", [128, 8, 2 * FFN], BF16)
                wdn = sb("wdn", [128, 22, 1024], BF16)
                gpo = sb("gpo", [128, 1024], F32)
                h2b = [sb(f"h2b{i}", [128, 1024], BF16) for i in range(2)]
                h2T = sb("h2T", [128, 8, TBLK], BF16)
                actT = sb("actT", [128, 22, TBLK], BF16)
                gT = [sb(f"gT{i}", [128, TBLK], BF16) for i in range(2)]
                x1b = [sb(f"x1b{i}", [128, 1024], F32) for i in range(2)]
                tmpf = sb("tmpf", [128, 1024], F32)
                jk = sb("jk", [128, 1024], F32)
                s8 = sb("s8", [128, 8], F32)
                yo = [sb(f"yo{i}", [128, 1024], F32) for i in range(2)]
                pT3 = ps("pT3", [128, 1024], BF16)
                pM = ps("pM", [128, 1024], F32)
                pG = [ps(f"pG{i}", [128, 512], F32) for i in range(2)]
                pU = [ps(f"pU{i}", [128, 512], F32) for i in range(2)]
                for c4 in range(11):
                    A("pool", lambda e, c4=c4: e.dma_start(out=wfi[:, :, c4 * 512:(c4 + 1) * 512],
                                                           in_=w_ffn_in[:, c4 * 512:(c4 + 1) * 512].rearrange("(k p) c -> p k c", p=128)),
                      writes=[("wfi", c4)], dma_key="wfi")
                for q2 in range(2):
                    A("pool", lambda e, q2=q2: e.dma_start(out=wdn[:, q2 * 11:(q2 + 1) * 11, :],
                                                           in_=w_ffn_down[q2 * 1408:(q2 + 1) * 1408, :].rearrange("(k p) c -> p k c", p=128)),
                      writes=["wdn"], dma_key="wdn")
                A("sp", lambda e: e.dma_start(out=gpo[:], in_=norm_ffn_post.partition_broadcast(128)), writes=["gpo"], dma_key="gpo")
                wfi_all = [("wfi", c4) for c4 in range(11)]
                cntt = 0
                for seq in range(NSEQ):
                    for blk in range(S // TBLK):
                        for tt in range(NTB):
                            t = blk * NTB + tt
                            b = cntt % 2
                            cntt += 1
                            rows = slice(t * 128, (t + 1) * 128)
                            A("sp", lambda e: e.dma_start(out=h2b[b][:], in_=scr_h2[seq, rows, :]), reads=[("scr_h2", seq, t)], writes=[f"h2b{b}"], dma_key=f"h2b{b}")
                            for k in range(8):
                                A("pe", lambda e, k=k: e.transpose(pT3[:, k * 128:(k + 1) * 128], h2b[b][:, k * 128:(k + 1) * 128], ident[:]), reads=[f"h2b{b}", "consts"], writes=["pT3"])
                            A("act", lambda e: e.copy(h2T[:, :, tt * 128:(tt + 1) * 128], pT3[:].rearrange("p (k t) -> p k t", k=8)), reads=["pT3"], writes=[("h2T", tt)])
                        h2T_all = [("h2T", tt) for tt in range(NTB)]
                        for f in range(22):
                            pb = f % 2
                            for k in range(8):
                                A("pe", lambda e, k=k: e.matmul(pG[pb][:], wfi[:, k, f * 128:(f + 1) * 128], h2T[:, k, :], start=(k == 0), stop=(k == 7)),
                                  reads=h2T_all + wfi_all, writes=[f"pG{pb}"])
                            for k in range(8):
                                A("pe", lambda e, k=k: e.matmul(pU[pb][:], wfi[:, k, FFN + f * 128:FFN + (f + 1) * 128], h2T[:, k, :], start=(k == 0), stop=(k == 7)),
                                  reads=h2T_all + wfi_all, writes=[f"pU{pb}"])
                            A("act", lambda e: e.activation(gT[pb][:], pG[pb][:], AF.Silu), reads=[f"pG{pb}"], writes=[f"gT{pb}"])
                            A("dve", lambda e: e.tensor_tensor(actT[:, f, :], pU[pb][:], gT[pb][:], ALU.mult),
                              reads=[f"pU{pb}", f"gT{pb}"], writes=[("actT", f)])
                        act_all = [("actT", f) for f in range(22)]
                        for tt in range(NTB):
                            t = blk * NTB + tt
                            b = (cntt + tt) % 2
                            rows = slice(t * 128, (t + 1) * 128)
                            A("sp", lambda e: e.dma_start(out=x1b[b][:], in_=y_out[seq, rows, :]), reads=[("y", seq, t)], writes=[f"x1b{b}"], dma_key=f"x1b{b}")
                            for f in range(22):
                                for hf in range(2):
                                    A("pe", lambda e, f=f, hf=hf: e.matmul(pM[:, hf * 512:(hf + 1) * 512], actT[:, f, tt * 128:(tt + 1) * 128], wdn[:, f, hf * 512:(hf + 1) * 512],
                                                                      start=(f == 0), stop=(f == 21)),
                                      reads=act_all + ["wdn"], writes=["pM"])
                            A("act", lambda e: e.activation(jk[:], pM[:], AF.Square, accum_out=s8[:, 0:1]), reads=["pM"], writes=["jk", "ss"])
                            A("act", lambda e: e.activation(s8[:, 1:2], s8[:, 0:1], AF.Sqrt, bias=EPS, scale=1.0 / D), reads=["ss"], writes=["sd"])
                            A("dve", lambda e: e.reciprocal(s8[:, 1:2], s8[:, 1:2]), reads=["sd"], writes=["sd"])
                            A("dve", lambda e: e.scalar_tensor_tensor(tmpf[:], pM[:], s8[:, 1:2], gpo[:], ALU.mult, ALU.mult), reads=["pM", "sd", "gpo"], writes=["tmpf"])
                            A("pool", lambda e: e.tensor_tensor(yo[b][:], tmpf[:], x1b[b][:], ALU.add), reads=["tmpf", f"x1b{b}"], writes=[f"yo{b}"])
                            A("act", lambda e: e.dma_start(out=y_out[seq, rows, :], in_=yo[b][:]), reads=[f"yo{b}"], writes=[("y", seq, t)], dma_key=f"yo{b}")
            S_.barrier()
        S_.emit(final_waits=list(S_.dma_count.keys()))
    nc._marks = getattr(S_, 'marks', [])
    return nc


def _host_consts(inp_conv_w, inp_conv_b, inp_norm_w, S):
    p = np.arange(128)
    m = p % 64
    rot = (m < 16)
    h2 = (m >= 8) & rot
    fi = np.where(rot, m % 8, 0)
    invf = (500000.0 ** (-(2.0 * fi) / 16.0)).astype(np.float32)
    ang = np.arange(S, dtype=np.float32)[None, :] * invf[:, None]
    cos = np.where(rot[:, None], np.cos(ang), 1.0).astype(np.float32)
    sgn = np.where(rot, np.where(h2, 1.0, -1.0), 0.0).astype(np.float32)
    sin = (np.sin(ang) * sgn[:, None]).astype(np.float32)
    cw = np.ascontiguousarray(inp_conv_w.reshape(5, 32, 128).transpose(2, 1, 0).reshape(128, 160), dtype=np.float32)
    cb = np.ascontiguousarray(inp_conv_b.reshape(32, 128).T, dtype=np.float32)
    nw = np.ascontiguousarray(inp_norm_w.reshape(16, 128).T, dtype=np.float32)
    return {"rot_cos": cos, "rot_sin": sin, "cw_l": cw, "cb_l": cb, "nw_l": nw}


_NC_CACHE = {}


def kernel(**inputs):
    x = np.asarray(inputs["x"], dtype=np.float32)
    B, S, _ = x.shape
    n_cores = 8
    NSEQ = B // n_cores
    key = (S, NSEQ)
    if key not in _NC_CACHE:
        _NC_CACHE[key] = build(S, NSEQ)
    nc = _NC_CACHE[key]
    f = lambda k: np.ascontiguousarray(np.asarray(inputs[k], dtype=np.float32)[0])
    shared = {
        "norm_mix_pre": f("norm_mix_pre").reshape(1, D), "w_in": f("w_in"),
        "ssd_conv_w": f("ssd_conv_w"), "ssd_conv_b": f("ssd_conv_b").reshape(1, 4096),
        "ssd_dt_bias": f("ssd_dt_bias").reshape(1, 64), "ssd_A_log": f("ssd_A_log").reshape(1, 64),
        "ssd_D": f("ssd_D").reshape(1, 32), "ssd_norm_w": f("ssd_norm_w").reshape(1, DIN),
        "w_ssd_branch": f("w_ssd_branch"), "w_attn_branch": f("w_attn_branch"), "w_out": f("w_out"),
        "norm_mix_post": f("norm_mix_post").reshape(1, D), "norm_ffn_pre": f("norm_ffn_pre").reshape(1, D),
        "w_ffn_in": f("w_ffn_in"), "w_ffn_down": f("w_ffn_down"), "norm_ffn_post": f("norm_ffn_post").reshape(1, D),
    }
    shared.update(_host_consts(shared["ssd_conv_w"], shared["ssd_conv_b"], shared["ssd_norm_w"], S))
    in_maps = []
    big = ("w_in", "w_ssd_branch", "w_attn_branch", "w_out", "w_ffn_in", "w_ffn_down", "rot_cos", "rot_sin")
    for c in range(n_cores):
        m = dict(shared)
        for k in big:
            a = shared[k]
            m[k] = np.concatenate([a, np.full((1, a.shape[1]), float(c), np.float32)], axis=0)
        m["x"] = np.ascontiguousarray(x[c * NSEQ:(c + 1) * NSEQ])
        in_maps.append(m)
    res = run_bass_kernel_spmd(nc, in_maps, core_ids=list(range(n_cores)))
    return np.concatenate([np.asarray(r["y"], dtype=np.float32) for r in res.results], axis=0)
```
